# Optimizing a Trainium2 kernel written in Bass

```python
import math
import jax, jax.numpy as jnp
from jax import lax
import numpy as np

D_MODEL = 2048
BATCH = 2
SEQ = 8192
DEPTH = 2
DEC_BATCH = 32
DEC_SEQ = 16
PAST_LEN = 1024

CHUNK = 64
EPS = 1e-6
R_HEADS = 8
R_DK = 128
R_DV = 256
R_QK = R_HEADS * R_DK
R_VAL = R_HEADS * R_DV
ROPE_BASE = 10000.0
S5_GROUP = 16
S5_WIDTH = D_MODEL
S5_GROUPS = S5_WIDTH // S5_GROUP
S5_STATE = 64
SSD_INNER = 2 * D_MODEL
SSD_HEADDIM = 64
SSD_HEADS = SSD_INNER // SSD_HEADDIM
SSD_GROUPS = 8
SSD_HPG = SSD_HEADS // SSD_GROUPS
SSD_STATE = 128
SSD_CONV_W = 4
SSD_CONV_DIM = SSD_INNER + 2 * SSD_GROUPS * SSD_STATE
MEM_LEN = 256
X_HEADS = 4
X_HEAD_DIM = D_MODEL // X_HEADS
D_FF = 4 * D_MODEL
N_BRANCH = 3
N_NORMS = 7
IN_SIZES = (R_QK, R_QK, R_VAL, R_VAL, S5_WIDTH, SSD_INNER, SSD_CONV_DIM, SSD_HEADS, N_BRANCH * D_MODEL)
IN_COLS = sum(IN_SIZES)
DT_MIN = 0.001
DT_MAX = 0.1

kernel_name = "hybrid_streaming_encoder_step"


def rmsnorm(x, g):
    xf = x.astype(jnp.float32)
    y = xf * lax.rsqrt(jnp.mean(xf * xf, axis=-1, keepdims=True) + EPS)
    return (y * g.astype(jnp.float32)).astype(x.dtype)


def to_chunks(t, n):
    b, l = t.shape[:2]
    return jnp.swapaxes(t.reshape(b, l // n, n, *t.shape[2:]), 0, 1)


def from_chunks(t):
    t = jnp.swapaxes(t, 0, 1)
    return t.reshape(t.shape[0], t.shape[1] * t.shape[2], *t.shape[3:])


def split_columns(proj):
    offsets = np.cumsum(IN_SIZES)[:-1].tolist()
    return jnp.split(proj, offsets, axis=-1)


def rope(x, pos):
    half = x.shape[-1] // 2
    inv_freq = jnp.exp(-math.log(ROPE_BASE) * jnp.arange(half, dtype=jnp.float32) / half)
    ang = pos.astype(jnp.float32)[:, None] * inv_freq[None]
    cos = jnp.cos(ang)[None, :, None]
    sin = jnp.sin(ang)[None, :, None]
    x = x.astype(jnp.float32)
    x1, x2 = x[..., :half], x[..., half:]
    return jnp.concatenate([x1 * cos - x2 * sin, x2 * cos + x1 * sin], axis=-1)


def retention_chunk(s, q, k, v, log_gamma):
    L = q.shape[1]
    idx = jnp.arange(L, dtype=jnp.float32)
    lg = log_gamma[:, None]
    intra = jnp.exp(jnp.abs(idx[:, None] - idx[None, :])[None] * lg[:, :, None])
    decay_in = jnp.exp((idx + 1.0)[None] * lg).T
    decay_up = jnp.exp((L - 1.0 - idx)[None] * lg).T
    decay_all = jnp.exp(L * log_gamma)
    q = q.astype(jnp.float32)
    k = k.astype(jnp.float32)
    v = v.astype(jnp.float32)
    scores = jnp.einsum('bihd,bjhd->bhij', q, k) * intra[None]
    out = (jnp.einsum('bhij,bjhe->bihe', scores, v)
           + jnp.einsum('bihd,bhde->bihe', q, s) * decay_in[None, :, :, None])
    s = decay_all[:, None, None] * s + jnp.einsum('bjhd,bjhe->bhde', k * decay_up[None, :, :, None], v)
    return s, out


def retention_branch(q, k, v, g, s0, pos, gn_gain, w_o):
    b, l = q.shape[:2]
    q = rope(q.reshape(b, l, R_HEADS, R_DK), pos)
    k = rope(k.reshape(b, l, R_HEADS, R_DK), pos) * (R_DK ** -0.5)
    v = v.reshape(b, l, R_HEADS, R_DV)
    n = min(l, CHUNK)
    log_gamma = jnp.log1p(-jnp.exp2(-5.0 - jnp.arange(R_HEADS, dtype=jnp.float32)))

    def step(s, qkv):
        qc, kc, vc = qkv
        return retention_chunk(s, qc, kc, vc, log_gamma)

    s_new, o = lax.scan(step, s0.astype(jnp.float32), (to_chunks(q, n), to_chunks(k, n), to_chunks(v, n)))
    o = from_chunks(o)
    mu = jnp.mean(o, axis=-1, keepdims=True)
    var = jnp.mean(jnp.square(o - mu), axis=-1, keepdims=True)
    o = ((o - mu) * lax.rsqrt(var + EPS)).reshape(b, l, R_VAL) * gn_gain.astype(jnp.float32)
    o = jax.nn.silu(g.astype(jnp.float32)) * o
    return o.astype(g.dtype) @ w_o, s_new


def cplx_affine(e1, e2):
    a1r, a1i, b1r, b1i = e1
    a2r, a2i, b2r, b2i = e2
    return (a1r * a2r - a1i * a2i, a1r * a2i + a1i * a2r,
            a2r * b1r - a2i * b1i + b2r, a2r * b1i + a2i * b1r + b2i)


def s5_discretize(a_re, a_im, b_re, b_im, log_dt):
    a_re = a_re.astype(jnp.float32)
    a_im = a_im.astype(jnp.float32)
    b_re = b_re.astype(jnp.float32)
    b_im = b_im.astype(jnp.float32)
    dt = jnp.exp(log_dt.astype(jnp.float32))[:, None]
    mag = jnp.exp(dt * a_re)
    ar = mag * jnp.cos(dt * a_im)
    ai = mag * jnp.sin(dt * a_im)
    nr, ni = ar - 1.0, ai
    den = a_re * a_re + a_im * a_im
    cr = (nr * a_re + ni * a_im) / den
    ci = (ni * a_re - nr * a_im) / den
    bbr = cr[..., None] * b_re - ci[..., None] * b_im
    bbi = cr[..., None] * b_im + ci[..., None] * b_re
    return ar, ai, bbr, bbi


def s5_branch(u, h0r, h0i, a_re, a_im, b_re, b_im, c_re, c_im, d, log_dt, w_glu):
    b, l = u.shape[:2]
    ar, ai, bbr, bbi = s5_discretize(a_re, a_im, b_re, b_im, log_dt)
    c_re = c_re.astype(jnp.float32)
    c_im = c_im.astype(jnp.float32)
    uf = u.astype(jnp.float32)
    n = min(l, CHUNK)

    def step(carry, uc):
        hr, hi = carry
        br = jnp.einsum('blgi,gpi->blgp', uc, bbr)
        bi = jnp.einsum('blgi,gpi->blgp', uc, bbi)
        br = br.at[:, 0].add(ar * hr - ai * hi)
        bi = bi.at[:, 0].add(ar * hi + ai * hr)
        elems = (jnp.broadcast_to(ar, br.shape), jnp.broadcast_to(ai, br.shape), br, bi)
        _, _, xr, xi = lax.associative_scan(cplx_affine, elems, axis=1)
        y = jnp.einsum('blgp,gip->blgi', xr, c_re) - jnp.einsum('blgp,gip->blgi', xi, c_im)
        return (xr[:, -1], xi[:, -1]), y

    (hr, hi), y = lax.scan(step, (h0r.astype(jnp.float32), h0i.astype(jnp.float32)),
                           to_chunks(uf.reshape(b, l, S5_GROUPS, S5_GROUP), n))
    y = from_chunks(y).reshape(b, l, S5_WIDTH) + d.astype(jnp.float32) * uf
    y = jax.nn.gelu(y).astype(u.dtype)
    val, gate = jnp.split(y @ w_glu, 2, axis=-1)
    return val * jax.nn.sigmoid(gate), hr, hi


def ssd_chunk(h, x, bm, cm, dt, a):
    L = x.shape[1]
    cs = jnp.cumsum(dt * a, axis=1)
    causal = jnp.tril(jnp.ones((L, L), dtype=bool))
    seg = cs[:, :, None] - cs[:, None, :]
    decay = jnp.exp(jnp.where(causal[None, :, :, None, None], seg, -jnp.inf))
    xdt = x * dt[..., None]
    cb = jnp.einsum('btgn,bsgn->btsg', cm, bm)
    y = (jnp.einsum('btsg,btsgr,bsgrp->btgrp', cb, decay, xdt)
         + jnp.einsum('btgn,bgrpn->btgrp', cm, h) * jnp.exp(cs)[..., None])
    w = jnp.exp(cs[:, -1:] - cs)
    h = jnp.exp(cs[:, -1])[..., None, None] * h + jnp.einsum('bsgn,bsgr,bsgrp->bgrpn', bm, w, xdt)
    return h, y


def ssd_branch(z, xbc, dt_raw, h0, conv0, conv_w, conv_b, dt_bias, a_log, d, norm_g, w_out):
    b, l = z.shape[:2]
    xpad = jnp.concatenate([conv0.astype(xbc.dtype), xbc], axis=1)
    conv_new = xpad[:, -(SSD_CONV_W - 1):]
    xc = lax.conv_general_dilated(xpad, conv_w.astype(xbc.dtype)[:, None, :], window_strides=(1,),
                                  padding='VALID', dimension_numbers=('NWC', 'WIO', 'NWC'),
                                  feature_group_count=SSD_CONV_DIM)
    xc = jax.nn.silu(xc + conv_b.astype(xc.dtype))
    xs, bm, cm = jnp.split(xc, [SSD_INNER, SSD_INNER + SSD_GROUPS * SSD_STATE], axis=-1)
    xs = xs.astype(jnp.float32).reshape(b, l, SSD_GROUPS, SSD_HPG, SSD_HEADDIM)
    bm = bm.astype(jnp.float32).reshape(b, l, SSD_GROUPS, SSD_STATE)
    cm = cm.astype(jnp.float32).reshape(b, l, SSD_GROUPS, SSD_STATE)
    dt = jax.nn.softplus(dt_raw.astype(jnp.float32) + dt_bias.astype(jnp.float32))
    dt = dt.reshape(b, l, SSD_GROUPS, SSD_HPG)
    a = -jnp.exp(a_log.astype(jnp.float32)).reshape(SSD_GROUPS, SSD_HPG)
    n = min(l, CHUNK)

    def step(h, inp):
        xch, bch, cch, dtch = inp
        return ssd_chunk(h, xch, bch, cch, dtch, a)

    h_init = h0.astype(jnp.float32).reshape(b, SSD_GROUPS, SSD_HPG, SSD_HEADDIM, SSD_STATE)
    h_new, y = lax.scan(step, h_init, (to_chunks(xs, n), to_chunks(bm, n), to_chunks(cm, n), to_chunks(dt, n)))
    y = from_chunks(y) + d.astype(jnp.float32).reshape(SSD_GROUPS, SSD_HPG)[..., None] * xs
    y = y.reshape(b, l, SSD_INNER) * jax.nn.silu(z.astype(jnp.float32))
    yg = y.reshape(b, l, SSD_GROUPS, SSD_INNER // SSD_GROUPS)
    yg = yg * lax.rsqrt(jnp.mean(yg * yg, axis=-1, keepdims=True) + EPS)
    y = (yg.reshape(b, l, SSD_INNER) * norm_g.astype(jnp.float32)).astype(z.dtype)
    return y @ w_out, h_new.reshape(b, SSD_HEADS, SSD_HEADDIM, SSD_STATE), conv_new


def memory_kv(mem, g, w_xkv):
    b, m = mem.shape[:2]
    k, v = jnp.split(rmsnorm(mem, g) @ w_xkv, 2, axis=-1)
    return k.reshape(b, m, X_HEADS, X_HEAD_DIM), v.reshape(b, m, X_HEADS, X_HEAD_DIM)


def cross_attn(x, mem_k, mem_v, w_q, w_o):
    b, l, _ = x.shape
    q = (x @ w_q).reshape(b, l, X_HEADS, X_HEAD_DIM)
    s = jnp.einsum('blhd,bmhd->bhlm', q, mem_k.astype(q.dtype)).astype(jnp.float32) * (X_HEAD_DIM ** -0.5)
    p = jax.nn.softmax(s, axis=-1).astype(x.dtype)
    o = jnp.einsum('bhlm,bmhd->blhd', p, mem_v.astype(x.dtype))
    return o.reshape(b, l, D_MODEL) @ w_o


def run_layer(x, pos, mem_k, mem_v, ret_s, s5_r, s5_i, ssd_s, conv_s,
              gains, w_in, ret_gn, w_ret_o, s5_a_re, s5_a_im, s5_b_re, s5_b_im, s5_c_re, s5_c_im,
              s5_d, s5_log_dt, w_s5_glu, ssd_conv_w, ssd_conv_b, ssd_dt_bias, ssd_a_log, ssd_d, ssd_norm,
              w_ssd_out, w_mix_out, w_xq, w_xo, w_up, w_down):
    hn = rmsnorm(x, gains[0])
    q, k, v, g, u, z, xbc, dt_raw, gate_logits = split_columns(hn @ w_in)
    ret_out, ret_s = retention_branch(q, k, v, g, ret_s, pos, ret_gn, w_ret_o)
    s5_out, s5_r, s5_i = s5_branch(u, s5_r, s5_i, s5_a_re, s5_a_im, s5_b_re, s5_b_im,
                                   s5_c_re, s5_c_im, s5_d, s5_log_dt, w_s5_glu)
    ssd_out, ssd_s, conv_s = ssd_branch(z, xbc, dt_raw, ssd_s, conv_s, ssd_conv_w, ssd_conv_b,
                                        ssd_dt_bias, ssd_a_log, ssd_d, ssd_norm, w_ssd_out)
    g_ret, g_s5, g_ssd = jnp.split(jax.nn.sigmoid(gate_logits.astype(jnp.float32)).astype(x.dtype), N_BRANCH, axis=-1)
    merged = g_ret * ret_out + g_s5 * s5_out + g_ssd * ssd_out
    x = x + rmsnorm(merged @ w_mix_out, gains[1])
    x = x + rmsnorm(cross_attn(rmsnorm(x, gains[2]), mem_k, mem_v, w_xq, w_xo), gains[3])
    hn = rmsnorm(x, gains[4])
    x = x + rmsnorm(jnp.square(jax.nn.relu(hn @ w_up)) @ w_down, gains[5])
    return x, (ret_s, s5_r, s5_i, ssd_s, conv_s)


def setup_inputs(seed: int = 0) -> dict:
    key = jax.random.key(seed)
    keys = list(jax.random.split(key, 48))

    def nrm(shape, scale=1.0):
        return scale * jax.random.normal(keys.pop(), shape, jnp.float32)

    def unif(shape, lo, hi):
        return jax.random.uniform(keys.pop(), shape, jnp.float32, lo, hi)

    dt_ssd = jnp.exp(unif((DEPTH, SSD_HEADS), math.log(DT_MIN), math.log(DT_MAX)))
    s5_a_im = (math.pi * jnp.arange(S5_STATE, dtype=jnp.float32))[None, None] + nrm((DEPTH, S5_GROUPS, S5_STATE), 0.01)
    inp = {}
    inp['x_prompt'] = nrm((BATCH, SEQ, D_MODEL))
    inp['x_sample'] = nrm((DEC_BATCH, DEC_SEQ, D_MODEL))
    inp['mem_prompt'] = nrm((BATCH, MEM_LEN, D_MODEL))
    inp['state_ret'] = nrm((DEPTH, DEC_BATCH, R_HEADS, R_DK, R_DV), 0.5)
    inp['state_s5_re'] = nrm((DEPTH, DEC_BATCH, S5_GROUPS, S5_STATE), 0.1)
    inp['state_s5_im'] = nrm((DEPTH, DEC_BATCH, S5_GROUPS, S5_STATE), 0.1)
    inp['state_ssd'] = nrm((DEPTH, DEC_BATCH, SSD_HEADS, SSD_HEADDIM, SSD_STATE), 0.1)
    inp['cache_ssd_conv'] = nrm((DEPTH, DEC_BATCH, SSD_CONV_W - 1, SSD_CONV_DIM))
    inp['cache_mem_k'] = nrm((DEPTH, DEC_BATCH, MEM_LEN, X_HEADS, X_HEAD_DIM))
    inp['cache_mem_v'] = nrm((DEPTH, DEC_BATCH, MEM_LEN, X_HEADS, X_HEAD_DIM))
    inp['norm_gains'] = 1.0 + nrm((DEPTH, N_NORMS, D_MODEL), 0.02)
    inp['w_in'] = nrm((DEPTH, D_MODEL, IN_COLS), D_MODEL ** -0.5)
    inp['ret_gn'] = 1.0 + nrm((DEPTH, R_VAL), 0.02)
    inp['w_ret_o'] = nrm((DEPTH, R_VAL, D_MODEL), R_VAL ** -0.5)
    inp['s5_a_re'] = -0.5 + nrm((DEPTH, S5_GROUPS, S5_STATE), 0.01)
    inp['s5_a_im'] = s5_a_im
    inp['s5_b_re'] = nrm((DEPTH, S5_GROUPS, S5_STATE, S5_GROUP), (2 * S5_GROUP) ** -0.5)
    inp['s5_b_im'] = nrm((DEPTH, S5_GROUPS, S5_STATE, S5_GROUP), (2 * S5_GROUP) ** -0.5)
    inp['s5_c_re'] = nrm((DEPTH, S5_GROUPS, S5_GROUP, S5_STATE), S5_STATE ** -0.5)
    inp['s5_c_im'] = nrm((DEPTH, S5_GROUPS, S5_GROUP, S5_STATE), S5_STATE ** -0.5)
    inp['s5_d'] = nrm((DEPTH, S5_WIDTH))
    inp['s5_log_dt'] = unif((DEPTH, S5_GROUPS), math.log(DT_MIN), math.log(DT_MAX))
    inp['w_s5_glu'] = nrm((DEPTH, S5_WIDTH, 2 * D_MODEL), S5_WIDTH ** -0.5)
    inp['ssd_conv_w'] = nrm((DEPTH, SSD_CONV_W, SSD_CONV_DIM), SSD_CONV_W ** -0.5)
    inp['ssd_conv_b'] = nrm((DEPTH, SSD_CONV_DIM), 0.01)
    inp['ssd_dt_bias'] = dt_ssd + jnp.log(-jnp.expm1(-dt_ssd))
    inp['ssd_a_log'] = jnp.log(unif((DEPTH, SSD_HEADS), 1.0, 16.0))
    inp['ssd_d'] = 1.0 + nrm((DEPTH, SSD_HEADS), 0.02)
    inp['ssd_norm'] = 1.0 + nrm((DEPTH, SSD_INNER), 0.02)
    inp['w_ssd_out'] = nrm((DEPTH, SSD_INNER, D_MODEL), SSD_INNER ** -0.5)
    inp['w_mix_out'] = nrm((DEPTH, D_MODEL, D_MODEL), D_MODEL ** -0.5)
    inp['w_xq'] = nrm((DEPTH, D_MODEL, D_MODEL), D_MODEL ** -0.5)
    inp['w_xkv'] = nrm((DEPTH, D_MODEL, 2 * D_MODEL), D_MODEL ** -0.5)
    inp['w_xo'] = nrm((DEPTH, D_MODEL, D_MODEL), D_MODEL ** -0.5)
    inp['w_up'] = nrm((DEPTH, D_MODEL, D_FF), D_MODEL ** -0.5)
    inp['w_down'] = nrm((DEPTH, D_FF, D_MODEL), D_FF ** -0.5)
    return inp


def reference(x_prompt, x_sample, mem_prompt, state_ret, state_s5_re, state_s5_im, state_ssd,
              cache_ssd_conv, cache_mem_k, cache_mem_v, norm_gains, w_in, ret_gn, w_ret_o,
              s5_a_re, s5_a_im, s5_b_re, s5_b_im, s5_c_re, s5_c_im, s5_d, s5_log_dt, w_s5_glu,
              ssd_conv_w, ssd_conv_b, ssd_dt_bias, ssd_a_log, ssd_d, ssd_norm, w_ssd_out,
              w_mix_out, w_xq, w_xkv, w_xo, w_up, w_down):
    b, l = x_prompt.shape[:2]
    dl = x_sample.shape[1]
    pos_p = jnp.arange(l, dtype=jnp.int32)
    pos_s = PAST_LEN + jnp.arange(dl, dtype=jnp.int32)
    yp, ys = x_prompt, x_sample
    p_states, s_states, p_mk, p_mv = [], [], [], []
    for i in range(DEPTH):
        lw = (norm_gains[i], w_in[i], ret_gn[i], w_ret_o[i], s5_a_re[i], s5_a_im[i], s5_b_re[i], s5_b_im[i],
              s5_c_re[i], s5_c_im[i], s5_d[i], s5_log_dt[i], w_s5_glu[i], ssd_conv_w[i], ssd_conv_b[i],
              ssd_dt_bias[i], ssd_a_log[i], ssd_d[i], ssd_norm[i], w_ssd_out[i], w_mix_out[i],
              w_xq[i], w_xo[i], w_up[i], w_down[i])
        mk, mv = memory_kv(mem_prompt, norm_gains[i, 6], w_xkv[i])
        yp, st_p = run_layer(yp, pos_p, mk, mv,
                             jnp.zeros((b, R_HEADS, R_DK, R_DV), jnp.float32),
                             jnp.zeros((b, S5_GROUPS, S5_STATE), jnp.float32),
                             jnp.zeros((b, S5_GROUPS, S5_STATE), jnp.float32),
                             jnp.zeros((b, SSD_HEADS, SSD_HEADDIM, SSD_STATE), jnp.float32),
                             jnp.zeros((b, SSD_CONV_W - 1, SSD_CONV_DIM), x_prompt.dtype),
                             *lw)
        p_states.append(st_p)
        p_mk.append(mk)
        p_mv.append(mv)
        ys, st_s = run_layer(ys, pos_s, cache_mem_k[i], cache_mem_v[i], state_ret[i], state_s5_re[i],
                             state_s5_im[i], state_ssd[i], cache_ssd_conv[i], *lw)
        s_states.append(st_s)
    p_ret, p_s5_re, p_s5_im, p_ssd, p_conv = [jnp.stack(t) for t in zip(*p_states)]
    s_ret, s_s5_re, s_s5_im, s_ssd, s_conv = [jnp.stack(t) for t in zip(*s_states)]
    p_mem_k = jnp.stack(p_mk)
    p_mem_v = jnp.stack(p_mv)
    return (yp, ys, p_ret, p_s5_re, p_s5_im, p_ssd, p_conv, p_mem_k, p_mem_v,
            s_ret, s_s5_re, s_s5_im, s_ssd, s_conv)
```

```python
import math
import numpy as np
import concourse.bass as bass
import concourse.mybir as mybir
from concourse.bass_utils import run_bass_kernel_spmd

F32 = mybir.dt.float32
BF16 = mybir.dt.bfloat16
I32 = mybir.dt.int32
AF = mybir.ActivationFunctionType
ALU = mybir.AluOpType
AX = mybir.AxisListType


class Sched:
    EPOCH = 24000
    NDMA = 6

    def __init__(self, nc):
        self.nc = nc
        self.engs = ["pe", "act", "dve", "pool", "sp"]
        self.items = {e: [] for e in self.engs}
        self.count = {e: 0 for e in self.engs}
        self.sems = {}
        self.dsems = {}
        self.dcount = {e: 0 for e in self.engs}
        self.seen = {e: {} for e in self.engs}
        self.res = {}
        self._semctx = []
        self.dlast = {}

    def _sem(self, key):
        d = self.sems if key[0] == "E" else self.dsems
        if key not in d:
            cm = self.nc.semaphore("s_%s_%s_%d" % key)
            d[key] = cm.__enter__()
            self._semctx.append(cm)
        return d[key]

    def _deps(self, reads, writes):
        ev = []
        for r in reads:
            st = self.res.get(r)
            if st and st[0] is not None:
                ev.append(st[0])
        for w in writes:
            st = self.res.get(w)
            if st:
                if st[0] is not None:
                    ev.append(st[0])
                ev.extend(st[1])
        return ev

    def _waits(self, eng, events):
        best = {}
        for (k, v) in events:
            if best.get(k, 0) < v:
                best[k] = v
        out = []
        for k, v in best.items():
            if self.seen[eng].get(k, 0) < v:
                self.seen[eng][k] = v
                out.append((k, v))
        return out

    def _mark(self, event, reads, writes):
        for r in reads:
            st = self.res.setdefault(r, [None, []])
            st[1].append(event)
        for w in writes:
            self.res[w] = [event, []]

    def op(self, eng, fn, reads=(), writes=()):
        waits = self._waits(eng, self._deps(reads, writes))
        n = self.count[eng]
        epoch, idx = divmod(n, self.EPOCH)
        self.count[eng] = n + 1
        key = ("E", eng, epoch)
        self._sem(key)
        event = (key, idx + 1)
        self.items[eng].append((waits, fn, key, 1))
        self._mark(event, reads, writes)
        return event

    def dma(self, eng, fn, reads=(), writes=()):
        n = self.dcount[eng]
        self.dcount[eng] = n + 1
        k, rnd = n % self.NDMA, n // self.NDMA
        key = ("D", eng, k)
        self._sem(key)
        ev = self._deps(reads, writes)
        if rnd > 0:
            ev.append((key, 16 * rnd))
        waits = self._waits(eng, ev)
        event = (key, 16 * (rnd + 1))
        self.dlast[key] = 16 * (rnd + 1)
        self.items[eng].append((waits, fn, key, 16))
        self._mark(event, reads, writes)
        return event

    def wait_all(self, eng, keys):
        ev = []
        for kk in keys:
            st = self.res.get(kk)
            if st:
                if st[0] is not None:
                    ev.append(st[0])
                ev.extend(st[1])
        waits = self._waits(eng, ev)
        self.items[eng].append((waits, None, None, 0))

    def barrier(self):
        ev = []
        for f in self.engs:
            n = self.count[f]
            if n > 0:
                epoch, idx = divmod(n - 1, self.EPOCH)
                ev.append((("E", f, epoch), idx + 1))
        for k, v in self.dlast.items():
            ev.append((k, v))
        for e in self.engs:
            self.items[e].append((self._waits(e, list(ev)), None, None, 0))

    def emit(self):
        nc = self.nc
        sched = self

        def run(engname, engine):
            for waits, fn, key, inc in sched.items[engname]:
                for (k, v) in waits:
                    engine.wait_ge(sched._sem(k), v)
                if fn is not None:
                    ins = fn(engine)
                    ins.then_inc(sched._sem(key), inc)

        with nc.Block() as block:
            @block.tensor
            def _(e):
                run("pe", e)

            @block.scalar
            def _(e):
                run("act", e)

            @block.vector
            def _(e):
                run("dve", e)

            @block.gpsimd
            def _(e):
                run("pool", e)

            @block.sync
            def _(e):
                run("sp", e)
        self.items = {e: [] for e in self.engs}

    def close(self):
        for cm in reversed(self._semctx):
            cm.__exit__(None, None, None)
        self._semctx = []


D = 2048
NS = 4
SL = 16
PAST = 1024
EPS = 1e-6
RH, RDK, RDV = 8, 128, 256
INC = 24640
C_Q, C_K, C_V, C_G, C_U, C_Z, C_XBC, C_DT, C_GATE = 0, 1024, 2048, 4096, 6144, 8192, 12288, 18432, 18496
GAMMA = [1.0 - 2.0 ** (-5 - h) for h in range(RH)]
WSHAPES = {"w_in": (D, INC), "w_ret_o": (D, D), "w_s5_glu": (D, 2 * D), "w_ssd_out": (2 * D, D), "w_mix_out": (D, D),
           "w_xq": (D, D), "w_xkv": (D, 2 * D), "w_xo": (D, D), "w_up": (D, 4 * D), "w_down": (4 * D, D)}
SMALLW = {"norm_gains": (7, D), "ret_gn": (D,), "s5_a_re": (128, 64), "s5_a_im": (128, 64), "s5_b_re": (128, 64, 16),
          "s5_b_im": (128, 64, 16), "s5_c_re": (128, 16, 64), "s5_c_im": (128, 16, 64), "s5_d": (D,), "s5_log_dt": (128,),
          "ssd_conv_w": (4, 6144), "ssd_conv_b": (6144,), "ssd_dt_bias": (64,), "ssd_a_log": (64,), "ssd_d": (64,),
          "ssd_norm": (4096,)}


def host_consts(TP):
    NT = TP + NS * SL
    pos = np.concatenate([np.arange(TP)] + [PAST + np.arange(SL)] * NS).astype(np.float32)
    half = 64
    inv = np.exp(-math.log(10000.0) * np.arange(half, dtype=np.float32) / half).astype(np.float32)
    ang = pos[:, None] * inv[None]
    cos, sin = np.cos(ang).astype(np.float32), np.sin(ang).astype(np.float32)
    cs = np.zeros((NT, 2, RH, 2, 64), np.float32)
    sn = np.zeros((NT, 2, RH, 2, 64), np.float32)
    for w in range(2):
        sc = 1.0 if w == 0 else RDK ** -0.5
        cs[:, w, :, :, :] = (cos * sc)[:, None, None, :]
        sn[:, w, :, 0, :] = (-sin * sc)[:, None, :]
        sn[:, w, :, 1, :] = (sin * sc)[:, None, :]
    lg = np.log1p(-np.exp2(-5.0 - np.arange(RH))).astype(np.float64)
    idx = np.arange(64)
    mask = np.exp(np.abs(idx[:, None] - idx[None, :])[:, None, :] * lg[None, :, None]).astype(np.float32)
    din = np.exp((idx + 1.0)[None, :] * lg[:, None])[None].repeat(128, 0).astype(np.float32)
    dup64 = np.exp((63.0 - idx)[:, None] * lg[None, :]).astype(np.float32)
    dup16 = np.exp((15.0 - np.arange(16))[:, None] * lg[None, :]).astype(np.float32)
    tv = np.arange(64, dtype=np.float32)[None].repeat(128, 0)
    ident = np.eye(128, dtype=np.float32)
    tri = (idx[:, None] <= idx[None, :]).astype(np.float32)
    return {"rope_cs": cs.reshape(NT, 2048), "rope_sn": sn.reshape(NT, 2048), "ret_mask": mask, "ret_din": din,
            "ret_dup64": dup64, "ret_dup16": dup16, "tvec": tv, "ident_in": ident, "tri_in": tri}


CONST_SHAPES = lambda NT: {"rope_cs": (NT, 2048), "rope_sn": (NT, 2048), "ret_mask": (64, 8, 64), "ret_din": (128, 8, 64),
                           "ret_dup64": (64, 8), "ret_dup16": (16, 8), "tvec": (128, 64), "ident_in": (128, 128),
                           "tri_in": (64, 64)}


from contextlib import ExitStack


class Builder:
    def __init__(self, TP, dbg_out=()):
        self.TP = TP
        self.NT = NT = TP + NS * SL
        self.nc = nc = bass.Bass("TRN2", target_bir_lowering=False)
        self.S = Sched(nc)
        self.uid = 0
        di = lambda name, shape, dt=F32: nc.dram_tensor(name, list(shape), dt, kind="ExternalInput").ap()
        do = lambda name, shape, dt=F32: nc.dram_tensor(name, list(shape), dt, kind="ExternalOutput").ap()
        ds = lambda name, shape, dt: nc.dram_tensor(name, list(shape), dt, kind=("ExternalOutput" if name in dbg_out else "Internal")).ap()
        self.x_in = di("x_in", (NT, D))
        self.mem = di("mem_in", (256, D))
        self.st_ret = di("st_ret", (2, NS, RH, RDK, RDV))
        self.st_s5r = di("st_s5r", (2, NS, 128, 64))
        self.st_s5i = di("st_s5i", (2, NS, 128, 64))
        self.st_ssd = di("st_ssd", (2, NS, 64, 64, 128))
        self.st_conv = di("st_conv", (2, NS, 3, 6144))
        self.st_mk = di("st_mk", (2, NS, 256, D))
        self.st_mv = di("st_mv", (2, NS, 256, D))
        self.w = {k: di(k, (2,) + v) for k, v in WSHAPES.items()}
        self.sw = {k: di(k, (2,) + v) for k, v in SMALLW.items()}
        self.cst = {k: di(k, v) for k, v in CONST_SHAPES(NT).items()}
        self.y = do("y", (NT, D))
        self.o_ret = do("o_ret", (2, 1 + NS, RH, RDK, RDV))
        self.o_s5r = do("o_s5r", (2, 1 + NS, 128, 64))
        self.o_s5i = do("o_s5i", (2, 1 + NS, 128, 64))
        self.o_ssd = do("o_ssd", (2, 1 + NS, 64, 64, 128))
        self.o_conv = do("o_conv", (2, 1 + NS, 3, 6144))
        self.o_mk = do("o_mk", (2, 256, D))
        self.o_mv = do("o_mv", (2, 256, D))
        self.wb = {k: ds(k + "_bf", (2,) + v, BF16) for k, v in WSHAPES.items()}
        self.xres = ds("xres", (NT, D), F32)
        self.hn = ds("hn", (NT, D), BF16)
        self.pj_qkvg = ds("pj_qkvg", (NT, 6144), BF16)
        self.pj_u = ds("pj_u", (NT, 2048), BF16)
        self.pj_z = ds("pj_z", (NT, 4096), BF16)
        self.pj_gate = ds("pj_gate", (NT, 6144), BF16)
        self.dtf = ds("dtf", (NT, 64), F32)
        self.xpad = ds("xpad", (16 + NT + 3 * (1 + NS) + 64, 6144), BF16)
        self.ret_y = ds("ret_y", (NT, D), BF16)
        self.s5_y = ds("s5_y", (NT, D), BF16)
        self.ssd_y = ds("ssd_y", (NT, 2 * D), BF16)
        self.br_ret = ds("br_ret", (NT, D), F32)
        self.br_glu = ds("br_glu", (NT, 2 * D), F32)
        self.br_ssd = ds("br_ssd", (NT, D), F32)
        self.merged = ds("merged", (NT, D), BF16)
        self.lin_out = ds("lin_out", (NT, D), F32)
        self.q_bf = ds("q_bf", (NT, D), BF16)
        self.att = ds("att", (NT, D), BF16)
        self.hmlp = ds("hmlp", (NT, 4 * D), BF16)
        self.memn = ds("memn", (256, D), BF16)
        self.mkv_f = ds("mkv_f", (256, 2 * D), F32)
        self.kvp_bf = ds("kvp_bf", (256, 2 * D), BF16)
        self.kv_bf = ds("kv_bf", (1 + NS, 2, 256, D), BF16)
        self.seqs = [(0, TP, 16, 0)] + [(TP + SL * j, SL, 16 + TP + 3 + (SL + 3) * j, 1 + j) for j in range(NS)]

    def name(self, p):
        self.uid += 1
        return "%s%d" % (p, self.uid)

    def stage(self):
        b = self

        class St:
            def __init__(s):
                s.stack = ExitStack()

            def sb(s, shape, dt=F32, nm="t"):
                return s.stack.enter_context(b.nc.sbuf_tensor(b.name(nm), list(shape), dt))

            def ps(s, shape, dt=F32, nm="p"):
                return s.stack.enter_context(b.nc.psum_tensor(b.name(nm), list(shape), dt))

            def done(s):
                b.S.barrier()
                b.S.emit()
                s.stack.close()
        return St()

    def tiles(self):
        out, r = [], 0
        while r < self.NT:
            n = min(128, self.NT - r)
            out.append((r, n))
            r += n
        return out

    def chunks(self):
        out = []
        for si, (r0, ln, _, _) in enumerate(self.seqs):
            c = min(64, ln)
            for k in range(ln // c):
                out.append((si, r0 + k * c, c, k == 0, k == ln // c - 1))
        return out

    def load_row_bcast(self, st, ap_row, width, npart=128, dt=F32, eng="sp"):
        t = st.sb([npart, width], dt, "rb")
        key = self.name("rbk")
        self.S.dma(eng, lambda e: e.dma_start(out=t[:], in_=ap_row.to_broadcast([npart, width])), writes=[key])
        return t, key

    def cast_weights(self):
        S = self.S
        for k, (rows, cols) in WSHAPES.items():
            step = max(1, (4 << 20) // cols)
            for l in range(2):
                for r in range(0, rows, step):
                    rr = min(step, rows - r)
                    S.dma("pool", lambda e, k=k, l=l, r=r, rr=rr: e.dma_start(out=self.wb[k][l, r:r + rr, :], in_=self.w[k][l, r:r + rr, :]),
                          writes=[("dram", k + "_bf")])
        S.barrier()
        S.emit()

    def init_copy(self):
        S = self.S
        S.dma("sp", lambda e: e.dma_start(out=self.xres, in_=self.x_in), writes=["xres"])
        for l in range(2):
            pass
        S.barrier()
        S.emit()

    def rms_stage(self, src, dst, gain_row, nrows):
        S = self.S
        st = self.stage()
        g, gk = self.load_row_bcast(st, gain_row, D)
        xt = [st.sb([128, D], F32, "xt") for _ in range(2)]
        hb = [st.sb([128, D], BF16, "hb") for _ in range(2)]
        junk = st.sb([128, D], BF16, "junk")
        ss = [st.sb([128, 1], F32, "ss") for _ in range(2)]
        r, i = 0, 0
        while r < nrows:
            n = min(128, nrows - r)
            b = i % 2
            S.dma("sp", lambda e, r=r, n=n, b=b: e.dma_start(out=xt[b][:n, :], in_=src[r:r + n, :]), reads=[("dram", src.name)], writes=[("xt", b)])
            S.op("dve", lambda e, n=n, b=b: e.memset(ss[b][:n, :], 0.0), writes=[("ss", b)])
            S.op("act", lambda e, n=n, b=b: e.activation(junk[:n, :], xt[b][:n, :], AF.Square, accum_out=ss[b][:n, :]), reads=[("xt", b), ("ss", b)], writes=["junk", ("ss", b)])
            S.op("act", lambda e, n=n, b=b: e.activation(ss[b][:n, :], ss[b][:n, :], AF.Sqrt, bias=EPS, scale=1.0 / D), reads=[("ss", b)], writes=[("ss", b)])
            S.op("dve", lambda e, n=n, b=b: e.reciprocal(ss[b][:n, :], ss[b][:n, :]), reads=[("ss", b)], writes=[("ss", b)])
            S.op("dve", lambda e, n=n, b=b: e.scalar_tensor_tensor(hb[b][:n, :], xt[b][:n, :], ss[b][:n, 0:1], g[:n, :], ALU.mult, ALU.mult),
                 reads=[("xt", b), ("ss", b), gk], writes=[("hb", b)])
            S.dma("sp", lambda e, r=r, n=n, b=b: e.dma_start(out=dst[r:r + n, :], in_=hb[b][:n, :]), reads=[("hb", b)], writes=[("dram", dst.name)])
            r += n
            i += 1
        st.done()

    def linear(self, A, K, W, N, evac, nrows=None, colblocks=None):
        S = self.S
        nrows = self.NT if nrows is None else nrows
        KC = K // 128
        TBL = {2048: 1024, 4096: 512, 8192: 256}[K]
        st = self.stage()
        AT = st.sb([128, KC, TBL], BF16, "AT")
        WT = [st.sb([128, KC, 512], BF16, "WT") for _ in range(2)]
        PS = [st.ps([128, 512], F32, "lps") for _ in range(2)]
        self.lin_st = st
        if colblocks is None:
            colblocks = [(c, min(512, N - c)) for c in range(0, N, 512)]
        Wv = W.rearrange("(kc p) n -> p kc n", p=128)
        widx, pidx = 0, 0
        rb = 0
        while rb < nrows:
            nb = min(TBL, nrows - rb)
            if nrows - (rb + nb) < 128 and nrows - (rb + nb) > 0 and nb + (nrows - rb - nb) <= TBL + 64 and False:
                pass
            for kc in range(KC):
                S.dma("sp", lambda e, kc=kc, rb=rb, nb=nb: e.dma_start(out=AT[:, kc, :nb], in_=A[rb:rb + nb, kc * 128:(kc + 1) * 128], transpose=True),
                      reads=[("dram", A.name)], writes=[("AT", kc)])
            for (c0, cw) in colblocks:
                wbuf = widx % 2
                widx += 1
                S.dma("act", lambda e, c0=c0, cw=cw, wbuf=wbuf: e.dma_start(out=WT[wbuf][:, :, :cw], in_=Wv[:, :, c0:c0 + cw]),
                      reads=[("dram", W.name)], writes=[("WT", wbuf)])
                t0 = 0
                while t0 < nb:
                    n = min(128, nb - t0)
                    pb = pidx % 2
                    pidx += 1

                    def mm(e, t0=t0, n=n, cw=cw, wbuf=wbuf, pb=pb):
                        for kc in range(KC):
                            ins = e.matmul(PS[pb][:n, :cw], AT[:, kc, t0:t0 + n], WT[wbuf][:, kc, :cw], start=(kc == 0), stop=(kc == KC - 1))
                        return ins
                    S.op("pe", mm, reads=[("AT", kc) for kc in range(KC)] + [("WT", wbuf)], writes=[("lps", pb)])
                    evac(st, rb + t0, n, c0, cw, PS[pb], pb)
                    t0 += n
            rb += nb
        st.done()

    def evac_store(self, dst, dt, coloff=0, func=None):
        S = self.S
        bufs = {}

        def evac(st, r0, n, c0, cw, ps, pb):
            if "ob" not in bufs:
                bufs["ob"] = [st.sb([128, 512], dt, "ob") for _ in range(2)]
                bufs["i"] = 0
            ob = bufs["ob"][bufs["i"] % 2]
            okey = ("ob", bufs["i"] % 2)
            bufs["i"] += 1
            if func is None:
                S.op("act", lambda e: e.copy(ob[:n, :cw], ps[:n, :cw]), reads=[("lps", pb)], writes=[okey])
            else:
                func(st, n, cw, ps, pb, ob, okey)
            S.dma("sp", lambda e: e.dma_start(out=dst[r0:r0 + n, coloff + c0:coloff + c0 + cw], in_=ob[:n, :cw]), reads=[okey], writes=[("dram", dst.name)])
        return evac

    def segs(self, r0, n):
        out = []
        for (s0, ln, p0, _) in self.seqs:
            a, b = max(r0, s0), min(r0 + n, s0 + ln)
            if a < b:
                out.append((a - r0, b - a, p0 + 3 + (a - s0)))
        return out

    def evac_inproj(self):
        S = self.S
        bufs = {}

        def evac(st, r0, n, c0, cw, ps, pb):
            if "ob" not in bufs:
                bufs["ob"] = [st.sb([128, 512], BF16, "ob") for _ in range(2)]
                bufs["of"] = st.sb([128, 64], F32, "of")
                bufs["i"] = 0
            ob = bufs["ob"][bufs["i"] % 2]
            okey = ("ob", bufs["i"] % 2)
            bufs["i"] += 1
            S.op("act", lambda e: e.copy(ob[:n, :cw], ps[:n, :cw]), reads=[("lps", pb)], writes=[okey])
            tgt = None
            if c0 < C_U:
                tgt, lc = self.pj_qkvg, c0
            elif c0 < C_Z:
                tgt, lc = self.pj_u, c0 - C_U
            elif c0 < C_XBC:
                tgt, lc = self.pj_z, c0 - C_Z
            elif c0 >= C_GATE:
                tgt, lc = self.pj_gate, c0 - C_GATE
            if tgt is not None:
                S.dma("sp", lambda e: e.dma_start(out=tgt[r0:r0 + n, lc:lc + cw], in_=ob[:n, :cw]), reads=[okey], writes=[("dram", tgt.name)])
            if C_XBC <= c0 < C_DT:
                for (o, cnt, prow) in self.segs(r0, n):
                    S.dma("sp", lambda e, o=o, cnt=cnt, prow=prow: e.dma_start(out=self.xpad[prow:prow + cnt, c0 - C_XBC:c0 - C_XBC + cw], in_=ob[o:o + cnt, :cw]),
                          reads=[okey], writes=[("dram", "xpad")])
            if c0 == C_DT:
                of = bufs["of"]
                S.op("dve", lambda e: e.tensor_copy(of[:n, :cw], ps[:n, :cw]), reads=[("lps", pb)], writes=["of"])
                S.dma("sp", lambda e: e.dma_start(out=self.dtf[r0:r0 + n, :], in_=of[:n, :cw]), reads=["of"], writes=[("dram", "dtf")])
        return evac

    def retention(self, l):
        S, nc = self.S, self.nc
        st = self.stage()
        sb, ps = st.sb, st.ps
        mask = sb([64, 8, 64], F32); din = sb([128, 8, 64], F32); dup64 = sb([64, 8], F32); dup16 = sb([16, 8], F32)
        identb = sb([128, 128], BF16); identf = sb([128, 128], F32)
        S.dma("sp", lambda e: e.dma_start(out=mask[:], in_=self.cst["ret_mask"]), writes=["mask"])
        S.dma("sp", lambda e: e.dma_start(out=din[:], in_=self.cst["ret_din"]), writes=["din"])
        S.dma("sp", lambda e: e.dma_start(out=dup64[:], in_=self.cst["ret_dup64"]), writes=["dup64"])
        S.dma("sp", lambda e: e.dma_start(out=dup16[:], in_=self.cst["ret_dup16"]), writes=["dup16"])
        S.dma("sp", lambda e: e.dma_start(out=identf[:], in_=self.cst["ident_in"]), writes=["identf"])
        S.op("dve", lambda e: e.tensor_copy(identb[:], identf[:]), reads=["identf"], writes=["identb"])
        gn, gnk = self.load_row_bcast(st, self.sw["ret_gn"][l:l + 1, :], D, 64)
        qk = sb([64, 2048], BF16); vt = sb([64, 2048], BF16); gt = sb([64, 2048], BF16)
        cs = sb([64, 2048], F32); sn = sb([64, 2048], F32)
        t1 = sb([64, 2048], F32); t2 = sb([64, 2048], F32)
        qkr = sb([64, 2048], BF16); kd = sb([64, 8, 128], BF16)
        qkT = sb([128, 16, 64], BF16); qdT = sb([128, 8, 64], BF16)
        sm = sb([64, 8, 64], BF16)
        o_sb = sb([64, 8, 256], F32); osq = sb([64, 8, 256], F32)
        Sf = sb([128, 8, 256], F32); Sb = sb([128, 8, 256], BF16)
        s1 = sb([64, 8], F32); s2 = sb([64, 8], F32); mean = sb([64, 8], F32); msq = sb([64, 8], F32)
        sg = sb([64, 2048], F32); yb = sb([64, 2048], BF16)
        tq = ps([128, 16, 64], BF16); ps_s = ps([64, 8, 64], F32)
        ps_o = [ps([64, 2, 256], F32) for _ in range(2)]; ps_S = [ps([128, 2, 256], F32) for _ in range(2)]
        for (si, r0, L, first, last) in self.chunks():
            slot = self.seqs[si][3]
            if first:
                if slot == 0:
                    S.op("dve", lambda e: e.memset(Sf[:], 0.0), writes=["Sf"])
                else:
                    S.dma("sp", lambda e, slot=slot: e.dma_start(out=Sf[:], in_=self.st_ret[l, slot - 1].rearrange("h d e -> d h e")), writes=["Sf"])
                S.op("act", lambda e: e.copy(Sb[:], Sf[:]), reads=["Sf"], writes=["Sb"])
            S.dma("sp", lambda e, r0=r0, L=L: e.dma_start(out=qk[:L, :], in_=self.pj_qkvg[r0:r0 + L, C_Q:C_Q + 2048]), reads=[("dram", "pj_qkvg")], writes=["qk"])
            S.dma("sp", lambda e, r0=r0, L=L: e.dma_start(out=vt[:L, :], in_=self.pj_qkvg[r0:r0 + L, C_V:C_V + 2048]), reads=[("dram", "pj_qkvg")], writes=["vt"])
            S.dma("sp", lambda e, r0=r0, L=L: e.dma_start(out=gt[:L, :], in_=self.pj_qkvg[r0:r0 + L, C_G:C_G + 2048]), reads=[("dram", "pj_qkvg")], writes=["gt"])
            S.dma("act", lambda e, r0=r0, L=L: e.dma_start(out=cs[:L, :], in_=self.cst["rope_cs"][r0:r0 + L, :]), writes=["cs"])
            S.dma("act", lambda e, r0=r0, L=L: e.dma_start(out=sn[:L, :], in_=self.cst["rope_sn"][r0:r0 + L, :]), writes=["sn"])
            v4 = lambda t, L=L: t[:L, :].rearrange("p (a two d) -> p a two d", two=2, d=64)
            S.op("dve", lambda e, L=L: e.tensor_tensor(t1[:L, :], qk[:L, :], cs[:L, :], ALU.mult), reads=["qk", "cs"], writes=["t1"])
            S.op("pool", lambda e, L=L, v4=v4: e.tensor_tensor(v4(t2)[:, :, 0, :], v4(qk)[:, :, 1, :], v4(sn)[:, :, 0, :], ALU.mult), reads=["qk", "sn"], writes=["t2a"])
            S.op("pool", lambda e, L=L, v4=v4: e.tensor_tensor(v4(t2)[:, :, 1, :], v4(qk)[:, :, 0, :], v4(sn)[:, :, 1, :], ALU.mult), reads=["qk", "sn"], writes=["t2b"])
            S.op("dve", lambda e, L=L: e.tensor_tensor(qkr[:L, :], t1[:L, :], t2[:L, :], ALU.add), reads=["t1", "t2a", "t2b"], writes=["qkr"])
            dup = dup64 if L == 64 else dup16
            S.op("dve", lambda e, L=L, dup=dup: e.tensor_tensor(kd[:L], qkr[:L, 1024:2048].rearrange("p (h d) -> p h d", h=8),
                                                               dup[:L, :].unsqueeze(2).to_broadcast([L, 8, 128]), ALU.mult),
                 reads=["qkr", "dup64", "dup16"], writes=["kd"])

            def tr(e, L=L):
                for i in range(16):
                    ins = e.transpose(tq[:, i, :L], qkr[:L, i * 128:(i + 1) * 128], identb[:L, :L])
                return ins
            S.op("pe", tr, reads=["qkr", "identb"], writes=["tq"])
            S.op("act", lambda e, L=L: e.copy(qkT[:, :, :L], tq[:, :, :L]), reads=["tq"], writes=["qkT"])
            S.op("dve", lambda e, L=L: e.tensor_tensor(qdT[:, :, :L], qkT[:, 0:8, :L], din[:, :, :L], ALU.mult), reads=["qkT", "din"], writes=["qdT"])

            def sc(e, L=L):
                for h in range(8):
                    ins = e.matmul(ps_s[:L, h, :L], qkT[:, 8 + h, :L], qkT[:, h, :L], start=True, stop=True)
                return ins
            S.op("pe", sc, reads=["qkT"], writes=["ps_s"])
            S.op("dve", lambda e, L=L: e.tensor_tensor(sm[:L, :, :L], ps_s[:L, :, :L], mask[:L, :, :L], ALU.mult), reads=["ps_s", "mask"], writes=["sm"])
            for hp in range(4):
                pb = hp % 2

                def om(e, L=L, hp=hp, pb=pb):
                    for hh in range(2):
                        h = 2 * hp + hh
                        e.matmul(ps_o[pb][:L, hh, :], sm[:L, h, :L], vt[:L, h * 256:(h + 1) * 256], start=True, stop=False)
                        ins = e.matmul(ps_o[pb][:L, hh, :], qdT[:, h, :L], Sb[:, h, :], start=False, stop=True)
                    return ins
                S.op("pe", om, reads=["sm", "vt", "qdT", "Sb"], writes=[("ps_o", pb)])
                S.op("act", lambda e, L=L, hp=hp, pb=pb: e.copy(o_sb[:L, 2 * hp:2 * hp + 2, :], ps_o[pb][:L, :, :]), reads=[("ps_o", pb)], writes=[("o_sb", hp)])
            for hp in range(4):
                pb = hp % 2

                def sm_(e, L=L, hp=hp, pb=pb):
                    for hh in range(2):
                        h = 2 * hp + hh
                        ins = e.matmul(ps_S[pb][:, hh, :], kd[:L, h, :], vt[:L, h * 256:(h + 1) * 256], start=True, stop=True)
                    return ins
                S.op("pe", sm_, reads=["kd", "vt"], writes=[("ps_S", pb)])
                for hh in range(2):
                    h = 2 * hp + hh
                    S.op("dve", lambda e, h=h, hh=hh, pb=pb, L=L: e.scalar_tensor_tensor(Sf[:, h, :], Sf[:, h, :], float(GAMMA[h] ** L), ps_S[pb][:, hh, :], ALU.mult, ALU.add),
                         reads=[("ps_S", pb), "Sf"], writes=["Sf"])
            S.op("act", lambda e: e.copy(Sb[:], Sf[:]), reads=["Sf"], writes=["Sb"])
            if last:
                dst = self.o_ret[l, slot].rearrange("h d e -> d h e")
                S.dma("sp", lambda e, dst=dst: e.dma_start(out=dst, in_=Sf[:]), reads=["Sf"], writes=[("dram", "o_ret")])
            okeys = [("o_sb", hp) for hp in range(4)]
            S.op("dve", lambda e, L=L: e.tensor_reduce(s1[:L, :], o_sb[:L], AX.X, ALU.add), reads=okeys, writes=["s1"])
            S.op("act", lambda e, L=L: e.activation(osq[:L], o_sb[:L], AF.Square), reads=okeys, writes=["osq"])
            S.op("dve", lambda e, L=L: e.tensor_reduce(s2[:L, :], osq[:L], AX.X, ALU.add), reads=["osq"], writes=["s2"])
            S.op("dve", lambda e, L=L: e.tensor_scalar(mean[:L, :], s1[:L, :], 1.0 / 256, None, ALU.mult), reads=["s1"], writes=["mean"])
            S.op("dve", lambda e, L=L: e.tensor_tensor(msq[:L, :], mean[:L, :], mean[:L, :], ALU.mult), reads=["mean"], writes=["msq"])
            S.op("dve", lambda e, L=L: e.scalar_tensor_tensor(s2[:L, :], s2[:L, :], 1.0 / 256, msq[:L, :], ALU.mult, ALU.subtract), reads=["s2", "msq"], writes=["s2"])
            S.op("act", lambda e, L=L: e.activation(s2[:L, :], s2[:L, :], AF.Sqrt, bias=EPS, scale=1.0), reads=["s2"], writes=["s2"])
            S.op("dve", lambda e, L=L: e.reciprocal(s2[:L, :], s2[:L, :]), reads=["s2"], writes=["s2"])
            S.op("dve", lambda e, L=L: e.tensor_tensor(osq[:L], o_sb[:L], mean[:L, :].unsqueeze(2).to_broadcast([L, 8, 256]), ALU.subtract), reads=okeys + ["mean", "osq"], writes=["osq"])
            S.op("dve", lambda e, L=L: e.tensor_tensor(osq[:L], osq[:L], s2[:L, :].unsqueeze(2).to_broadcast([L, 8, 256]), ALU.mult), reads=["osq", "s2"], writes=["osq"])
            S.op("act", lambda e, L=L: e.activation(sg[:L, :], gt[:L, :], AF.Silu), reads=["gt"], writes=["sg"])
            S.op("pool", lambda e, L=L: e.tensor_tensor(sg[:L, :], sg[:L, :], gn[:L, :], ALU.mult), reads=["sg", gnk], writes=["sg"])
            S.op("dve", lambda e, L=L: e.tensor_tensor(yb[:L, :], osq[:L].rearrange("p h e -> p (h e)"), sg[:L, :], ALU.mult), reads=["osq", "sg"], writes=["yb"])
            S.dma("sp", lambda e, r0=r0, L=L: e.dma_start(out=self.ret_y[r0:r0 + L, :], in_=yb[:L, :]), reads=["yb"], writes=[("dram", "ret_y")])
        st.done()


    def trig(self, st, ang, cos_o, sin_o, shape, key_in, key_c, key_s):
        S = self.S
        ki = st.sb(shape, I32, "ki"); kf = st.sb(shape, F32, "kf"); s2 = st.sb(shape, F32, "s2"); s4 = st.sb(shape, F32, "s4")
        k = self.name("trg")
        S.op("dve", lambda e: e.tensor_scalar(ki[:], ang(), 1.0 / (2 * math.pi), None, ALU.mult), reads=[key_in], writes=[k + "ki"])
        S.op("dve", lambda e: e.tensor_copy(kf[:], ki[:]), reads=[k + "ki"], writes=[k + "kf"])
        S.op("dve", lambda e: e.scalar_tensor_tensor(kf[:], kf[:], -2 * math.pi, ang(), ALU.mult, ALU.add), reads=[k + "kf", key_in], writes=[k + "kf"])
        S.op("act", lambda e: e.activation(s2[:], kf[:], AF.Sin, scale=0.5), reads=[k + "kf"], writes=[k + "s2"])
        S.op("act", lambda e: e.activation(s4[:], kf[:], AF.Sin, scale=0.25), reads=[k + "kf"], writes=[k + "s4"])
        S.op("dve", lambda e: e.tensor_tensor(s4[:], s4[:], s4[:], ALU.mult), reads=[k + "s4"], writes=[k + "s4"])
        S.op("dve", lambda e: e.tensor_scalar(s4[:], s4[:], -2.0, 1.0, ALU.mult, ALU.add), reads=[k + "s4"], writes=[k + "s4"])
        S.op("dve", lambda e: e.scalar_tensor_tensor(sin_o(), s2[:], 2.0, s4[:], ALU.mult, ALU.mult), reads=[k + "s2", k + "s4"], writes=[key_s])
        S.op("dve", lambda e: e.tensor_tensor(s2[:], s2[:], s2[:], ALU.mult), reads=[k + "s2", key_s], writes=[k + "s2"])
        S.op("dve", lambda e: e.tensor_scalar(cos_o(), s2[:], -2.0, 1.0, ALU.mult, ALU.add), reads=[k + "s2"], writes=[key_c])

    def s5(self, l):
        S, nc = self.S, self.nc
        st = self.stage()
        sb, ps = st.sb, st.ps
        QH = 32
        identf = sb([128, 128], F32); identb = sb([128, 128], BF16)
        S.dma("sp", lambda e: e.dma_start(out=identf[:], in_=self.cst["ident_in"]), writes=["identf"])
        S.op("dve", lambda e: e.tensor_copy(identb[:], identf[:]), reads=["identf"], writes=["identb"])
        tv = sb([128, 64], F32)
        S.dma("sp", lambda e: e.dma_start(out=tv[:], in_=self.cst["tvec"]), writes=["tv"])
        COS = sb([128, 64, 64], F32); SIN = sb([128, 64, 64], F32); MT = sb([128, 64, 64], F32)
        MT16 = sb([128, 64, 16], F32)
        LB = sb([128, 64, 2, 128], BF16); CB = sb([128, 64, 2, 32], BF16)
        arT = sb([128, 64], F32); aiT = sb([128, 64], F32)
        pst = self.stage()
        psb = pst.sb
        are = psb([64, 128], F32); aim = psb([64, 128], F32); dtq = psb([64, 2], F32); dtE = psb([64, 128], F32)
        S.dma("sp", lambda e: e.dma_start(out=are[:], in_=self.sw["s5_a_re"][l].rearrange("(q g) p -> q (g p)", g=2)), writes=["are"])
        S.dma("sp", lambda e: e.dma_start(out=aim[:], in_=self.sw["s5_a_im"][l].rearrange("(q g) p -> q (g p)", g=2)), writes=["aim"])
        S.dma("sp", lambda e: e.dma_start(out=dtq[:], in_=self.sw["s5_log_dt"][l].rearrange("(q g) -> q g", g=2)), writes=["dtq"])
        S.op("act", lambda e: e.activation(dtq[:], dtq[:], AF.Exp), reads=["dtq"], writes=["dtq"])
        for g2 in range(2):
            S.op("dve", lambda e, g2=g2: e.tensor_copy(dtE[:, g2 * 64:(g2 + 1) * 64], dtq[:, g2:g2 + 1].to_broadcast([64, 64])), reads=["dtq"], writes=["dtE%d" % g2])
        dk = ["dtE0", "dtE1"]
        mag = psb([64, 128], F32); th = psb([64, 128], F32); cth = psb([64, 128], F32); sth = psb([64, 128], F32)
        S.op("dve", lambda e: e.tensor_tensor(mag[:], dtE[:], are[:], ALU.mult), reads=dk + ["are"], writes=["mag"])
        S.op("act", lambda e: e.activation(mag[:], mag[:], AF.Exp), reads=["mag"], writes=["mag"])
        S.op("dve", lambda e: e.tensor_tensor(th[:], dtE[:], aim[:], ALU.mult), reads=dk + ["aim"], writes=["th"])
        self.trig(pst, lambda: th[:], lambda: cth[:], lambda: sth[:], [64, 128], "th", "cth", "sth")
        ar = psb([64, 128], F32); ai = psb([64, 128], F32); nr = psb([64, 128], F32); den = psb([64, 128], F32)
        cr = psb([64, 128], F32); ci = psb([64, 128], F32); tmp = psb([64, 128], F32)
        S.op("dve", lambda e: e.tensor_tensor(ar[:], mag[:], cth[:], ALU.mult), reads=["mag", "cth"], writes=["ar"])
        S.op("dve", lambda e: e.tensor_tensor(ai[:], mag[:], sth[:], ALU.mult), reads=["mag", "sth"], writes=["ai"])
        S.op("dve", lambda e: e.tensor_scalar(nr[:], ar[:], -1.0, None, ALU.add), reads=["ar"], writes=["nr"])
        S.op("dve", lambda e: e.tensor_tensor(den[:], are[:], are[:], ALU.mult), reads=["are"], writes=["den"])
        S.op("dve", lambda e: e.tensor_tensor(tmp[:], aim[:], aim[:], ALU.mult), reads=["aim"], writes=["tmp"])
        S.op("dve", lambda e: e.tensor_tensor(den[:], den[:], tmp[:], ALU.add), reads=["den", "tmp"], writes=["den"])
        S.op("dve", lambda e: e.reciprocal(den[:], den[:]), reads=["den"], writes=["den"])
        S.op("dve", lambda e: e.tensor_tensor(cr[:], nr[:], are[:], ALU.mult), reads=["nr", "are"], writes=["cr"])
        S.op("dve", lambda e: e.tensor_tensor(tmp[:], ai[:], aim[:], ALU.mult), reads=["ai", "aim", "den"], writes=["tmp"])
        S.op("dve", lambda e: e.tensor_tensor(cr[:], cr[:], tmp[:], ALU.add), reads=["cr", "tmp"], writes=["cr"])
        S.op("dve", lambda e: e.tensor_tensor(cr[:], cr[:], den[:], ALU.mult), reads=["cr", "den"], writes=["cr"])
        S.op("dve", lambda e: e.tensor_tensor(ci[:], ai[:], are[:], ALU.mult), reads=["ai", "are"], writes=["ci"])
        S.op("dve", lambda e: e.tensor_tensor(tmp[:], nr[:], aim[:], ALU.mult), reads=["nr", "aim", "cr"], writes=["tmp"])
        S.op("dve", lambda e: e.tensor_tensor(ci[:], ci[:], tmp[:], ALU.subtract), reads=["ci", "tmp"], writes=["ci"])
        S.op("dve", lambda e: e.tensor_tensor(ci[:], ci[:], den[:], ALU.mult), reads=["ci", "den"], writes=["ci"])
        thT = psb([128, 64], F32); mT = psb([128, 64], F32); crT = psb([128, 64], F32); ciT = psb([128, 64], F32)
        ptr = pst.ps([128, 64], F32)
        for (src, sk, dst, dkk) in [(ar, "ar", arT, "arT"), (ai, "ai", aiT, "aiT"), (th, "th", thT, "thT"), (mag, "mag", mT, "mT"), (cr, "cr", crT, "crT"), (ci, "ci", ciT, "ciT")]:
            S.op("pe", lambda e, src=src: e.matmul(ptr[:], src[:], identf[:64, :64], start=True, stop=True), reads=[sk, "identf"], writes=["ptr"])
            S.op("act", lambda e, dst=dst: e.copy(dst[:], ptr[:]), reads=["ptr"], writes=[dkk])
        S.op("dve", lambda e: e.tensor_tensor(MT[:], thT[:].unsqueeze(2).to_broadcast([128, 64, 64]), tv[:].unsqueeze(1).to_broadcast([128, 64, 64]), ALU.mult), reads=["thT", "tv"], writes=["ANG"])
        tst = self.stage()
        self.trig(tst, lambda: MT[:], lambda: COS[:], lambda: SIN[:], [128, 64, 64], "ANG", "COS", "SIN")
        S.barrier(); S.emit(); tst.stack.close()
        S.op("dve", lambda e: e.tensor_copy(MT[:], mT[:].unsqueeze(2).to_broadcast([128, 64, 64])), reads=["mT", "COS", "SIN", "ANG"], writes=["MT"])
        S.op("dve", lambda e: e.memset(MT[:, :, 0:1], 0.0), reads=["MT"], writes=["MT"])
        S.op("dve", lambda e: e.tensor_copy(MT16[:], MT[:, :, 0:16]), reads=["MT"], writes=["MT"])
        Bn = [psb([128, 64, 16], F32, "Bn") for _ in range(2)]
        S.dma("sp", lambda e: e.dma_start(out=Bn[0][:], in_=self.sw["s5_b_re"][l].rearrange("(q g) p j -> (g p) q j", g=2)), writes=["Bn0"])
        S.dma("sp", lambda e: e.dma_start(out=Bn[1][:], in_=self.sw["s5_b_im"][l].rearrange("(q g) p j -> (g p) q j", g=2)), writes=["Bn1"])
        bb = [psb([128, 64, 16], F32, "bb") for _ in range(2)]
        t16 = psb([128, 64, 16], F32)
        bc = lambda t: t[:].unsqueeze(2).to_broadcast([128, 64, 16])
        S.op("dve", lambda e: e.tensor_tensor(bb[0][:], Bn[0][:], bc(crT), ALU.mult), reads=["Bn0", "crT"], writes=["bb0"])
        S.op("dve", lambda e: e.tensor_tensor(t16[:], Bn[1][:], bc(ciT), ALU.mult), reads=["Bn1", "ciT"], writes=["t16"])
        S.op("dve", lambda e: e.tensor_tensor(bb[0][:], bb[0][:], t16[:], ALU.subtract), reads=["bb0", "t16"], writes=["bb0"])
        S.op("dve", lambda e: e.tensor_tensor(bb[1][:], Bn[1][:], bc(crT), ALU.mult), reads=["Bn1", "crT"], writes=["bb1"])
        S.op("dve", lambda e: e.tensor_tensor(t16[:], Bn[0][:], bc(ciT), ALU.mult), reads=["Bn0", "ciT", "bb0"], writes=["t16"])
        S.op("dve", lambda e: e.tensor_tensor(bb[1][:], bb[1][:], t16[:], ALU.add), reads=["bb1", "t16"], writes=["bb1"])
        Z = psb([128, 64, 128], BF16)
        ptz = pst.ps([128, 8, 128], BF16)
        for ri in range(2):
            S.op("dve", lambda e: e.memset(Z[:], 0.0), reads=["Z"], writes=["Z"])
            Zv = Z[:].rearrange("p (c qq) (gl j) -> p c qq gl j", qq=4, j=16)
            bv = bb[ri][:].rearrange("p (c qq) j -> p c qq j", qq=4)
            for qq in range(4):
                for g2 in range(2):
                    S.op("dve", lambda e, qq=qq, g2=g2, Zv=Zv, bv=bv: e.tensor_copy(Zv[g2 * 64:(g2 + 1) * 64, :, qq, 2 * qq + g2, :], bv[g2 * 64:(g2 + 1) * 64, :, qq, :]),
                         reads=["bb%d" % ri, "Z"], writes=["Z"])
            for q8 in range(8):
                def trz(e, q8=q8):
                    for k in range(8):
                        ins = e.transpose(ptz[:, k, :], Z[:, q8 * 8 + k, :], identb[:])
                    return ins
                S.op("pe", trz, reads=["Z", "identb"], writes=["ptz"])
                S.op("act", lambda e, q8=q8, ri=ri: e.copy(LB[:, q8 * 8:(q8 + 1) * 8, ri, :], ptz[:]), reads=["ptz"], writes=["LB"])
        Cn = psb([64, 64, 64], F32); Y = psb([64, 64, 128], BF16)
        ptc = pst.ps([128, 8, 64], BF16)
        for ri, nm in enumerate(["s5_c_re", "s5_c_im"]):
            S.op("dve", lambda e: e.memset(Cn[:], 0.0), reads=["Cn"], writes=["Cn"])
            S.op("dve", lambda e: e.memset(Y[:], 0.0), reads=["Y"], writes=["Y"])
            cv = self.sw[nm][l].rearrange("(q g) i p -> g i q p", g=2)
            for g2 in range(2):
                S.dma("sp", lambda e, g2=g2, cv=cv: e.dma_start(out=Cn[g2 * 32:g2 * 32 + 16, :, :], in_=cv[g2]), reads=["Cn"], writes=["Cn"])
            for g2 in range(2):
                S.op("dve", lambda e, g2=g2, ri=ri: e.tensor_scalar(Y[g2 * 32:(g2 + 1) * 32, :, g2 * 64:(g2 + 1) * 64], Cn[g2 * 32:(g2 + 1) * 32, :, :], (1.0 if ri == 0 else -1.0), None, ALU.mult),
                     reads=["Cn", "Y"], writes=["Y"])
            for q8 in range(8):
                def trc(e, q8=q8):
                    for k in range(8):
                        ins = e.transpose(ptc[:, k, :], Y[:, q8 * 8 + k, :], identb[:64, :64])
                    return ins
                S.op("pe", trc, reads=["Y", "identb"], writes=["ptc"])
                S.op("act", lambda e, q8=q8, ri=ri: e.copy(CB[:, q8 * 8:(q8 + 1) * 8, ri, :].rearrange("p k (g i) -> p k g i", g=2),
                                                         ptc[:].rearrange("p k (g i) -> p k g i", g=2)[:, :, :, 0:16]), reads=["ptc"], writes=["CB"])
        S.barrier(); S.emit(); pst.stack.close()
        dT, dTk = self.load_row_bcast(st, self.sw["s5_d"][l:l + 1, :], D, 64)
        uT = sb([128, 16, 64], BF16); utok = sb([64, 2048], BF16)
        Braw = sb([128, QH, 2, 64], F32); BR = sb([128, QH, 64], F32); BI = sb([128, QH, 64], F32); tmpb = sb([128, QH, 64], F32)
        XR = sb([128, QH, 64], BF16); XI = sb([128, QH, 64], BF16)
        xpr = sb([128, 64], F32); xpi = sb([128, 64], F32); fr = sb([128, 64], F32); fi = sb([128, 64], F32); f2 = sb([128, 64], F32)
        yt = sb([64, 2048], F32); yb = sb([64, 2048], BF16)
        xo = sb([64, 128], F32)
        pb_ = [ps([128, 4, 2, 64], F32) for _ in range(2)]
        py = ps([64, 2048], F32)
        pxo = ps([64, 128], F32)
        for (si, r0, n, first, last) in self.chunks():
            slot = self.seqs[si][3]
            if first:
                if slot == 0:
                    S.op("dve", lambda e: e.memset(xpr[:], 0.0), writes=["xpr"])
                    S.op("dve", lambda e: e.memset(xpi[:], 0.0), writes=["xpi"])
                else:
                    for (srcst, dstt, kk) in [(self.st_s5r, xpr, "xpr"), (self.st_s5i, xpi, "xpi")]:
                        S.dma("sp", lambda e, srcst=srcst, slot=slot: e.dma_start(out=xo[:, :], in_=srcst[l, slot - 1].rearrange("(q g) p -> q (g p)", g=2)), reads=["xo"], writes=["xo"])
                        S.op("pe", lambda e: e.matmul(pb_[0][:, 0, 0, :], xo[:, :], identf[:64, :64], start=True, stop=True), reads=["xo", "identf"], writes=[("pb", 0)])
                        S.op("act", lambda e, dstt=dstt: e.copy(dstt[:], pb_[0][:, 0, 0, :]), reads=[("pb", 0)], writes=[kk])
            for c in range(16):
                S.dma("sp", lambda e, c=c, r0=r0, n=n: e.dma_start(out=uT[:, c, :n], in_=self.pj_u[r0:r0 + n, c * 128:(c + 1) * 128], transpose=True),
                      reads=[("dram", "pj_u")], writes=[("uT", c)])
            S.dma("act", lambda e, r0=r0, n=n: e.dma_start(out=utok[:n, :], in_=self.pj_u[r0:r0 + n, :]), reads=[("dram", "pj_u")], writes=["utok"])
            S.op("dve", lambda e: e.tensor_tensor(fr[:], arT[:], xpr[:], ALU.mult), reads=["arT", "xpr"], writes=["fr"])
            S.op("dve", lambda e: e.tensor_tensor(f2[:], aiT[:], xpi[:], ALU.mult), reads=["aiT", "xpi"], writes=["f2"])
            S.op("dve", lambda e: e.tensor_tensor(fr[:], fr[:], f2[:], ALU.subtract), reads=["fr", "f2"], writes=["fr"])
            S.op("dve", lambda e: e.tensor_tensor(fi[:], arT[:], xpi[:], ALU.mult), reads=["arT", "xpi"], writes=["fi"])
            S.op("dve", lambda e: e.tensor_tensor(f2[:], aiT[:], xpr[:], ALU.mult), reads=["aiT", "xpr", "fr"], writes=["f2"])
            S.op("dve", lambda e: e.tensor_tensor(fi[:], fi[:], f2[:], ALU.add), reads=["fi", "f2"], writes=["fi"])
            V = lambda t, n=n: t[:].rearrange("p q t -> p (q t)")[:, :QH * n].rearrange("p (q t) -> p q t", t=n)
            F2 = lambda t, n=n: t[:].rearrange("p q t -> p (q t)")[:, :QH * n]
            BRv, BIv, tmv = V(BR), V(BI), V(tmpb)
            for hf in range(2):
                q0 = hf * QH
                for c8 in range(8):
                    c = hf * 8 + c8
                    pbi = c % 2

                    def bm(e, c=c, pbi=pbi, n=n):
                        for qq in range(4):
                            for ri in range(2):
                                ins = e.matmul(pb_[pbi][:, qq, ri, :n], LB[:, 4 * c + qq, ri, :], uT[:, c, :n], start=True, stop=True)
                        return ins
                    S.op("pe", bm, reads=["LB", ("uT", c)], writes=[("pb", pbi)])
                    S.op("act", lambda e, c8=c8, pbi=pbi, n=n: e.copy(Braw[:, 4 * c8:4 * c8 + 4, :, :n], pb_[pbi][:, :, :, :n]), reads=[("pb", pbi)], writes=["Braw"])
                cosv = COS[:, q0:q0 + QH, :n]; sinv = SIN[:, q0:q0 + QH, :n]; mtv = MT[:, q0:q0 + QH, :n]
                br_, bi_ = Braw[:, :, 0, :n], Braw[:, :, 1, :n]
                S.op("dve", lambda e, cosv=cosv, br_=br_, n=n, BRv=BRv, BIv=BIv, tmv=tmv: e.tensor_tensor(BRv, br_, cosv, ALU.mult), reads=["Braw", "COS"], writes=["BR"])
                S.op("pool", lambda e, sinv=sinv, bi_=bi_, n=n, BRv=BRv, BIv=BIv, tmv=tmv: e.tensor_tensor(tmv, bi_, sinv, ALU.mult), reads=["Braw", "SIN"], writes=["tmpb"])
                S.op("dve", lambda e, n=n, BRv=BRv, BIv=BIv, tmv=tmv: e.tensor_tensor(BRv, BRv, tmv, ALU.add), reads=["BR", "tmpb"], writes=["BR"])
                S.op("dve", lambda e, cosv=cosv, bi_=bi_, n=n, BRv=BRv, BIv=BIv, tmv=tmv: e.tensor_tensor(BIv, bi_, cosv, ALU.mult), reads=["Braw", "COS"], writes=["BI"])
                S.op("pool", lambda e, sinv=sinv, br_=br_, n=n, BRv=BRv, BIv=BIv, tmv=tmv: e.tensor_tensor(tmv, br_, sinv, ALU.mult), reads=["Braw", "SIN", "BR"], writes=["tmpb"])
                S.op("dve", lambda e, n=n, BRv=BRv, BIv=BIv, tmv=tmv: e.tensor_tensor(BIv, BIv, tmv, ALU.subtract), reads=["BI", "tmpb"], writes=["BI"])
                S.op("dve", lambda e, q0=q0, BRv=BRv: e.tensor_tensor(BRv[:, :, 0], BRv[:, :, 0], fr[:, q0:q0 + QH], ALU.add), reads=["BR", "fr"], writes=["BR"])
                S.op("dve", lambda e, q0=q0, BIv=BIv: e.tensor_tensor(BIv[:, :, 0], BIv[:, :, 0], fi[:, q0:q0 + QH], ALU.add), reads=["BI", "fi"], writes=["BI"])
                mt2 = (MT if n == 64 else MT16)[:, q0:q0 + QH, :].rearrange("p q t -> p (q t)")
                S.op("dve", lambda e, mt2=mt2, F2=F2: e.tensor_tensor_scan(F2(BR), mt2, F2(BR), 0.0, ALU.mult, ALU.add), reads=["BR", "MT"], writes=["BR"])
                S.op("dve", lambda e, mt2=mt2, F2=F2: e.tensor_tensor_scan(F2(BI), mt2, F2(BI), 0.0, ALU.mult, ALU.add), reads=["BI", "MT"], writes=["BI"])
                TA, TB = Braw[:, :, 0, :n], Braw[:, :, 1, :n]
                S.op("dve", lambda e, TA=TA, cosv=cosv, n=n, BRv=BRv: e.tensor_tensor(TA, BRv, cosv, ALU.mult), reads=["BR", "COS", "Braw"], writes=["TA"])
                S.op("pool", lambda e, TB=TB, sinv=sinv, n=n, BIv=BIv: e.tensor_tensor(TB, BIv, sinv, ALU.mult), reads=["BI", "SIN", "Braw"], writes=["TB"])
                S.op("dve", lambda e, TA=TA, TB=TB, n=n: e.tensor_tensor(XR[:, :, :n], TA, TB, ALU.subtract), reads=["TA", "TB"], writes=["XR"])
                S.op("dve", lambda e, TA=TA, TB=TB, q0=q0, n=n: e.tensor_tensor(xpr[:, q0:q0 + QH], TA[:, :, n - 1], TB[:, :, n - 1], ALU.subtract), reads=["TA", "TB"], writes=["xpr"])
                S.op("dve", lambda e, TA=TA, sinv=sinv, n=n, BRv=BRv: e.tensor_tensor(TA, BRv, sinv, ALU.mult), reads=["BR", "SIN", "XR", "xpr"], writes=["TA"])
                S.op("pool", lambda e, TB=TB, cosv=cosv, n=n, BIv=BIv: e.tensor_tensor(TB, BIv, cosv, ALU.mult), reads=["BI", "COS", "XR", "xpr"], writes=["TB"])
                S.op("dve", lambda e, TA=TA, TB=TB, n=n: e.tensor_tensor(XI[:, :, :n], TA, TB, ALU.add), reads=["TA", "TB"], writes=["XI"])
                S.op("dve", lambda e, TA=TA, TB=TB, q0=q0, n=n: e.tensor_tensor(xpi[:, q0:q0 + QH], TA[:, :, n - 1], TB[:, :, n - 1], ALU.add), reads=["TA", "TB"], writes=["xpi"])

                def cm(e, q0=q0, n=n):
                    for q in range(QH):
                        e.matmul(py[:n, (q0 + q) * 32:(q0 + q + 1) * 32], XR[:, q, :n], CB[:, q0 + q, 0, :], start=True, stop=False)
                        ins = e.matmul(py[:n, (q0 + q) * 32:(q0 + q + 1) * 32], XI[:, q, :n], CB[:, q0 + q, 1, :], start=False, stop=True)
                    return ins
                S.op("pe", cm, reads=["XR", "XI", "CB"], writes=["py"])
                S.op("dve", lambda e: e.tensor_copy(Braw[:, 0, 0, 0:1], Braw[:, 0, 0, 0:1]), reads=[], writes=["Braw", "TA", "TB"])
            S.op("dve", lambda e, n=n: e.tensor_tensor(yt[:n, :], utok[:n, :], dT[:n, :], ALU.mult), reads=["utok", dTk], writes=["yt"])
            S.op("dve", lambda e, n=n: e.tensor_tensor(yt[:n, :], yt[:n, :], py[:n, :], ALU.add), reads=["yt", "py"], writes=["yt"])
            S.op("act", lambda e, n=n: e.activation(yb[:n, :], yt[:n, :], AF.Gelu), reads=["yt"], writes=["yb"])
            S.dma("sp", lambda e, r0=r0, n=n: e.dma_start(out=self.s5_y[r0:r0 + n, :], in_=yb[:n, :]), reads=["yb"], writes=[("dram", "s5_y")])
            if last:
                for (srct, dsto, kk) in [(xpr, self.o_s5r, "xpr"), (xpi, self.o_s5i, "xpi")]:
                    S.op("pe", lambda e, srct=srct: e.matmul(pxo[:], srct[:], identf[:], start=True, stop=True), reads=[kk, "identf"], writes=["pxo"])
                    S.op("act", lambda e: e.copy(xo[:], pxo[:]), reads=["pxo"], writes=["xo"])
                    S.dma("sp", lambda e, dsto=dsto, slot=slot: e.dma_start(out=dsto[l, slot].rearrange("(q g) p -> q (g p)", g=2), in_=xo[:]), reads=["xo"], writes=[("dram", dsto.name)])
        st.done()


    def ssd(self, l):
        S, nc = self.S, self.nc
        st = self.stage()
        sb, ps = st.sb, st.ps
        identf = sb([128, 128], F32); identb = sb([128, 128], BF16); tri = sb([64, 64], F32); ones = sb([64, 64], F32)
        sel = {64: sb([64, 128], F32), 16: sb([64, 128], F32)}
        S.dma("sp", lambda e: e.dma_start(out=identf[:], in_=self.cst["ident_in"]), writes=["identf"])
        S.op("dve", lambda e: e.tensor_copy(identb[:], identf[:]), reads=["identf"], writes=["identb"])
        S.dma("sp", lambda e: e.dma_start(out=tri[:], in_=self.cst["tri_in"]), writes=["tri"])
        S.op("dve", lambda e: e.memset(ones[:], 1.0), writes=["ones"])
        for LL in (64, 16):
            S.op("dve", lambda e, LL=LL: e.tensor_copy(sel[LL][:, :], identf[0:64, LL - 1:LL].to_broadcast([64, 128])), reads=["identf"], writes=["sel%d" % LL])
        cw = sb([128, 48, 4], F32); cb = sb([128, 48], F32)
        for w in range(4):
            S.dma("sp", lambda e, w=w: e.dma_start(out=cw[:, :, w], in_=self.sw["ssd_conv_w"][l, w].rearrange("(c p) -> p c", p=128), allow_slow_non_contiguous=True), writes=["cw%d" % w])
        S.dma("sp", lambda e: e.dma_start(out=cb[:], in_=self.sw["ssd_conv_b"][l].rearrange("(c p) -> p c", p=128), allow_slow_non_contiguous=True), writes=["cb"])
        cwk = ["cw%d" % w for w in range(4)]
        dtb, dtbk = self.load_row_bcast(st, self.sw["ssd_dt_bias"][l:l + 1, :], 64, 64)
        aneg, alk = self.load_row_bcast(st, self.sw["ssd_a_log"][l:l + 1, :], 64, 64)
        dsk, dskk = self.load_row_bcast(st, self.sw["ssd_d"][l:l + 1, :], 64, 64)
        ng, ngk = self.load_row_bcast(st, self.sw["ssd_norm"][l:l + 1, :], 4096, 64)
        S.op("act", lambda e: e.activation(aneg[:], aneg[:], AF.Exp), reads=[alk], writes=[alk])
        S.op("dve", lambda e: e.tensor_scalar(aneg[:], aneg[:], -1.0, None, ALU.mult), reads=[alk], writes=[alk])
        xT = sb([128, 48, 80], BF16); acc = sb([128, 48, 64], F32); ctmp = sb([128, 48, 64], F32); xcT = sb([128, 48, 64], BF16)
        xtok = sb([64, 4096], BF16); btok = sb([64, 1024], BF16); zt = sb([64, 4096], BF16)
        dtr = sb([64, 64], F32); dtx = sb([64, 64], F32); dta_ = sb([64, 64], F32); t64 = sb([64, 64], F32); dt = sb([64, 64], F32)
        cs_col = sb([64, 64], F32); ecs = sb([64, 64], F32); wdec = sb([64, 64], F32); csl = sb([128, 64], F32); ecl = sb([128, 64], F32)
        Rg = sb([64, 8, 64], F32); d1 = sb([64, 8, 64], F32); cbm = sb([64, 8, 64], F32); M = sb([64, 64, 64], BF16)
        xdt = sb([64, 4096], BF16); xdtw = sb([64, 4096], BF16)
        hT = sb([128, 4096], F32); hTb = sb([128, 4096], BF16)
        yv = sb([64, 4096], F32); ytmp = sb([64, 512], F32); sz = sb([64, 4096], BF16); ssq = sb([64, 8], F32); yb = sb([64, 4096], BF16)
        hio = sb([128, 4, 128], F32)
        p_tr = ps([64, 2048], BF16); p_small = ps([128, 64], F32); p_cs = ps([64, 8, 64], F32); p_cb = ps([64, 8, 64], F32)
        py = ps([64, 512], F32); pys = ps([64, 512], F32); ph = ps([128, 512], F32)
        v3 = lambda t, L: t[:L, :].rearrange("p (h q) -> p h q", q=64)
        for (si, r0, L, first, last) in self.chunks():
            r00, ln, p0, slot = self.seqs[si]
            prow = p0 + 3 + (r0 - r00)
            NR = L + 16
            if first:
                if slot == 0:
                    S.op("dve", lambda e: e.memset(hT[:], 0.0), writes=["hT"])
                else:
                    hv = self.st_ssd[l, slot - 1].rearrange("(k q) p n -> (q p) k n", q=2)
                    for k4 in range(8):
                        S.dma("sp", lambda e, k4=k4, hv=hv: e.dma_start(out=hio[:], in_=hv[:, 4 * k4:4 * k4 + 4, :]), reads=["hio"], writes=["hio"])

                        def trh(e):
                            for k in range(4):
                                ins = e.matmul(ph[:, k * 128:(k + 1) * 128], hio[:, k, :], identf[:], start=True, stop=True)
                            return ins
                        S.op("pe", trh, reads=["hio", "identf"], writes=["ph"])
                        S.op("act", lambda e, k4=k4: e.copy(hT[:, k4 * 512:(k4 + 1) * 512], ph[:]), reads=["ph"], writes=["hT"])
                S.op("act", lambda e: e.copy(hTb[:], hT[:]), reads=["hT"], writes=["hTb"])
            for c in range(48):
                q = "sp"
                S.dma(q, lambda e, c=c, prow=prow, NR=NR: e.dma_start(out=xT[:, c, :NR], in_=self.xpad[prow - 16:prow - 16 + NR, c * 128:(c + 1) * 128], transpose=True),
                      reads=[("dram", "xpad")], writes=[("xT", c)])
            S.dma("sp", lambda e, r0=r0, L=L: e.dma_start(out=zt[:L, :], in_=self.pj_z[r0:r0 + L, :]), reads=[("dram", "pj_z")], writes=["zt"])
            S.dma("sp", lambda e, r0=r0, L=L: e.dma_start(out=dtr[:L, :], in_=self.dtf[r0:r0 + L, :]), reads=[("dram", "dtf")], writes=["dtr"])
            xk = [("xT", c) for c in range(48)]
            bw = lambda w, L=L: cw[:, :, w:w + 1].to_broadcast([128, 48, L])
            S.op("dve", lambda e, L=L, bw=bw: e.tensor_tensor(acc[:, :, :L], xT[:, :, 13:13 + L], bw(0), ALU.mult), reads=xk + cwk, writes=["acc"])
            for w in range(1, 4):
                S.op("pool", lambda e, L=L, bw=bw, w=w: e.tensor_tensor(ctmp[:, :, :L], xT[:, :, 13 + w:13 + w + L], bw(w), ALU.mult), reads=xk + cwk, writes=["ctmp"])
                S.op("dve", lambda e, L=L: e.tensor_tensor(acc[:, :, :L], acc[:, :, :L], ctmp[:, :, :L], ALU.add), reads=["acc", "ctmp"], writes=["acc"])
            S.op("dve", lambda e, L=L: e.tensor_tensor(acc[:, :, :L], acc[:, :, :L], cb[:].unsqueeze(2).to_broadcast([128, 48, L]), ALU.add), reads=["acc", "cb"], writes=["acc"])
            S.op("act", lambda e, L=L: e.activation(xcT[:, :, :L], acc[:, :, :L], AF.Silu), reads=["acc"], writes=["xcT"])
            for rnd in range(2):
                def trx(e, rnd=rnd, L=L):
                    for k in range(16):
                        ins = e.transpose(p_tr[:L, k * 128:(k + 1) * 128], xcT[:, rnd * 16 + k, :L], identb[:])
                    return ins
                S.op("pe", trx, reads=["xcT", "identb"], writes=["p_tr"])
                S.op("act", lambda e, rnd=rnd, L=L: e.copy(xtok[:L, rnd * 2048:(rnd + 1) * 2048], p_tr[:L, :]), reads=["p_tr"], writes=[("xtok", rnd)])

            def trb(e, L=L):
                for k in range(8):
                    ins = e.transpose(p_tr[:L, k * 128:(k + 1) * 128], xcT[:, 32 + k, :L], identb[:])
                return ins
            S.op("pe", trb, reads=["xcT", "identb"], writes=["p_tr"])
            S.op("act", lambda e, L=L: e.copy(btok[:L, :], p_tr[:L, 0:1024]), reads=["p_tr"], writes=["btok"])
            xtk = [("xtok", 0), ("xtok", 1)]
            S.op("dve", lambda e, L=L: e.tensor_tensor(dtx[:L, :], dtr[:L, :], dtb[:L, :], ALU.add), reads=["dtr", dtbk], writes=["dtx"])
            S.op("act", lambda e, L=L: e.activation(t64[:L, :], dtx[:L, :], AF.Abs), reads=["dtx"], writes=["t64"])
            S.op("act", lambda e, L=L: e.activation(t64[:L, :], t64[:L, :], AF.Exp, scale=-1.0), reads=["t64"], writes=["t64"])
            S.op("act", lambda e, L=L: e.activation(t64[:L, :], t64[:L, :], AF.Ln, bias=1.0), reads=["t64"], writes=["t64"])
            S.op("dve", lambda e, L=L: e.tensor_scalar(dt[:L, :], dtx[:L, :], 0.0, None, ALU.max), reads=["dtx"], writes=["dt"])
            S.op("dve", lambda e, L=L: e.tensor_tensor(dt[:L, :], dt[:L, :], t64[:L, :], ALU.add), reads=["dt", "t64"], writes=["dt"])
            S.op("dve", lambda e, L=L: e.tensor_tensor(dta_[:L, :], dt[:L, :], aneg[:L, :], ALU.mult), reads=["dt", alk], writes=["dta"])
            S.op("pe", lambda e, L=L: e.matmul(p_small[:L, :], tri[:L, :L], dta_[:L, :], start=True, stop=True), reads=["tri", "dta"], writes=["p_small"])
            S.op("act", lambda e, L=L: e.copy(cs_col[:L, :], p_small[:L, :]), reads=["p_small"], writes=["cs_col"])
            S.op("act", lambda e, L=L: e.activation(ecs[:L, :], cs_col[:L, :], AF.Exp), reads=["cs_col"], writes=["ecs"])
            S.op("pe", lambda e, L=L: e.matmul(p_small[:, :], sel[L][:L, :], cs_col[:L, :], start=True, stop=True), reads=["sel%d" % L, "cs_col"], writes=["p_small"])
            S.op("act", lambda e: e.copy(csl[:], p_small[:]), reads=["p_small"], writes=["csl"])
            S.op("act", lambda e: e.activation(ecl[:], csl[:], AF.Exp), reads=["csl"], writes=["ecl"])
            S.op("dve", lambda e, L=L: e.tensor_tensor(wdec[:L, :], csl[:L, :], cs_col[:L, :], ALU.subtract), reads=["csl", "cs_col"], writes=["wdec"])
            S.op("act", lambda e, L=L: e.activation(wdec[:L, :], wdec[:L, :], AF.Exp), reads=["wdec"], writes=["wdec"])
            def cbf(e, L=L):
                for g in range(8):
                    ins = e.matmul(p_cb[:L, g, :L], xcT[:, 32 + g, :L], xcT[:, 40 + g, :L], start=True, stop=True)
                return ins
            S.op("pe", cbf, reads=["xcT"], writes=["p_cb"])
            S.op("dve", lambda e, L=L: e.tensor_tensor(cbm[:L, :, :L], p_cb[:L, :, :L], tri[:L, :L].unsqueeze(1).to_broadcast([L, 8, L]), ALU.mult), reads=["p_cb", "tri"], writes=["cbm"])
            S.op("dve", lambda e, L=L: e.tensor_tensor(v3(xdt, L), v3(xtok, L), dt[:L, :].unsqueeze(2).to_broadcast([L, 64, 64]), ALU.mult), reads=xtk + ["dt"], writes=["xdt"])
            S.op("pool", lambda e, L=L: e.tensor_tensor(v3(xdtw, L), v3(xdt, L), wdec[:L, :].unsqueeze(2).to_broadcast([L, 64, 64]), ALU.mult), reads=["xdt", "wdec"], writes=["xdtw"])
            for g in range(8):
                hs = slice(8 * g, 8 * g + 8)
                S.op("dve", lambda e, L=L, hs=hs: e.tensor_tensor(Rg[:L, :, :L], dta_[:L, hs].unsqueeze(2).to_broadcast([L, 8, L]), tri[:L, :L].unsqueeze(1).to_broadcast([L, 8, L]), ALU.mult),
                     reads=["dta", "tri"], writes=["Rg"])
                S.op("pe", lambda e, L=L: e.matmul(p_cs[:L, :, :L], ones[:L, :L], Rg[:L, :, :L], start=True, stop=True), reads=["ones", "Rg"], writes=["p_cs"])
                S.op("dve", lambda e, L=L, hs=hs: e.tensor_tensor(d1[:L, :, :L], p_cs[:L, :, :L], cs_col[:L, hs].unsqueeze(2).to_broadcast([L, 8, L]), ALU.subtract), reads=["p_cs", "cs_col"], writes=["d1"])
                S.op("dve", lambda e, L=L: e.tensor_scalar(d1[:L, :, :L], d1[:L, :, :L], 0.0, None, ALU.min), reads=["d1"], writes=["d1"])
                S.op("act", lambda e, L=L: e.activation(d1[:L, :, :L], d1[:L, :, :L], AF.Exp), reads=["d1"], writes=["d1"])
                S.op("dve", lambda e, L=L, g=g, hs=hs: e.tensor_tensor(M[:L, hs, :L], d1[:L, :, :L], cbm[:L, g:g + 1, :L].to_broadcast([L, 8, L]), ALU.mult), reads=["d1", "cbm"], writes=[("M", g)])

                def ym(e, L=L, g=g):
                    for r in range(8):
                        h = 8 * g + r
                        ins = e.matmul(py[:L, r * 64:(r + 1) * 64], M[:L, h, :L], xdt[:L, h * 64:(h + 1) * 64], start=True, stop=True)
                    return ins
                S.op("pe", ym, reads=[("M", g), "xdt"], writes=["py"])
                S.op("pe", lambda e, L=L, g=g: e.matmul(pys[:L, :], xcT[:, 40 + g, :L], hTb[:, g * 512:(g + 1) * 512], start=True, stop=True), reads=["xcT", "hTb"], writes=["pys"])
                S.op("dve", lambda e, L=L, hs=hs: e.tensor_tensor(ytmp[:L, :].rearrange("p (r q) -> p r q", q=64), pys[:L, :].rearrange("p (r q) -> p r q", q=64),
                                                              ecs[:L, hs].unsqueeze(2).to_broadcast([L, 8, 64]), ALU.mult), reads=["pys", "ecs"], writes=["ytmp"])
                S.op("dve", lambda e, L=L, g=g: e.tensor_tensor(yv[:L, g * 512:(g + 1) * 512], ytmp[:L, :], py[:L, :], ALU.add), reads=["ytmp", "py"], writes=[("yv", g)])
                S.op("pe", lambda e, L=L, g=g: e.matmul(ph[:, :], btok[:L, g * 128:(g + 1) * 128], xdtw[:L, g * 512:(g + 1) * 512], start=True, stop=True), reads=["btok", "xdtw"], writes=["ph"])
                hg = hT[:, g * 512:(g + 1) * 512]
                S.op("pool", lambda e, hg=hg, hs=hs: e.tensor_tensor(hg.rearrange("p (r q) -> p r q", q=64), hg.rearrange("p (r q) -> p r q", q=64),
                                                                    ecl[:, hs].unsqueeze(2).to_broadcast([128, 8, 64]), ALU.mult), reads=["hT", "ecl", "hTb"], writes=["hT"])
                S.op("dve", lambda e, hg=hg: e.tensor_tensor(hg, hg, ph[:, :], ALU.add), reads=["hT", "ph"], writes=["hT"])
            S.op("act", lambda e: e.copy(hTb[:], hT[:]), reads=["hT", "pys"], writes=["hTb"])
            yk = [("yv", g) for g in range(8)]
            S.op("pool", lambda e, L=L: e.tensor_tensor(v3(xdtw, L), v3(xtok, L), dsk[:L, :].unsqueeze(2).to_broadcast([L, 64, 64]), ALU.mult), reads=xtk + [dskk, "xdtw", "ph"], writes=["xdtw"])
            S.op("dve", lambda e, L=L: e.tensor_tensor(yv[:L, :], yv[:L, :], xdtw[:L, :], ALU.add), reads=yk + ["xdtw"], writes=["yv"])
            S.op("act", lambda e, L=L: e.activation(sz[:L, :], zt[:L, :], AF.Silu), reads=["zt"], writes=["sz"])
            S.op("dve", lambda e, L=L: e.tensor_tensor(yv[:L, :], yv[:L, :], sz[:L, :], ALU.mult), reads=["yv", "sz"], writes=["yv"])
            S.op("act", lambda e, L=L: e.activation(xdt[:L, :], yv[:L, :], AF.Square), reads=["yv", "xdt", "py"], writes=["xdt"])
            S.op("dve", lambda e, L=L: e.tensor_reduce(ssq[:L, :], xdt[:L, :].rearrange("p (g q) -> p g q", g=8), AX.X, ALU.add), reads=["xdt"], writes=["ssq"])
            S.op("act", lambda e, L=L: e.activation(ssq[:L, :], ssq[:L, :], AF.Sqrt, bias=EPS, scale=1.0 / 512), reads=["ssq"], writes=["ssq"])
            S.op("dve", lambda e, L=L: e.reciprocal(ssq[:L, :], ssq[:L, :]), reads=["ssq"], writes=["ssq"])
            S.op("dve", lambda e, L=L: e.tensor_tensor(yv[:L, :].rearrange("p (g q) -> p g q", g=8), yv[:L, :].rearrange("p (g q) -> p g q", g=8),
                                                     ssq[:L, :].unsqueeze(2).to_broadcast([L, 8, 512]), ALU.mult), reads=["yv", "ssq"], writes=["yv"])
            S.op("dve", lambda e, L=L: e.tensor_tensor(yb[:L, :], yv[:L, :], ng[:L, :], ALU.mult), reads=["yv", ngk], writes=["yb"])
            S.dma("sp", lambda e, r0=r0, L=L: e.dma_start(out=self.ssd_y[r0:r0 + L, :], in_=yb[:L, :]), reads=["yb"], writes=[("dram", "ssd_y")])
            if last:
                ov = self.o_ssd[l, slot].rearrange("(k q) p n -> (q p) k n", q=2)
                for k4 in range(8):
                    def tro(e, k4=k4):
                        for k in range(4):
                            ins = e.matmul(ph[:, k * 128:(k + 1) * 128], hT[:, (4 * k4 + k) * 128:(4 * k4 + k + 1) * 128], identf[:], start=True, stop=True)
                        return ins
                    S.op("pe", tro, reads=["hT", "identf"], writes=["ph"])
                    S.op("act", lambda e: e.copy(hio[:].rearrange("p k n -> p (k n)"), ph[:]), reads=["ph", "hio"], writes=["hio"])
                    S.dma("sp", lambda e, k4=k4, ov=ov: e.dma_start(out=ov[:, 4 * k4:4 * k4 + 4, :], in_=hio[:]), reads=["hio"], writes=[("dram", "o_ssd")])
        st.done()

    def merge(self, l):
        S = self.S
        st = self.stage()
        sb = st.sb
        rt = sb([128, D], F32); gl = sb([128, 2 * D], F32); sd = sb([128, D], F32); gb = sb([128, 3 * D], BF16)
        sg = sb([128, 3 * D], F32); tmp = sb([128, D], F32); ob = sb([128, D], BF16)
        for (r, n) in self.tiles():
            S.dma("sp", lambda e, r=r, n=n: e.dma_start(out=rt[:n, :], in_=self.br_ret[r:r + n, :]), reads=[("dram", "br_ret")], writes=["rt"])
            S.dma("act", lambda e, r=r, n=n: e.dma_start(out=gl[:n, :], in_=self.br_glu[r:r + n, :]), reads=[("dram", "br_glu")], writes=["gl"])
            S.dma("sp", lambda e, r=r, n=n: e.dma_start(out=sd[:n, :], in_=self.br_ssd[r:r + n, :]), reads=[("dram", "br_ssd")], writes=["sd"])
            S.dma("act", lambda e, r=r, n=n: e.dma_start(out=gb[:n, :], in_=self.pj_gate[r:r + n, :]), reads=[("dram", "pj_gate")], writes=["gb"])
            S.op("act", lambda e, n=n: e.activation(sg[:n, :], gb[:n, :], AF.Sigmoid), reads=["gb"], writes=["sg"])
            S.op("act", lambda e, n=n: e.activation(tmp[:n, :], gl[:n, D:2 * D], AF.Sigmoid), reads=["gl"], writes=["tmp"])
            S.op("dve", lambda e, n=n: e.tensor_tensor(tmp[:n, :], tmp[:n, :], gl[:n, 0:D], ALU.mult), reads=["tmp", "gl"], writes=["tmp"])
            S.op("dve", lambda e, n=n: e.tensor_tensor(tmp[:n, :], tmp[:n, :], sg[:n, D:2 * D], ALU.mult), reads=["tmp", "sg"], writes=["tmp"])
            S.op("pool", lambda e, n=n: e.tensor_tensor(rt[:n, :], rt[:n, :], sg[:n, 0:D], ALU.mult), reads=["rt", "sg"], writes=["rt"])
            S.op("pool", lambda e, n=n: e.tensor_tensor(sd[:n, :], sd[:n, :], sg[:n, 2 * D:3 * D], ALU.mult), reads=["sd", "sg"], writes=["sd"])
            S.op("dve", lambda e, n=n: e.tensor_tensor(tmp[:n, :], tmp[:n, :], rt[:n, :], ALU.add), reads=["tmp", "rt"], writes=["tmp"])
            S.op("dve", lambda e, n=n: e.tensor_tensor(ob[:n, :], tmp[:n, :], sd[:n, :], ALU.add), reads=["tmp", "sd"], writes=["ob"])
            S.dma("sp", lambda e, r=r, n=n: e.dma_start(out=self.merged[r:r + n, :], in_=ob[:n, :]), reads=["ob"], writes=[("dram", "merged")])
        st.done()

    def normadd(self, ysrc, g_post, g_next, final=False):
        S = self.S
        st = self.stage()
        sb = st.sb
        gp, gpk = self.load_row_bcast(st, g_post, D)
        if g_next is not None:
            gn, gnk = self.load_row_bcast(st, g_next, D)
        yt = sb([128, D], F32); xt = sb([128, D], F32); junk = sb([128, D], BF16); hb = sb([128, D], BF16)
        ss = sb([128, 1], F32); ss2 = sb([128, 1], F32)
        for (r, n) in self.tiles():
            S.dma("sp", lambda e, r=r, n=n: e.dma_start(out=yt[:n, :], in_=ysrc[r:r + n, :]), reads=[("dram", ysrc.name)], writes=["yt"])
            S.dma("act", lambda e, r=r, n=n: e.dma_start(out=xt[:n, :], in_=self.xres[r:r + n, :]), reads=[("dram", "xres")], writes=["xt"])
            S.op("dve", lambda e, n=n: e.memset(ss[:n, :], 0.0), writes=["ss"])
            S.op("act", lambda e, n=n: e.activation(junk[:n, :], yt[:n, :], AF.Square, accum_out=ss[:n, :]), reads=["yt", "ss"], writes=["junk", "ss"])
            S.op("act", lambda e, n=n: e.activation(ss[:n, :], ss[:n, :], AF.Sqrt, bias=EPS, scale=1.0 / D), reads=["ss"], writes=["ss"])
            S.op("dve", lambda e, n=n: e.reciprocal(ss[:n, :], ss[:n, :]), reads=["ss"], writes=["ss"])
            S.op("dve", lambda e, n=n: e.scalar_tensor_tensor(yt[:n, :], yt[:n, :], ss[:n, 0:1], gp[:n, :], ALU.mult, ALU.mult), reads=["yt", "ss", gpk], writes=["yt"])
            S.op("dve", lambda e, n=n: e.tensor_tensor(xt[:n, :], xt[:n, :], yt[:n, :], ALU.add), reads=["xt", "yt"], writes=["xt"])
            S.dma("sp", lambda e, r=r, n=n: e.dma_start(out=self.xres[r:r + n, :], in_=xt[:n, :]), reads=["xt"], writes=[("dram", "xres")])
            if final:
                S.dma("sp", lambda e, r=r, n=n: e.dma_start(out=self.y[r:r + n, :], in_=xt[:n, :]), reads=["xt"], writes=[("dram", "y")])
            if g_next is not None:
                S.op("dve", lambda e, n=n: e.memset(ss2[:n, :], 0.0), writes=["ss2"])
                S.op("act", lambda e, n=n: e.activation(junk[:n, :], xt[:n, :], AF.Square, accum_out=ss2[:n, :]), reads=["xt", "junk", "ss2"], writes=["junk", "ss2"])
                S.op("act", lambda e, n=n: e.activation(ss2[:n, :], ss2[:n, :], AF.Sqrt, bias=EPS, scale=1.0 / D), reads=["ss2"], writes=["ss2"])
                S.op("dve", lambda e, n=n: e.reciprocal(ss2[:n, :], ss2[:n, :]), reads=["ss2"], writes=["ss2"])
                S.op("dve", lambda e, n=n: e.scalar_tensor_tensor(hb[:n, :], xt[:n, :], ss2[:n, 0:1], gn[:n, :], ALU.mult, ALU.mult), reads=["xt", "ss2", gnk], writes=["hb"])
                S.dma("sp", lambda e, r=r, n=n: e.dma_start(out=self.hn[r:r + n, :], in_=hb[:n, :]), reads=["hb"], writes=[("dram", "hn")])
        st.done()

    def attention(self):
        S = self.S
        st = self.stage()
        sb, ps = st.sb, st.ps
        identf = sb([128, 128], F32); identb = sb([128, 128], BF16)
        S.dma("sp", lambda e: e.dma_start(out=identf[:], in_=self.cst["ident_in"]), writes=["identf"])
        S.op("dve", lambda e: e.tensor_copy(identb[:], identf[:]), reads=["identf"], writes=["identb"])
        kT = sb([128, 16, 256], BF16); v = sb([128, 2, D], BF16); qT = sb([128, 16, 128], BF16)
        pexp = sb([128, 4, 256], BF16); pT = sb([128, 8, 128], BF16); ob = sb([128, D], BF16)
        mx = sb([128, 4], F32); sm = sb([128, 4], F32)
        ps_sc = ps([128, 4, 256], F32); pt = ps([128, 8, 128], BF16); po = ps([128, D], F32)
        SC = 512 ** -0.5
        for (r0, ln, p0, slot) in self.seqs:
            for c in range(16):
                ksrc = self.kvp_bf[:, 0:D] if slot == 0 else self.kv_bf[slot, 0]
                S.dma("sp", lambda e, c=c, ksrc=ksrc: e.dma_start(out=kT[:, c, :], in_=ksrc[:, c * 128:(c + 1) * 128], transpose=True),
                      reads=[("dram", "kv_bf")], writes=[("kT", c)])
            vsrc = self.kvp_bf[:, D:2 * D] if slot == 0 else self.kv_bf[slot, 1]
            S.dma("sp", lambda e, vsrc=vsrc: e.dma_start(out=v[:], in_=vsrc.rearrange("(mc p) d -> p mc d", p=128)), reads=[("dram", "kv_bf")], writes=["v"])
            t = 0
            while t < ln:
                n = min(128, ln - t)
                rr = r0 + t
                for c in range(16):
                    S.dma("sp", lambda e, c=c, rr=rr, n=n: e.dma_start(out=qT[:, c, :n], in_=self.q_bf[rr:rr + n, c * 128:(c + 1) * 128], transpose=True),
                          reads=[("dram", "q_bf")], writes=[("qT", c)])

                def scm(e, n=n):
                    for h in range(4):
                        for dc in range(4):
                            ins = e.matmul(ps_sc[:n, h, :], qT[:, h * 4 + dc, :n], kT[:, h * 4 + dc, :], start=(dc == 0), stop=(dc == 3))
                    return ins
                S.op("pe", scm, reads=[("qT", c) for c in range(16)] + [("kT", c) for c in range(16)], writes=["ps_sc"])
                S.op("dve", lambda e, n=n: e.tensor_reduce(mx[:n, :], ps_sc[:n], AX.X, ALU.max), reads=["ps_sc"], writes=["mx"])
                S.op("dve", lambda e, n=n: e.tensor_scalar(mx[:n, :], mx[:n, :], -SC, None, ALU.mult), reads=["mx"], writes=["mx"])
                for h in range(4):
                    S.op("act", lambda e, n=n, h=h: e.activation(pexp[:n, h, :], ps_sc[:n, h, :], AF.Exp, bias=mx[:n, h:h + 1], scale=SC),
                         reads=["ps_sc", "mx"], writes=[("pexp", h)])
                S.op("dve", lambda e, n=n: e.tensor_reduce(sm[:n, :], pexp[:n], AX.X, ALU.add), reads=[("pexp", h) for h in range(4)], writes=[("sm", h) for h in range(4)])

                def trp(e, n=n):
                    for h in range(4):
                        for mc in range(2):
                            ins = e.transpose(pt[:, h * 2 + mc, :n], pexp[:n, h, mc * 128:(mc + 1) * 128], identb[:n, :n])
                    return ins
                S.op("pe", trp, reads=[("pexp", h) for h in range(4)] + ["identb"], writes=["pt"])
                S.op("act", lambda e, n=n: e.copy(pT[:, :, :n], pt[:, :, :n]), reads=["pt"], writes=["pT"])

                def om(e, n=n):
                    for h in range(4):
                        for mc in range(2):
                            ins = e.matmul(po[:n, h * 512:(h + 1) * 512], pT[:, h * 2 + mc, :n], v[:, mc, h * 512:(h + 1) * 512], start=(mc == 0), stop=(mc == 1))
                    return ins
                S.op("pe", om, reads=["pT", "v"], writes=["po"])
                smk = [("sm", h) for h in range(4)]
                S.op("dve", lambda e, n=n: e.reciprocal(sm[:n, :], sm[:n, :]), reads=smk, writes=smk)
                S.op("dve", lambda e, n=n: e.tensor_tensor(ob[:n, :].rearrange("p (h d) -> p h d", h=4), po[:n, :].rearrange("p (h d) -> p h d", h=4),
                                                         sm[:n, :].unsqueeze(2).to_broadcast([n, 4, 512]), ALU.mult), reads=["po"] + smk, writes=["ob"])
                S.dma("sp", lambda e, rr=rr, n=n: e.dma_start(out=self.att[rr:rr + n, :], in_=ob[:n, :]), reads=["ob"], writes=[("dram", "att")])
                t += n
        st.done()

    def evac_relu2(self):
        S = self.S

        def f(st, n, cw, ps, pb, ob, okey):
            if not hasattr(st, "r2"):
                st.r2 = st.sb([128, 512], F32, "r2")
            r2 = st.r2
            S.op("act", lambda e: e.activation(r2[:n, :cw], ps[:n, :cw], AF.Relu), reads=[("lps", pb)], writes=["r2"])
            S.op("dve", lambda e: e.tensor_tensor(ob[:n, :cw], r2[:n, :cw], r2[:n, :cw], ALU.mult), reads=["r2"], writes=[okey])
        return f

    def conv_out(self, l):
        S = self.S
        for (r0, ln, p0, slot) in self.seqs:
            S.dma("pool", lambda e, p0=p0, ln=ln, slot=slot: e.dma_start(out=self.o_conv[l, slot], in_=self.xpad[p0 + ln:p0 + ln + 3, :]),
                  reads=[("dram", "xpad")], writes=[("dram", "o_conv")])
        S.barrier()
        S.emit()

    def conv_init(self, l):
        S = self.S
        st = self.stage()
        z = st.sb([96, 192], BF16)
        S.op("dve", lambda e: e.memset(z[:], 0.0), writes=["z"])
        S.dma("sp", lambda e: e.dma_start(out=self.xpad[16:19, :].rearrange("r (a f) -> (r a) f", f=192), in_=z[:]), reads=["z"], writes=[("dram", "xpad")])
        for j in range(NS):
            p0 = self.seqs[1 + j][2]
            S.dma("pool", lambda e, j=j, p0=p0: e.dma_start(out=self.xpad[p0:p0 + 3, :], in_=self.st_conv[l, j]), writes=[("dram", "xpad")])
        st.done()

    def mem_kv(self, l):
        S = self.S
        self.rms_stage(self.mem, self.memn, self.sw["norm_gains"][l, 6:7, :], 256)
        self.linear(self.memn, D, self.wb["w_xkv"][l], 2 * D, self.evac_store(self.mkv_f, F32), nrows=256)
        S.dma("sp", lambda e: e.dma_start(out=self.o_mk[l], in_=self.mkv_f[:, 0:D]), reads=[("dram", "mkv_f")], writes=[("dram", "o_mk")])
        S.dma("sp", lambda e: e.dma_start(out=self.o_mv[l], in_=self.mkv_f[:, D:2 * D]), reads=[("dram", "mkv_f")], writes=[("dram", "o_mv")])
        for kv in range(2):
            if kv == 0:
                S.dma("pool", lambda e: e.dma_start(out=self.kvp_bf, in_=self.mkv_f), reads=[("dram", "mkv_f")], writes=[("dram", "kv_bf")])
            src = self.st_mk if kv == 0 else self.st_mv
            for j in range(NS):
                S.dma("pool", lambda e, kv=kv, j=j, src=src: e.dma_start(out=self.kv_bf[1 + j, kv], in_=src[l, j]), writes=[("dram", "kv_bf")])
        S.barrier()
        S.emit()

    def build(self, upto="all"):
        S = self.S
        flags = upto.split(",")
        G = lambda l, i: self.sw["norm_gains"][l, i:i + 1, :]
        self.cast_weights()
        self.init_copy()
        self.rms_stage(self.xres, self.hn, G(0, 0), self.NT)
        for l in range(2):
            self.mem_kv(l)
            self.conv_init(l)
            cbs = [(c, 512) for c in range(0, C_DT, 512)] + [(C_DT, 64)] + [(c, 512) for c in range(C_GATE, INC, 512)]
            self.linear(self.hn, D, self.wb["w_in"][l], INC, self.evac_inproj(), colblocks=cbs)
            self.conv_out(l)
            self.retention(l)
            self.s5(l)
            self.ssd(l)
            if "mix" in flags:
                break
            self.linear(self.ret_y, D, self.wb["w_ret_o"][l], D, self.evac_store(self.br_ret, F32))
            self.linear(self.s5_y, D, self.wb["w_s5_glu"][l], 2 * D, self.evac_store(self.br_glu, F32))
            self.linear(self.ssd_y, 2 * D, self.wb["w_ssd_out"][l], D, self.evac_store(self.br_ssd, F32))
            self.merge(l)
            self.linear(self.merged, D, self.wb["w_mix_out"][l], D, self.evac_store(self.lin_out, F32))
            self.normadd(self.lin_out, G(l, 1), G(l, 2))
            self.linear(self.hn, D, self.wb["w_xq"][l], D, self.evac_store(self.q_bf, BF16))
            self.attention()
            self.linear(self.att, D, self.wb["w_xo"][l], D, self.evac_store(self.lin_out, F32))
            self.normadd(self.lin_out, G(l, 3), G(l, 4))
            self.linear(self.hn, D, self.wb["w_up"][l], 4 * D, self.evac_store(self.hmlp, BF16, func=self.evac_relu2()))
            self.linear(self.hmlp, 4 * D, self.wb["w_down"][l], D, self.evac_store(self.lin_out, F32))
            self.normadd(self.lin_out, G(l, 5), G(l + 1, 0) if l == 0 else None, final=(l == 1))
            if "l0" in flags:
                break
        if "l0" in flags or "mix" in flags:
            st = self.stage()
            xt = st.sb([128, D], F32)
            for (r, n) in self.tiles():
                S.dma("sp", lambda e, r=r, n=n: e.dma_start(out=xt[:n, :], in_=self.xres[r:r + n, :]), reads=[("dram", "xres")], writes=["xt"])
                S.dma("sp", lambda e, r=r, n=n: e.dma_start(out=self.y[r:r + n, :], in_=xt[:n, :]), reads=["xt"], writes=[("dram", "y")])
            st.done()
        S.barrier()
        S.emit()
        S.close()
        return self.nc


def make_in_maps(inputs, TP, cores):
    consts = host_consts(TP)
    maps = []
    for c in cores:
        b = c % 2
        sl = slice(NS * c, NS * (c + 1))
        m = {}
        m["x_in"] = np.concatenate([inputs["x_prompt"][b, :TP], inputs["x_sample"][sl].reshape(NS * SL, D)], axis=0)
        m["mem_in"] = inputs["mem_prompt"][b]
        m["st_ret"] = inputs["state_ret"][:, sl]
        m["st_s5r"] = inputs["state_s5_re"][:, sl]
        m["st_s5i"] = inputs["state_s5_im"][:, sl]
        m["st_ssd"] = inputs["state_ssd"][:, sl]
        m["st_conv"] = inputs["cache_ssd_conv"][:, sl]
        m["st_mk"] = inputs["cache_mem_k"][:, sl].reshape(2, NS, 256, D)
        m["st_mv"] = inputs["cache_mem_v"][:, sl].reshape(2, NS, 256, D)
        for k in WSHAPES:
            m[k] = inputs[k]
        for k in SMALLW:
            m[k] = inputs[k]
        m.update(consts)
        maps.append({k: np.ascontiguousarray(v, dtype=np.float32) for k, v in m.items()})
    return maps


KERNEL_FLAGS = "all"


def kernel(**inputs):
    TP = 8192
    inputs = {k: np.asarray(v) for k, v in inputs.items()}
    b = Builder(TP)
    nc = b.build(KERNEL_FLAGS)
    cores = list(range(8))
    maps = make_in_maps(inputs, TP, cores)
    res = run_bass_kernel_spmd(nc, maps, core_ids=cores)
    R = res.results
    f32 = np.float32
    y_p = np.stack([np.asarray(R[bb]["y"], f32)[:TP] for bb in range(2)])
    y_s = np.concatenate([np.asarray(R[c]["y"], f32)[TP:].reshape(NS, SL, D) for c in cores], axis=0)

    def pstate(name, shape):
        return np.stack([np.asarray(R[bb][name], f32)[:, 0] for bb in range(2)], axis=1).reshape(shape)

    def sstate(name, shape):
        return np.concatenate([np.asarray(R[c][name], f32)[:, 1:] for c in cores], axis=1).reshape(shape)
    p_ret = pstate("o_ret", (2, 2, RH, RDK, RDV))
    p_s5r = pstate("o_s5r", (2, 2, 128, 64))
    p_s5i = pstate("o_s5i", (2, 2, 128, 64))
    p_ssd = pstate("o_ssd", (2, 2, 64, 64, 128))
    p_conv = pstate("o_conv", (2, 2, 3, 6144))
    p_mk = np.stack([np.asarray(R[bb]["o_mk"], f32) for bb in range(2)], axis=1).reshape(2, 2, 256, 4, 512)
    p_mv = np.stack([np.asarray(R[bb]["o_mv"], f32) for bb in range(2)], axis=1).reshape(2, 2, 256, 4, 512)
    s_ret = sstate("o_ret", (2, 32, RH, RDK, RDV))
    s_s5r = sstate("o_s5r", (2, 32, 128, 64))
    s_s5i = sstate("o_s5i", (2, 32, 128, 64))
    s_ssd = sstate("o_ssd", (2, 32, 64, 64, 128))
    s_conv = sstate("o_conv", (2, 32, 3, 6144))
    return (y_p, y_s, p_ret, p_s5r, p_s5i, p_ssd, p_conv, p_mk, p_mv, s_ret, s_s5r, s_s5i, s_ssd, s_conv)
```

```python
import math
import numpy as np
import concourse.bass as bass
import concourse.mybir as mybir
from concourse.bass_utils import run_bass_kernel_spmd

F32 = mybir.dt.float32
BF16 = mybir.dt.bfloat16
I32 = mybir.dt.int32
AF = mybir.ActivationFunctionType
ALU = mybir.AluOpType
AX = mybir.AxisListType


class Sched:
    EPOCH = 24000
    NDMA = 16

    def __init__(self, nc):
        self.nc = nc
        self.engs = ["pe", "act", "dve", "pool", "sp"]
        self.items = {e: [] for e in self.engs}
        self.count = {e: 0 for e in self.engs}
        self.sems = {}
        self.dsems = {}
        self.dcount = {e: 0 for e in self.engs}
        self.seen = {e: {} for e in self.engs}
        self.res = {}
        self._semctx = []
        self.dlast = {}

    def _sem(self, key):
        d = self.sems if key[0] == "E" else self.dsems
        if key not in d:
            cm = self.nc.semaphore("s_%s_%s_%d" % key)
            d[key] = cm.__enter__()
            self._semctx.append(cm)
        return d[key]

    def _deps(self, reads, writes):
        ev = []
        for r in reads:
            st = self.res.get(r)
            if st and st[0] is not None:
                ev.append(st[0])
        for w in writes:
            st = self.res.get(w)
            if st:
                if st[0] is not None:
                    ev.append(st[0])
                ev.extend(st[1])
        return ev

    def _waits(self, eng, events):
        best = {}
        for (k, v) in events:
            if best.get(k, 0) < v:
                best[k] = v
        out = []
        for k, v in best.items():
            if self.seen[eng].get(k, 0) < v:
                self.seen[eng][k] = v
                out.append((k, v))
        return out

    def _mark(self, event, reads, writes):
        for r in reads:
            st = self.res.setdefault(r, [None, []])
            st[1].append(event)
        for w in writes:
            self.res[w] = [event, []]

    def op(self, eng, fn, reads=(), writes=()):
        waits = self._waits(eng, self._deps(reads, writes))
        n = self.count[eng]
        epoch, idx = divmod(n, self.EPOCH)
        self.count[eng] = n + 1
        key = ("E", eng, epoch)
        self._sem(key)
        event = (key, idx + 1)
        self.items[eng].append((waits, fn, key, 1))
        self._mark(event, reads, writes)
        return event

    def dma(self, eng, fn, reads=(), writes=()):
        n = self.dcount[eng]
        self.dcount[eng] = n + 1
        k, rnd = n % self.NDMA, n // self.NDMA
        key = ("D", eng, k)
        self._sem(key)
        ev = self._deps(reads, writes)
        if rnd > 0:
            ev.append((key, 16 * rnd))
        waits = self._waits(eng, ev)
        event = (key, 16 * (rnd + 1))
        self.dlast[key] = 16 * (rnd + 1)
        self.items[eng].append((waits, fn, key, 16))
        self._mark(event, reads, writes)
        return event

    def wait_all(self, eng, keys):
        ev = []
        for kk in keys:
            st = self.res.get(kk)
            if st:
                if st[0] is not None:
                    ev.append(st[0])
                ev.extend(st[1])
        waits = self._waits(eng, ev)
        self.items[eng].append((waits, None, None, 0))

    def barrier(self):
        ev = []
        for f in self.engs:
            n = self.count[f]
            if n > 0:
                epoch, idx = divmod(n - 1, self.EPOCH)
                ev.append((("E", f, epoch), idx + 1))
        for k, v in self.dlast.items():
            ev.append((k, v))
        for e in self.engs:
            self.items[e].append((self._waits(e, list(ev)), None, None, 0))

    def emit(self):
        nc = self.nc
        sched = self

        def run(engname, engine):
            for waits, fn, key, inc in sched.items[engname]:
                for (k, v) in waits:
                    engine.wait_ge(sched._sem(k), v)
                if fn is not None:
                    ins = fn(engine)
                    ins.then_inc(sched._sem(key), inc)

        with nc.Block() as block:
            @block.tensor
            def _(e):
                run("pe", e)

            @block.scalar
            def _(e):
                run("act", e)

            @block.vector
            def _(e):
                run("dve", e)

            @block.gpsimd
            def _(e):
                run("pool", e)

            @block.sync
            def _(e):
                run("sp", e)
        self.items = {e: [] for e in self.engs}

    def close(self):
        for cm in reversed(self._semctx):
            cm.__exit__(None, None, None)
        self._semctx = []


D = 2048
NS = 4
SL = 16
PAST = 1024
EPS = 1e-6
RH, RDK, RDV = 8, 128, 256
INC = 24640
C_Q, C_K, C_V, C_G, C_U, C_Z, C_XBC, C_DT, C_GATE = 0, 1024, 2048, 4096, 6144, 8192, 12288, 18432, 18496
GAMMA = [1.0 - 2.0 ** (-5 - h) for h in range(RH)]
WSHAPES = {"w_in": (D, INC), "w_ret_o": (D, D), "w_s5_glu": (D, 2 * D), "w_ssd_out": (2 * D, D), "w_mix_out": (D, D),
           "w_xq": (D, D), "w_xkv": (D, 2 * D), "w_xo": (D, D), "w_up": (D, 4 * D), "w_down": (4 * D, D)}
SMALLW = {"norm_gains": (7, D), "ret_gn": (D,), "s5_a_re": (128, 64), "s5_a_im": (128, 64), "s5_b_re": (128, 64, 16),
          "s5_b_im": (128, 64, 16), "s5_c_re": (128, 16, 64), "s5_c_im": (128, 16, 64), "s5_d": (D,), "s5_log_dt": (128,),
          "ssd_conv_w": (4, 6144), "ssd_conv_b": (6144,), "ssd_dt_bias": (64,), "ssd_a_log": (64,), "ssd_d": (64,),
          "ssd_norm": (4096,)}


def host_consts(TP):
    NT = TP + NS * SL
    pos = np.concatenate([np.arange(TP)] + [PAST + np.arange(SL)] * NS).astype(np.float32)
    half = 64
    inv = np.exp(-math.log(10000.0) * np.arange(half, dtype=np.float32) / half).astype(np.float32)
    ang = pos[:, None] * inv[None]
    cos, sin = np.cos(ang).astype(np.float32), np.sin(ang).astype(np.float32)
    cs = np.zeros((NT, 2, RH, 2, 64), np.float32)
    sn = np.zeros((NT, 2, RH, 2, 64), np.float32)
    for w in range(2):
        sc = 1.0 if w == 0 else RDK ** -0.5
        cs[:, w, :, :, :] = (cos * sc)[:, None, None, :]
        sn[:, w, :, 0, :] = (-sin * sc)[:, None, :]
        sn[:, w, :, 1, :] = (sin * sc)[:, None, :]
    lg = np.log1p(-np.exp2(-5.0 - np.arange(RH))).astype(np.float64)
    idx = np.arange(64)
    mask = np.exp(np.abs(idx[:, None] - idx[None, :])[:, None, :] * lg[None, :, None]).astype(np.float32)
    din = np.exp((idx + 1.0)[None, :] * lg[:, None])[None].repeat(128, 0).astype(np.float32)
    dup64 = np.exp((63.0 - idx)[:, None] * lg[None, :]).astype(np.float32)
    dup16 = np.exp((15.0 - np.arange(16))[:, None] * lg[None, :]).astype(np.float32)
    tv = np.arange(64, dtype=np.float32)[None].repeat(128, 0)
    ident = np.eye(128, dtype=np.float32)
    tri = (idx[:, None] <= idx[None, :]).astype(np.float32)
    return {"rope_cs": cs.reshape(NT, 2048), "rope_sn": sn.reshape(NT, 2048), "ret_mask": mask, "ret_din": din,
            "ret_dup64": dup64, "ret_dup16": dup16, "tvec": tv, "ident_in": ident, "tri_in": tri}


CONST_SHAPES = lambda NT: {"rope_cs": (NT, 2048), "rope_sn": (NT, 2048), "ret_mask": (64, 8, 64), "ret_din": (128, 8, 64),
                           "ret_dup64": (64, 8), "ret_dup16": (16, 8), "tvec": (128, 64), "ident_in": (128, 128),
                           "tri_in": (64, 64)}


from contextlib import ExitStack


class Builder:
    def __init__(self, TP, dbg_out=()):
        self.TP = TP
        self.NT = NT = TP + NS * SL
        self.nc = nc = bass.Bass("TRN2", target_bir_lowering=False)
        self.S = Sched(nc)
        self.uid = 0
        di = lambda name, shape, dt=F32: nc.dram_tensor(name, list(shape), dt, kind="ExternalInput").ap()
        do = lambda name, shape, dt=F32: nc.dram_tensor(name, list(shape), dt, kind="ExternalOutput").ap()
        ds = lambda name, shape, dt: nc.dram_tensor(name, list(shape), dt, kind=("ExternalOutput" if name in dbg_out else "Internal")).ap()
        self.x_in = di("x_in", (NT, D))
        self.mem = di("mem_in", (256, D))
        self.st_ret = di("st_ret", (2, NS, RH, RDK, RDV))
        self.st_s5r = di("st_s5r", (2, NS, 128, 64))
        self.st_s5i = di("st_s5i", (2, NS, 128, 64))
        self.st_ssd = di("st_ssd", (2, NS, 64, 64, 128))
        self.st_conv = di("st_conv", (2, NS, 3, 6144))
        self.st_mk = di("st_mk", (2, NS, 256, D))
        self.st_mv = di("st_mv", (2, NS, 256, D))
        self.w = {k: di(k, (2,) + v) for k, v in WSHAPES.items()}
        self.sw = {k: di(k, (2,) + v) for k, v in SMALLW.items()}
        self.cst = {k: di(k, v) for k, v in CONST_SHAPES(NT).items()}
        self.y = do("y", (NT, D))
        self.o_ret = do("o_ret", (2, 1 + NS, RH, RDK, RDV))
        self.o_s5r = do("o_s5r", (2, 1 + NS, 128, 64))
        self.o_s5i = do("o_s5i", (2, 1 + NS, 128, 64))
        self.o_ssd = do("o_ssd", (2, 1 + NS, 64, 64, 128))
        self.o_conv = do("o_conv", (2, 1 + NS, 3, 6144))
        self.o_mk = do("o_mk", (2, 256, D))
        self.o_mv = do("o_mv", (2, 256, D))
        self.wb = {k: ds(k + "_bf", (2,) + v, BF16) for k, v in WSHAPES.items()}
        self.xres = ds("xres", (NT, D), F32)
        self.hn = ds("hn", (NT, D), BF16)
        self.pj_qkvg = ds("pj_qkvg", (NT, 6144), BF16)
        self.pj_u = ds("pj_u", (NT, 2048), BF16)
        self.pj_z = ds("pj_z", (NT, 4096), BF16)
        self.pj_gate = ds("pj_gate", (NT, 6144), BF16)
        self.dtf = ds("dtf", (NT, 64), F32)
        self.xpad = ds("xpad", (16 + NT + 3 * (1 + NS) + 64, 6144), BF16)
        self.ret_y = ds("ret_y", (NT, D), BF16)
        self.s5_y = ds("s5_y", (NT, D), BF16)
        self.ssd_y = ds("ssd_y", (NT, 2 * D), BF16)
        self.br_ret = ds("br_ret", (NT, D), F32)
        self.br_glu = ds("br_glu", (NT, 2 * D), F32)
        self.br_ssd = ds("br_ssd", (NT, D), F32)
        self.merged = ds("merged", (NT, D), BF16)
        self.lin_out = ds("lin_out", (NT, D), F32)
        self.q_bf = ds("q_bf", (NT, D), BF16)
        self.att = ds("att", (NT, D), BF16)
        self.hmlp = ds("hmlp", (NT, 4 * D), BF16)
        self.memn = ds("memn", (256, D), BF16)
        self.mkv_f = ds("mkv_f", (256, 2 * D), F32)
        self.kvp_bf = ds("kvp_bf", (256, 2 * D), BF16)
        self.kv_bf = ds("kv_bf", (1 + NS, 2, 256, D), BF16)
        self.seqs = [(0, TP, 16, 0)] + [(TP + SL * j, SL, 16 + TP + 3 + (SL + 3) * j, 1 + j) for j in range(NS)]

    def name(self, p):
        self.uid += 1
        return "%s%d" % (p, self.uid)

    def stage(self):
        b = self

        class St:
            def __init__(s):
                s.stack = ExitStack()

            def sb(s, shape, dt=F32, nm="t"):
                return s.stack.enter_context(b.nc.sbuf_tensor(b.name(nm), list(shape), dt))

            def ps(s, shape, dt=F32, nm="p"):
                return s.stack.enter_context(b.nc.psum_tensor(b.name(nm), list(shape), dt))

            def done(s):
                b.S.barrier()
                b.S.emit()
                s.stack.close()
        return St()

    def tiles(self):
        out, r = [], 0
        while r < self.NT:
            n = min(128, self.NT - r)
            out.append((r, n))
            r += n
        return out

    def chunks(self):
        out = []
        for si, (r0, ln, _, _) in enumerate(self.seqs):
            c = min(64, ln)
            for k in range(ln // c):
                out.append((si, r0 + k * c, c, k == 0, k == ln // c - 1))
        return out

    def load_row_bcast(self, st, ap_row, width, npart=128, dt=F32, eng="sp"):
        t = st.sb([npart, width], dt, "rb")
        key = self.name("rbk")
        self.S.dma(eng, lambda e: e.dma_start(out=t[:], in_=ap_row.to_broadcast([npart, width])), writes=[key])
        return t, key

    def cast_weights(self):
        S = self.S
        for k, (rows, cols) in WSHAPES.items():
            step = max(1, (4 << 20) // cols)
            for l in range(2):
                for r in range(0, rows, step):
                    rr = min(step, rows - r)
                    S.dma("pool", lambda e, k=k, l=l, r=r, rr=rr: e.dma_start(out=self.wb[k][l, r:r + rr, :], in_=self.w[k][l, r:r + rr, :]),
                          writes=[("dram", k + "_bf")])
        S.barrier()
        S.emit()

    def init_copy(self):
        S = self.S
        S.dma("sp", lambda e: e.dma_start(out=self.xres, in_=self.x_in), writes=["xres"])
        for l in range(2):
            pass
        S.barrier()
        S.emit()

    def rms_stage(self, src, dst, gain_row, nrows):
        S = self.S
        st = self.stage()
        g, gk = self.load_row_bcast(st, gain_row, D)
        xt = [st.sb([128, D], F32, "xt") for _ in range(2)]
        hb = [st.sb([128, D], BF16, "hb") for _ in range(2)]
        junk = st.sb([128, D], BF16, "junk")
        ss = [st.sb([128, 1], F32, "ss") for _ in range(2)]
        r, i = 0, 0
        while r < nrows:
            n = min(128, nrows - r)
            b = i % 2
            S.dma("sp", lambda e, r=r, n=n, b=b: e.dma_start(out=xt[b][:n, :], in_=src[r:r + n, :]), reads=[("dram", src.name)], writes=[("xt", b)])
            S.op("dve", lambda e, n=n, b=b: e.memset(ss[b][:n, :], 0.0), writes=[("ss", b)])
            S.op("act", lambda e, n=n, b=b: e.activation(junk[:n, :], xt[b][:n, :], AF.Square, accum_out=ss[b][:n, :]), reads=[("xt", b), ("ss", b)], writes=["junk", ("ss", b)])
            S.op("act", lambda e, n=n, b=b: e.activation(ss[b][:n, :], ss[b][:n, :], AF.Sqrt, bias=EPS, scale=1.0 / D), reads=[("ss", b)], writes=[("ss", b)])
            S.op("dve", lambda e, n=n, b=b: e.reciprocal(ss[b][:n, :], ss[b][:n, :]), reads=[("ss", b)], writes=[("ss", b)])
            S.op("dve", lambda e, n=n, b=b: e.scalar_tensor_tensor(hb[b][:n, :], xt[b][:n, :], ss[b][:n, 0:1], g[:n, :], ALU.mult, ALU.mult),
                 reads=[("xt", b), ("ss", b), gk], writes=[("hb", b)])
            S.dma("sp", lambda e, r=r, n=n, b=b: e.dma_start(out=dst[r:r + n, :], in_=hb[b][:n, :]), reads=[("hb", b)], writes=[("dram", dst.name)])
            r += n
            i += 1
        st.done()

    def linear(self, A, K, W, N, evac, nrows=None, colblocks=None):
        S = self.S
        nrows = self.NT if nrows is None else nrows
        KC = K // 128
        TBL = {2048: 1024, 4096: 512, 8192: 256}[K]
        st = self.stage()
        NAT = 2 if K <= 4096 else 1
        ATs = [st.sb([128, KC, TBL], BF16, "AT") for _ in range(NAT)]
        WT = [st.sb([128, KC, 512], BF16, "WT") for _ in range(2)]
        PS = [st.ps([128, 512], F32, "lps") for _ in range(2)]
        self.lin_st = st
        if colblocks is None:
            colblocks = [(c, min(512, N - c)) for c in range(0, N, 512)]
        Wv = W.rearrange("(kc p) n -> p kc n", p=128)
        widx, pidx = 0, 0
        blocks = []
        rb = 0
        while rb < nrows:
            nb = min(TBL, nrows - rb)
            blocks.append((rb, nb))
            rb += nb

        def load_at(bi):
            rb, nb = blocks[bi]
            ab = bi % NAT
            for kc in range(KC):
                S.dma("sp", lambda e, kc=kc, rb=rb, nb=nb, ab=ab: e.dma_start(out=ATs[ab][:, kc, :nb], in_=A[rb:rb + nb, kc * 128:(kc + 1) * 128], transpose=True),
                      reads=[("dram", A.name)], writes=[("AT", ab, kc)])
        load_at(0)
        for bi, (rb, nb) in enumerate(blocks):
            ab = bi % NAT
            AT = ATs[ab]
            if NAT == 2 and bi + 1 < len(blocks):
                load_at(bi + 1)
            for (c0, cw) in colblocks:
                wbuf = widx % 2
                widx += 1
                S.dma("act", lambda e, c0=c0, cw=cw, wbuf=wbuf: e.dma_start(out=WT[wbuf][:, :, :cw], in_=Wv[:, :, c0:c0 + cw]),
                      reads=[("dram", W.name)], writes=[("WT", wbuf)])
                t0 = 0
                while t0 < nb:
                    n = min(128, nb - t0)
                    pb = pidx % 2
                    pidx += 1

                    def mm(e, t0=t0, n=n, cw=cw, wbuf=wbuf, pb=pb, AT=AT):
                        for kc in range(KC):
                            ins = e.matmul(PS[pb][:n, :cw], AT[:, kc, t0:t0 + n], WT[wbuf][:, kc, :cw], start=(kc == 0), stop=(kc == KC - 1))
                        return ins
                    S.op("pe", mm, reads=[("AT", ab, kc) for kc in range(KC)] + [("WT", wbuf)], writes=[("lps", pb)])
                    evac(st, rb + t0, n, c0, cw, PS[pb], pb)
                    t0 += n
            if NAT == 1 and bi + 1 < len(blocks):
                load_at(bi + 1)
        st.done()

    def evac_store(self, dst, dt, coloff=0, func=None):
        S = self.S
        bufs = {}

        def evac(st, r0, n, c0, cw, ps, pb):
            if "ob" not in bufs:
                bufs["ob"] = [st.sb([128, 512], dt, "ob") for _ in range(2)]
                bufs["i"] = 0
            ob = bufs["ob"][bufs["i"] % 2]
            okey = ("ob", bufs["i"] % 2)
            bufs["i"] += 1
            if func is None:
                S.op("act", lambda e: e.copy(ob[:n, :cw], ps[:n, :cw]), reads=[("lps", pb)], writes=[okey])
            else:
                func(st, n, cw, ps, pb, ob, okey)
            S.dma("sp", lambda e: e.dma_start(out=dst[r0:r0 + n, coloff + c0:coloff + c0 + cw], in_=ob[:n, :cw]), reads=[okey], writes=[("dram", dst.name)])
        return evac

    def segs(self, r0, n):
        out = []
        for (s0, ln, p0, _) in self.seqs:
            a, b = max(r0, s0), min(r0 + n, s0 + ln)
            if a < b:
                out.append((a - r0, b - a, p0 + 3 + (a - s0)))
        return out

    def evac_inproj(self):
        S = self.S
        bufs = {}

        def evac(st, r0, n, c0, cw, ps, pb):
            if "ob" not in bufs:
                bufs["ob"] = [st.sb([128, 512], BF16, "ob") for _ in range(2)]
                bufs["of"] = st.sb([128, 64], F32, "of")
                bufs["i"] = 0
            ob = bufs["ob"][bufs["i"] % 2]
            okey = ("ob", bufs["i"] % 2)
            bufs["i"] += 1
            S.op("act", lambda e: e.copy(ob[:n, :cw], ps[:n, :cw]), reads=[("lps", pb)], writes=[okey])
            tgt = None
            if c0 < C_U:
                tgt, lc = self.pj_qkvg, c0
            elif c0 < C_Z:
                tgt, lc = self.pj_u, c0 - C_U
            elif c0 < C_XBC:
                tgt, lc = self.pj_z, c0 - C_Z
            elif c0 >= C_GATE:
                tgt, lc = self.pj_gate, c0 - C_GATE
            if tgt is not None:
                S.dma("sp", lambda e: e.dma_start(out=tgt[r0:r0 + n, lc:lc + cw], in_=ob[:n, :cw]), reads=[okey], writes=[("dram", tgt.name)])
            if C_XBC <= c0 < C_DT:
                for (o, cnt, prow) in self.segs(r0, n):
                    S.dma("sp", lambda e, o=o, cnt=cnt, prow=prow: e.dma_start(out=self.xpad[prow:prow + cnt, c0 - C_XBC:c0 - C_XBC + cw], in_=ob[o:o + cnt, :cw]),
                          reads=[okey], writes=[("dram", "xpad")])
            if c0 == C_DT:
                of = bufs["of"]
                S.op("dve", lambda e: e.tensor_copy(of[:n, :cw], ps[:n, :cw]), reads=[("lps", pb)], writes=["of"])
                S.dma("sp", lambda e: e.dma_start(out=self.dtf[r0:r0 + n, :], in_=of[:n, :cw]), reads=["of"], writes=[("dram", "dtf")])
        return evac

    def retention(self, l):
        S, nc = self.S, self.nc
        st = self.stage()
        sb, ps = st.sb, st.ps
        mask = sb([64, 8, 64], F32); din = sb([128, 8, 64], F32); dup64 = sb([64, 8], F32); dup16 = sb([16, 8], F32)
        identb = sb([128, 128], BF16); identf = sb([128, 128], F32)
        S.dma("sp", lambda e: e.dma_start(out=mask[:], in_=self.cst["ret_mask"]), writes=["mask"])
        S.dma("sp", lambda e: e.dma_start(out=din[:], in_=self.cst["ret_din"]), writes=["din"])
        S.dma("sp", lambda e: e.dma_start(out=dup64[:], in_=self.cst["ret_dup64"]), writes=["dup64"])
        S.dma("sp", lambda e: e.dma_start(out=dup16[:], in_=self.cst["ret_dup16"]), writes=["dup16"])
        S.dma("sp", lambda e: e.dma_start(out=identf[:], in_=self.cst["ident_in"]), writes=["identf"])
        S.op("dve", lambda e: e.tensor_copy(identb[:], identf[:]), reads=["identf"], writes=["identb"])
        gn, gnk = self.load_row_bcast(st, self.sw["ret_gn"][l:l + 1, :], D, 64)
        qk = sb([64, 2048], BF16); vt = sb([64, 2048], BF16); gt = sb([64, 2048], BF16)
        cs = sb([64, 2048], F32); sn = sb([64, 2048], F32)
        t1 = sb([64, 2048], F32); t2 = sb([64, 2048], F32)
        qkr = sb([64, 2048], BF16); kd = sb([64, 8, 128], BF16)
        qkT = sb([128, 16, 64], BF16); qdT = sb([128, 8, 64], BF16)
        sm = sb([64, 8, 64], BF16)
        o_sb = sb([64, 8, 256], F32); osq = sb([64, 8, 256], F32)
        Sf = sb([128, 8, 256], F32); Sb = sb([128, 8, 256], BF16)
        s1 = sb([64, 8], F32); s2 = sb([64, 8], F32); mean = sb([64, 8], F32); msq = sb([64, 8], F32)
        sg = sb([64, 2048], F32); yb = sb([64, 2048], BF16)
        tq = ps([128, 16, 64], BF16); ps_s = ps([64, 8, 64], F32)
        ps_o = [ps([64, 2, 256], F32) for _ in range(2)]; ps_S = [ps([128, 2, 256], F32) for _ in range(2)]
        for (si, r0, L, first, last) in self.chunks():
            slot = self.seqs[si][3]
            if first:
                if slot == 0:
                    S.op("dve", lambda e: e.memset(Sf[:], 0.0), writes=["Sf"])
                else:
                    S.dma("sp", lambda e, slot=slot: e.dma_start(out=Sf[:], in_=self.st_ret[l, slot - 1].rearrange("h d e -> d h e")), writes=["Sf"])
                S.op("act", lambda e: e.copy(Sb[:], Sf[:]), reads=["Sf"], writes=["Sb"])
            S.dma("sp", lambda e, r0=r0, L=L: e.dma_start(out=qk[:L, :], in_=self.pj_qkvg[r0:r0 + L, C_Q:C_Q + 2048]), reads=[("dram", "pj_qkvg")], writes=["qk"])
            S.dma("sp", lambda e, r0=r0, L=L: e.dma_start(out=vt[:L, :], in_=self.pj_qkvg[r0:r0 + L, C_V:C_V + 2048]), reads=[("dram", "pj_qkvg")], writes=["vt"])
            S.dma("sp", lambda e, r0=r0, L=L: e.dma_start(out=gt[:L, :], in_=self.pj_qkvg[r0:r0 + L, C_G:C_G + 2048]), reads=[("dram", "pj_qkvg")], writes=["gt"])
            S.dma("act", lambda e, r0=r0, L=L: e.dma_start(out=cs[:L, :], in_=self.cst["rope_cs"][r0:r0 + L, :]), writes=["cs"])
            S.dma("act", lambda e, r0=r0, L=L: e.dma_start(out=sn[:L, :], in_=self.cst["rope_sn"][r0:r0 + L, :]), writes=["sn"])
            v4 = lambda t, L=L: t[:L, :].rearrange("p (a two d) -> p a two d", two=2, d=64)
            S.op("dve", lambda e, L=L: e.tensor_tensor(t1[:L, :], qk[:L, :], cs[:L, :], ALU.mult), reads=["qk", "cs"], writes=["t1"])
            S.op("pool", lambda e, L=L, v4=v4: e.tensor_tensor(v4(t2)[:, :, 0, :], v4(qk)[:, :, 1, :], v4(sn)[:, :, 0, :], ALU.mult), reads=["qk", "sn"], writes=["t2a"])
            S.op("pool", lambda e, L=L, v4=v4: e.tensor_tensor(v4(t2)[:, :, 1, :], v4(qk)[:, :, 0, :], v4(sn)[:, :, 1, :], ALU.mult), reads=["qk", "sn"], writes=["t2b"])
            S.op("dve", lambda e, L=L: e.tensor_tensor(qkr[:L, :], t1[:L, :], t2[:L, :], ALU.add), reads=["t1", "t2a", "t2b"], writes=["qkr"])
            dup = dup64 if L == 64 else dup16
            S.op("dve", lambda e, L=L, dup=dup: e.tensor_tensor(kd[:L], qkr[:L, 1024:2048].rearrange("p (h d) -> p h d", h=8),
                                                               dup[:L, :].unsqueeze(2).to_broadcast([L, 8, 128]), ALU.mult),
                 reads=["qkr", "dup64", "dup16"], writes=["kd"])

            def tr(e, L=L):
                for i in range(16):
                    ins = e.transpose(tq[:, i, :L], qkr[:L, i * 128:(i + 1) * 128], identb[:L, :L])
                return ins
            S.op("pe", tr, reads=["qkr", "identb"], writes=["tq"])
            S.op("act", lambda e, L=L: e.copy(qkT[:, :, :L], tq[:, :, :L]), reads=["tq"], writes=["qkT"])
            S.op("dve", lambda e, L=L: e.tensor_tensor(qdT[:, :, :L], qkT[:, 0:8, :L], din[:, :, :L], ALU.mult), reads=["qkT", "din"], writes=["qdT"])

            def sc(e, L=L):
                for h in range(8):
                    ins = e.matmul(ps_s[:L, h, :L], qkT[:, 8 + h, :L], qkT[:, h, :L], start=True, stop=True)
                return ins
            S.op("pe", sc, reads=["qkT"], writes=["ps_s"])
            S.op("dve", lambda e, L=L: e.tensor_tensor(sm[:L, :, :L], ps_s[:L, :, :L], mask[:L, :, :L], ALU.mult), reads=["ps_s", "mask"], writes=["sm"])
            for hp in range(4):
                pb = hp % 2

                def om(e, L=L, hp=hp, pb=pb):
                    for hh in range(2):
                        h = 2 * hp + hh
                        e.matmul(ps_o[pb][:L, hh, :], sm[:L, h, :L], vt[:L, h * 256:(h + 1) * 256], start=True, stop=False)
                        ins = e.matmul(ps_o[pb][:L, hh, :], qdT[:, h, :L], Sb[:, h, :], start=False, stop=True)
                    return ins
                S.op("pe", om, reads=["sm", "vt", "qdT", "Sb"], writes=[("ps_o", pb)])
                S.op("act", lambda e, L=L, hp=hp, pb=pb: e.copy(o_sb[:L, 2 * hp:2 * hp + 2, :], ps_o[pb][:L, :, :]), reads=[("ps_o", pb)], writes=[("o_sb", hp)])
            for hp in range(4):
                pb = hp % 2

                def sm_(e, L=L, hp=hp, pb=pb):
                    for hh in range(2):
                        h = 2 * hp + hh
                        ins = e.matmul(ps_S[pb][:, hh, :], kd[:L, h, :], vt[:L, h * 256:(h + 1) * 256], start=True, stop=True)
                    return ins
                S.op("pe", sm_, reads=["kd", "vt"], writes=[("ps_S", pb)])
                for hh in range(2):
                    h = 2 * hp + hh
                    S.op("dve", lambda e, h=h, hh=hh, pb=pb, L=L: e.scalar_tensor_tensor(Sf[:, h, :], Sf[:, h, :], float(GAMMA[h] ** L), ps_S[pb][:, hh, :], ALU.mult, ALU.add),
                         reads=[("ps_S", pb), "Sf"], writes=["Sf"])
            S.op("act", lambda e: e.copy(Sb[:], Sf[:]), reads=["Sf"], writes=["Sb"])
            if last:
                dst = self.o_ret[l, slot].rearrange("h d e -> d h e")
                S.dma("sp", lambda e, dst=dst: e.dma_start(out=dst, in_=Sf[:]), reads=["Sf"], writes=[("dram", "o_ret")])
            okeys = [("o_sb", hp) for hp in range(4)]
            S.op("dve", lambda e, L=L: e.tensor_reduce(s1[:L, :], o_sb[:L], AX.X, ALU.add), reads=okeys, writes=["s1"])
            S.op("act", lambda e, L=L: e.activation(osq[:L], o_sb[:L], AF.Square), reads=okeys, writes=["osq"])
            S.op("dve", lambda e, L=L: e.tensor_reduce(s2[:L, :], osq[:L], AX.X, ALU.add), reads=["osq"], writes=["s2"])
            S.op("dve", lambda e, L=L: e.tensor_scalar(mean[:L, :], s1[:L, :], 1.0 / 256, None, ALU.mult), reads=["s1"], writes=["mean"])
            S.op("dve", lambda e, L=L: e.tensor_tensor(msq[:L, :], mean[:L, :], mean[:L, :], ALU.mult), reads=["mean"], writes=["msq"])
            S.op("dve", lambda e, L=L: e.scalar_tensor_tensor(s2[:L, :], s2[:L, :], 1.0 / 256, msq[:L, :], ALU.mult, ALU.subtract), reads=["s2", "msq"], writes=["s2"])
            S.op("act", lambda e, L=L: e.activation(s2[:L, :], s2[:L, :], AF.Sqrt, bias=EPS, scale=1.0), reads=["s2"], writes=["s2"])
            S.op("dve", lambda e, L=L: e.reciprocal(s2[:L, :], s2[:L, :]), reads=["s2"], writes=["s2"])
            S.op("dve", lambda e, L=L: e.tensor_tensor(osq[:L], o_sb[:L], mean[:L, :].unsqueeze(2).to_broadcast([L, 8, 256]), ALU.subtract), reads=okeys + ["mean", "osq"], writes=["osq"])
            S.op("dve", lambda e, L=L: e.tensor_tensor(osq[:L], osq[:L], s2[:L, :].unsqueeze(2).to_broadcast([L, 8, 256]), ALU.mult), reads=["osq", "s2"], writes=["osq"])
            S.op("act", lambda e, L=L: e.activation(sg[:L, :], gt[:L, :], AF.Silu), reads=["gt"], writes=["sg"])
            S.op("pool", lambda e, L=L: e.tensor_tensor(sg[:L, :], sg[:L, :], gn[:L, :], ALU.mult), reads=["sg", gnk], writes=["sg"])
            S.op("dve", lambda e, L=L: e.tensor_tensor(yb[:L, :], osq[:L].rearrange("p h e -> p (h e)"), sg[:L, :], ALU.mult), reads=["osq", "sg"], writes=["yb"])
            S.dma("sp", lambda e, r0=r0, L=L: e.dma_start(out=self.ret_y[r0:r0 + L, :], in_=yb[:L, :]), reads=["yb"], writes=[("dram", "ret_y")])
        st.done()


    def trig(self, st, ang, cos_o, sin_o, shape, key_in, key_c, key_s):
        S = self.S
        ki = st.sb(shape, I32, "ki"); kf = st.sb(shape, F32, "kf"); s2 = st.sb(shape, F32, "s2"); s4 = st.sb(shape, F32, "s4")
        k = self.name("trg")
        S.op("dve", lambda e: e.tensor_scalar(ki[:], ang(), 1.0 / (2 * math.pi), None, ALU.mult), reads=[key_in], writes=[k + "ki"])
        S.op("dve", lambda e: e.tensor_copy(kf[:], ki[:]), reads=[k + "ki"], writes=[k + "kf"])
        S.op("dve", lambda e: e.scalar_tensor_tensor(kf[:], kf[:], -2 * math.pi, ang(), ALU.mult, ALU.add), reads=[k + "kf", key_in], writes=[k + "kf"])
        S.op("act", lambda e: e.activation(s2[:], kf[:], AF.Sin, scale=0.5), reads=[k + "kf"], writes=[k + "s2"])
        S.op("act", lambda e: e.activation(s4[:], kf[:], AF.Sin, scale=0.25), reads=[k + "kf"], writes=[k + "s4"])
        S.op("dve", lambda e: e.tensor_tensor(s4[:], s4[:], s4[:], ALU.mult), reads=[k + "s4"], writes=[k + "s4"])
        S.op("dve", lambda e: e.tensor_scalar(s4[:], s4[:], -2.0, 1.0, ALU.mult, ALU.add), reads=[k + "s4"], writes=[k + "s4"])
        S.op("dve", lambda e: e.scalar_tensor_tensor(sin_o(), s2[:], 2.0, s4[:], ALU.mult, ALU.mult), reads=[k + "s2", k + "s4"], writes=[key_s])
        S.op("dve", lambda e: e.tensor_tensor(s2[:], s2[:], s2[:], ALU.mult), reads=[k + "s2", key_s], writes=[k + "s2"])
        S.op("dve", lambda e: e.tensor_scalar(cos_o(), s2[:], -2.0, 1.0, ALU.mult, ALU.add), reads=[k + "s2"], writes=[key_c])

    def s5(self, l):
        S, nc = self.S, self.nc
        st = self.stage()
        sb, ps = st.sb, st.ps
        QH = 32
        identf = sb([128, 128], F32); identb = sb([128, 128], BF16)
        S.dma("sp", lambda e: e.dma_start(out=identf[:], in_=self.cst["ident_in"]), writes=["identf"])
        S.op("dve", lambda e: e.tensor_copy(identb[:], identf[:]), reads=["identf"], writes=["identb"])
        tv = sb([128, 64], F32)
        S.dma("sp", lambda e: e.dma_start(out=tv[:], in_=self.cst["tvec"]), writes=["tv"])
        COS = sb([128, 64, 64], F32); SIN = sb([128, 64, 64], F32); MT = sb([128, 64, 64], F32)
        MT16 = sb([128, 64, 16], F32)
        LB = sb([128, 64, 2, 128], BF16); CB = sb([128, 64, 2, 32], BF16)
        arT = sb([128, 64], F32); aiT = sb([128, 64], F32)
        pst = self.stage()
        psb = pst.sb
        are = psb([64, 128], F32); aim = psb([64, 128], F32); dtq = psb([64, 2], F32); dtE = psb([64, 128], F32)
        S.dma("sp", lambda e: e.dma_start(out=are[:], in_=self.sw["s5_a_re"][l].rearrange("(q g) p -> q (g p)", g=2)), writes=["are"])
        S.dma("sp", lambda e: e.dma_start(out=aim[:], in_=self.sw["s5_a_im"][l].rearrange("(q g) p -> q (g p)", g=2)), writes=["aim"])
        S.dma("sp", lambda e: e.dma_start(out=dtq[:], in_=self.sw["s5_log_dt"][l].rearrange("(q g) -> q g", g=2)), writes=["dtq"])
        S.op("act", lambda e: e.activation(dtq[:], dtq[:], AF.Exp), reads=["dtq"], writes=["dtq"])
        for g2 in range(2):
            S.op("dve", lambda e, g2=g2: e.tensor_copy(dtE[:, g2 * 64:(g2 + 1) * 64], dtq[:, g2:g2 + 1].to_broadcast([64, 64])), reads=["dtq"], writes=["dtE%d" % g2])
        dk = ["dtE0", "dtE1"]
        mag = psb([64, 128], F32); th = psb([64, 128], F32); cth = psb([64, 128], F32); sth = psb([64, 128], F32)
        S.op("dve", lambda e: e.tensor_tensor(mag[:], dtE[:], are[:], ALU.mult), reads=dk + ["are"], writes=["mag"])
        S.op("act", lambda e: e.activation(mag[:], mag[:], AF.Exp), reads=["mag"], writes=["mag"])
        S.op("dve", lambda e: e.tensor_tensor(th[:], dtE[:], aim[:], ALU.mult), reads=dk + ["aim"], writes=["th"])
        self.trig(pst, lambda: th[:], lambda: cth[:], lambda: sth[:], [64, 128], "th", "cth", "sth")
        ar = psb([64, 128], F32); ai = psb([64, 128], F32); nr = psb([64, 128], F32); den = psb([64, 128], F32)
        cr = psb([64, 128], F32); ci = psb([64, 128], F32); tmp = psb([64, 128], F32)
        S.op("dve", lambda e: e.tensor_tensor(ar[:], mag[:], cth[:], ALU.mult), reads=["mag", "cth"], writes=["ar"])
        S.op("dve", lambda e: e.tensor_tensor(ai[:], mag[:], sth[:], ALU.mult), reads=["mag", "sth"], writes=["ai"])
        S.op("dve", lambda e: e.tensor_scalar(nr[:], ar[:], -1.0, None, ALU.add), reads=["ar"], writes=["nr"])
        S.op("dve", lambda e: e.tensor_tensor(den[:], are[:], are[:], ALU.mult), reads=["are"], writes=["den"])
        S.op("dve", lambda e: e.tensor_tensor(tmp[:], aim[:], aim[:], ALU.mult), reads=["aim"], writes=["tmp"])
        S.op("dve", lambda e: e.tensor_tensor(den[:], den[:], tmp[:], ALU.add), reads=["den", "tmp"], writes=["den"])
        S.op("dve", lambda e: e.reciprocal(den[:], den[:]), reads=["den"], writes=["den"])
        S.op("dve", lambda e: e.tensor_tensor(cr[:], nr[:], are[:], ALU.mult), reads=["nr", "are"], writes=["cr"])
        S.op("dve", lambda e: e.tensor_tensor(tmp[:], ai[:], aim[:], ALU.mult), reads=["ai", "aim", "den"], writes=["tmp"])
        S.op("dve", lambda e: e.tensor_tensor(cr[:], cr[:], tmp[:], ALU.add), reads=["cr", "tmp"], writes=["cr"])
        S.op("dve", lambda e: e.tensor_tensor(cr[:], cr[:], den[:], ALU.mult), reads=["cr", "den"], writes=["cr"])
        S.op("dve", lambda e: e.tensor_tensor(ci[:], ai[:], are[:], ALU.mult), reads=["ai", "are"], writes=["ci"])
        S.op("dve", lambda e: e.tensor_tensor(tmp[:], nr[:], aim[:], ALU.mult), reads=["nr", "aim", "cr"], writes=["tmp"])
        S.op("dve", lambda e: e.tensor_tensor(ci[:], ci[:], tmp[:], ALU.subtract), reads=["ci", "tmp"], writes=["ci"])
        S.op("dve", lambda e: e.tensor_tensor(ci[:], ci[:], den[:], ALU.mult), reads=["ci", "den"], writes=["ci"])
        thT = psb([128, 64], F32); mT = psb([128, 64], F32); crT = psb([128, 64], F32); ciT = psb([128, 64], F32)
        ptr = pst.ps([128, 64], F32)
        for (src, sk, dst, dkk) in [(ar, "ar", arT, "arT"), (ai, "ai", aiT, "aiT"), (th, "th", thT, "thT"), (mag, "mag", mT, "mT"), (cr, "cr", crT, "crT"), (ci, "ci", ciT, "ciT")]:
            S.op("pe", lambda e, src=src: e.matmul(ptr[:], src[:], identf[:64, :64], start=True, stop=True), reads=[sk, "identf"], writes=["ptr"])
            S.op("act", lambda e, dst=dst: e.copy(dst[:], ptr[:]), reads=["ptr"], writes=[dkk])
        S.op("dve", lambda e: e.tensor_tensor(MT[:], thT[:].unsqueeze(2).to_broadcast([128, 64, 64]), tv[:].unsqueeze(1).to_broadcast([128, 64, 64]), ALU.mult), reads=["thT", "tv"], writes=["ANG"])
        tst = self.stage()
        self.trig(tst, lambda: MT[:], lambda: COS[:], lambda: SIN[:], [128, 64, 64], "ANG", "COS", "SIN")
        S.barrier(); S.emit(); tst.stack.close()
        S.op("dve", lambda e: e.tensor_copy(MT[:], mT[:].unsqueeze(2).to_broadcast([128, 64, 64])), reads=["mT", "COS", "SIN", "ANG"], writes=["MT"])
        S.op("dve", lambda e: e.memset(MT[:, :, 0:1], 0.0), reads=["MT"], writes=["MT"])
        S.op("dve", lambda e: e.tensor_copy(MT16[:], MT[:, :, 0:16]), reads=["MT"], writes=["MT"])
        Bn = [psb([128, 64, 16], F32, "Bn") for _ in range(2)]
        S.dma("sp", lambda e: e.dma_start(out=Bn[0][:], in_=self.sw["s5_b_re"][l].rearrange("(q g) p j -> (g p) q j", g=2)), writes=["Bn0"])
        S.dma("sp", lambda e: e.dma_start(out=Bn[1][:], in_=self.sw["s5_b_im"][l].rearrange("(q g) p j -> (g p) q j", g=2)), writes=["Bn1"])
        bb = [psb([128, 64, 16], F32, "bb") for _ in range(2)]
        t16 = psb([128, 64, 16], F32)
        bc = lambda t: t[:].unsqueeze(2).to_broadcast([128, 64, 16])
        S.op("dve", lambda e: e.tensor_tensor(bb[0][:], Bn[0][:], bc(crT), ALU.mult), reads=["Bn0", "crT"], writes=["bb0"])
        S.op("dve", lambda e: e.tensor_tensor(t16[:], Bn[1][:], bc(ciT), ALU.mult), reads=["Bn1", "ciT"], writes=["t16"])
        S.op("dve", lambda e: e.tensor_tensor(bb[0][:], bb[0][:], t16[:], ALU.subtract), reads=["bb0", "t16"], writes=["bb0"])
        S.op("dve", lambda e: e.tensor_tensor(bb[1][:], Bn[1][:], bc(crT), ALU.mult), reads=["Bn1", "crT"], writes=["bb1"])
        S.op("dve", lambda e: e.tensor_tensor(t16[:], Bn[0][:], bc(ciT), ALU.mult), reads=["Bn0", "ciT", "bb0"], writes=["t16"])
        S.op("dve", lambda e: e.tensor_tensor(bb[1][:], bb[1][:], t16[:], ALU.add), reads=["bb1", "t16"], writes=["bb1"])
        Z = psb([128, 64, 128], BF16)
        ptz = pst.ps([128, 8, 128], BF16)
        for ri in range(2):
            S.op("dve", lambda e: e.memset(Z[:], 0.0), reads=["Z"], writes=["Z"])
            Zv = Z[:].rearrange("p (c qq) (gl j) -> p c qq gl j", qq=4, j=16)
            bv = bb[ri][:].rearrange("p (c qq) j -> p c qq j", qq=4)
            for qq in range(4):
                for g2 in range(2):
                    S.op("dve", lambda e, qq=qq, g2=g2, Zv=Zv, bv=bv: e.tensor_copy(Zv[g2 * 64:(g2 + 1) * 64, :, qq, 2 * qq + g2, :], bv[g2 * 64:(g2 + 1) * 64, :, qq, :]),
                         reads=["bb%d" % ri, "Z"], writes=["Z"])
            for q8 in range(8):
                def trz(e, q8=q8):
                    for k in range(8):
                        ins = e.transpose(ptz[:, k, :], Z[:, q8 * 8 + k, :], identb[:])
                    return ins
                S.op("pe", trz, reads=["Z", "identb"], writes=["ptz"])
                S.op("act", lambda e, q8=q8, ri=ri: e.copy(LB[:, q8 * 8:(q8 + 1) * 8, ri, :], ptz[:]), reads=["ptz"], writes=["LB"])
        Cn = psb([64, 64, 64], F32); Y = psb([64, 64, 128], BF16)
        ptc = pst.ps([128, 8, 64], BF16)
        for ri, nm in enumerate(["s5_c_re", "s5_c_im"]):
            S.op("dve", lambda e: e.memset(Cn[:], 0.0), reads=["Cn"], writes=["Cn"])
            S.op("dve", lambda e: e.memset(Y[:], 0.0), reads=["Y"], writes=["Y"])
            cv = self.sw[nm][l].rearrange("(q g) i p -> g i q p", g=2)
            for g2 in range(2):
                S.dma("sp", lambda e, g2=g2, cv=cv: e.dma_start(out=Cn[g2 * 32:g2 * 32 + 16, :, :], in_=cv[g2]), reads=["Cn"], writes=["Cn"])
            for g2 in range(2):
                S.op("dve", lambda e, g2=g2, ri=ri: e.tensor_scalar(Y[g2 * 32:(g2 + 1) * 32, :, g2 * 64:(g2 + 1) * 64], Cn[g2 * 32:(g2 + 1) * 32, :, :], (1.0 if ri == 0 else -1.0), None, ALU.mult),
                     reads=["Cn", "Y"], writes=["Y"])
            for q8 in range(8):
                def trc(e, q8=q8):
                    for k in range(8):
                        ins = e.transpose(ptc[:, k, :], Y[:, q8 * 8 + k, :], identb[:64, :64])
                    return ins
                S.op("pe", trc, reads=["Y", "identb"], writes=["ptc"])
                S.op("act", lambda e, q8=q8, ri=ri: e.copy(CB[:, q8 * 8:(q8 + 1) * 8, ri, :].rearrange("p k (g i) -> p k g i", g=2),
                                                         ptc[:].rearrange("p k (g i) -> p k g i", g=2)[:, :, :, 0:16]), reads=["ptc"], writes=["CB"])
        S.barrier(); S.emit(); pst.stack.close()
        dT, dTk = self.load_row_bcast(st, self.sw["s5_d"][l:l + 1, :], D, 64)
        uTs = [sb([128, 16, 64], BF16) for _ in range(2)]; utoks = [sb([64, 2048], BF16) for _ in range(2)]
        Braw = sb([128, QH, 2, 64], F32); BR = sb([128, QH, 64], F32); BI = sb([128, QH, 64], F32); tmpb = sb([128, QH, 64], F32)
        XR = sb([128, QH, 64], BF16); XI = sb([128, QH, 64], BF16)
        xpr = sb([128, 64], F32); xpi = sb([128, 64], F32); fr = sb([128, 64], F32); fi = sb([128, 64], F32); f2 = sb([128, 64], F32)
        yt = sb([64, 2048], F32); yb = sb([64, 2048], BF16)
        xo = sb([64, 128], F32)
        pb_ = [ps([128, 4, 2, 64], F32) for _ in range(2)]
        py = ps([64, 2048], F32)
        pxo = ps([64, 128], F32)
        chs = self.chunks()

        def s5_loads(ci):
            (si_, r0_, n_, _, _) = chs[ci]
            ub_ = ci % 2
            for c in range(16):
                S.dma("sp", lambda e, c=c, r0_=r0_, n_=n_, ub_=ub_: e.dma_start(out=uTs[ub_][:, c, :n_], in_=self.pj_u[r0_:r0_ + n_, c * 128:(c + 1) * 128], transpose=True),
                      reads=[("dram", "pj_u")], writes=[("uT", ub_, c)])
            S.dma("act", lambda e, r0_=r0_, n_=n_, ub_=ub_: e.dma_start(out=utoks[ub_][:n_, :], in_=self.pj_u[r0_:r0_ + n_, :]), reads=[("dram", "pj_u")], writes=[("utok", ub_)])
        for ci, (si, r0, n, first, last) in enumerate(chs):
            slot = self.seqs[si][3]
            if first:
                if slot == 0:
                    S.op("dve", lambda e: e.memset(xpr[:], 0.0), writes=["xpr"])
                    S.op("dve", lambda e: e.memset(xpi[:], 0.0), writes=["xpi"])
                else:
                    for (srcst, dstt, kk) in [(self.st_s5r, xpr, "xpr"), (self.st_s5i, xpi, "xpi")]:
                        S.dma("sp", lambda e, srcst=srcst, slot=slot: e.dma_start(out=xo[:, :], in_=srcst[l, slot - 1].rearrange("(q g) p -> q (g p)", g=2)), reads=["xo"], writes=["xo"])
                        S.op("pe", lambda e: e.matmul(pb_[0][:, 0, 0, :], xo[:, :], identf[:64, :64], start=True, stop=True), reads=["xo", "identf"], writes=[("pb", 0)])
                        S.op("act", lambda e, dstt=dstt: e.copy(dstt[:], pb_[0][:, 0, 0, :]), reads=[("pb", 0)], writes=[kk])
            ub = ci % 2
            uT, utok = uTs[ub], utoks[ub]
            if ci == 0:
                s5_loads(0)
            if ci + 1 < len(chs):
                s5_loads(ci + 1)
            S.op("dve", lambda e: e.tensor_tensor(fr[:], arT[:], xpr[:], ALU.mult), reads=["arT", "xpr"], writes=["fr"])
            S.op("dve", lambda e: e.tensor_tensor(f2[:], aiT[:], xpi[:], ALU.mult), reads=["aiT", "xpi"], writes=["f2"])
            S.op("dve", lambda e: e.tensor_tensor(fr[:], fr[:], f2[:], ALU.subtract), reads=["fr", "f2"], writes=["fr"])
            S.op("dve", lambda e: e.tensor_tensor(fi[:], arT[:], xpi[:], ALU.mult), reads=["arT", "xpi"], writes=["fi"])
            S.op("dve", lambda e: e.tensor_tensor(f2[:], aiT[:], xpr[:], ALU.mult), reads=["aiT", "xpr", "fr"], writes=["f2"])
            S.op("dve", lambda e: e.tensor_tensor(fi[:], fi[:], f2[:], ALU.add), reads=["fi", "f2"], writes=["fi"])
            V = lambda t, n=n: t[:].rearrange("p q t -> p (q t)")[:, :QH * n].rearrange("p (q t) -> p q t", t=n)
            F2 = lambda t, n=n: t[:].rearrange("p q t -> p (q t)")[:, :QH * n]
            BRv, BIv, tmv = V(BR), V(BI), V(tmpb)
            for hf in range(2):
                q0 = hf * QH
                for c8 in range(8):
                    c = hf * 8 + c8
                    pbi = c % 2

                    def bm(e, c=c, pbi=pbi, n=n, uT=uT):
                        for qq in range(4):
                            for ri in range(2):
                                ins = e.matmul(pb_[pbi][:, qq, ri, :n], LB[:, 4 * c + qq, ri, :], uT[:, c, :n], start=True, stop=True)
                        return ins
                    S.op("pe", bm, reads=["LB", ("uT", ub, c)], writes=[("pb", pbi)])
                    S.op("act", lambda e, c8=c8, pbi=pbi, n=n: e.copy(Braw[:, 4 * c8:4 * c8 + 4, :, :n], pb_[pbi][:, :, :, :n]), reads=[("pb", pbi)], writes=["Braw"])
                cosv = COS[:, q0:q0 + QH, :n]; sinv = SIN[:, q0:q0 + QH, :n]; mtv = MT[:, q0:q0 + QH, :n]
                br_, bi_ = Braw[:, :, 0, :n], Braw[:, :, 1, :n]
                S.op("dve", lambda e, cosv=cosv, br_=br_, n=n, BRv=BRv, BIv=BIv, tmv=tmv: e.tensor_tensor(BRv, br_, cosv, ALU.mult), reads=["Braw", "COS"], writes=["BR"])
                S.op("pool", lambda e, sinv=sinv, bi_=bi_, n=n, BRv=BRv, BIv=BIv, tmv=tmv: e.tensor_tensor(tmv, bi_, sinv, ALU.mult), reads=["Braw", "SIN"], writes=["tmpb"])
                S.op("dve", lambda e, n=n, BRv=BRv, BIv=BIv, tmv=tmv: e.tensor_tensor(BRv, BRv, tmv, ALU.add), reads=["BR", "tmpb"], writes=["BR"])
                S.op("dve", lambda e, cosv=cosv, bi_=bi_, n=n, BRv=BRv, BIv=BIv, tmv=tmv: e.tensor_tensor(BIv, bi_, cosv, ALU.mult), reads=["Braw", "COS"], writes=["BI"])
                S.op("pool", lambda e, sinv=sinv, br_=br_, n=n, BRv=BRv, BIv=BIv, tmv=tmv: e.tensor_tensor(tmv, br_, sinv, ALU.mult), reads=["Braw", "SIN", "BR"], writes=["tmpb"])
                S.op("dve", lambda e, n=n, BRv=BRv, BIv=BIv, tmv=tmv: e.tensor_tensor(BIv, BIv, tmv, ALU.subtract), reads=["BI", "tmpb"], writes=["BI"])
                S.op("dve", lambda e, q0=q0, BRv=BRv: e.tensor_tensor(BRv[:, :, 0], BRv[:, :, 0], fr[:, q0:q0 + QH], ALU.add), reads=["BR", "fr"], writes=["BR"])
                S.op("dve", lambda e, q0=q0, BIv=BIv: e.tensor_tensor(BIv[:, :, 0], BIv[:, :, 0], fi[:, q0:q0 + QH], ALU.add), reads=["BI", "fi"], writes=["BI"])
                mt2 = (MT if n == 64 else MT16)[:, q0:q0 + QH, :].rearrange("p q t -> p (q t)")
                S.op("dve", lambda e, mt2=mt2, F2=F2: e.tensor_tensor_scan(F2(BR), mt2, F2(BR), 0.0, ALU.mult, ALU.add), reads=["BR", "MT"], writes=["BR"])
                S.op("dve", lambda e, mt2=mt2, F2=F2: e.tensor_tensor_scan(F2(BI), mt2, F2(BI), 0.0, ALU.mult, ALU.add), reads=["BI", "MT"], writes=["BI"])
                TA, TB = Braw[:, :, 0, :n], Braw[:, :, 1, :n]
                S.op("dve", lambda e, TA=TA, cosv=cosv, n=n, BRv=BRv: e.tensor_tensor(TA, BRv, cosv, ALU.mult), reads=["BR", "COS", "Braw"], writes=["TA"])
                S.op("pool", lambda e, TB=TB, sinv=sinv, n=n, BIv=BIv: e.tensor_tensor(TB, BIv, sinv, ALU.mult), reads=["BI", "SIN", "Braw"], writes=["TB"])
                S.op("dve", lambda e, TA=TA, TB=TB, n=n: e.tensor_tensor(XR[:, :, :n], TA, TB, ALU.subtract), reads=["TA", "TB"], writes=["XR"])
                S.op("dve", lambda e, TA=TA, TB=TB, q0=q0, n=n: e.tensor_tensor(xpr[:, q0:q0 + QH], TA[:, :, n - 1], TB[:, :, n - 1], ALU.subtract), reads=["TA", "TB"], writes=["xpr"])
                S.op("dve", lambda e, TA=TA, sinv=sinv, n=n, BRv=BRv: e.tensor_tensor(TA, BRv, sinv, ALU.mult), reads=["BR", "SIN", "XR", "xpr"], writes=["TA"])
                S.op("pool", lambda e, TB=TB, cosv=cosv, n=n, BIv=BIv: e.tensor_tensor(TB, BIv, cosv, ALU.mult), reads=["BI", "COS", "XR", "xpr"], writes=["TB"])
                S.op("dve", lambda e, TA=TA, TB=TB, n=n: e.tensor_tensor(XI[:, :, :n], TA, TB, ALU.add), reads=["TA", "TB"], writes=["XI"])
                S.op("dve", lambda e, TA=TA, TB=TB, q0=q0, n=n: e.tensor_tensor(xpi[:, q0:q0 + QH], TA[:, :, n - 1], TB[:, :, n - 1], ALU.add), reads=["TA", "TB"], writes=["xpi"])

                def cm(e, q0=q0, n=n):
                    for q in range(QH):
                        e.matmul(py[:n, (q0 + q) * 32:(q0 + q + 1) * 32], XR[:, q, :n], CB[:, q0 + q, 0, :], start=True, stop=False)
                        ins = e.matmul(py[:n, (q0 + q) * 32:(q0 + q + 1) * 32], XI[:, q, :n], CB[:, q0 + q, 1, :], start=False, stop=True)
                    return ins
                S.op("pe", cm, reads=["XR", "XI", "CB"], writes=["py"])
                S.op("dve", lambda e: e.tensor_copy(Braw[:, 0, 0, 0:1], Braw[:, 0, 0, 0:1]), reads=[], writes=["Braw", "TA", "TB"])
            S.op("dve", lambda e, n=n, utok=utok: e.tensor_tensor(yt[:n, :], utok[:n, :], dT[:n, :], ALU.mult), reads=[("utok", ub), dTk], writes=["yt"])
            S.op("dve", lambda e, n=n: e.tensor_tensor(yt[:n, :], yt[:n, :], py[:n, :], ALU.add), reads=["yt", "py"], writes=["yt"])
            S.op("act", lambda e, n=n: e.activation(yb[:n, :], yt[:n, :], AF.Gelu), reads=["yt"], writes=["yb"])
            S.dma("sp", lambda e, r0=r0, n=n: e.dma_start(out=self.s5_y[r0:r0 + n, :], in_=yb[:n, :]), reads=["yb"], writes=[("dram", "s5_y")])
            if last:
                for (srct, dsto, kk) in [(xpr, self.o_s5r, "xpr"), (xpi, self.o_s5i, "xpi")]:
                    S.op("pe", lambda e, srct=srct: e.matmul(pxo[:], srct[:], identf[:], start=True, stop=True), reads=[kk, "identf"], writes=["pxo"])
                    S.op("act", lambda e: e.copy(xo[:], pxo[:]), reads=["pxo"], writes=["xo"])
                    S.dma("sp", lambda e, dsto=dsto, slot=slot: e.dma_start(out=dsto[l, slot].rearrange("(q g) p -> q (g p)", g=2), in_=xo[:]), reads=["xo"], writes=[("dram", dsto.name)])
        st.done()


    def ssd(self, l):
        S, nc = self.S, self.nc
        st = self.stage()
        sb, ps = st.sb, st.ps
        identf = sb([128, 128], F32); identb = sb([128, 128], BF16); tri = sb([64, 64], F32); ones = sb([64, 64], F32)
        sel = {64: sb([64, 128], F32), 16: sb([64, 128], F32)}
        S.dma("sp", lambda e: e.dma_start(out=identf[:], in_=self.cst["ident_in"]), writes=["identf"])
        S.op("dve", lambda e: e.tensor_copy(identb[:], identf[:]), reads=["identf"], writes=["identb"])
        S.dma("sp", lambda e: e.dma_start(out=tri[:], in_=self.cst["tri_in"]), writes=["tri"])
        S.op("dve", lambda e: e.memset(ones[:], 1.0), writes=["ones"])
        for LL in (64, 16):
            S.op("dve", lambda e, LL=LL: e.tensor_copy(sel[LL][:, :], identf[0:64, LL - 1:LL].to_broadcast([64, 128])), reads=["identf"], writes=["sel%d" % LL])
        cw = sb([128, 48, 4], F32); cb = sb([128, 48], F32)
        for w in range(4):
            S.dma("sp", lambda e, w=w: e.dma_start(out=cw[:, :, w], in_=self.sw["ssd_conv_w"][l, w].rearrange("(c p) -> p c", p=128), allow_slow_non_contiguous=True), writes=["cw%d" % w])
        S.dma("sp", lambda e: e.dma_start(out=cb[:], in_=self.sw["ssd_conv_b"][l].rearrange("(c p) -> p c", p=128), allow_slow_non_contiguous=True), writes=["cb"])
        cwk = ["cw%d" % w for w in range(4)]
        dtb, dtbk = self.load_row_bcast(st, self.sw["ssd_dt_bias"][l:l + 1, :], 64, 64)
        aneg, alk = self.load_row_bcast(st, self.sw["ssd_a_log"][l:l + 1, :], 64, 64)
        dsk, dskk = self.load_row_bcast(st, self.sw["ssd_d"][l:l + 1, :], 64, 64)
        ng, ngk = self.load_row_bcast(st, self.sw["ssd_norm"][l:l + 1, :], 4096, 64)
        S.op("act", lambda e: e.activation(aneg[:], aneg[:], AF.Exp), reads=[alk], writes=[alk])
        S.op("dve", lambda e: e.tensor_scalar(aneg[:], aneg[:], -1.0, None, ALU.mult), reads=[alk], writes=[alk])
        xTs = [sb([128, 48, 80], BF16) for _ in range(2)]; acc = sb([128, 48, 64], F32); ctmp = sb([128, 48, 64], F32); xcT = sb([128, 48, 64], BF16)
        xtok = sb([64, 4096], BF16); btok = sb([64, 1024], BF16); zt = sb([64, 4096], BF16)
        dtrs = [sb([64, 64], F32) for _ in range(2)]; dtx = sb([64, 64], F32); dta_ = sb([64, 64], F32); t64 = sb([64, 64], F32); dt = sb([64, 64], F32)
        cs_col = sb([64, 64], F32); ecs = sb([64, 64], F32); wdec = sb([64, 64], F32); csl = sb([128, 64], F32); ecl = sb([128, 64], F32)
        Rg = sb([64, 8, 64], F32); d1 = sb([64, 8, 64], F32); cbm = sb([64, 8, 64], F32); M = sb([64, 64, 64], BF16)
        xdt = sb([64, 4096], BF16); xdtw = sb([64, 4096], BF16)
        hT = sb([128, 4096], F32); hTb = sb([128, 4096], BF16)
        yv = sb([64, 4096], F32); ytmp = sb([64, 512], F32); sz = sb([64, 4096], BF16); ssq = sb([64, 8], F32); yb = sb([64, 4096], BF16)
        hio = sb([128, 4, 128], F32)
        p_tr = ps([64, 2048], BF16); p_small = ps([128, 64], F32); p_cs = ps([64, 8, 64], F32); p_cb = ps([64, 8, 64], F32)
        py = ps([64, 512], F32); pys = ps([64, 512], F32); ph = ps([128, 512], F32)
        v3 = lambda t, L: t[:L, :].rearrange("p (h q) -> p h q", q=64)
        chs = self.chunks()

        def ssd_loads(ci):
            (si_, r0_, L_, _, _) = chs[ci]
            r00_, _, p0_, _ = self.seqs[si_]
            prow_ = p0_ + 3 + (r0_ - r00_)
            NR_ = L_ + 16
            xb_ = ci % 2
            for c in range(48):
                S.dma("sp", lambda e, c=c, prow_=prow_, NR_=NR_, xb_=xb_: e.dma_start(out=xTs[xb_][:, c, :NR_], in_=self.xpad[prow_ - 16:prow_ - 16 + NR_, c * 128:(c + 1) * 128], transpose=True),
                      reads=[("dram", "xpad")], writes=[("xT", xb_, c)])
            S.dma("act", lambda e, r0_=r0_, L_=L_, xb_=xb_: e.dma_start(out=dtrs[xb_][:L_, :], in_=self.dtf[r0_:r0_ + L_, :]), reads=[("dram", "dtf")], writes=[("dtr", xb_)])
        for ci, (si, r0, L, first, last) in enumerate(chs):
            r00, ln, p0, slot = self.seqs[si]
            prow = p0 + 3 + (r0 - r00)
            NR = L + 16
            if first:
                if slot == 0:
                    S.op("dve", lambda e: e.memset(hT[:], 0.0), writes=["hT"])
                else:
                    hv = self.st_ssd[l, slot - 1].rearrange("(k q) p n -> (q p) k n", q=2)
                    for k4 in range(8):
                        S.dma("sp", lambda e, k4=k4, hv=hv: e.dma_start(out=hio[:], in_=hv[:, 4 * k4:4 * k4 + 4, :]), reads=["hio"], writes=["hio"])

                        def trh(e):
                            for k in range(4):
                                ins = e.matmul(ph[:, k * 128:(k + 1) * 128], hio[:, k, :], identf[:], start=True, stop=True)
                            return ins
                        S.op("pe", trh, reads=["hio", "identf"], writes=["ph"])
                        S.op("act", lambda e, k4=k4: e.copy(hT[:, k4 * 512:(k4 + 1) * 512], ph[:]), reads=["ph"], writes=["hT"])
                S.op("act", lambda e: e.copy(hTb[:], hT[:]), reads=["hT"], writes=["hTb"])
            xb = ci % 2
            xT, dtr = xTs[xb], dtrs[xb]
            if ci == 0:
                ssd_loads(0)
            S.dma("act", lambda e, r0=r0, L=L: e.dma_start(out=zt[:L, :], in_=self.pj_z[r0:r0 + L, :]), reads=[("dram", "pj_z")], writes=["zt"])
            if ci + 1 < len(chs):
                ssd_loads(ci + 1)
            xk = [("xT", xb, c) for c in range(48)]
            bw = lambda w, L=L: cw[:, :, w:w + 1].to_broadcast([128, 48, L])
            S.op("dve", lambda e, L=L, bw=bw, xT=xT: e.tensor_tensor(acc[:, :, :L], xT[:, :, 13:13 + L], bw(0), ALU.mult), reads=xk + cwk, writes=["acc"])
            for w in range(1, 4):
                S.op("pool", lambda e, L=L, bw=bw, w=w, xT=xT: e.tensor_tensor(ctmp[:, :, :L], xT[:, :, 13 + w:13 + w + L], bw(w), ALU.mult), reads=xk + cwk, writes=["ctmp"])
                S.op("dve", lambda e, L=L: e.tensor_tensor(acc[:, :, :L], acc[:, :, :L], ctmp[:, :, :L], ALU.add), reads=["acc", "ctmp"], writes=["acc"])
            S.op("dve", lambda e, L=L: e.tensor_tensor(acc[:, :, :L], acc[:, :, :L], cb[:].unsqueeze(2).to_broadcast([128, 48, L]), ALU.add), reads=["acc", "cb"], writes=["acc"])
            S.op("act", lambda e, L=L: e.activation(xcT[:, :, :L], acc[:, :, :L], AF.Silu), reads=["acc"], writes=["xcT"])
            for rnd in range(2):
                def trx(e, rnd=rnd, L=L):
                    for k in range(16):
                        ins = e.transpose(p_tr[:L, k * 128:(k + 1) * 128], xcT[:, rnd * 16 + k, :L], identb[:])
                    return ins
                S.op("pe", trx, reads=["xcT", "identb"], writes=["p_tr"])
                S.op("act", lambda e, rnd=rnd, L=L: e.copy(xtok[:L, rnd * 2048:(rnd + 1) * 2048], p_tr[:L, :]), reads=["p_tr"], writes=[("xtok", rnd)])

            def trb(e, L=L):
                for k in range(8):
                    ins = e.transpose(p_tr[:L, k * 128:(k + 1) * 128], xcT[:, 32 + k, :L], identb[:])
                return ins
            S.op("pe", trb, reads=["xcT", "identb"], writes=["p_tr"])
            S.op("act", lambda e, L=L: e.copy(btok[:L, :], p_tr[:L, 0:1024]), reads=["p_tr"], writes=["btok"])
            xtk = [("xtok", 0), ("xtok", 1)]
            S.op("dve", lambda e, L=L, dtr=dtr: e.tensor_tensor(dtx[:L, :], dtr[:L, :], dtb[:L, :], ALU.add), reads=[("dtr", xb), dtbk], writes=["dtx"])
            S.op("act", lambda e, L=L: e.activation(t64[:L, :], dtx[:L, :], AF.Abs), reads=["dtx"], writes=["t64"])
            S.op("act", lambda e, L=L: e.activation(t64[:L, :], t64[:L, :], AF.Exp, scale=-1.0), reads=["t64"], writes=["t64"])
            S.op("act", lambda e, L=L: e.activation(t64[:L, :], t64[:L, :], AF.Ln, bias=1.0), reads=["t64"], writes=["t64"])
            S.op("dve", lambda e, L=L: e.tensor_scalar(dt[:L, :], dtx[:L, :], 0.0, None, ALU.max), reads=["dtx"], writes=["dt"])
            S.op("dve", lambda e, L=L: e.tensor_tensor(dt[:L, :], dt[:L, :], t64[:L, :], ALU.add), reads=["dt", "t64"], writes=["dt"])
            S.op("dve", lambda e, L=L: e.tensor_tensor(dta_[:L, :], dt[:L, :], aneg[:L, :], ALU.mult), reads=["dt", alk], writes=["dta"])
            S.op("pe", lambda e, L=L: e.matmul(p_small[:L, :], tri[:L, :L], dta_[:L, :], start=True, stop=True), reads=["tri", "dta"], writes=["p_small"])
            S.op("act", lambda e, L=L: e.copy(cs_col[:L, :], p_small[:L, :]), reads=["p_small"], writes=["cs_col"])
            S.op("act", lambda e, L=L: e.activation(ecs[:L, :], cs_col[:L, :], AF.Exp), reads=["cs_col"], writes=["ecs"])
            S.op("pe", lambda e, L=L: e.matmul(p_small[:, :], sel[L][:L, :], cs_col[:L, :], start=True, stop=True), reads=["sel%d" % L, "cs_col"], writes=["p_small"])
            S.op("act", lambda e: e.copy(csl[:], p_small[:]), reads=["p_small"], writes=["csl"])
            S.op("act", lambda e: e.activation(ecl[:], csl[:], AF.Exp), reads=["csl"], writes=["ecl"])
            S.op("dve", lambda e, L=L: e.tensor_tensor(wdec[:L, :], csl[:L, :], cs_col[:L, :], ALU.subtract), reads=["csl", "cs_col"], writes=["wdec"])
            S.op("act", lambda e, L=L: e.activation(wdec[:L, :], wdec[:L, :], AF.Exp), reads=["wdec"], writes=["wdec"])
            def cbf(e, L=L):
                for g in range(8):
                    ins = e.matmul(p_cb[:L, g, :L], xcT[:, 32 + g, :L], xcT[:, 40 + g, :L], start=True, stop=True)
                return ins
            S.op("pe", cbf, reads=["xcT"], writes=["p_cb"])
            S.op("dve", lambda e, L=L: e.tensor_tensor(cbm[:L, :, :L], p_cb[:L, :, :L], tri[:L, :L].unsqueeze(1).to_broadcast([L, 8, L]), ALU.mult), reads=["p_cb", "tri"], writes=["cbm"])
            S.op("dve", lambda e, L=L: e.tensor_tensor(v3(xdt, L), v3(xtok, L), dt[:L, :].unsqueeze(2).to_broadcast([L, 64, 64]), ALU.mult), reads=xtk + ["dt"], writes=["xdt"])
            S.op("pool", lambda e, L=L: e.tensor_tensor(v3(xdtw, L), v3(xdt, L), wdec[:L, :].unsqueeze(2).to_broadcast([L, 64, 64]), ALU.mult), reads=["xdt", "wdec"], writes=["xdtw"])
            for g in range(8):
                hs = slice(8 * g, 8 * g + 8)
                S.op("dve", lambda e, L=L, hs=hs: e.tensor_tensor(Rg[:L, :, :L], dta_[:L, hs].unsqueeze(2).to_broadcast([L, 8, L]), tri[:L, :L].unsqueeze(1).to_broadcast([L, 8, L]), ALU.mult),
                     reads=["dta", "tri"], writes=["Rg"])
                S.op("pe", lambda e, L=L: e.matmul(p_cs[:L, :, :L], ones[:L, :L], Rg[:L, :, :L], start=True, stop=True), reads=["ones", "Rg"], writes=["p_cs"])
                S.op("dve", lambda e, L=L, hs=hs: e.tensor_tensor(d1[:L, :, :L], p_cs[:L, :, :L], cs_col[:L, hs].unsqueeze(2).to_broadcast([L, 8, L]), ALU.subtract), reads=["p_cs", "cs_col"], writes=["d1"])
                S.op("dve", lambda e, L=L: e.tensor_scalar(d1[:L, :, :L], d1[:L, :, :L], 0.0, None, ALU.min), reads=["d1"], writes=["d1"])
                S.op("act", lambda e, L=L: e.activation(d1[:L, :, :L], d1[:L, :, :L], AF.Exp), reads=["d1"], writes=["d1"])
                S.op("dve", lambda e, L=L, g=g, hs=hs: e.tensor_tensor(M[:L, hs, :L], d1[:L, :, :L], cbm[:L, g:g + 1, :L].to_broadcast([L, 8, L]), ALU.mult), reads=["d1", "cbm"], writes=[("M", g)])

                def ym(e, L=L, g=g):
                    for r in range(8):
                        h = 8 * g + r
                        ins = e.matmul(py[:L, r * 64:(r + 1) * 64], M[:L, h, :L], xdt[:L, h * 64:(h + 1) * 64], start=True, stop=True)
                    return ins
                S.op("pe", ym, reads=[("M", g), "xdt"], writes=["py"])
                S.op("pe", lambda e, L=L, g=g: e.matmul(pys[:L, :], xcT[:, 40 + g, :L], hTb[:, g * 512:(g + 1) * 512], start=True, stop=True), reads=["xcT", "hTb"], writes=["pys"])
                S.op("dve", lambda e, L=L, hs=hs: e.tensor_tensor(ytmp[:L, :].rearrange("p (r q) -> p r q", q=64), pys[:L, :].rearrange("p (r q) -> p r q", q=64),
                                                              ecs[:L, hs].unsqueeze(2).to_broadcast([L, 8, 64]), ALU.mult), reads=["pys", "ecs"], writes=["ytmp"])
                S.op("dve", lambda e, L=L, g=g: e.tensor_tensor(yv[:L, g * 512:(g + 1) * 512], ytmp[:L, :], py[:L, :], ALU.add), reads=["ytmp", "py"], writes=[("yv", g)])
                S.op("pe", lambda e, L=L, g=g: e.matmul(ph[:, :], btok[:L, g * 128:(g + 1) * 128], xdtw[:L, g * 512:(g + 1) * 512], start=True, stop=True), reads=["btok", "xdtw"], writes=["ph"])
                hg = hT[:, g * 512:(g + 1) * 512]
                S.op("pool", lambda e, hg=hg, hs=hs: e.tensor_tensor(hg.rearrange("p (r q) -> p r q", q=64), hg.rearrange("p (r q) -> p r q", q=64),
                                                                    ecl[:, hs].unsqueeze(2).to_broadcast([128, 8, 64]), ALU.mult), reads=["hT", "ecl", "hTb"], writes=["hT"])
                S.op("dve", lambda e, hg=hg: e.tensor_tensor(hg, hg, ph[:, :], ALU.add), reads=["hT", "ph"], writes=["hT"])
            S.op("act", lambda e: e.copy(hTb[:], hT[:]), reads=["hT", "pys"], writes=["hTb"])
            yk = [("yv", g) for g in range(8)]
            S.op("pool", lambda e, L=L: e.tensor_tensor(v3(xdtw, L), v3(xtok, L), dsk[:L, :].unsqueeze(2).to_broadcast([L, 64, 64]), ALU.mult), reads=xtk + [dskk, "xdtw", "ph"], writes=["xdtw"])
            S.op("dve", lambda e, L=L: e.tensor_tensor(yv[:L, :], yv[:L, :], xdtw[:L, :], ALU.add), reads=yk + ["xdtw"], writes=["yv"])
            S.op("act", lambda e, L=L: e.activation(sz[:L, :], zt[:L, :], AF.Silu), reads=["zt"], writes=["sz"])
            S.op("dve", lambda e, L=L: e.tensor_tensor(yv[:L, :], yv[:L, :], sz[:L, :], ALU.mult), reads=["yv", "sz"], writes=["yv"])
            S.op("act", lambda e, L=L: e.activation(xdt[:L, :], yv[:L, :], AF.Square), reads=["yv", "xdt", "py"], writes=["xdt"])
            S.op("dve", lambda e, L=L: e.tensor_reduce(ssq[:L, :], xdt[:L, :].rearrange("p (g q) -> p g q", g=8), AX.X, ALU.add), reads=["xdt"], writes=["ssq"])
            S.op("act", lambda e, L=L: e.activation(ssq[:L, :], ssq[:L, :], AF.Sqrt, bias=EPS, scale=1.0 / 512), reads=["ssq"], writes=["ssq"])
            S.op("dve", lambda e, L=L: e.reciprocal(ssq[:L, :], ssq[:L, :]), reads=["ssq"], writes=["ssq"])
            S.op("dve", lambda e, L=L: e.tensor_tensor(yv[:L, :].rearrange("p (g q) -> p g q", g=8), yv[:L, :].rearrange("p (g q) -> p g q", g=8),
                                                     ssq[:L, :].unsqueeze(2).to_broadcast([L, 8, 512]), ALU.mult), reads=["yv", "ssq"], writes=["yv"])
            S.op("dve", lambda e, L=L: e.tensor_tensor(yb[:L, :], yv[:L, :], ng[:L, :], ALU.mult), reads=["yv", ngk], writes=["yb"])
            S.dma("sp", lambda e, r0=r0, L=L: e.dma_start(out=self.ssd_y[r0:r0 + L, :], in_=yb[:L, :]), reads=["yb"], writes=[("dram", "ssd_y")])
            if last:
                ov = self.o_ssd[l, slot].rearrange("(k q) p n -> (q p) k n", q=2)
                for k4 in range(8):
                    def tro(e, k4=k4):
                        for k in range(4):
                            ins = e.matmul(ph[:, k * 128:(k + 1) * 128], hT[:, (4 * k4 + k) * 128:(4 * k4 + k + 1) * 128], identf[:], start=True, stop=True)
                        return ins
                    S.op("pe", tro, reads=["hT", "identf"], writes=["ph"])
                    S.op("act", lambda e: e.copy(hio[:].rearrange("p k n -> p (k n)"), ph[:]), reads=["ph", "hio"], writes=["hio"])
                    S.dma("sp", lambda e, k4=k4, ov=ov: e.dma_start(out=ov[:, 4 * k4:4 * k4 + 4, :], in_=hio[:]), reads=["hio"], writes=[("dram", "o_ssd")])
        st.done()

    def merge(self, l):
        S = self.S
        st = self.stage()
        sb = st.sb
        rt = sb([128, D], F32); gl = sb([128, 2 * D], F32); sd = sb([128, D], F32); gb = sb([128, 3 * D], BF16)
        sg = sb([128, 3 * D], F32); tmp = sb([128, D], F32); ob = sb([128, D], BF16)
        for (r, n) in self.tiles():
            S.dma("sp", lambda e, r=r, n=n: e.dma_start(out=rt[:n, :], in_=self.br_ret[r:r + n, :]), reads=[("dram", "br_ret")], writes=["rt"])
            S.dma("act", lambda e, r=r, n=n: e.dma_start(out=gl[:n, :], in_=self.br_glu[r:r + n, :]), reads=[("dram", "br_glu")], writes=["gl"])
            S.dma("sp", lambda e, r=r, n=n: e.dma_start(out=sd[:n, :], in_=self.br_ssd[r:r + n, :]), reads=[("dram", "br_ssd")], writes=["sd"])
            S.dma("act", lambda e, r=r, n=n: e.dma_start(out=gb[:n, :], in_=self.pj_gate[r:r + n, :]), reads=[("dram", "pj_gate")], writes=["gb"])
            S.op("act", lambda e, n=n: e.activation(sg[:n, :], gb[:n, :], AF.Sigmoid), reads=["gb"], writes=["sg"])
            S.op("act", lambda e, n=n: e.activation(tmp[:n, :], gl[:n, D:2 * D], AF.Sigmoid), reads=["gl"], writes=["tmp"])
            S.op("dve", lambda e, n=n: e.tensor_tensor(tmp[:n, :], tmp[:n, :], gl[:n, 0:D], ALU.mult), reads=["tmp", "gl"], writes=["tmp"])
            S.op("dve", lambda e, n=n: e.tensor_tensor(tmp[:n, :], tmp[:n, :], sg[:n, D:2 * D], ALU.mult), reads=["tmp", "sg"], writes=["tmp"])
            S.op("pool", lambda e, n=n: e.tensor_tensor(rt[:n, :], rt[:n, :], sg[:n, 0:D], ALU.mult), reads=["rt", "sg"], writes=["rt"])
            S.op("pool", lambda e, n=n: e.tensor_tensor(sd[:n, :], sd[:n, :], sg[:n, 2 * D:3 * D], ALU.mult), reads=["sd", "sg"], writes=["sd"])
            S.op("dve", lambda e, n=n: e.tensor_tensor(tmp[:n, :], tmp[:n, :], rt[:n, :], ALU.add), reads=["tmp", "rt"], writes=["tmp"])
            S.op("dve", lambda e, n=n: e.tensor_tensor(ob[:n, :], tmp[:n, :], sd[:n, :], ALU.add), reads=["tmp", "sd"], writes=["ob"])
            S.dma("sp", lambda e, r=r, n=n: e.dma_start(out=self.merged[r:r + n, :], in_=ob[:n, :]), reads=["ob"], writes=[("dram", "merged")])
        st.done()

    def normadd(self, ysrc, g_post, g_next, final=False):
        S = self.S
        st = self.stage()
        sb = st.sb
        gp, gpk = self.load_row_bcast(st, g_post, D)
        if g_next is not None:
            gn, gnk = self.load_row_bcast(st, g_next, D)
        yt = sb([128, D], F32); xt = sb([128, D], F32); junk = sb([128, D], BF16); hb = sb([128, D], BF16)
        ss = sb([128, 1], F32); ss2 = sb([128, 1], F32)
        for (r, n) in self.tiles():
            S.dma("sp", lambda e, r=r, n=n: e.dma_start(out=yt[:n, :], in_=ysrc[r:r + n, :]), reads=[("dram", ysrc.name)], writes=["yt"])
            S.dma("act", lambda e, r=r, n=n: e.dma_start(out=xt[:n, :], in_=self.xres[r:r + n, :]), reads=[("dram", "xres")], writes=["xt"])
            S.op("dve", lambda e, n=n: e.memset(ss[:n, :], 0.0), writes=["ss"])
            S.op("act", lambda e, n=n: e.activation(junk[:n, :], yt[:n, :], AF.Square, accum_out=ss[:n, :]), reads=["yt", "ss"], writes=["junk", "ss"])
            S.op("act", lambda e, n=n: e.activation(ss[:n, :], ss[:n, :], AF.Sqrt, bias=EPS, scale=1.0 / D), reads=["ss"], writes=["ss"])
            S.op("dve", lambda e, n=n: e.reciprocal(ss[:n, :], ss[:n, :]), reads=["ss"], writes=["ss"])
            S.op("dve", lambda e, n=n: e.scalar_tensor_tensor(yt[:n, :], yt[:n, :], ss[:n, 0:1], gp[:n, :], ALU.mult, ALU.mult), reads=["yt", "ss", gpk], writes=["yt"])
            S.op("dve", lambda e, n=n: e.tensor_tensor(xt[:n, :], xt[:n, :], yt[:n, :], ALU.add), reads=["xt", "yt"], writes=["xt"])
            S.dma("sp", lambda e, r=r, n=n: e.dma_start(out=self.xres[r:r + n, :], in_=xt[:n, :]), reads=["xt"], writes=[("dram", "xres")])
            if final:
                S.dma("sp", lambda e, r=r, n=n: e.dma_start(out=self.y[r:r + n, :], in_=xt[:n, :]), reads=["xt"], writes=[("dram", "y")])
            if g_next is not None:
                S.op("dve", lambda e, n=n: e.memset(ss2[:n, :], 0.0), writes=["ss2"])
                S.op("act", lambda e, n=n: e.activation(junk[:n, :], xt[:n, :], AF.Square, accum_out=ss2[:n, :]), reads=["xt", "junk", "ss2"], writes=["junk", "ss2"])
                S.op("act", lambda e, n=n: e.activation(ss2[:n, :], ss2[:n, :], AF.Sqrt, bias=EPS, scale=1.0 / D), reads=["ss2"], writes=["ss2"])
                S.op("dve", lambda e, n=n: e.reciprocal(ss2[:n, :], ss2[:n, :]), reads=["ss2"], writes=["ss2"])
                S.op("dve", lambda e, n=n: e.scalar_tensor_tensor(hb[:n, :], xt[:n, :], ss2[:n, 0:1], gn[:n, :], ALU.mult, ALU.mult), reads=["xt", "ss2", gnk], writes=["hb"])
                S.dma("sp", lambda e, r=r, n=n: e.dma_start(out=self.hn[r:r + n, :], in_=hb[:n, :]), reads=["hb"], writes=[("dram", "hn")])
        st.done()

    def attention(self):
        S = self.S
        st = self.stage()
        sb, ps = st.sb, st.ps
        identf = sb([128, 128], F32); identb = sb([128, 128], BF16)
        S.dma("sp", lambda e: e.dma_start(out=identf[:], in_=self.cst["ident_in"]), writes=["identf"])
        S.op("dve", lambda e: e.tensor_copy(identb[:], identf[:]), reads=["identf"], writes=["identb"])
        kT = sb([128, 16, 256], BF16); v = sb([128, 2, D], BF16); qT = sb([128, 16, 128], BF16)
        pexp = sb([128, 4, 256], BF16); pT = sb([128, 8, 128], BF16); ob = sb([128, D], BF16)
        mx = sb([128, 4], F32); sm = sb([128, 4], F32)
        ps_sc = ps([128, 4, 256], F32); pt = ps([128, 8, 128], BF16); po = ps([128, D], F32)
        SC = 512 ** -0.5
        for (r0, ln, p0, slot) in self.seqs:
            for c in range(16):
                ksrc = self.kvp_bf[:, 0:D] if slot == 0 else self.kv_bf[slot, 0]
                S.dma("sp", lambda e, c=c, ksrc=ksrc: e.dma_start(out=kT[:, c, :], in_=ksrc[:, c * 128:(c + 1) * 128], transpose=True),
                      reads=[("dram", "kv_bf")], writes=[("kT", c)])
            vsrc = self.kvp_bf[:, D:2 * D] if slot == 0 else self.kv_bf[slot, 1]
            S.dma("sp", lambda e, vsrc=vsrc: e.dma_start(out=v[:], in_=vsrc.rearrange("(mc p) d -> p mc d", p=128)), reads=[("dram", "kv_bf")], writes=["v"])
            t = 0
            while t < ln:
                n = min(128, ln - t)
                rr = r0 + t
                for c in range(16):
                    S.dma("sp", lambda e, c=c, rr=rr, n=n: e.dma_start(out=qT[:, c, :n], in_=self.q_bf[rr:rr + n, c * 128:(c + 1) * 128], transpose=True),
                          reads=[("dram", "q_bf")], writes=[("qT", c)])

                def scm(e, n=n):
                    for h in range(4):
                        for dc in range(4):
                            ins = e.matmul(ps_sc[:n, h, :], qT[:, h * 4 + dc, :n], kT[:, h * 4 + dc, :], start=(dc == 0), stop=(dc == 3))
                    return ins
                S.op("pe", scm, reads=[("qT", c) for c in range(16)] + [("kT", c) for c in range(16)], writes=["ps_sc"])
                S.op("dve", lambda e, n=n: e.tensor_reduce(mx[:n, :], ps_sc[:n], AX.X, ALU.max), reads=["ps_sc"], writes=["mx"])
                S.op("dve", lambda e, n=n: e.tensor_scalar(mx[:n, :], mx[:n, :], -SC, None, ALU.mult), reads=["mx"], writes=["mx"])
                for h in range(4):
                    S.op("act", lambda e, n=n, h=h: e.activation(pexp[:n, h, :], ps_sc[:n, h, :], AF.Exp, bias=mx[:n, h:h + 1], scale=SC),
                         reads=["ps_sc", "mx"], writes=[("pexp", h)])
                S.op("dve", lambda e, n=n: e.tensor_reduce(sm[:n, :], pexp[:n], AX.X, ALU.add), reads=[("pexp", h) for h in range(4)], writes=[("sm", h) for h in range(4)])

                def trp(e, n=n):
                    for h in range(4):
                        for mc in range(2):
                            ins = e.transpose(pt[:, h * 2 + mc, :n], pexp[:n, h, mc * 128:(mc + 1) * 128], identb[:n, :n])
                    return ins
                S.op("pe", trp, reads=[("pexp", h) for h in range(4)] + ["identb"], writes=["pt"])
                S.op("act", lambda e, n=n: e.copy(pT[:, :, :n], pt[:, :, :n]), reads=["pt"], writes=["pT"])

                def om(e, n=n):
                    for h in range(4):
                        for mc in range(2):
                            ins = e.matmul(po[:n, h * 512:(h + 1) * 512], pT[:, h * 2 + mc, :n], v[:, mc, h * 512:(h + 1) * 512], start=(mc == 0), stop=(mc == 1))
                    return ins
                S.op("pe", om, reads=["pT", "v"], writes=["po"])
                smk = [("sm", h) for h in range(4)]
                S.op("dve", lambda e, n=n: e.reciprocal(sm[:n, :], sm[:n, :]), reads=smk, writes=smk)
                S.op("dve", lambda e, n=n: e.tensor_tensor(ob[:n, :].rearrange("p (h d) -> p h d", h=4), po[:n, :].rearrange("p (h d) -> p h d", h=4),
                                                         sm[:n, :].unsqueeze(2).to_broadcast([n, 4, 512]), ALU.mult), reads=["po"] + smk, writes=["ob"])
                S.dma("sp", lambda e, rr=rr, n=n: e.dma_start(out=self.att[rr:rr + n, :], in_=ob[:n, :]), reads=["ob"], writes=[("dram", "att")])
                t += n
        st.done()

    def evac_relu2(self):
        S = self.S

        def f(st, n, cw, ps, pb, ob, okey):
            if not hasattr(st, "r2"):
                st.r2 = st.sb([128, 512], F32, "r2")
            r2 = st.r2
            S.op("act", lambda e: e.activation(r2[:n, :cw], ps[:n, :cw], AF.Relu), reads=[("lps", pb)], writes=["r2"])
            S.op("dve", lambda e: e.tensor_tensor(ob[:n, :cw], r2[:n, :cw], r2[:n, :cw], ALU.mult), reads=["r2"], writes=[okey])
        return f

    def conv_out(self, l):
        S = self.S
        for (r0, ln, p0, slot) in self.seqs:
            S.dma("pool", lambda e, p0=p0, ln=ln, slot=slot: e.dma_start(out=self.o_conv[l, slot], in_=self.xpad[p0 + ln:p0 + ln + 3, :]),
                  reads=[("dram", "xpad")], writes=[("dram", "o_conv")])
        S.barrier()
        S.emit()

    def conv_init(self, l):
        S = self.S
        st = self.stage()
        z = st.sb([96, 192], BF16)
        S.op("dve", lambda e: e.memset(z[:], 0.0), writes=["z"])
        S.dma("sp", lambda e: e.dma_start(out=self.xpad[16:19, :].rearrange("r (a f) -> (r a) f", f=192), in_=z[:]), reads=["z"], writes=[("dram", "xpad")])
        for j in range(NS):
            p0 = self.seqs[1 + j][2]
            S.dma("pool", lambda e, j=j, p0=p0: e.dma_start(out=self.xpad[p0:p0 + 3, :], in_=self.st_conv[l, j]), writes=[("dram", "xpad")])
        st.done()

    def mem_kv(self, l):
        S = self.S
        self.rms_stage(self.mem, self.memn, self.sw["norm_gains"][l, 6:7, :], 256)
        self.linear(self.memn, D, self.wb["w_xkv"][l], 2 * D, self.evac_store(self.mkv_f, F32), nrows=256)
        S.dma("sp", lambda e: e.dma_start(out=self.o_mk[l], in_=self.mkv_f[:, 0:D]), reads=[("dram", "mkv_f")], writes=[("dram", "o_mk")])
        S.dma("sp", lambda e: e.dma_start(out=self.o_mv[l], in_=self.mkv_f[:, D:2 * D]), reads=[("dram", "mkv_f")], writes=[("dram", "o_mv")])
        for kv in range(2):
            if kv == 0:
                S.dma("pool", lambda e: e.dma_start(out=self.kvp_bf, in_=self.mkv_f), reads=[("dram", "mkv_f")], writes=[("dram", "kv_bf")])
            src = self.st_mk if kv == 0 else self.st_mv
            for j in range(NS):
                S.dma("pool", lambda e, kv=kv, j=j, src=src: e.dma_start(out=self.kv_bf[1 + j, kv], in_=src[l, j]), writes=[("dram", "kv_bf")])
        S.barrier()
        S.emit()

    def build(self, upto="all"):
        S = self.S
        flags = upto.split(",")
        G = lambda l, i: self.sw["norm_gains"][l, i:i + 1, :]
        self.cast_weights()
        self.init_copy()
        if "castonly" in flags:
            S.barrier(); S.emit(); S.close()
            return self.nc
        self.rms_stage(self.xres, self.hn, G(0, 0), self.NT)
        for l in range(2):
            self.mem_kv(l)
            self.conv_init(l)
            cbs = [(c, 512) for c in range(0, C_DT, 512)] + [(C_DT, 64)] + [(c, 512) for c in range(C_GATE, INC, 512)]
            self.linear(self.hn, D, self.wb["w_in"][l], INC, self.evac_inproj(), colblocks=cbs)
            self.conv_out(l)
            if "noret" not in flags:
                self.retention(l)
            if "nos5" not in flags:
                self.s5(l)
            if "nossd" not in flags:
                self.ssd(l)
            if "mix" in flags:
                break
            self.linear(self.ret_y, D, self.wb["w_ret_o"][l], D, self.evac_store(self.br_ret, F32))
            self.linear(self.s5_y, D, self.wb["w_s5_glu"][l], 2 * D, self.evac_store(self.br_glu, F32))
            self.linear(self.ssd_y, 2 * D, self.wb["w_ssd_out"][l], D, self.evac_store(self.br_ssd, F32))
            self.merge(l)
            self.linear(self.merged, D, self.wb["w_mix_out"][l], D, self.evac_store(self.lin_out, F32))
            self.normadd(self.lin_out, G(l, 1), G(l, 2))
            self.linear(self.hn, D, self.wb["w_xq"][l], D, self.evac_store(self.q_bf, BF16))
            self.attention()
            self.linear(self.att, D, self.wb["w_xo"][l], D, self.evac_store(self.lin_out, F32))
            self.normadd(self.lin_out, G(l, 3), G(l, 4))
            self.linear(self.hn, D, self.wb["w_up"][l], 4 * D, self.evac_store(self.hmlp, BF16, func=self.evac_relu2()))
            self.linear(self.hmlp, 4 * D, self.wb["w_down"][l], D, self.evac_store(self.lin_out, F32))
            self.normadd(self.lin_out, G(l, 5), G(l + 1, 0) if l == 0 else None, final=(l == 1))
            if "l0" in flags:
                break
        if "l0" in flags or "mix" in flags:
            st = self.stage()
            xt = st.sb([128, D], F32)
            for (r, n) in self.tiles():
                S.dma("sp", lambda e, r=r, n=n: e.dma_start(out=xt[:n, :], in_=self.xres[r:r + n, :]), reads=[("dram", "xres")], writes=["xt"])
                S.dma("sp", lambda e, r=r, n=n: e.dma_start(out=self.y[r:r + n, :], in_=xt[:n, :]), reads=["xt"], writes=[("dram", "y")])
            st.done()
        S.barrier()
        S.emit()
        S.close()
        return self.nc


def make_in_maps(inputs, TP, cores):
    consts = host_consts(TP)
    maps = []
    for c in cores:
        b = c % 2
        sl = slice(NS * c, NS * (c + 1))
        m = {}
        m["x_in"] = np.concatenate([inputs["x_prompt"][b, :TP], inputs["x_sample"][sl].reshape(NS * SL, D)], axis=0)
        m["mem_in"] = inputs["mem_prompt"][b]
        m["st_ret"] = inputs["state_ret"][:, sl]
        m["st_s5r"] = inputs["state_s5_re"][:, sl]
        m["st_s5i"] = inputs["state_s5_im"][:, sl]
        m["st_ssd"] = inputs["state_ssd"][:, sl]
        m["st_conv"] = inputs["cache_ssd_conv"][:, sl]
        m["st_mk"] = inputs["cache_mem_k"][:, sl].reshape(2, NS, 256, D)
        m["st_mv"] = inputs["cache_mem_v"][:, sl].reshape(2, NS, 256, D)
        for k in WSHAPES:
            m[k] = inputs[k]
        for k in SMALLW:
            m[k] = inputs[k]
        m.update(consts)
        maps.append({k: np.ascontiguousarray(v, dtype=np.float32) for k, v in m.items()})
    return maps


KERNEL_FLAGS = "all"


def kernel(**inputs):
    TP = 8192
    inputs = {k: np.asarray(v) for k, v in inputs.items()}
    b = Builder(TP)
    nc = b.build(KERNEL_FLAGS)
    cores = list(range(8))
    maps = make_in_maps(inputs, TP, cores)
    res = run_bass_kernel_spmd(nc, maps, core_ids=cores)
    R = res.results
    f32 = np.float32
    y_p = np.stack([np.asarray(R[bb]["y"], f32)[:TP] for bb in range(2)])
    y_s = np.concatenate([np.asarray(R[c]["y"], f32)[TP:].reshape(NS, SL, D) for c in cores], axis=0)

    def pstate(name, shape):
        return np.stack([np.asarray(R[bb][name], f32)[:, 0] for bb in range(2)], axis=1).reshape(shape)

    def sstate(name, shape):
        return np.concatenate([np.asarray(R[c][name], f32)[:, 1:] for c in cores], axis=1).reshape(shape)
    p_ret = pstate("o_ret", (2, 2, RH, RDK, RDV))
    p_s5r = pstate("o_s5r", (2, 2, 128, 64))
    p_s5i = pstate("o_s5i", (2, 2, 128, 64))
    p_ssd = pstate("o_ssd", (2, 2, 64, 64, 128))
    p_conv = pstate("o_conv", (2, 2, 3, 6144))
    p_mk = np.stack([np.asarray(R[bb]["o_mk"], f32) for bb in range(2)], axis=1).reshape(2, 2, 256, 4, 512)
    p_mv = np.stack([np.asarray(R[bb]["o_mv"], f32) for bb in range(2)], axis=1).reshape(2, 2, 256, 4, 512)
    s_ret = sstate("o_ret", (2, 32, RH, RDK, RDV))
    s_s5r = sstate("o_s5r", (2, 32, 128, 64))
    s_s5i = sstate("o_s5i", (2, 32, 128, 64))
    s_ssd = sstate("o_ssd", (2, 32, 64, 64, 128))
    s_conv = sstate("o_conv", (2, 32, 3, 6144))
    return (y_p, y_s, p_ret, p_s5r, p_s5i, p_ssd, p_conv, p_mk, p_mv, s_ret, s_s5r, s_s5i, s_ssd, s_conv)
```

```python
import math
import numpy as np
import concourse.bass as bass
import concourse.mybir as mybir
from concourse.bass_utils import run_bass_kernel_spmd

F32 = mybir.dt.float32
BF16 = mybir.dt.bfloat16
I32 = mybir.dt.int32
AF = mybir.ActivationFunctionType
ALU = mybir.AluOpType
AX = mybir.AxisListType


class Sched:
    EPOCH = 24000
    NDMA = 16

    def __init__(self, nc):
        self.nc = nc
        self.engs = ["pe", "act", "dve", "pool", "sp"]
        self.items = {e: [] for e in self.engs}
        self.count = {e: 0 for e in self.engs}
        self.sems = {}
        self.dsems = {}
        self.dcount = {e: 0 for e in self.engs}
        self.seen = {e: {} for e in self.engs}
        self.res = {}
        self._semctx = []
        self.dlast = {}

    def _sem(self, key):
        d = self.sems if key[0] == "E" else self.dsems
        if key not in d:
            cm = self.nc.semaphore("s_%s_%s_%d" % key)
            d[key] = cm.__enter__()
            self._semctx.append(cm)
        return d[key]

    def _deps(self, reads, writes):
        ev = []
        for r in reads:
            st = self.res.get(r)
            if st and st[0] is not None:
                ev.append(st[0])
        for w in writes:
            st = self.res.get(w)
            if st:
                if st[0] is not None:
                    ev.append(st[0])
                ev.extend(st[1])
        return ev

    def _waits(self, eng, events):
        best = {}
        for (k, v) in events:
            if best.get(k, 0) < v:
                best[k] = v
        out = []
        for k, v in best.items():
            if self.seen[eng].get(k, 0) < v:
                self.seen[eng][k] = v
                out.append((k, v))
        return out

    def _mark(self, event, reads, writes):
        for r in reads:
            st = self.res.setdefault(r, [None, []])
            st[1].append(event)
        for w in writes:
            self.res[w] = [event, []]

    def op(self, eng, fn, reads=(), writes=()):
        waits = self._waits(eng, self._deps(reads, writes))
        n = self.count[eng]
        epoch, idx = divmod(n, self.EPOCH)
        self.count[eng] = n + 1
        key = ("E", eng, epoch)
        self._sem(key)
        event = (key, idx + 1)
        self.items[eng].append((waits, fn, key, 1))
        self._mark(event, reads, writes)
        return event

    def dma(self, eng, fn, reads=(), writes=()):
        n = self.dcount[eng]
        self.dcount[eng] = n + 1
        k, rnd = n % self.NDMA, n // self.NDMA
        key = ("D", eng, k)
        self._sem(key)
        ev = self._deps(reads, writes)
        if rnd > 0:
            ev.append((key, 16 * rnd))
        waits = self._waits(eng, ev)
        event = (key, 16 * (rnd + 1))
        self.dlast[key] = 16 * (rnd + 1)
        self.items[eng].append((waits, fn, key, 16))
        self._mark(event, reads, writes)
        return event

    def wait_all(self, eng, keys):
        ev = []
        for kk in keys:
            st = self.res.get(kk)
            if st:
                if st[0] is not None:
                    ev.append(st[0])
                ev.extend(st[1])
        waits = self._waits(eng, ev)
        self.items[eng].append((waits, None, None, 0))

    def barrier(self):
        ev = []
        for f in self.engs:
            n = self.count[f]
            if n > 0:
                epoch, idx = divmod(n - 1, self.EPOCH)
                ev.append((("E", f, epoch), idx + 1))
        for k, v in self.dlast.items():
            ev.append((k, v))
        for e in self.engs:
            self.items[e].append((self._waits(e, list(ev)), None, None, 0))

    def emit(self):
        nc = self.nc
        sched = self

        def run(engname, engine):
            for waits, fn, key, inc in sched.items[engname]:
                for (k, v) in waits:
                    engine.wait_ge(sched._sem(k), v)
                if fn is not None:
                    ins = fn(engine)
                    ins.then_inc(sched._sem(key), inc)

        with nc.Block() as block:
            @block.tensor
            def _(e):
                run("pe", e)

            @block.scalar
            def _(e):
                run("act", e)

            @block.vector
            def _(e):
                run("dve", e)

            @block.gpsimd
            def _(e):
                run("pool", e)

            @block.sync
            def _(e):
                run("sp", e)
        self.items = {e: [] for e in self.engs}

    def close(self):
        for cm in reversed(self._semctx):
            cm.__exit__(None, None, None)
        self._semctx = []


D = 2048
NS = 4
SL = 16
PAST = 1024
EPS = 1e-6
RH, RDK, RDV = 8, 128, 256
INC = 24640
C_Q, C_K, C_V, C_G, C_U, C_Z, C_XBC, C_DT, C_GATE = 0, 1024, 2048, 4096, 6144, 8192, 12288, 18432, 18496
GAMMA = [1.0 - 2.0 ** (-5 - h) for h in range(RH)]
WSHAPES = {"w_in": (D, INC), "w_ret_o": (D, D), "w_s5_glu": (D, 2 * D), "w_ssd_out": (2 * D, D), "w_mix_out": (D, D),
           "w_xq": (D, D), "w_xkv": (D, 2 * D), "w_xo": (D, D), "w_up": (D, 4 * D), "w_down": (4 * D, D)}
SMALLW = {"norm_gains": (7, D), "ret_gn": (D,), "s5_a_re": (128, 64), "s5_a_im": (128, 64), "s5_b_re": (128, 64, 16),
          "s5_b_im": (128, 64, 16), "s5_c_re": (128, 16, 64), "s5_c_im": (128, 16, 64), "s5_d": (D,), "s5_log_dt": (128,),
          "ssd_conv_w": (4, 6144), "ssd_conv_b": (6144,), "ssd_dt_bias": (64,), "ssd_a_log": (64,), "ssd_d": (64,),
          "ssd_norm": (4096,)}


def host_consts(TP):
    NT = TP + NS * SL
    pos = np.concatenate([np.arange(TP)] + [PAST + np.arange(SL)] * NS).astype(np.float32)
    half = 64
    inv = np.exp(-math.log(10000.0) * np.arange(half, dtype=np.float32) / half).astype(np.float32)
    ang = pos[:, None] * inv[None]
    cos, sin = np.cos(ang).astype(np.float32), np.sin(ang).astype(np.float32)
    cs = np.zeros((NT, 2, RH, 2, 64), np.float32)
    sn = np.zeros((NT, 2, RH, 2, 64), np.float32)
    for w in range(2):
        sc = 1.0 if w == 0 else RDK ** -0.5
        cs[:, w, :, :, :] = (cos * sc)[:, None, None, :]
        sn[:, w, :, 0, :] = (-sin * sc)[:, None, :]
        sn[:, w, :, 1, :] = (sin * sc)[:, None, :]
    lg = np.log1p(-np.exp2(-5.0 - np.arange(RH))).astype(np.float64)
    idx = np.arange(64)
    mask = np.exp(np.abs(idx[:, None] - idx[None, :])[:, None, :] * lg[None, :, None]).astype(np.float32)
    din = np.exp((idx + 1.0)[None, :] * lg[:, None])[None].repeat(128, 0).astype(np.float32)
    dup64 = np.exp((63.0 - idx)[:, None] * lg[None, :]).astype(np.float32)
    dup16 = np.exp((15.0 - np.arange(16))[:, None] * lg[None, :]).astype(np.float32)
    tv = np.arange(64, dtype=np.float32)[None].repeat(128, 0)
    ident = np.eye(128, dtype=np.float32)
    tri = (idx[:, None] <= idx[None, :]).astype(np.float32)
    return {"rope_cs": cs.reshape(NT, 2048), "rope_sn": sn.reshape(NT, 2048), "ret_mask": mask, "ret_din": din,
            "ret_dup64": dup64, "ret_dup16": dup16, "tvec": tv, "ident_in": ident, "tri_in": tri}


CONST_SHAPES = lambda NT: {"rope_cs": (NT, 2048), "rope_sn": (NT, 2048), "ret_mask": (64, 8, 64), "ret_din": (128, 8, 64),
                           "ret_dup64": (64, 8), "ret_dup16": (16, 8), "tvec": (128, 64), "ident_in": (128, 128),
                           "tri_in": (64, 64)}


from contextlib import ExitStack


class Builder:
    def __init__(self, TP, dbg_out=()):
        self.TP = TP
        self.NT = NT = TP + NS * SL
        self.nc = nc = bass.Bass("TRN2", target_bir_lowering=False)
        self.S = Sched(nc)
        self.uid = 0
        di = lambda name, shape, dt=F32: nc.dram_tensor(name, list(shape), dt, kind="ExternalInput").ap()
        do = lambda name, shape, dt=F32: nc.dram_tensor(name, list(shape), dt, kind="ExternalOutput").ap()
        ds = lambda name, shape, dt: nc.dram_tensor(name, list(shape), dt, kind=("ExternalOutput" if name in dbg_out else "Internal")).ap()
        self.x_in = di("x_in", (NT, D))
        self.mem = di("mem_in", (256, D))
        self.st_ret = di("st_ret", (2, NS, RH, RDK, RDV))
        self.st_s5r = di("st_s5r", (2, NS, 128, 64))
        self.st_s5i = di("st_s5i", (2, NS, 128, 64))
        self.st_ssd = di("st_ssd", (2, NS, 64, 64, 128))
        self.st_conv = di("st_conv", (2, NS, 3, 6144))
        self.st_mk = di("st_mk", (2, NS, 256, D))
        self.st_mv = di("st_mv", (2, NS, 256, D))
        self.w = {k: di(k, (2,) + v) for k, v in WSHAPES.items()}
        self.sw = {k: di(k, (2,) + v) for k, v in SMALLW.items()}
        self.cst = {k: di(k, v) for k, v in CONST_SHAPES(NT).items()}
        self.y = do("y", (NT, D))
        self.o_ret = do("o_ret", (2, 1 + NS, RH, RDK, RDV))
        self.o_s5r = do("o_s5r", (2, 1 + NS, 128, 64))
        self.o_s5i = do("o_s5i", (2, 1 + NS, 128, 64))
        self.o_ssd = do("o_ssd", (2, 1 + NS, 64, 64, 128))
        self.o_conv = do("o_conv", (2, 1 + NS, 3, 6144))
        self.o_mk = do("o_mk", (2, 256, D))
        self.o_mv = do("o_mv", (2, 256, D))
        self.wb = {k: ds(k + "_bf", (2,) + v, BF16) for k, v in WSHAPES.items()}
        self.xres = ds("xres", (NT, D), F32)
        self.hn = ds("hn", (NT, D), BF16)
        self.pj_qkvg = ds("pj_qkvg", (NT, 6144), BF16)
        self.pj_u = ds("pj_u", (NT, 2048), BF16)
        self.pj_z = ds("pj_z", (NT, 4096), BF16)
        self.pj_gate = ds("pj_gate", (NT, 6144), BF16)
        self.dtf = ds("dtf", (NT, 64), F32)
        self.xpad = ds("xpad", (16 + NT + 3 * (1 + NS) + 64, 6144), BF16)
        self.ret_y = ds("ret_y", (NT, D), BF16)
        self.s5_y = ds("s5_y", (NT, D), BF16)
        self.ssd_y = ds("ssd_y", (NT, 2 * D), BF16)
        self.br_ret = ds("br_ret", (NT, D), F32)
        self.br_glu = ds("br_glu", (NT, 2 * D), F32)
        self.br_ssd = ds("br_ssd", (NT, D), F32)
        self.merged = ds("merged", (NT, D), BF16)
        self.lin_out = ds("lin_out", (NT, D), F32)
        self.q_bf = ds("q_bf", (NT, D), BF16)
        self.att = ds("att", (NT, D), BF16)
        self.hmlp = ds("hmlp", (NT, 4 * D), BF16)
        self.memn = ds("memn", (256, D), BF16)
        self.mkv_f = ds("mkv_f", (256, 2 * D), F32)
        self.kvp_bf = ds("kvp_bf", (256, 2 * D), BF16)
        self.kv_bf = ds("kv_bf", (1 + NS, 2, 256, D), BF16)
        self.seqs = [(0, TP, 16, 0)] + [(TP + SL * j, SL, 16 + TP + 3 + (SL + 3) * j, 1 + j) for j in range(NS)]

    def name(self, p):
        self.uid += 1
        return "%s%d" % (p, self.uid)

    def stage(self):
        b = self

        class St:
            def __init__(s):
                s.stack = ExitStack()

            def sb(s, shape, dt=F32, nm="t"):
                return s.stack.enter_context(b.nc.sbuf_tensor(b.name(nm), list(shape), dt))

            def ps(s, shape, dt=F32, nm="p"):
                return s.stack.enter_context(b.nc.psum_tensor(b.name(nm), list(shape), dt))

            def done(s):
                b.S.barrier()
                b.S.emit()
                s.stack.close()
        return St()

    def tiles(self):
        out, r = [], 0
        while r < self.NT:
            n = min(128, self.NT - r)
            out.append((r, n))
            r += n
        return out

    def chunks(self):
        out = []
        for si, (r0, ln, _, _) in enumerate(self.seqs):
            c = min(64, ln)
            for k in range(ln // c):
                out.append((si, r0 + k * c, c, k == 0, k == ln // c - 1))
        return out

    def load_row_bcast(self, st, ap_row, width, npart=128, dt=F32, eng="sp"):
        t = st.sb([npart, width], dt, "rb")
        key = self.name("rbk")
        self.S.dma(eng, lambda e: e.dma_start(out=t[:], in_=ap_row.to_broadcast([npart, width])), writes=[key])
        return t, key

    def cast_weights(self):
        S = self.S
        for k, (rows, cols) in WSHAPES.items():
            step = max(1, (4 << 20) // cols)
            for l in range(2):
                for r in range(0, rows, step):
                    rr = min(step, rows - r)
                    S.dma("pool", lambda e, k=k, l=l, r=r, rr=rr: e.dma_start(out=self.wb[k][l, r:r + rr, :], in_=self.w[k][l, r:r + rr, :]),
                          writes=[("dram", k + "_bf")])
        S.barrier()
        S.emit()

    def init_copy(self):
        S = self.S
        S.dma("sp", lambda e: e.dma_start(out=self.xres, in_=self.x_in), writes=["xres"])
        for l in range(2):
            pass
        S.barrier()
        S.emit()

    def rms_stage(self, src, dst, gain_row, nrows):
        S = self.S
        st = self.stage()
        g, gk = self.load_row_bcast(st, gain_row, D)
        xt = [st.sb([128, D], F32, "xt") for _ in range(2)]
        hb = [st.sb([128, D], BF16, "hb") for _ in range(2)]
        junk = st.sb([128, D], BF16, "junk")
        ss = [st.sb([128, 1], F32, "ss") for _ in range(2)]
        r, i = 0, 0
        while r < nrows:
            n = min(128, nrows - r)
            b = i % 2
            S.dma("sp", lambda e, r=r, n=n, b=b: e.dma_start(out=xt[b][:n, :], in_=src[r:r + n, :]), reads=[("dram", src.name)], writes=[("xt", b)])
            S.op("dve", lambda e, n=n, b=b: e.memset(ss[b][:n, :], 0.0), writes=[("ss", b)])
            S.op("act", lambda e, n=n, b=b: e.activation(junk[:n, :], xt[b][:n, :], AF.Square, accum_out=ss[b][:n, :]), reads=[("xt", b), ("ss", b)], writes=["junk", ("ss", b)])
            S.op("act", lambda e, n=n, b=b: e.activation(ss[b][:n, :], ss[b][:n, :], AF.Sqrt, bias=EPS, scale=1.0 / D), reads=[("ss", b)], writes=[("ss", b)])
            S.op("dve", lambda e, n=n, b=b: e.reciprocal(ss[b][:n, :], ss[b][:n, :]), reads=[("ss", b)], writes=[("ss", b)])
            S.op("dve", lambda e, n=n, b=b: e.scalar_tensor_tensor(hb[b][:n, :], xt[b][:n, :], ss[b][:n, 0:1], g[:n, :], ALU.mult, ALU.mult),
                 reads=[("xt", b), ("ss", b), gk], writes=[("hb", b)])
            S.dma("sp", lambda e, r=r, n=n, b=b: e.dma_start(out=dst[r:r + n, :], in_=hb[b][:n, :]), reads=[("hb", b)], writes=[("dram", dst.name)])
            r += n
            i += 1
        st.done()

    def linear(self, A, K, W, N, evac, nrows=None, colblocks=None):
        S = self.S
        nrows = self.NT if nrows is None else nrows
        KC = K // 128
        TBL = {2048: 1024, 4096: 512, 8192: 256}[K]
        st = self.stage()
        NAT = 2 if K <= 4096 else 1
        ATs = [st.sb([128, KC, TBL + 64], BF16, "AT") for _ in range(NAT)]
        WT = [st.sb([128, KC, 512], BF16, "WT") for _ in range(2)]
        PS = [st.ps([128, 512], F32, "lps") for _ in range(2)]
        self.lin_st = st
        if colblocks is None:
            colblocks = [(c, min(512, N - c)) for c in range(0, N, 512)]
        Wv = W.rearrange("(kc p) n -> p kc n", p=128)
        widx, pidx = 0, 0
        blocks = []
        rb = 0
        while rb < nrows:
            nb = min(TBL, nrows - rb)
            if 0 < nrows - (rb + nb) <= 64:
                nb = nrows - rb
            blocks.append((rb, nb))
            rb += nb

        def load_at(bi):
            rb, nb = blocks[bi]
            ab = bi % NAT
            for kc in range(KC):
                S.dma("sp", lambda e, kc=kc, rb=rb, nb=nb, ab=ab: e.dma_start(out=ATs[ab][:, kc, :nb], in_=A[rb:rb + nb, kc * 128:(kc + 1) * 128], transpose=True),
                      reads=[("dram", A.name)], writes=[("AT", ab, kc)])
        load_at(0)
        for bi, (rb, nb) in enumerate(blocks):
            ab = bi % NAT
            AT = ATs[ab]
            if NAT == 2 and bi + 1 < len(blocks):
                load_at(bi + 1)
            for (c0, cw) in colblocks:
                wbuf = widx % 2
                widx += 1
                S.dma("act", lambda e, c0=c0, cw=cw, wbuf=wbuf: e.dma_start(out=WT[wbuf][:, :, :cw], in_=Wv[:, :, c0:c0 + cw]),
                      reads=[("dram", W.name)], writes=[("WT", wbuf)])
                t0 = 0
                while t0 < nb:
                    n = min(128, nb - t0)
                    pb = pidx % 2
                    pidx += 1

                    def mm(e, t0=t0, n=n, cw=cw, wbuf=wbuf, pb=pb, AT=AT):
                        for kc in range(KC):
                            ins = e.matmul(PS[pb][:n, :cw], AT[:, kc, t0:t0 + n], WT[wbuf][:, kc, :cw], start=(kc == 0), stop=(kc == KC - 1))
                        return ins
                    S.op("pe", mm, reads=[("AT", ab, kc) for kc in range(KC)] + [("WT", wbuf)], writes=[("lps", pb)])
                    evac(st, rb + t0, n, c0, cw, PS[pb], pb)
                    t0 += n
            if NAT == 1 and bi + 1 < len(blocks):
                load_at(bi + 1)
        st.done()

    def evac_store(self, dst, dt, coloff=0, func=None):
        S = self.S
        bufs = {}

        def evac(st, r0, n, c0, cw, ps, pb):
            if "ob" not in bufs:
                bufs["ob"] = [st.sb([128, 512], dt, "ob") for _ in range(2)]
                bufs["i"] = 0
            ob = bufs["ob"][bufs["i"] % 2]
            okey = ("ob", bufs["i"] % 2)
            bufs["i"] += 1
            if func is None:
                S.op("act", lambda e: e.copy(ob[:n, :cw], ps[:n, :cw]), reads=[("lps", pb)], writes=[okey])
            else:
                func(st, n, cw, ps, pb, ob, okey)
            S.dma("sp", lambda e: e.dma_start(out=dst[r0:r0 + n, coloff + c0:coloff + c0 + cw], in_=ob[:n, :cw]), reads=[okey], writes=[("dram", dst.name)])
        return evac

    def segs(self, r0, n):
        out = []
        for (s0, ln, p0, _) in self.seqs:
            a, b = max(r0, s0), min(r0 + n, s0 + ln)
            if a < b:
                out.append((a - r0, b - a, p0 + 3 + (a - s0)))
        return out

    def evac_inproj(self):
        S = self.S
        bufs = {}

        def evac(st, r0, n, c0, cw, ps, pb):
            if "ob" not in bufs:
                bufs["ob"] = [st.sb([128, 512], BF16, "ob") for _ in range(2)]
                bufs["of"] = st.sb([128, 64], F32, "of")
                bufs["i"] = 0
            ob = bufs["ob"][bufs["i"] % 2]
            okey = ("ob", bufs["i"] % 2)
            bufs["i"] += 1
            S.op("act", lambda e: e.copy(ob[:n, :cw], ps[:n, :cw]), reads=[("lps", pb)], writes=[okey])
            tgt = None
            if c0 < C_U:
                tgt, lc = self.pj_qkvg, c0
            elif c0 < C_Z:
                tgt, lc = self.pj_u, c0 - C_U
            elif c0 < C_XBC:
                tgt, lc = self.pj_z, c0 - C_Z
            elif c0 >= C_GATE:
                tgt, lc = self.pj_gate, c0 - C_GATE
            if tgt is not None:
                S.dma("sp", lambda e: e.dma_start(out=tgt[r0:r0 + n, lc:lc + cw], in_=ob[:n, :cw]), reads=[okey], writes=[("dram", tgt.name)])
            if C_XBC <= c0 < C_DT:
                for (o, cnt, prow) in self.segs(r0, n):
                    S.dma("sp", lambda e, o=o, cnt=cnt, prow=prow: e.dma_start(out=self.xpad[prow:prow + cnt, c0 - C_XBC:c0 - C_XBC + cw], in_=ob[o:o + cnt, :cw]),
                          reads=[okey], writes=[("dram", "xpad")])
            if c0 == C_DT:
                of = bufs["of"]
                S.op("dve", lambda e: e.tensor_copy(of[:n, :cw], ps[:n, :cw]), reads=[("lps", pb)], writes=["of"])
                S.dma("sp", lambda e: e.dma_start(out=self.dtf[r0:r0 + n, :], in_=of[:n, :cw]), reads=["of"], writes=[("dram", "dtf")])
        return evac

    def retention(self, l):
        S, nc = self.S, self.nc
        st = self.stage()
        sb, ps = st.sb, st.ps
        mask = sb([64, 8, 64], F32); din = sb([128, 8, 64], F32); dup64 = sb([64, 8], F32); dup16 = sb([16, 8], F32)
        identb = sb([128, 128], BF16); identf = sb([128, 128], F32)
        S.dma("sp", lambda e: e.dma_start(out=mask[:], in_=self.cst["ret_mask"]), writes=["mask"])
        S.dma("sp", lambda e: e.dma_start(out=din[:], in_=self.cst["ret_din"]), writes=["din"])
        S.dma("sp", lambda e: e.dma_start(out=dup64[:], in_=self.cst["ret_dup64"]), writes=["dup64"])
        S.dma("sp", lambda e: e.dma_start(out=dup16[:], in_=self.cst["ret_dup16"]), writes=["dup16"])
        S.dma("sp", lambda e: e.dma_start(out=identf[:], in_=self.cst["ident_in"]), writes=["identf"])
        S.op("dve", lambda e: e.tensor_copy(identb[:], identf[:]), reads=["identf"], writes=["identb"])
        gn, gnk = self.load_row_bcast(st, self.sw["ret_gn"][l:l + 1, :], D, 64)
        qk = sb([64, 2048], BF16); vt = sb([64, 2048], BF16); gt = sb([64, 2048], BF16)
        cs = sb([64, 2048], F32); sn = sb([64, 2048], F32)
        t1 = sb([64, 2048], F32); t2 = sb([64, 2048], F32)
        qkr = sb([64, 2048], BF16); kd = sb([64, 8, 128], BF16)
        qkT = sb([128, 16, 64], BF16); qdT = sb([128, 8, 64], BF16)
        sm = sb([64, 8, 64], BF16)
        o_sb = sb([64, 8, 256], F32); osq = sb([64, 8, 256], F32)
        Sf = sb([128, 8, 256], F32); Sb = sb([128, 8, 256], BF16)
        s1 = sb([64, 8], F32); s2 = sb([64, 8], F32); mean = sb([64, 8], F32); msq = sb([64, 8], F32)
        sg = sb([64, 2048], F32); yb = sb([64, 2048], BF16)
        tq = ps([128, 16, 64], BF16); ps_s = ps([64, 8, 64], F32)
        ps_o = [ps([64, 2, 256], F32) for _ in range(2)]; ps_S = [ps([128, 2, 256], F32) for _ in range(2)]
        for (si, r0, L, first, last) in self.chunks():
            slot = self.seqs[si][3]
            if first:
                if slot == 0:
                    S.op("dve", lambda e: e.memset(Sf[:], 0.0), writes=["Sf"])
                else:
                    S.dma("sp", lambda e, slot=slot: e.dma_start(out=Sf[:], in_=self.st_ret[l, slot - 1].rearrange("h d e -> d h e")), writes=["Sf"])
                S.op("act", lambda e: e.copy(Sb[:], Sf[:]), reads=["Sf"], writes=["Sb"])
            S.dma("sp", lambda e, r0=r0, L=L: e.dma_start(out=qk[:L, :], in_=self.pj_qkvg[r0:r0 + L, C_Q:C_Q + 2048]), reads=[("dram", "pj_qkvg")], writes=["qk"])
            S.dma("sp", lambda e, r0=r0, L=L: e.dma_start(out=vt[:L, :], in_=self.pj_qkvg[r0:r0 + L, C_V:C_V + 2048]), reads=[("dram", "pj_qkvg")], writes=["vt"])
            S.dma("sp", lambda e, r0=r0, L=L: e.dma_start(out=gt[:L, :], in_=self.pj_qkvg[r0:r0 + L, C_G:C_G + 2048]), reads=[("dram", "pj_qkvg")], writes=["gt"])
            S.dma("act", lambda e, r0=r0, L=L: e.dma_start(out=cs[:L, :], in_=self.cst["rope_cs"][r0:r0 + L, :]), writes=["cs"])
            S.dma("act", lambda e, r0=r0, L=L: e.dma_start(out=sn[:L, :], in_=self.cst["rope_sn"][r0:r0 + L, :]), writes=["sn"])
            v4 = lambda t, L=L: t[:L, :].rearrange("p (a two d) -> p a two d", two=2, d=64)
            S.op("dve", lambda e, L=L: e.tensor_tensor(t1[:L, :], qk[:L, :], cs[:L, :], ALU.mult), reads=["qk", "cs"], writes=["t1"])
            S.op("pool", lambda e, L=L, v4=v4: e.tensor_tensor(v4(t2)[:, :, 0, :], v4(qk)[:, :, 1, :], v4(sn)[:, :, 0, :], ALU.mult), reads=["qk", "sn"], writes=["t2a"])
            S.op("pool", lambda e, L=L, v4=v4: e.tensor_tensor(v4(t2)[:, :, 1, :], v4(qk)[:, :, 0, :], v4(sn)[:, :, 1, :], ALU.mult), reads=["qk", "sn"], writes=["t2b"])
            S.op("dve", lambda e, L=L: e.tensor_tensor(qkr[:L, :], t1[:L, :], t2[:L, :], ALU.add), reads=["t1", "t2a", "t2b"], writes=["qkr"])
            dup = dup64 if L == 64 else dup16
            S.op("dve", lambda e, L=L, dup=dup: e.tensor_tensor(kd[:L], qkr[:L, 1024:2048].rearrange("p (h d) -> p h d", h=8),
                                                               dup[:L, :].unsqueeze(2).to_broadcast([L, 8, 128]), ALU.mult),
                 reads=["qkr", "dup64", "dup16"], writes=["kd"])

            def tr(e, L=L):
                for i in range(16):
                    ins = e.transpose(tq[:, i, :L], qkr[:L, i * 128:(i + 1) * 128], identb[:L, :L])
                return ins
            S.op("pe", tr, reads=["qkr", "identb"], writes=["tq"])
            S.op("act", lambda e, L=L: e.copy(qkT[:, :, :L], tq[:, :, :L]), reads=["tq"], writes=["qkT"])
            S.op("dve", lambda e, L=L: e.tensor_tensor(qdT[:, :, :L], qkT[:, 0:8, :L], din[:, :, :L], ALU.mult), reads=["qkT", "din"], writes=["qdT"])

            def sc(e, L=L):
                for h in range(8):
                    ins = e.matmul(ps_s[:L, h, :L], qkT[:, 8 + h, :L], qkT[:, h, :L], start=True, stop=True)
                return ins
            S.op("pe", sc, reads=["qkT"], writes=["ps_s"])
            S.op("dve", lambda e, L=L: e.tensor_tensor(sm[:L, :, :L], ps_s[:L, :, :L], mask[:L, :, :L], ALU.mult), reads=["ps_s", "mask"], writes=["sm"])
            for hp in range(4):
                pb = hp % 2

                def om(e, L=L, hp=hp, pb=pb):
                    for hh in range(2):
                        h = 2 * hp + hh
                        e.matmul(ps_o[pb][:L, hh, :], sm[:L, h, :L], vt[:L, h * 256:(h + 1) * 256], start=True, stop=False)
                        ins = e.matmul(ps_o[pb][:L, hh, :], qdT[:, h, :L], Sb[:, h, :], start=False, stop=True)
                    return ins
                S.op("pe", om, reads=["sm", "vt", "qdT", "Sb"], writes=[("ps_o", pb)])
                S.op("act", lambda e, L=L, hp=hp, pb=pb: e.copy(o_sb[:L, 2 * hp:2 * hp + 2, :], ps_o[pb][:L, :, :]), reads=[("ps_o", pb)], writes=[("o_sb", hp)])
            for hp in range(4):
                pb = hp % 2

                def sm_(e, L=L, hp=hp, pb=pb):
                    for hh in range(2):
                        h = 2 * hp + hh
                        ins = e.matmul(ps_S[pb][:, hh, :], kd[:L, h, :], vt[:L, h * 256:(h + 1) * 256], start=True, stop=True)
                    return ins
                S.op("pe", sm_, reads=["kd", "vt"], writes=[("ps_S", pb)])
                for hh in range(2):
                    h = 2 * hp + hh
                    S.op("dve", lambda e, h=h, hh=hh, pb=pb, L=L: e.scalar_tensor_tensor(Sf[:, h, :], Sf[:, h, :], float(GAMMA[h] ** L), ps_S[pb][:, hh, :], ALU.mult, ALU.add),
                         reads=[("ps_S", pb), "Sf"], writes=["Sf"])
            S.op("act", lambda e: e.copy(Sb[:], Sf[:]), reads=["Sf"], writes=["Sb"])
            if last:
                dst = self.o_ret[l, slot].rearrange("h d e -> d h e")
                S.dma("sp", lambda e, dst=dst: e.dma_start(out=dst, in_=Sf[:]), reads=["Sf"], writes=[("dram", "o_ret")])
            okeys = [("o_sb", hp) for hp in range(4)]
            S.op("dve", lambda e, L=L: e.tensor_reduce(s1[:L, :], o_sb[:L], AX.X, ALU.add), reads=okeys, writes=["s1"])
            S.op("act", lambda e, L=L: e.activation(osq[:L], o_sb[:L], AF.Square), reads=okeys, writes=["osq"])
            S.op("dve", lambda e, L=L: e.tensor_reduce(s2[:L, :], osq[:L], AX.X, ALU.add), reads=["osq"], writes=["s2"])
            S.op("dve", lambda e, L=L: e.tensor_scalar(mean[:L, :], s1[:L, :], 1.0 / 256, None, ALU.mult), reads=["s1"], writes=["mean"])
            S.op("dve", lambda e, L=L: e.tensor_tensor(msq[:L, :], mean[:L, :], mean[:L, :], ALU.mult), reads=["mean"], writes=["msq"])
            S.op("dve", lambda e, L=L: e.scalar_tensor_tensor(s2[:L, :], s2[:L, :], 1.0 / 256, msq[:L, :], ALU.mult, ALU.subtract), reads=["s2", "msq"], writes=["s2"])
            S.op("act", lambda e, L=L: e.activation(s2[:L, :], s2[:L, :], AF.Sqrt, bias=EPS, scale=1.0), reads=["s2"], writes=["s2"])
            S.op("dve", lambda e, L=L: e.reciprocal(s2[:L, :], s2[:L, :]), reads=["s2"], writes=["s2"])
            S.op("dve", lambda e, L=L: e.tensor_tensor(osq[:L], o_sb[:L], mean[:L, :].unsqueeze(2).to_broadcast([L, 8, 256]), ALU.subtract), reads=okeys + ["mean", "osq"], writes=["osq"])
            S.op("dve", lambda e, L=L: e.tensor_tensor(osq[:L], osq[:L], s2[:L, :].unsqueeze(2).to_broadcast([L, 8, 256]), ALU.mult), reads=["osq", "s2"], writes=["osq"])
            S.op("act", lambda e, L=L: e.activation(sg[:L, :], gt[:L, :], AF.Silu), reads=["gt"], writes=["sg"])
            S.op("pool", lambda e, L=L: e.tensor_tensor(sg[:L, :], sg[:L, :], gn[:L, :], ALU.mult), reads=["sg", gnk], writes=["sg"])
            S.op("dve", lambda e, L=L: e.tensor_tensor(yb[:L, :], osq[:L].rearrange("p h e -> p (h e)"), sg[:L, :], ALU.mult), reads=["osq", "sg"], writes=["yb"])
            S.dma("sp", lambda e, r0=r0, L=L: e.dma_start(out=self.ret_y[r0:r0 + L, :], in_=yb[:L, :]), reads=["yb"], writes=[("dram", "ret_y")])
        st.done()


    def trig(self, st, ang, cos_o, sin_o, shape, key_in, key_c, key_s):
        S = self.S
        ki = st.sb(shape, I32, "ki"); kf = st.sb(shape, F32, "kf"); s2 = st.sb(shape, F32, "s2"); s4 = st.sb(shape, F32, "s4")
        k = self.name("trg")
        S.op("dve", lambda e: e.tensor_scalar(ki[:], ang(), 1.0 / (2 * math.pi), None, ALU.mult), reads=[key_in], writes=[k + "ki"])
        S.op("dve", lambda e: e.tensor_copy(kf[:], ki[:]), reads=[k + "ki"], writes=[k + "kf"])
        S.op("dve", lambda e: e.scalar_tensor_tensor(kf[:], kf[:], -2 * math.pi, ang(), ALU.mult, ALU.add), reads=[k + "kf", key_in], writes=[k + "kf"])
        S.op("act", lambda e: e.activation(s2[:], kf[:], AF.Sin, scale=0.5), reads=[k + "kf"], writes=[k + "s2"])
        S.op("act", lambda e: e.activation(s4[:], kf[:], AF.Sin, scale=0.25), reads=[k + "kf"], writes=[k + "s4"])
        S.op("dve", lambda e: e.tensor_tensor(s4[:], s4[:], s4[:], ALU.mult), reads=[k + "s4"], writes=[k + "s4"])
        S.op("dve", lambda e: e.tensor_scalar(s4[:], s4[:], -2.0, 1.0, ALU.mult, ALU.add), reads=[k + "s4"], writes=[k + "s4"])
        S.op("dve", lambda e: e.scalar_tensor_tensor(sin_o(), s2[:], 2.0, s4[:], ALU.mult, ALU.mult), reads=[k + "s2", k + "s4"], writes=[key_s])
        S.op("dve", lambda e: e.tensor_tensor(s2[:], s2[:], s2[:], ALU.mult), reads=[k + "s2", key_s], writes=[k + "s2"])
        S.op("dve", lambda e: e.tensor_scalar(cos_o(), s2[:], -2.0, 1.0, ALU.mult, ALU.add), reads=[k + "s2"], writes=[key_c])

    def s5(self, l):
        S, nc = self.S, self.nc
        st = self.stage()
        sb, ps = st.sb, st.ps
        QH = 32
        identf = sb([128, 128], F32); identb = sb([128, 128], BF16)
        S.dma("sp", lambda e: e.dma_start(out=identf[:], in_=self.cst["ident_in"]), writes=["identf"])
        S.op("dve", lambda e: e.tensor_copy(identb[:], identf[:]), reads=["identf"], writes=["identb"])
        tv = sb([128, 64], F32)
        S.dma("sp", lambda e: e.dma_start(out=tv[:], in_=self.cst["tvec"]), writes=["tv"])
        COS = sb([128, 64, 64], F32); SIN = sb([128, 64, 64], F32); MT = sb([128, 64, 64], F32)
        MT16 = sb([128, 64, 16], F32)
        LB = sb([128, 64, 2, 128], BF16); CB = sb([128, 64, 2, 32], BF16)
        arT = sb([128, 64], F32); aiT = sb([128, 64], F32)
        pst = self.stage()
        psb = pst.sb
        are = psb([64, 128], F32); aim = psb([64, 128], F32); dtq = psb([64, 2], F32); dtE = psb([64, 128], F32)
        S.dma("sp", lambda e: e.dma_start(out=are[:], in_=self.sw["s5_a_re"][l].rearrange("(q g) p -> q (g p)", g=2)), writes=["are"])
        S.dma("sp", lambda e: e.dma_start(out=aim[:], in_=self.sw["s5_a_im"][l].rearrange("(q g) p -> q (g p)", g=2)), writes=["aim"])
        S.dma("sp", lambda e: e.dma_start(out=dtq[:], in_=self.sw["s5_log_dt"][l].rearrange("(q g) -> q g", g=2)), writes=["dtq"])
        S.op("act", lambda e: e.activation(dtq[:], dtq[:], AF.Exp), reads=["dtq"], writes=["dtq"])
        for g2 in range(2):
            S.op("dve", lambda e, g2=g2: e.tensor_copy(dtE[:, g2 * 64:(g2 + 1) * 64], dtq[:, g2:g2 + 1].to_broadcast([64, 64])), reads=["dtq"], writes=["dtE%d" % g2])
        dk = ["dtE0", "dtE1"]
        mag = psb([64, 128], F32); th = psb([64, 128], F32); cth = psb([64, 128], F32); sth = psb([64, 128], F32)
        S.op("dve", lambda e: e.tensor_tensor(mag[:], dtE[:], are[:], ALU.mult), reads=dk + ["are"], writes=["mag"])
        S.op("act", lambda e: e.activation(mag[:], mag[:], AF.Exp), reads=["mag"], writes=["mag"])
        S.op("dve", lambda e: e.tensor_tensor(th[:], dtE[:], aim[:], ALU.mult), reads=dk + ["aim"], writes=["th"])
        self.trig(pst, lambda: th[:], lambda: cth[:], lambda: sth[:], [64, 128], "th", "cth", "sth")
        ar = psb([64, 128], F32); ai = psb([64, 128], F32); nr = psb([64, 128], F32); den = psb([64, 128], F32)
        cr = psb([64, 128], F32); ci = psb([64, 128], F32); tmp = psb([64, 128], F32)
        S.op("dve", lambda e: e.tensor_tensor(ar[:], mag[:], cth[:], ALU.mult), reads=["mag", "cth"], writes=["ar"])
        S.op("dve", lambda e: e.tensor_tensor(ai[:], mag[:], sth[:], ALU.mult), reads=["mag", "sth"], writes=["ai"])
        S.op("dve", lambda e: e.tensor_scalar(nr[:], ar[:], -1.0, None, ALU.add), reads=["ar"], writes=["nr"])
        S.op("dve", lambda e: e.tensor_tensor(den[:], are[:], are[:], ALU.mult), reads=["are"], writes=["den"])
        S.op("dve", lambda e: e.tensor_tensor(tmp[:], aim[:], aim[:], ALU.mult), reads=["aim"], writes=["tmp"])
        S.op("dve", lambda e: e.tensor_tensor(den[:], den[:], tmp[:], ALU.add), reads=["den", "tmp"], writes=["den"])
        S.op("dve", lambda e: e.reciprocal(den[:], den[:]), reads=["den"], writes=["den"])
        S.op("dve", lambda e: e.tensor_tensor(cr[:], nr[:], are[:], ALU.mult), reads=["nr", "are"], writes=["cr"])
        S.op("dve", lambda e: e.tensor_tensor(tmp[:], ai[:], aim[:], ALU.mult), reads=["ai", "aim", "den"], writes=["tmp"])
        S.op("dve", lambda e: e.tensor_tensor(cr[:], cr[:], tmp[:], ALU.add), reads=["cr", "tmp"], writes=["cr"])
        S.op("dve", lambda e: e.tensor_tensor(cr[:], cr[:], den[:], ALU.mult), reads=["cr", "den"], writes=["cr"])
        S.op("dve", lambda e: e.tensor_tensor(ci[:], ai[:], are[:], ALU.mult), reads=["ai", "are"], writes=["ci"])
        S.op("dve", lambda e: e.tensor_tensor(tmp[:], nr[:], aim[:], ALU.mult), reads=["nr", "aim", "cr"], writes=["tmp"])
        S.op("dve", lambda e: e.tensor_tensor(ci[:], ci[:], tmp[:], ALU.subtract), reads=["ci", "tmp"], writes=["ci"])
        S.op("dve", lambda e: e.tensor_tensor(ci[:], ci[:], den[:], ALU.mult), reads=["ci", "den"], writes=["ci"])
        thT = psb([128, 64], F32); mT = psb([128, 64], F32); crT = psb([128, 64], F32); ciT = psb([128, 64], F32)
        ptr = pst.ps([128, 64], F32)
        for (src, sk, dst, dkk) in [(ar, "ar", arT, "arT"), (ai, "ai", aiT, "aiT"), (th, "th", thT, "thT"), (mag, "mag", mT, "mT"), (cr, "cr", crT, "crT"), (ci, "ci", ciT, "ciT")]:
            S.op("pe", lambda e, src=src: e.matmul(ptr[:], src[:], identf[:64, :64], start=True, stop=True), reads=[sk, "identf"], writes=["ptr"])
            S.op("act", lambda e, dst=dst: e.copy(dst[:], ptr[:]), reads=["ptr"], writes=[dkk])
        S.op("dve", lambda e: e.tensor_tensor(MT[:], thT[:].unsqueeze(2).to_broadcast([128, 64, 64]), tv[:].unsqueeze(1).to_broadcast([128, 64, 64]), ALU.mult), reads=["thT", "tv"], writes=["ANG"])
        tst = self.stage()
        self.trig(tst, lambda: MT[:], lambda: COS[:], lambda: SIN[:], [128, 64, 64], "ANG", "COS", "SIN")
        S.barrier(); S.emit(); tst.stack.close()
        S.op("dve", lambda e: e.tensor_copy(MT[:], mT[:].unsqueeze(2).to_broadcast([128, 64, 64])), reads=["mT", "COS", "SIN", "ANG"], writes=["MT"])
        S.op("dve", lambda e: e.memset(MT[:, :, 0:1], 0.0), reads=["MT"], writes=["MT"])
        S.op("dve", lambda e: e.tensor_copy(MT16[:], MT[:, :, 0:16]), reads=["MT"], writes=["MT"])
        Bn = [psb([128, 64, 16], F32, "Bn") for _ in range(2)]
        S.dma("sp", lambda e: e.dma_start(out=Bn[0][:], in_=self.sw["s5_b_re"][l].rearrange("(q g) p j -> (g p) q j", g=2)), writes=["Bn0"])
        S.dma("sp", lambda e: e.dma_start(out=Bn[1][:], in_=self.sw["s5_b_im"][l].rearrange("(q g) p j -> (g p) q j", g=2)), writes=["Bn1"])
        bb = [psb([128, 64, 16], F32, "bb") for _ in range(2)]
        t16 = psb([128, 64, 16], F32)
        bc = lambda t: t[:].unsqueeze(2).to_broadcast([128, 64, 16])
        S.op("dve", lambda e: e.tensor_tensor(bb[0][:], Bn[0][:], bc(crT), ALU.mult), reads=["Bn0", "crT"], writes=["bb0"])
        S.op("dve", lambda e: e.tensor_tensor(t16[:], Bn[1][:], bc(ciT), ALU.mult), reads=["Bn1", "ciT"], writes=["t16"])
        S.op("dve", lambda e: e.tensor_tensor(bb[0][:], bb[0][:], t16[:], ALU.subtract), reads=["bb0", "t16"], writes=["bb0"])
        S.op("dve", lambda e: e.tensor_tensor(bb[1][:], Bn[1][:], bc(crT), ALU.mult), reads=["Bn1", "crT"], writes=["bb1"])
        S.op("dve", lambda e: e.tensor_tensor(t16[:], Bn[0][:], bc(ciT), ALU.mult), reads=["Bn0", "ciT", "bb0"], writes=["t16"])
        S.op("dve", lambda e: e.tensor_tensor(bb[1][:], bb[1][:], t16[:], ALU.add), reads=["bb1", "t16"], writes=["bb1"])
        Z = psb([128, 64, 128], BF16)
        ptz = pst.ps([128, 8, 128], BF16)
        for ri in range(2):
            S.op("dve", lambda e: e.memset(Z[:], 0.0), reads=["Z"], writes=["Z"])
            Zv = Z[:].rearrange("p (c qq) (gl j) -> p c qq gl j", qq=4, j=16)
            bv = bb[ri][:].rearrange("p (c qq) j -> p c qq j", qq=4)
            for qq in range(4):
                for g2 in range(2):
                    S.op("dve", lambda e, qq=qq, g2=g2, Zv=Zv, bv=bv: e.tensor_copy(Zv[g2 * 64:(g2 + 1) * 64, :, qq, 2 * qq + g2, :], bv[g2 * 64:(g2 + 1) * 64, :, qq, :]),
                         reads=["bb%d" % ri, "Z"], writes=["Z"])
            for q8 in range(8):
                def trz(e, q8=q8):
                    for k in range(8):
                        ins = e.transpose(ptz[:, k, :], Z[:, q8 * 8 + k, :], identb[:])
                    return ins
                S.op("pe", trz, reads=["Z", "identb"], writes=["ptz"])
                S.op("act", lambda e, q8=q8, ri=ri: e.copy(LB[:, q8 * 8:(q8 + 1) * 8, ri, :], ptz[:]), reads=["ptz"], writes=["LB"])
        Cn = psb([64, 64, 64], F32); Y = psb([64, 64, 128], BF16)
        ptc = pst.ps([128, 8, 64], BF16)
        for ri, nm in enumerate(["s5_c_re", "s5_c_im"]):
            S.op("dve", lambda e: e.memset(Cn[:], 0.0), reads=["Cn"], writes=["Cn"])
            S.op("dve", lambda e: e.memset(Y[:], 0.0), reads=["Y"], writes=["Y"])
            cv = self.sw[nm][l].rearrange("(q g) i p -> g i q p", g=2)
            for g2 in range(2):
                S.dma("sp", lambda e, g2=g2, cv=cv: e.dma_start(out=Cn[g2 * 32:g2 * 32 + 16, :, :], in_=cv[g2]), reads=["Cn"], writes=["Cn"])
            for g2 in range(2):
                S.op("dve", lambda e, g2=g2, ri=ri: e.tensor_scalar(Y[g2 * 32:(g2 + 1) * 32, :, g2 * 64:(g2 + 1) * 64], Cn[g2 * 32:(g2 + 1) * 32, :, :], (1.0 if ri == 0 else -1.0), None, ALU.mult),
                     reads=["Cn", "Y"], writes=["Y"])
            for q8 in range(8):
                def trc(e, q8=q8):
                    for k in range(8):
                        ins = e.transpose(ptc[:, k, :], Y[:, q8 * 8 + k, :], identb[:64, :64])
                    return ins
                S.op("pe", trc, reads=["Y", "identb"], writes=["ptc"])
                S.op("act", lambda e, q8=q8, ri=ri: e.copy(CB[:, q8 * 8:(q8 + 1) * 8, ri, :].rearrange("p k (g i) -> p k g i", g=2),
                                                         ptc[:].rearrange("p k (g i) -> p k g i", g=2)[:, :, :, 0:16]), reads=["ptc"], writes=["CB"])
        S.barrier(); S.emit(); pst.stack.close()
        dT, dTk = self.load_row_bcast(st, self.sw["s5_d"][l:l + 1, :], D, 64)
        uTs = [sb([128, 16, 64], BF16) for _ in range(2)]; utoks = [sb([64, 2048], BF16) for _ in range(2)]
        Braw = sb([128, QH, 2, 64], F32); BR = sb([128, QH, 64], F32); BI = sb([128, QH, 64], F32); tmpb = sb([128, QH, 64], F32)
        XR = sb([128, QH, 64], BF16); XI = sb([128, QH, 64], BF16)
        xpr = sb([128, 64], F32); xpi = sb([128, 64], F32); fr = sb([128, 64], F32); fi = sb([128, 64], F32); f2 = sb([128, 64], F32)
        yt = sb([64, 2048], F32); yb = sb([64, 2048], BF16)
        xo = sb([64, 128], F32)
        pb_ = [ps([128, 4, 2, 64], F32) for _ in range(2)]
        py = ps([64, 2048], F32)
        pxo = ps([64, 128], F32)
        chs = self.chunks()

        def s5_loads(ci):
            (si_, r0_, n_, _, _) = chs[ci]
            ub_ = ci % 2
            for c in range(16):
                S.dma("sp", lambda e, c=c, r0_=r0_, n_=n_, ub_=ub_: e.dma_start(out=uTs[ub_][:, c, :n_], in_=self.pj_u[r0_:r0_ + n_, c * 128:(c + 1) * 128], transpose=True),
                      reads=[("dram", "pj_u")], writes=[("uT", ub_, c)])
            S.dma("act", lambda e, r0_=r0_, n_=n_, ub_=ub_: e.dma_start(out=utoks[ub_][:n_, :], in_=self.pj_u[r0_:r0_ + n_, :]), reads=[("dram", "pj_u")], writes=[("utok", ub_)])
        for ci, (si, r0, n, first, last) in enumerate(chs):
            slot = self.seqs[si][3]
            if first:
                if slot == 0:
                    S.op("dve", lambda e: e.memset(xpr[:], 0.0), writes=["xpr"])
                    S.op("dve", lambda e: e.memset(xpi[:], 0.0), writes=["xpi"])
                else:
                    for (srcst, dstt, kk) in [(self.st_s5r, xpr, "xpr"), (self.st_s5i, xpi, "xpi")]:
                        S.dma("sp", lambda e, srcst=srcst, slot=slot: e.dma_start(out=xo[:, :], in_=srcst[l, slot - 1].rearrange("(q g) p -> q (g p)", g=2)), reads=["xo"], writes=["xo"])
                        S.op("pe", lambda e: e.matmul(pb_[0][:, 0, 0, :], xo[:, :], identf[:64, :64], start=True, stop=True), reads=["xo", "identf"], writes=[("pb", 0)])
                        S.op("act", lambda e, dstt=dstt: e.copy(dstt[:], pb_[0][:, 0, 0, :]), reads=[("pb", 0)], writes=[kk])
            ub = ci % 2
            uT, utok = uTs[ub], utoks[ub]
            if ci == 0:
                s5_loads(0)
            if ci + 1 < len(chs):
                s5_loads(ci + 1)
            S.op("dve", lambda e: e.tensor_tensor(fr[:], arT[:], xpr[:], ALU.mult), reads=["arT", "xpr"], writes=["fr"])
            S.op("dve", lambda e: e.tensor_tensor(f2[:], aiT[:], xpi[:], ALU.mult), reads=["aiT", "xpi"], writes=["f2"])
            S.op("dve", lambda e: e.tensor_tensor(fr[:], fr[:], f2[:], ALU.subtract), reads=["fr", "f2"], writes=["fr"])
            S.op("dve", lambda e: e.tensor_tensor(fi[:], arT[:], xpi[:], ALU.mult), reads=["arT", "xpi"], writes=["fi"])
            S.op("dve", lambda e: e.tensor_tensor(f2[:], aiT[:], xpr[:], ALU.mult), reads=["aiT", "xpr", "fr"], writes=["f2"])
            S.op("dve", lambda e: e.tensor_tensor(fi[:], fi[:], f2[:], ALU.add), reads=["fi", "f2"], writes=["fi"])
            V = lambda t, n=n: t[:].rearrange("p q t -> p (q t)")[:, :QH * n].rearrange("p (q t) -> p q t", t=n)
            F2 = lambda t, n=n: t[:].rearrange("p q t -> p (q t)")[:, :QH * n]
            BRv, BIv, tmv = V(BR), V(BI), V(tmpb)
            for hf in range(2):
                q0 = hf * QH
                for c8 in range(8):
                    c = hf * 8 + c8
                    pbi = c % 2

                    def bm(e, c=c, pbi=pbi, n=n, uT=uT):
                        for qq in range(4):
                            for ri in range(2):
                                ins = e.matmul(pb_[pbi][:, qq, ri, :n], LB[:, 4 * c + qq, ri, :], uT[:, c, :n], start=True, stop=True)
                        return ins
                    S.op("pe", bm, reads=["LB", ("uT", ub, c)], writes=[("pb", pbi)])
                    S.op("act", lambda e, c8=c8, pbi=pbi, n=n: e.copy(Braw[:, 4 * c8:4 * c8 + 4, :, :n], pb_[pbi][:, :, :, :n]), reads=[("pb", pbi)], writes=["Braw"])
                cosv = COS[:, q0:q0 + QH, :n]; sinv = SIN[:, q0:q0 + QH, :n]; mtv = MT[:, q0:q0 + QH, :n]
                br_, bi_ = Braw[:, :, 0, :n], Braw[:, :, 1, :n]
                S.op("dve", lambda e, cosv=cosv, br_=br_, n=n, BRv=BRv, BIv=BIv, tmv=tmv: e.tensor_tensor(BRv, br_, cosv, ALU.mult), reads=["Braw", "COS"], writes=["BR"])
                S.op("pool", lambda e, sinv=sinv, bi_=bi_, n=n, BRv=BRv, BIv=BIv, tmv=tmv: e.tensor_tensor(tmv, bi_, sinv, ALU.mult), reads=["Braw", "SIN"], writes=["tmpb"])
                S.op("dve", lambda e, n=n, BRv=BRv, BIv=BIv, tmv=tmv: e.tensor_tensor(BRv, BRv, tmv, ALU.add), reads=["BR", "tmpb"], writes=["BR"])
                S.op("dve", lambda e, cosv=cosv, bi_=bi_, n=n, BRv=BRv, BIv=BIv, tmv=tmv: e.tensor_tensor(BIv, bi_, cosv, ALU.mult), reads=["Braw", "COS"], writes=["BI"])
                S.op("pool", lambda e, sinv=sinv, br_=br_, n=n, BRv=BRv, BIv=BIv, tmv=tmv: e.tensor_tensor(tmv, br_, sinv, ALU.mult), reads=["Braw", "SIN", "BR"], writes=["tmpb"])
                S.op("dve", lambda e, n=n, BRv=BRv, BIv=BIv, tmv=tmv: e.tensor_tensor(BIv, BIv, tmv, ALU.subtract), reads=["BI", "tmpb"], writes=["BI"])
                S.op("dve", lambda e, q0=q0, BRv=BRv: e.tensor_tensor(BRv[:, :, 0], BRv[:, :, 0], fr[:, q0:q0 + QH], ALU.add), reads=["BR", "fr"], writes=["BR"])
                S.op("dve", lambda e, q0=q0, BIv=BIv: e.tensor_tensor(BIv[:, :, 0], BIv[:, :, 0], fi[:, q0:q0 + QH], ALU.add), reads=["BI", "fi"], writes=["BI"])
                mt2 = (MT if n == 64 else MT16)[:, q0:q0 + QH, :].rearrange("p q t -> p (q t)")
                S.op("dve", lambda e, mt2=mt2, F2=F2: e.tensor_tensor_scan(F2(BR), mt2, F2(BR), 0.0, ALU.mult, ALU.add), reads=["BR", "MT"], writes=["BR"])
                S.op("dve", lambda e, mt2=mt2, F2=F2: e.tensor_tensor_scan(F2(BI), mt2, F2(BI), 0.0, ALU.mult, ALU.add), reads=["BI", "MT"], writes=["BI"])
                TA, TB = Braw[:, :, 0, :n], Braw[:, :, 1, :n]
                S.op("dve", lambda e, TA=TA, cosv=cosv, n=n, BRv=BRv: e.tensor_tensor(TA, BRv, cosv, ALU.mult), reads=["BR", "COS", "Braw"], writes=["TA"])
                S.op("pool", lambda e, TB=TB, sinv=sinv, n=n, BIv=BIv: e.tensor_tensor(TB, BIv, sinv, ALU.mult), reads=["BI", "SIN", "Braw"], writes=["TB"])
                S.op("dve", lambda e, TA=TA, TB=TB, n=n: e.tensor_tensor(XR[:, :, :n], TA, TB, ALU.subtract), reads=["TA", "TB"], writes=["XR"])
                S.op("dve", lambda e, TA=TA, TB=TB, q0=q0, n=n: e.tensor_tensor(xpr[:, q0:q0 + QH], TA[:, :, n - 1], TB[:, :, n - 1], ALU.subtract), reads=["TA", "TB"], writes=["xpr"])
                S.op("dve", lambda e, TA=TA, sinv=sinv, n=n, BRv=BRv: e.tensor_tensor(TA, BRv, sinv, ALU.mult), reads=["BR", "SIN", "XR", "xpr"], writes=["TA"])
                S.op("pool", lambda e, TB=TB, cosv=cosv, n=n, BIv=BIv: e.tensor_tensor(TB, BIv, cosv, ALU.mult), reads=["BI", "COS", "XR", "xpr"], writes=["TB"])
                S.op("dve", lambda e, TA=TA, TB=TB, n=n: e.tensor_tensor(XI[:, :, :n], TA, TB, ALU.add), reads=["TA", "TB"], writes=["XI"])
                S.op("dve", lambda e, TA=TA, TB=TB, q0=q0, n=n: e.tensor_tensor(xpi[:, q0:q0 + QH], TA[:, :, n - 1], TB[:, :, n - 1], ALU.add), reads=["TA", "TB"], writes=["xpi"])

                def cm(e, q0=q0, n=n):
                    for q in range(QH):
                        e.matmul(py[:n, (q0 + q) * 32:(q0 + q + 1) * 32], XR[:, q, :n], CB[:, q0 + q, 0, :], start=True, stop=False)
                        ins = e.matmul(py[:n, (q0 + q) * 32:(q0 + q + 1) * 32], XI[:, q, :n], CB[:, q0 + q, 1, :], start=False, stop=True)
                    return ins
                S.op("pe", cm, reads=["XR", "XI", "CB"], writes=["py"])
                S.op("dve", lambda e: e.tensor_copy(Braw[:, 0, 0, 0:1], Braw[:, 0, 0, 0:1]), reads=[], writes=["Braw", "TA", "TB"])
            S.op("dve", lambda e, n=n, utok=utok: e.tensor_tensor(yt[:n, :], utok[:n, :], dT[:n, :], ALU.mult), reads=[("utok", ub), dTk], writes=["yt"])
            S.op("dve", lambda e, n=n: e.tensor_tensor(yt[:n, :], yt[:n, :], py[:n, :], ALU.add), reads=["yt", "py"], writes=["yt"])
            S.op("act", lambda e, n=n: e.activation(yb[:n, :], yt[:n, :], AF.Gelu), reads=["yt"], writes=["yb"])
            S.dma("sp", lambda e, r0=r0, n=n: e.dma_start(out=self.s5_y[r0:r0 + n, :], in_=yb[:n, :]), reads=["yb"], writes=[("dram", "s5_y")])
            if last:
                for (srct, dsto, kk) in [(xpr, self.o_s5r, "xpr"), (xpi, self.o_s5i, "xpi")]:
                    S.op("pe", lambda e, srct=srct: e.matmul(pxo[:], srct[:], identf[:], start=True, stop=True), reads=[kk, "identf"], writes=["pxo"])
                    S.op("act", lambda e: e.copy(xo[:], pxo[:]), reads=["pxo"], writes=["xo"])
                    S.dma("sp", lambda e, dsto=dsto, slot=slot: e.dma_start(out=dsto[l, slot].rearrange("(q g) p -> q (g p)", g=2), in_=xo[:]), reads=["xo"], writes=[("dram", dsto.name)])
        st.done()


    def ssd(self, l):
        S, nc = self.S, self.nc
        st = self.stage()
        sb, ps = st.sb, st.ps
        identf = sb([128, 128], F32); identb = sb([128, 128], BF16); tri = sb([64, 64], F32); ones = sb([64, 64], F32)
        sel = {64: sb([64, 128], F32), 16: sb([64, 128], F32)}
        S.dma("sp", lambda e: e.dma_start(out=identf[:], in_=self.cst["ident_in"]), writes=["identf"])
        S.op("dve", lambda e: e.tensor_copy(identb[:], identf[:]), reads=["identf"], writes=["identb"])
        S.dma("sp", lambda e: e.dma_start(out=tri[:], in_=self.cst["tri_in"]), writes=["tri"])
        S.op("dve", lambda e: e.memset(ones[:], 1.0), writes=["ones"])
        for LL in (64, 16):
            S.op("dve", lambda e, LL=LL: e.tensor_copy(sel[LL][:, :], identf[0:64, LL - 1:LL].to_broadcast([64, 128])), reads=["identf"], writes=["sel%d" % LL])
        cw = sb([128, 48, 4], F32); cb = sb([128, 48], F32)
        for w in range(4):
            S.dma("sp", lambda e, w=w: e.dma_start(out=cw[:, :, w], in_=self.sw["ssd_conv_w"][l, w].rearrange("(c p) -> p c", p=128), allow_slow_non_contiguous=True), writes=["cw%d" % w])
        S.dma("sp", lambda e: e.dma_start(out=cb[:], in_=self.sw["ssd_conv_b"][l].rearrange("(c p) -> p c", p=128), allow_slow_non_contiguous=True), writes=["cb"])
        cwk = ["cw%d" % w for w in range(4)]
        dtb, dtbk = self.load_row_bcast(st, self.sw["ssd_dt_bias"][l:l + 1, :], 64, 64)
        aneg, alk = self.load_row_bcast(st, self.sw["ssd_a_log"][l:l + 1, :], 64, 64)
        dsk, dskk = self.load_row_bcast(st, self.sw["ssd_d"][l:l + 1, :], 64, 64)
        ng, ngk = self.load_row_bcast(st, self.sw["ssd_norm"][l:l + 1, :], 4096, 64)
        S.op("act", lambda e: e.activation(aneg[:], aneg[:], AF.Exp), reads=[alk], writes=[alk])
        S.op("dve", lambda e: e.tensor_scalar(aneg[:], aneg[:], -1.0, None, ALU.mult), reads=[alk], writes=[alk])
        xTs = [sb([128, 48, 80], BF16) for _ in range(2)]; acc = sb([128, 48, 64], F32); ctmp = sb([128, 48, 64], F32); xcT = sb([128, 48, 64], BF16)
        xtok = sb([64, 4096], BF16); btok = sb([64, 1024], BF16); zt = sb([64, 4096], BF16)
        dtrs = [sb([64, 64], F32) for _ in range(2)]; dtx = sb([64, 64], F32); dta_ = sb([64, 64], F32); t64 = sb([64, 64], F32); dt = sb([64, 64], F32)
        cs_col = sb([64, 64], F32); ecs = sb([64, 64], F32); wdec = sb([64, 64], F32); csl = sb([128, 64], F32); ecl = sb([128, 64], F32)
        Rg = sb([64, 8, 64], F32); d1 = sb([64, 8, 64], F32); cbm = sb([64, 8, 64], F32); M = sb([64, 64, 64], BF16)
        xdt = sb([64, 4096], BF16); xdtw = sb([64, 4096], BF16)
        hT = sb([128, 4096], F32); hTb = sb([128, 4096], BF16)
        yv = sb([64, 4096], F32); ytmp = sb([64, 512], F32); sz = sb([64, 4096], BF16); ssq = sb([64, 8], F32); yb = sb([64, 4096], BF16)
        hio = sb([128, 4, 128], F32)
        p_tr = ps([64, 2048], BF16); p_small = ps([128, 64], F32); p_cs = ps([64, 8, 64], F32); p_cb = ps([64, 8, 64], F32)
        py = ps([64, 512], F32); pys = ps([64, 512], F32); ph = ps([128, 512], F32)
        v3 = lambda t, L: t[:L, :].rearrange("p (h q) -> p h q", q=64)
        chs = self.chunks()

        def ssd_loads(ci):
            (si_, r0_, L_, _, _) = chs[ci]
            r00_, _, p0_, _ = self.seqs[si_]
            prow_ = p0_ + 3 + (r0_ - r00_)
            NR_ = L_ + 16
            xb_ = ci % 2
            for c in range(48):
                S.dma("sp", lambda e, c=c, prow_=prow_, NR_=NR_, xb_=xb_: e.dma_start(out=xTs[xb_][:, c, :NR_], in_=self.xpad[prow_ - 16:prow_ - 16 + NR_, c * 128:(c + 1) * 128], transpose=True),
                      reads=[("dram", "xpad")], writes=[("xT", xb_, c)])
            S.dma("act", lambda e, r0_=r0_, L_=L_, xb_=xb_: e.dma_start(out=dtrs[xb_][:L_, :], in_=self.dtf[r0_:r0_ + L_, :]), reads=[("dram", "dtf")], writes=[("dtr", xb_)])
        for ci, (si, r0, L, first, last) in enumerate(chs):
            r00, ln, p0, slot = self.seqs[si]
            prow = p0 + 3 + (r0 - r00)
            NR = L + 16
            if first:
                if slot == 0:
                    S.op("dve", lambda e: e.memset(hT[:], 0.0), writes=["hT"])
                else:
                    hv = self.st_ssd[l, slot - 1].rearrange("(k q) p n -> (q p) k n", q=2)
                    for k4 in range(8):
                        S.dma("sp", lambda e, k4=k4, hv=hv: e.dma_start(out=hio[:], in_=hv[:, 4 * k4:4 * k4 + 4, :]), reads=["hio"], writes=["hio"])

                        def trh(e):
                            for k in range(4):
                                ins = e.matmul(ph[:, k * 128:(k + 1) * 128], hio[:, k, :], identf[:], start=True, stop=True)
                            return ins
                        S.op("pe", trh, reads=["hio", "identf"], writes=["ph"])
                        S.op("act", lambda e, k4=k4: e.copy(hT[:, k4 * 512:(k4 + 1) * 512], ph[:]), reads=["ph"], writes=["hT"])
                S.op("act", lambda e: e.copy(hTb[:], hT[:]), reads=["hT"], writes=["hTb"])
            xb = ci % 2
            xT, dtr = xTs[xb], dtrs[xb]
            if ci == 0:
                ssd_loads(0)
            S.dma("act", lambda e, r0=r0, L=L: e.dma_start(out=zt[:L, :], in_=self.pj_z[r0:r0 + L, :]), reads=[("dram", "pj_z")], writes=["zt"])
            if ci + 1 < len(chs):
                ssd_loads(ci + 1)
            xk = [("xT", xb, c) for c in range(48)]
            bw = lambda w, L=L: cw[:, :, w:w + 1].to_broadcast([128, 48, L])
            S.op("dve", lambda e, L=L, bw=bw, xT=xT: e.tensor_tensor(acc[:, :, :L], xT[:, :, 13:13 + L], bw(0), ALU.mult), reads=xk + cwk, writes=["acc"])
            for w in range(1, 4):
                S.op("pool", lambda e, L=L, bw=bw, w=w, xT=xT: e.tensor_tensor(ctmp[:, :, :L], xT[:, :, 13 + w:13 + w + L], bw(w), ALU.mult), reads=xk + cwk, writes=["ctmp"])
                S.op("dve", lambda e, L=L: e.tensor_tensor(acc[:, :, :L], acc[:, :, :L], ctmp[:, :, :L], ALU.add), reads=["acc", "ctmp"], writes=["acc"])
            S.op("dve", lambda e, L=L: e.tensor_tensor(acc[:, :, :L], acc[:, :, :L], cb[:].unsqueeze(2).to_broadcast([128, 48, L]), ALU.add), reads=["acc", "cb"], writes=["acc"])
            S.op("act", lambda e, L=L: e.activation(xcT[:, :, :L], acc[:, :, :L], AF.Silu), reads=["acc"], writes=["xcT"])
            for rnd in range(2):
                def trx(e, rnd=rnd, L=L):
                    for k in range(16):
                        ins = e.transpose(p_tr[:L, k * 128:(k + 1) * 128], xcT[:, rnd * 16 + k, :L], identb[:])
                    return ins
                S.op("pe", trx, reads=["xcT", "identb"], writes=["p_tr"])
                S.op("act", lambda e, rnd=rnd, L=L: e.copy(xtok[:L, rnd * 2048:(rnd + 1) * 2048], p_tr[:L, :]), reads=["p_tr"], writes=[("xtok", rnd)])

            def trb(e, L=L):
                for k in range(8):
                    ins = e.transpose(p_tr[:L, k * 128:(k + 1) * 128], xcT[:, 32 + k, :L], identb[:])
                return ins
            S.op("pe", trb, reads=["xcT", "identb"], writes=["p_tr"])
            S.op("act", lambda e, L=L: e.copy(btok[:L, :], p_tr[:L, 0:1024]), reads=["p_tr"], writes=["btok"])
            xtk = [("xtok", 0), ("xtok", 1)]
            S.op("dve", lambda e, L=L, dtr=dtr: e.tensor_tensor(dtx[:L, :], dtr[:L, :], dtb[:L, :], ALU.add), reads=[("dtr", xb), dtbk], writes=["dtx"])
            S.op("act", lambda e, L=L: e.activation(t64[:L, :], dtx[:L, :], AF.Abs), reads=["dtx"], writes=["t64"])
            S.op("act", lambda e, L=L: e.activation(t64[:L, :], t64[:L, :], AF.Exp, scale=-1.0), reads=["t64"], writes=["t64"])
            S.op("act", lambda e, L=L: e.activation(t64[:L, :], t64[:L, :], AF.Ln, bias=1.0), reads=["t64"], writes=["t64"])
            S.op("dve", lambda e, L=L: e.tensor_scalar(dt[:L, :], dtx[:L, :], 0.0, None, ALU.max), reads=["dtx"], writes=["dt"])
            S.op("dve", lambda e, L=L: e.tensor_tensor(dt[:L, :], dt[:L, :], t64[:L, :], ALU.add), reads=["dt", "t64"], writes=["dt"])
            S.op("dve", lambda e, L=L: e.tensor_tensor(dta_[:L, :], dt[:L, :], aneg[:L, :], ALU.mult), reads=["dt", alk], writes=["dta"])
            S.op("pe", lambda e, L=L: e.matmul(p_small[:L, :], tri[:L, :L], dta_[:L, :], start=True, stop=True), reads=["tri", "dta"], writes=["p_small"])
            S.op("act", lambda e, L=L: e.copy(cs_col[:L, :], p_small[:L, :]), reads=["p_small"], writes=["cs_col"])
            S.op("act", lambda e, L=L: e.activation(ecs[:L, :], cs_col[:L, :], AF.Exp), reads=["cs_col"], writes=["ecs"])
            S.op("pe", lambda e, L=L: e.matmul(p_small[:, :], sel[L][:L, :], cs_col[:L, :], start=True, stop=True), reads=["sel%d" % L, "cs_col"], writes=["p_small"])
            S.op("act", lambda e: e.copy(csl[:], p_small[:]), reads=["p_small"], writes=["csl"])
            S.op("act", lambda e: e.activation(ecl[:], csl[:], AF.Exp), reads=["csl"], writes=["ecl"])
            S.op("dve", lambda e, L=L: e.tensor_tensor(wdec[:L, :], csl[:L, :], cs_col[:L, :], ALU.subtract), reads=["csl", "cs_col"], writes=["wdec"])
            S.op("act", lambda e, L=L: e.activation(wdec[:L, :], wdec[:L, :], AF.Exp), reads=["wdec"], writes=["wdec"])
            def cbf(e, L=L):
                for g in range(8):
                    ins = e.matmul(p_cb[:L, g, :L], xcT[:, 32 + g, :L], xcT[:, 40 + g, :L], start=True, stop=True)
                return ins
            S.op("pe", cbf, reads=["xcT"], writes=["p_cb"])
            S.op("dve", lambda e, L=L: e.tensor_tensor(cbm[:L, :, :L], p_cb[:L, :, :L], tri[:L, :L].unsqueeze(1).to_broadcast([L, 8, L]), ALU.mult), reads=["p_cb", "tri"], writes=["cbm"])
            S.op("dve", lambda e, L=L: e.tensor_tensor(v3(xdt, L), v3(xtok, L), dt[:L, :].unsqueeze(2).to_broadcast([L, 64, 64]), ALU.mult), reads=xtk + ["dt"], writes=["xdt"])
            S.op("pool", lambda e, L=L: e.tensor_tensor(v3(xdtw, L), v3(xdt, L), wdec[:L, :].unsqueeze(2).to_broadcast([L, 64, 64]), ALU.mult), reads=["xdt", "wdec"], writes=["xdtw"])
            for g in range(8):
                hs = slice(8 * g, 8 * g + 8)
                S.op("dve", lambda e, L=L, hs=hs: e.tensor_tensor(Rg[:L, :, :L], dta_[:L, hs].unsqueeze(2).to_broadcast([L, 8, L]), tri[:L, :L].unsqueeze(1).to_broadcast([L, 8, L]), ALU.mult),
                     reads=["dta", "tri"], writes=["Rg"])
                S.op("pe", lambda e, L=L: e.matmul(p_cs[:L, :, :L], ones[:L, :L], Rg[:L, :, :L], start=True, stop=True), reads=["ones", "Rg"], writes=["p_cs"])
                S.op("dve", lambda e, L=L, hs=hs: e.tensor_tensor(d1[:L, :, :L], p_cs[:L, :, :L], cs_col[:L, hs].unsqueeze(2).to_broadcast([L, 8, L]), ALU.subtract), reads=["p_cs", "cs_col"], writes=["d1"])
                S.op("dve", lambda e, L=L: e.tensor_scalar(d1[:L, :, :L], d1[:L, :, :L], 0.0, None, ALU.min), reads=["d1"], writes=["d1"])
                S.op("act", lambda e, L=L: e.activation(d1[:L, :, :L], d1[:L, :, :L], AF.Exp), reads=["d1"], writes=["d1"])
                S.op("dve", lambda e, L=L, g=g, hs=hs: e.tensor_tensor(M[:L, hs, :L], d1[:L, :, :L], cbm[:L, g:g + 1, :L].to_broadcast([L, 8, L]), ALU.mult), reads=["d1", "cbm"], writes=[("M", g)])

                def ym(e, L=L, g=g):
                    for r in range(8):
                        h = 8 * g + r
                        ins = e.matmul(py[:L, r * 64:(r + 1) * 64], M[:L, h, :L], xdt[:L, h * 64:(h + 1) * 64], start=True, stop=True)
                    return ins
                S.op("pe", ym, reads=[("M", g), "xdt"], writes=["py"])
                S.op("pe", lambda e, L=L, g=g: e.matmul(pys[:L, :], xcT[:, 40 + g, :L], hTb[:, g * 512:(g + 1) * 512], start=True, stop=True), reads=["xcT", "hTb"], writes=["pys"])
                S.op("dve", lambda e, L=L, hs=hs: e.tensor_tensor(ytmp[:L, :].rearrange("p (r q) -> p r q", q=64), pys[:L, :].rearrange("p (r q) -> p r q", q=64),
                                                              ecs[:L, hs].unsqueeze(2).to_broadcast([L, 8, 64]), ALU.mult), reads=["pys", "ecs"], writes=["ytmp"])
                S.op("dve", lambda e, L=L, g=g: e.tensor_tensor(yv[:L, g * 512:(g + 1) * 512], ytmp[:L, :], py[:L, :], ALU.add), reads=["ytmp", "py"], writes=[("yv", g)])
                S.op("pe", lambda e, L=L, g=g: e.matmul(ph[:, :], btok[:L, g * 128:(g + 1) * 128], xdtw[:L, g * 512:(g + 1) * 512], start=True, stop=True), reads=["btok", "xdtw"], writes=["ph"])
                hg = hT[:, g * 512:(g + 1) * 512]
                S.op("pool", lambda e, hg=hg, hs=hs: e.tensor_tensor(hg.rearrange("p (r q) -> p r q", q=64), hg.rearrange("p (r q) -> p r q", q=64),
                                                                    ecl[:, hs].unsqueeze(2).to_broadcast([128, 8, 64]), ALU.mult), reads=["hT", "ecl", "hTb"], writes=["hT"])
                S.op("dve", lambda e, hg=hg: e.tensor_tensor(hg, hg, ph[:, :], ALU.add), reads=["hT", "ph"], writes=["hT"])
            S.op("act", lambda e: e.copy(hTb[:], hT[:]), reads=["hT", "pys"], writes=["hTb"])
            yk = [("yv", g) for g in range(8)]
            S.op("pool", lambda e, L=L: e.tensor_tensor(v3(xdtw, L), v3(xtok, L), dsk[:L, :].unsqueeze(2).to_broadcast([L, 64, 64]), ALU.mult), reads=xtk + [dskk, "xdtw", "ph"], writes=["xdtw"])
            S.op("dve", lambda e, L=L: e.tensor_tensor(yv[:L, :], yv[:L, :], xdtw[:L, :], ALU.add), reads=yk + ["xdtw"], writes=["yv"])
            S.op("act", lambda e, L=L: e.activation(sz[:L, :], zt[:L, :], AF.Silu), reads=["zt"], writes=["sz"])
            S.op("dve", lambda e, L=L: e.tensor_tensor(yv[:L, :], yv[:L, :], sz[:L, :], ALU.mult), reads=["yv", "sz"], writes=["yv"])
            S.op("act", lambda e, L=L: e.activation(xdt[:L, :], yv[:L, :], AF.Square), reads=["yv", "xdt", "py"], writes=["xdt"])
            S.op("dve", lambda e, L=L: e.tensor_reduce(ssq[:L, :], xdt[:L, :].rearrange("p (g q) -> p g q", g=8), AX.X, ALU.add), reads=["xdt"], writes=["ssq"])
            S.op("act", lambda e, L=L: e.activation(ssq[:L, :], ssq[:L, :], AF.Sqrt, bias=EPS, scale=1.0 / 512), reads=["ssq"], writes=["ssq"])
            S.op("dve", lambda e, L=L: e.reciprocal(ssq[:L, :], ssq[:L, :]), reads=["ssq"], writes=["ssq"])
            S.op("dve", lambda e, L=L: e.tensor_tensor(yv[:L, :].rearrange("p (g q) -> p g q", g=8), yv[:L, :].rearrange("p (g q) -> p g q", g=8),
                                                     ssq[:L, :].unsqueeze(2).to_broadcast([L, 8, 512]), ALU.mult), reads=["yv", "ssq"], writes=["yv"])
            S.op("dve", lambda e, L=L: e.tensor_tensor(yb[:L, :], yv[:L, :], ng[:L, :], ALU.mult), reads=["yv", ngk], writes=["yb"])
            S.dma("sp", lambda e, r0=r0, L=L: e.dma_start(out=self.ssd_y[r0:r0 + L, :], in_=yb[:L, :]), reads=["yb"], writes=[("dram", "ssd_y")])
            if last:
                ov = self.o_ssd[l, slot].rearrange("(k q) p n -> (q p) k n", q=2)
                for k4 in range(8):
                    def tro(e, k4=k4):
                        for k in range(4):
                            ins = e.matmul(ph[:, k * 128:(k + 1) * 128], hT[:, (4 * k4 + k) * 128:(4 * k4 + k + 1) * 128], identf[:], start=True, stop=True)
                        return ins
                    S.op("pe", tro, reads=["hT", "identf"], writes=["ph"])
                    S.op("act", lambda e: e.copy(hio[:].rearrange("p k n -> p (k n)"), ph[:]), reads=["ph", "hio"], writes=["hio"])
                    S.dma("sp", lambda e, k4=k4, ov=ov: e.dma_start(out=ov[:, 4 * k4:4 * k4 + 4, :], in_=hio[:]), reads=["hio"], writes=[("dram", "o_ssd")])
        st.done()

    def merge(self, l):
        S = self.S
        st = self.stage()
        sb = st.sb
        rt = sb([128, D], F32); gl = sb([128, 2 * D], F32); sd = sb([128, D], F32); gb = sb([128, 3 * D], BF16)
        sg = sb([128, 3 * D], F32); tmp = sb([128, D], F32); ob = sb([128, D], BF16)
        for (r, n) in self.tiles():
            S.dma("sp", lambda e, r=r, n=n: e.dma_start(out=rt[:n, :], in_=self.br_ret[r:r + n, :]), reads=[("dram", "br_ret")], writes=["rt"])
            S.dma("act", lambda e, r=r, n=n: e.dma_start(out=gl[:n, :], in_=self.br_glu[r:r + n, :]), reads=[("dram", "br_glu")], writes=["gl"])
            S.dma("sp", lambda e, r=r, n=n: e.dma_start(out=sd[:n, :], in_=self.br_ssd[r:r + n, :]), reads=[("dram", "br_ssd")], writes=["sd"])
            S.dma("act", lambda e, r=r, n=n: e.dma_start(out=gb[:n, :], in_=self.pj_gate[r:r + n, :]), reads=[("dram", "pj_gate")], writes=["gb"])
            S.op("act", lambda e, n=n: e.activation(sg[:n, :], gb[:n, :], AF.Sigmoid), reads=["gb"], writes=["sg"])
            S.op("act", lambda e, n=n: e.activation(tmp[:n, :], gl[:n, D:2 * D], AF.Sigmoid), reads=["gl"], writes=["tmp"])
            S.op("dve", lambda e, n=n: e.tensor_tensor(tmp[:n, :], tmp[:n, :], gl[:n, 0:D], ALU.mult), reads=["tmp", "gl"], writes=["tmp"])
            S.op("dve", lambda e, n=n: e.tensor_tensor(tmp[:n, :], tmp[:n, :], sg[:n, D:2 * D], ALU.mult), reads=["tmp", "sg"], writes=["tmp"])
            S.op("pool", lambda e, n=n: e.tensor_tensor(rt[:n, :], rt[:n, :], sg[:n, 0:D], ALU.mult), reads=["rt", "sg"], writes=["rt"])
            S.op("pool", lambda e, n=n: e.tensor_tensor(sd[:n, :], sd[:n, :], sg[:n, 2 * D:3 * D], ALU.mult), reads=["sd", "sg"], writes=["sd"])
            S.op("dve", lambda e, n=n: e.tensor_tensor(tmp[:n, :], tmp[:n, :], rt[:n, :], ALU.add), reads=["tmp", "rt"], writes=["tmp"])
            S.op("dve", lambda e, n=n: e.tensor_tensor(ob[:n, :], tmp[:n, :], sd[:n, :], ALU.add), reads=["tmp", "sd"], writes=["ob"])
            S.dma("sp", lambda e, r=r, n=n: e.dma_start(out=self.merged[r:r + n, :], in_=ob[:n, :]), reads=["ob"], writes=[("dram", "merged")])
        st.done()

    def normadd(self, ysrc, g_post, g_next, final=False):
        S = self.S
        st = self.stage()
        sb = st.sb
        gp, gpk = self.load_row_bcast(st, g_post, D)
        if g_next is not None:
            gn, gnk = self.load_row_bcast(st, g_next, D)
        yts = [sb([128, D], F32) for _ in range(2)]; xts = [sb([128, D], F32) for _ in range(2)]
        junk = sb([128, D], BF16); hbs = [sb([128, D], BF16) for _ in range(2)]
        ss = sb([128, 1], F32); ss2 = sb([128, 1], F32)
        tl = self.tiles()

        def loads(i):
            r, n = tl[i]
            b = i % 2
            S.dma("sp", lambda e, r=r, n=n, b=b: e.dma_start(out=yts[b][:n, :], in_=ysrc[r:r + n, :]), reads=[("dram", ysrc.name)], writes=[("yt", b)])
            S.dma("act", lambda e, r=r, n=n, b=b: e.dma_start(out=xts[b][:n, :], in_=self.xres[r:r + n, :]), reads=[("dram", "xres", r)], writes=[("xt", b)])
        loads(0)
        for i, (r, n) in enumerate(tl):
            b = i % 2
            yt, xt, hb = yts[b], xts[b], hbs[b]
            if i + 1 < len(tl):
                loads(i + 1)
            S.op("dve", lambda e, n=n: e.memset(ss[:n, :], 0.0), writes=["ss"])
            S.op("act", lambda e, n=n, yt=yt: e.activation(junk[:n, :], yt[:n, :], AF.Square, accum_out=ss[:n, :]), reads=[("yt", b), "ss"], writes=["junk", "ss"])
            S.op("act", lambda e, n=n: e.activation(ss[:n, :], ss[:n, :], AF.Sqrt, bias=EPS, scale=1.0 / D), reads=["ss"], writes=["ss"])
            S.op("dve", lambda e, n=n: e.reciprocal(ss[:n, :], ss[:n, :]), reads=["ss"], writes=["ss"])
            S.op("dve", lambda e, n=n, yt=yt: e.scalar_tensor_tensor(yt[:n, :], yt[:n, :], ss[:n, 0:1], gp[:n, :], ALU.mult, ALU.mult), reads=[("yt", b), "ss", gpk], writes=[("yt", b)])
            S.op("pool", lambda e, n=n, yt=yt, xt=xt: e.tensor_tensor(xt[:n, :], xt[:n, :], yt[:n, :], ALU.add), reads=[("xt", b), ("yt", b)], writes=[("xt", b)])
            S.dma("sp", lambda e, r=r, n=n, xt=xt: e.dma_start(out=self.xres[r:r + n, :], in_=xt[:n, :]), reads=[("xt", b)], writes=[("dram", "xres", r)])
            if final:
                S.dma("sp", lambda e, r=r, n=n, xt=xt: e.dma_start(out=self.y[r:r + n, :], in_=xt[:n, :]), reads=[("xt", b)], writes=[("dram", "y")])
            if g_next is not None:
                S.op("dve", lambda e, n=n: e.memset(ss2[:n, :], 0.0), writes=["ss2"])
                S.op("act", lambda e, n=n, xt=xt: e.activation(junk[:n, :], xt[:n, :], AF.Square, accum_out=ss2[:n, :]), reads=[("xt", b), "junk", "ss2"], writes=["junk", "ss2"])
                S.op("act", lambda e, n=n: e.activation(ss2[:n, :], ss2[:n, :], AF.Sqrt, bias=EPS, scale=1.0 / D), reads=["ss2"], writes=["ss2"])
                S.op("dve", lambda e, n=n: e.reciprocal(ss2[:n, :], ss2[:n, :]), reads=["ss2"], writes=["ss2"])
                S.op("dve", lambda e, n=n, xt=xt, hb=hb: e.scalar_tensor_tensor(hb[:n, :], xt[:n, :], ss2[:n, 0:1], gn[:n, :], ALU.mult, ALU.mult), reads=[("xt", b), "ss2", gnk], writes=[("hb", b)])
                S.dma("sp", lambda e, r=r, n=n, hb=hb: e.dma_start(out=self.hn[r:r + n, :], in_=hb[:n, :]), reads=[("hb", b)], writes=[("dram", "hn")])
        S.op("dve", lambda e: e.memset(ss[:1, :], 0.0), reads=[("dram", "xres", r) for (r, n) in tl], writes=[("dram", "xres"), "ss"])
        st.done()

    def attention(self):
        S = self.S
        st = self.stage()
        sb, ps = st.sb, st.ps
        identf = sb([128, 128], F32); identb = sb([128, 128], BF16)
        S.dma("sp", lambda e: e.dma_start(out=identf[:], in_=self.cst["ident_in"]), writes=["identf"])
        S.op("dve", lambda e: e.tensor_copy(identb[:], identf[:]), reads=["identf"], writes=["identb"])
        kT = sb([128, 16, 256], BF16); v = sb([128, 2, D], BF16); qT = sb([128, 16, 128], BF16)
        pexp = sb([128, 4, 256], BF16); pT = sb([128, 8, 128], BF16); ob = sb([128, D], BF16)
        mx = sb([128, 4], F32); sm = sb([128, 4], F32)
        ps_sc = ps([128, 4, 256], F32); pt = ps([128, 8, 128], BF16); po = ps([128, D], F32)
        SC = 512 ** -0.5
        for (r0, ln, p0, slot) in self.seqs:
            for c in range(16):
                ksrc = self.kvp_bf[:, 0:D] if slot == 0 else self.kv_bf[slot, 0]
                S.dma("sp", lambda e, c=c, ksrc=ksrc: e.dma_start(out=kT[:, c, :], in_=ksrc[:, c * 128:(c + 1) * 128], transpose=True),
                      reads=[("dram", "kv_bf")], writes=[("kT", c)])
            vsrc = self.kvp_bf[:, D:2 * D] if slot == 0 else self.kv_bf[slot, 1]
            S.dma("sp", lambda e, vsrc=vsrc: e.dma_start(out=v[:], in_=vsrc.rearrange("(mc p) d -> p mc d", p=128)), reads=[("dram", "kv_bf")], writes=["v"])
            t = 0
            while t < ln:
                n = min(128, ln - t)
                rr = r0 + t
                for c in range(16):
                    S.dma("sp", lambda e, c=c, rr=rr, n=n: e.dma_start(out=qT[:, c, :n], in_=self.q_bf[rr:rr + n, c * 128:(c + 1) * 128], transpose=True),
                          reads=[("dram", "q_bf")], writes=[("qT", c)])

                def scm(e, n=n):
                    for h in range(4):
                        for dc in range(4):
                            ins = e.matmul(ps_sc[:n, h, :], qT[:, h * 4 + dc, :n], kT[:, h * 4 + dc, :], start=(dc == 0), stop=(dc == 3))
                    return ins
                S.op("pe", scm, reads=[("qT", c) for c in range(16)] + [("kT", c) for c in range(16)], writes=["ps_sc"])
                S.op("dve", lambda e, n=n: e.tensor_reduce(mx[:n, :], ps_sc[:n], AX.X, ALU.max), reads=["ps_sc"], writes=["mx"])
                S.op("dve", lambda e, n=n: e.tensor_scalar(mx[:n, :], mx[:n, :], -SC, None, ALU.mult), reads=["mx"], writes=["mx"])
                for h in range(4):
                    S.op("act", lambda e, n=n, h=h: e.activation(pexp[:n, h, :], ps_sc[:n, h, :], AF.Exp, bias=mx[:n, h:h + 1], scale=SC),
                         reads=["ps_sc", "mx"], writes=[("pexp", h)])
                S.op("dve", lambda e, n=n: e.tensor_reduce(sm[:n, :], pexp[:n], AX.X, ALU.add), reads=[("pexp", h) for h in range(4)], writes=[("sm", h) for h in range(4)])

                def trp(e, n=n):
                    for h in range(4):
                        for mc in range(2):
                            ins = e.transpose(pt[:, h * 2 + mc, :n], pexp[:n, h, mc * 128:(mc + 1) * 128], identb[:n, :n])
                    return ins
                S.op("pe", trp, reads=[("pexp", h) for h in range(4)] + ["identb"], writes=["pt"])
                S.op("act", lambda e, n=n: e.copy(pT[:, :, :n], pt[:, :, :n]), reads=["pt"], writes=["pT"])

                def om(e, n=n):
                    for h in range(4):
                        for mc in range(2):
                            ins = e.matmul(po[:n, h * 512:(h + 1) * 512], pT[:, h * 2 + mc, :n], v[:, mc, h * 512:(h + 1) * 512], start=(mc == 0), stop=(mc == 1))
                    return ins
                S.op("pe", om, reads=["pT", "v"], writes=["po"])
                smk = [("sm", h) for h in range(4)]
                S.op("dve", lambda e, n=n: e.reciprocal(sm[:n, :], sm[:n, :]), reads=smk, writes=smk)
                S.op("dve", lambda e, n=n: e.tensor_tensor(ob[:n, :].rearrange("p (h d) -> p h d", h=4), po[:n, :].rearrange("p (h d) -> p h d", h=4),
                                                         sm[:n, :].unsqueeze(2).to_broadcast([n, 4, 512]), ALU.mult), reads=["po"] + smk, writes=["ob"])
                S.dma("sp", lambda e, rr=rr, n=n: e.dma_start(out=self.att[rr:rr + n, :], in_=ob[:n, :]), reads=["ob"], writes=[("dram", "att")])
                t += n
        st.done()

    def evac_relu2(self):
        S = self.S

        def f(st, n, cw, ps, pb, ob, okey):
            if not hasattr(st, "r2"):
                st.r2 = st.sb([128, 512], F32, "r2")
            r2 = st.r2
            S.op("act", lambda e: e.activation(r2[:n, :cw], ps[:n, :cw], AF.Relu), reads=[("lps", pb)], writes=["r2"])
            S.op("dve", lambda e: e.tensor_tensor(ob[:n, :cw], r2[:n, :cw], r2[:n, :cw], ALU.mult), reads=["r2"], writes=[okey])
        return f

    def conv_out(self, l):
        S = self.S
        for (r0, ln, p0, slot) in self.seqs:
            S.dma("pool", lambda e, p0=p0, ln=ln, slot=slot: e.dma_start(out=self.o_conv[l, slot], in_=self.xpad[p0 + ln:p0 + ln + 3, :]),
                  reads=[("dram", "xpad")], writes=[("dram", "o_conv")])
        S.barrier()
        S.emit()

    def conv_init(self, l):
        S = self.S
        st = self.stage()
        z = st.sb([96, 192], BF16)
        S.op("dve", lambda e: e.memset(z[:], 0.0), writes=["z"])
        S.dma("sp", lambda e: e.dma_start(out=self.xpad[16:19, :].rearrange("r (a f) -> (r a) f", f=192), in_=z[:]), reads=["z"], writes=[("dram", "xpad")])
        for j in range(NS):
            p0 = self.seqs[1 + j][2]
            S.dma("pool", lambda e, j=j, p0=p0: e.dma_start(out=self.xpad[p0:p0 + 3, :], in_=self.st_conv[l, j]), writes=[("dram", "xpad")])
        st.done()

    def mem_kv(self, l):
        S = self.S
        self.rms_stage(self.mem, self.memn, self.sw["norm_gains"][l, 6:7, :], 256)
        self.linear(self.memn, D, self.wb["w_xkv"][l], 2 * D, self.evac_store(self.mkv_f, F32), nrows=256)
        S.dma("sp", lambda e: e.dma_start(out=self.o_mk[l], in_=self.mkv_f[:, 0:D]), reads=[("dram", "mkv_f")], writes=[("dram", "o_mk")])
        S.dma("sp", lambda e: e.dma_start(out=self.o_mv[l], in_=self.mkv_f[:, D:2 * D]), reads=[("dram", "mkv_f")], writes=[("dram", "o_mv")])
        for kv in range(2):
            if kv == 0:
                S.dma("pool", lambda e: e.dma_start(out=self.kvp_bf, in_=self.mkv_f), reads=[("dram", "mkv_f")], writes=[("dram", "kv_bf")])
            src = self.st_mk if kv == 0 else self.st_mv
            for j in range(NS):
                S.dma("pool", lambda e, kv=kv, j=j, src=src: e.dma_start(out=self.kv_bf[1 + j, kv], in_=src[l, j]), writes=[("dram", "kv_bf")])
        S.barrier()
        S.emit()

    def build(self, upto="all"):
        S = self.S
        flags = upto.split(",")
        G = lambda l, i: self.sw["norm_gains"][l, i:i + 1, :]
        self.cast_weights()
        self.init_copy()
        if "castonly" in flags:
            S.barrier(); S.emit(); S.close()
            return self.nc
        self.rms_stage(self.xres, self.hn, G(0, 0), self.NT)
        for l in range(2):
            self.mem_kv(l)
            self.conv_init(l)
            cbs = [(c, 512) for c in range(0, C_DT, 512)] + [(C_DT, 64)] + [(c, 512) for c in range(C_GATE, INC, 512)]
            self.linear(self.hn, D, self.wb["w_in"][l], INC, self.evac_inproj(), colblocks=cbs)
            self.conv_out(l)
            if "noret" not in flags:
                self.retention(l)
            if "nos5" not in flags:
                self.s5(l)
            if "nossd" not in flags:
                self.ssd(l)
            if "mix" in flags:
                break
            self.linear(self.ret_y, D, self.wb["w_ret_o"][l], D, self.evac_store(self.br_ret, F32))
            self.linear(self.s5_y, D, self.wb["w_s5_glu"][l], 2 * D, self.evac_store(self.br_glu, F32))
            self.linear(self.ssd_y, 2 * D, self.wb["w_ssd_out"][l], D, self.evac_store(self.br_ssd, F32))
            self.merge(l)
            self.linear(self.merged, D, self.wb["w_mix_out"][l], D, self.evac_store(self.lin_out, F32))
            self.normadd(self.lin_out, G(l, 1), G(l, 2))
            self.linear(self.hn, D, self.wb["w_xq"][l], D, self.evac_store(self.q_bf, BF16))
            self.attention()
            self.linear(self.att, D, self.wb["w_xo"][l], D, self.evac_store(self.lin_out, F32))
            self.normadd(self.lin_out, G(l, 3), G(l, 4))
            self.linear(self.hn, D, self.wb["w_up"][l], 4 * D, self.evac_store(self.hmlp, BF16, func=self.evac_relu2()))
            self.linear(self.hmlp, 4 * D, self.wb["w_down"][l], D, self.evac_store(self.lin_out, F32))
            self.normadd(self.lin_out, G(l, 5), G(l + 1, 0) if l == 0 else None, final=(l == 1))
            if "l0" in flags:
                break
        if "l0" in flags or "mix" in flags:
            st = self.stage()
            xt = st.sb([128, D], F32)
            for (r, n) in self.tiles():
                S.dma("sp", lambda e, r=r, n=n: e.dma_start(out=xt[:n, :], in_=self.xres[r:r + n, :]), reads=[("dram", "xres")], writes=["xt"])
                S.dma("sp", lambda e, r=r, n=n: e.dma_start(out=self.y[r:r + n, :], in_=xt[:n, :]), reads=["xt"], writes=[("dram", "y")])
            st.done()
        S.barrier()
        S.emit()
        S.close()
        return self.nc


def make_in_maps(inputs, TP, cores):
    consts = host_consts(TP)
    maps = []
    for c in cores:
        b = c % 2
        sl = slice(NS * c, NS * (c + 1))
        m = {}
        m["x_in"] = np.concatenate([inputs["x_prompt"][b, :TP], inputs["x_sample"][sl].reshape(NS * SL, D)], axis=0)
        m["mem_in"] = inputs["mem_prompt"][b]
        m["st_ret"] = inputs["state_ret"][:, sl]
        m["st_s5r"] = inputs["state_s5_re"][:, sl]
        m["st_s5i"] = inputs["state_s5_im"][:, sl]
        m["st_ssd"] = inputs["state_ssd"][:, sl]
        m["st_conv"] = inputs["cache_ssd_conv"][:, sl]
        m["st_mk"] = inputs["cache_mem_k"][:, sl].reshape(2, NS, 256, D)
        m["st_mv"] = inputs["cache_mem_v"][:, sl].reshape(2, NS, 256, D)
        for k in WSHAPES:
            m[k] = inputs[k]
        for k in SMALLW:
            m[k] = inputs[k]
        m.update(consts)
        maps.append({k: np.ascontiguousarray(v, dtype=np.float32) for k, v in m.items()})
    return maps


KERNEL_FLAGS = "all"


def kernel(**inputs):
    TP = 8192
    inputs = {k: np.asarray(v) for k, v in inputs.items()}
    b = Builder(TP)
    nc = b.build(KERNEL_FLAGS)
    cores = list(range(8))
    maps = make_in_maps(inputs, TP, cores)
    res = run_bass_kernel_spmd(nc, maps, core_ids=cores)
    R = res.results
    f32 = np.float32
    y_p = np.stack([np.asarray(R[bb]["y"], f32)[:TP] for bb in range(2)])
    y_s = np.concatenate([np.asarray(R[c]["y"], f32)[TP:].reshape(NS, SL, D) for c in cores], axis=0)

    def pstate(name, shape):
        return np.stack([np.asarray(R[bb][name], f32)[:, 0] for bb in range(2)], axis=1).reshape(shape)

    def sstate(name, shape):
        return np.concatenate([np.asarray(R[c][name], f32)[:, 1:] for c in cores], axis=1).reshape(shape)
    p_ret = pstate("o_ret", (2, 2, RH, RDK, RDV))
    p_s5r = pstate("o_s5r", (2, 2, 128, 64))
    p_s5i = pstate("o_s5i", (2, 2, 128, 64))
    p_ssd = pstate("o_ssd", (2, 2, 64, 64, 128))
    p_conv = pstate("o_conv", (2, 2, 3, 6144))
    p_mk = np.stack([np.asarray(R[bb]["o_mk"], f32) for bb in range(2)], axis=1).reshape(2, 2, 256, 4, 512)
    p_mv = np.stack([np.asarray(R[bb]["o_mv"], f32) for bb in range(2)], axis=1).reshape(2, 2, 256, 4, 512)
    s_ret = sstate("o_ret", (2, 32, RH, RDK, RDV))
    s_s5r = sstate("o_s5r", (2, 32, 128, 64))
    s_s5i = sstate("o_s5i", (2, 32, 128, 64))
    s_ssd = sstate("o_ssd", (2, 32, 64, 64, 128))
    s_conv = sstate("o_conv", (2, 32, 3, 6144))
    return (y_p, y_s, p_ret, p_s5r, p_s5i, p_ssd, p_conv, p_mk, p_mv, s_ret, s_s5r, s_s5i, s_ssd, s_conv)
```

```python
import math
import numpy as np
import concourse.bass as bass
import concourse.mybir as mybir
from concourse.bass_utils import run_bass_kernel_spmd

F32 = mybir.dt.float32
BF16 = mybir.dt.bfloat16
I32 = mybir.dt.int32
AF = mybir.ActivationFunctionType
ALU = mybir.AluOpType
AX = mybir.AxisListType


class Sched:
    EPOCH = 24000
    NDMA = 16

    def __init__(self, nc):
        self.nc = nc
        self.engs = ["pe", "act", "dve", "pool", "sp"]
        self.items = {e: [] for e in self.engs}
        self.count = {e: 0 for e in self.engs}
        self.sems = {}
        self.dsems = {}
        self.dcount = {e: 0 for e in self.engs}
        self.seen = {e: {} for e in self.engs}
        self.res = {}
        self._semctx = []
        self.dlast = {}

    def _sem(self, key):
        d = self.sems if key[0] == "E" else self.dsems
        if key not in d:
            cm = self.nc.semaphore("s_%s_%s_%d" % key)
            d[key] = cm.__enter__()
            self._semctx.append(cm)
        return d[key]

    def _deps(self, reads, writes):
        ev = []
        for r in reads:
            st = self.res.get(r)
            if st and st[0] is not None:
                ev.append(st[0])
        for w in writes:
            st = self.res.get(w)
            if st:
                if st[0] is not None:
                    ev.append(st[0])
                ev.extend(st[1])
        return ev

    def _waits(self, eng, events):
        best = {}
        for (k, v) in events:
            if best.get(k, 0) < v:
                best[k] = v
        out = []
        for k, v in best.items():
            if self.seen[eng].get(k, 0) < v:
                self.seen[eng][k] = v
                out.append((k, v))
        return out

    def _mark(self, event, reads, writes):
        for r in reads:
            st = self.res.setdefault(r, [None, []])
            st[1].append(event)
        for w in writes:
            self.res[w] = [event, []]

    def op(self, eng, fn, reads=(), writes=()):
        waits = self._waits(eng, self._deps(reads, writes))
        n = self.count[eng]
        epoch, idx = divmod(n, self.EPOCH)
        self.count[eng] = n + 1
        key = ("E", eng, epoch)
        self._sem(key)
        event = (key, idx + 1)
        self.items[eng].append((waits, fn, key, 1))
        self._mark(event, reads, writes)
        return event

    def dma(self, eng, fn, reads=(), writes=()):
        n = self.dcount[eng]
        self.dcount[eng] = n + 1
        k, rnd = n % self.NDMA, n // self.NDMA
        key = ("D", eng, k)
        self._sem(key)
        ev = self._deps(reads, writes)
        if rnd > 0:
            ev.append((key, 16 * rnd))
        waits = self._waits(eng, ev)
        event = (key, 16 * (rnd + 1))
        self.dlast[key] = 16 * (rnd + 1)
        self.items[eng].append((waits, fn, key, 16))
        self._mark(event, reads, writes)
        return event

    def wait_all(self, eng, keys):
        ev = []
        for kk in keys:
            st = self.res.get(kk)
            if st:
                if st[0] is not None:
                    ev.append(st[0])
                ev.extend(st[1])
        waits = self._waits(eng, ev)
        self.items[eng].append((waits, None, None, 0))

    def barrier(self):
        ev = []
        for f in self.engs:
            n = self.count[f]
            if n > 0:
                epoch, idx = divmod(n - 1, self.EPOCH)
                ev.append((("E", f, epoch), idx + 1))
        for k, v in self.dlast.items():
            ev.append((k, v))
        for e in self.engs:
            self.items[e].append((self._waits(e, list(ev)), None, None, 0))

    def emit(self):
        nc = self.nc
        sched = self

        def run(engname, engine):
            for waits, fn, key, inc in sched.items[engname]:
                for (k, v) in waits:
                    engine.wait_ge(sched._sem(k), v)
                if fn is not None:
                    ins = fn(engine)
                    ins.then_inc(sched._sem(key), inc)

        with nc.Block() as block:
            @block.tensor
            def _(e):
                run("pe", e)

            @block.scalar
            def _(e):
                run("act", e)

            @block.vector
            def _(e):
                run("dve", e)

            @block.gpsimd
            def _(e):
                run("pool", e)

            @block.sync
            def _(e):
                run("sp", e)
        self.items = {e: [] for e in self.engs}

    def close(self):
        for cm in reversed(self._semctx):
            cm.__exit__(None, None, None)
        self._semctx = []


D = 2048
NS = 4
SL = 16
PAST = 1024
EPS = 1e-6
RH, RDK, RDV = 8, 128, 256
INC = 24640
C_Q, C_K, C_V, C_G, C_U, C_Z, C_XBC, C_DT, C_GATE = 0, 1024, 2048, 4096, 6144, 8192, 12288, 18432, 18496
GAMMA = [1.0 - 2.0 ** (-5 - h) for h in range(RH)]
WSHAPES = {"w_in": (D, INC), "w_ret_o": (D, D), "w_s5_glu": (D, 2 * D), "w_ssd_out": (2 * D, D), "w_mix_out": (D, D),
           "w_xq": (D, D), "w_xkv": (D, 2 * D), "w_xo": (D, D), "w_up": (D, 4 * D), "w_down": (4 * D, D)}
SMALLW = {"norm_gains": (7, D), "ret_gn": (D,), "s5_a_re": (128, 64), "s5_a_im": (128, 64), "s5_b_re": (128, 64, 16),
          "s5_b_im": (128, 64, 16), "s5_c_re": (128, 16, 64), "s5_c_im": (128, 16, 64), "s5_d": (D,), "s5_log_dt": (128,),
          "ssd_conv_w": (4, 6144), "ssd_conv_b": (6144,), "ssd_dt_bias": (64,), "ssd_a_log": (64,), "ssd_d": (64,),
          "ssd_norm": (4096,)}


def host_consts(TP):
    NT = TP + NS * SL
    pos = np.concatenate([np.arange(TP)] + [PAST + np.arange(SL)] * NS).astype(np.float32)
    half = 64
    inv = np.exp(-math.log(10000.0) * np.arange(half, dtype=np.float32) / half).astype(np.float32)
    ang = pos[:, None] * inv[None]
    cos, sin = np.cos(ang).astype(np.float32), np.sin(ang).astype(np.float32)
    cs = np.zeros((NT, 2, RH, 2, 64), np.float32)
    sn = np.zeros((NT, 2, RH, 2, 64), np.float32)
    for w in range(2):
        sc = 1.0 if w == 0 else RDK ** -0.5
        cs[:, w, :, :, :] = (cos * sc)[:, None, None, :]
        sn[:, w, :, 0, :] = (-sin * sc)[:, None, :]
        sn[:, w, :, 1, :] = (sin * sc)[:, None, :]
    lg = np.log1p(-np.exp2(-5.0 - np.arange(RH))).astype(np.float64)
    idx = np.arange(64)
    mask = np.exp(np.abs(idx[:, None] - idx[None, :])[:, None, :] * lg[None, :, None]).astype(np.float32)
    din = np.exp((idx + 1.0)[None, :] * lg[:, None])[None].repeat(128, 0).astype(np.float32)
    dup64 = np.exp((63.0 - idx)[:, None] * lg[None, :]).astype(np.float32)
    dup16 = np.exp((15.0 - np.arange(16))[:, None] * lg[None, :]).astype(np.float32)
    tv = np.arange(64, dtype=np.float32)[None].repeat(128, 0)
    ident = np.eye(128, dtype=np.float32)
    tri = (idx[:, None] <= idx[None, :]).astype(np.float32)
    return {"rope_cs": cs.reshape(NT, 2048), "rope_sn": sn.reshape(NT, 2048), "ret_mask": mask, "ret_din": din,
            "ret_dup64": dup64, "ret_dup16": dup16, "tvec": tv, "ident_in": ident, "tri_in": tri}


CONST_SHAPES = lambda NT: {"rope_cs": (NT, 2048), "rope_sn": (NT, 2048), "ret_mask": (64, 8, 64), "ret_din": (128, 8, 64),
                           "ret_dup64": (64, 8), "ret_dup16": (16, 8), "tvec": (128, 64), "ident_in": (128, 128),
                           "tri_in": (64, 64)}


from contextlib import ExitStack


class Builder:
    def __init__(self, TP, dbg_out=()):
        self.TP = TP
        self.NT = NT = TP + NS * SL
        self.nc = nc = bass.Bass("TRN2", target_bir_lowering=False)
        self.S = Sched(nc)
        self.uid = 0
        di = lambda name, shape, dt=F32: nc.dram_tensor(name, list(shape), dt, kind="ExternalInput").ap()
        do = lambda name, shape, dt=F32: nc.dram_tensor(name, list(shape), dt, kind="ExternalOutput").ap()
        ds = lambda name, shape, dt: nc.dram_tensor(name, list(shape), dt, kind=("ExternalOutput" if name in dbg_out else "Internal")).ap()
        self.x_in = di("x_in", (NT, D))
        self.mem = di("mem_in", (256, D))
        self.st_ret = di("st_ret", (2, NS, RH, RDK, RDV))
        self.st_s5r = di("st_s5r", (2, NS, 128, 64))
        self.st_s5i = di("st_s5i", (2, NS, 128, 64))
        self.st_ssd = di("st_ssd", (2, NS, 64, 64, 128))
        self.st_conv = di("st_conv", (2, NS, 3, 6144))
        self.st_mk = di("st_mk", (2, NS, 256, D))
        self.st_mv = di("st_mv", (2, NS, 256, D))
        self.w = {k: di(k, (2,) + v) for k, v in WSHAPES.items()}
        self.sw = {k: di(k, (2,) + v) for k, v in SMALLW.items()}
        self.cst = {k: di(k, v) for k, v in CONST_SHAPES(NT).items()}
        self.y = do("y", (NT, D))
        self.o_ret = do("o_ret", (2, 1 + NS, RH, RDK, RDV))
        self.o_s5r = do("o_s5r", (2, 1 + NS, 128, 64))
        self.o_s5i = do("o_s5i", (2, 1 + NS, 128, 64))
        self.o_ssd = do("o_ssd", (2, 1 + NS, 64, 64, 128))
        self.o_conv = do("o_conv", (2, 1 + NS, 3, 6144))
        self.o_mk = do("o_mk", (2, 256, D))
        self.o_mv = do("o_mv", (2, 256, D))
        self.wb = {k: ds(k + "_bf", (2,) + v, BF16) for k, v in WSHAPES.items()}
        self.xres = ds("xres", (NT, D), F32)
        self.hn = ds("hn", (NT, D), BF16)
        self.pj_qkvg = ds("pj_qkvg", (NT, 6144), BF16)
        self.pj_u = ds("pj_u", (NT, 2048), BF16)
        self.pj_z = ds("pj_z", (NT, 4096), BF16)
        self.pj_gate = ds("pj_gate", (NT, 6144), BF16)
        self.dtf = ds("dtf", (NT, 64), F32)
        self.xpad = ds("xpad", (16 + NT + 3 * (1 + NS) + 64, 6144), BF16)
        self.ret_y = ds("ret_y", (NT, D), BF16)
        self.s5_y = ds("s5_y", (NT, D), BF16)
        self.ssd_y = ds("ssd_y", (NT, 2 * D), BF16)
        self.br_ret = ds("br_ret", (NT, D), F32)
        self.br_glu = ds("br_glu", (NT, 2 * D), F32)
        self.br_ssd = ds("br_ssd", (NT, D), F32)
        self.merged = ds("merged", (NT, D), BF16)
        self.lin_out = ds("lin_out", (NT, D), F32)
        self.q_bf = ds("q_bf", (NT, D), BF16)
        self.att = ds("att", (NT, D), BF16)
        self.hmlp = ds("hmlp", (NT, 4 * D), BF16)
        self.memn = ds("memn", (256, D), BF16)
        self.mkv_f = ds("mkv_f", (256, 2 * D), F32)
        self.kvp_bf = ds("kvp_bf", (256, 2 * D), BF16)
        self.kv_bf = ds("kv_bf", (1 + NS, 2, 256, D), BF16)
        self.seqs = [(0, TP, 16, 0)] + [(TP + SL * j, SL, 16 + TP + 3 + (SL + 3) * j, 1 + j) for j in range(NS)]

    def name(self, p):
        self.uid += 1
        return "%s%d" % (p, self.uid)

    def stage(self):
        b = self

        class St:
            def __init__(s):
                s.stack = ExitStack()

            def sb(s, shape, dt=F32, nm="t"):
                return s.stack.enter_context(b.nc.sbuf_tensor(b.name(nm), list(shape), dt))

            def ps(s, shape, dt=F32, nm="p"):
                return s.stack.enter_context(b.nc.psum_tensor(b.name(nm), list(shape), dt))

            def done(s):
                b.S.barrier()
                b.S.emit()
                s.stack.close()
        return St()

    def tiles(self):
        out, r = [], 0
        while r < self.NT:
            n = min(128, self.NT - r)
            out.append((r, n))
            r += n
        return out

    def chunks(self):
        out = []
        for si, (r0, ln, _, _) in enumerate(self.seqs):
            c = min(64, ln)
            for k in range(ln // c):
                out.append((si, r0 + k * c, c, k == 0, k == ln // c - 1))
        return out

    def load_row_bcast(self, st, ap_row, width, npart=128, dt=F32, eng="sp"):
        t = st.sb([npart, width], dt, "rb")
        key = self.name("rbk")
        self.S.dma(eng, lambda e: e.dma_start(out=t[:], in_=ap_row.to_broadcast([npart, width])), writes=[key])
        return t, key

    def cast_weights(self):
        S = self.S
        for k, (rows, cols) in WSHAPES.items():
            step = max(1, (4 << 20) // cols)
            for l in range(2):
                for r in range(0, rows, step):
                    rr = min(step, rows - r)
                    S.dma("pool", lambda e, k=k, l=l, r=r, rr=rr: e.dma_start(out=self.wb[k][l, r:r + rr, :], in_=self.w[k][l, r:r + rr, :]),
                          writes=[("dram", k + "_bf")])
        S.barrier()
        S.emit()

    def init_copy(self):
        S = self.S
        S.dma("sp", lambda e: e.dma_start(out=self.xres, in_=self.x_in), writes=["xres"])
        for l in range(2):
            pass
        S.barrier()
        S.emit()

    def rms_stage(self, src, dst, gain_row, nrows):
        S = self.S
        st = self.stage()
        g, gk = self.load_row_bcast(st, gain_row, D)
        xt = [st.sb([128, D], F32, "xt") for _ in range(2)]
        hb = [st.sb([128, D], BF16, "hb") for _ in range(2)]
        junk = st.sb([128, D], BF16, "junk")
        ss = [st.sb([128, 1], F32, "ss") for _ in range(2)]
        r, i = 0, 0
        while r < nrows:
            n = min(128, nrows - r)
            b = i % 2
            S.dma("sp", lambda e, r=r, n=n, b=b: e.dma_start(out=xt[b][:n, :], in_=src[r:r + n, :]), reads=[("dram", src.name)], writes=[("xt", b)])
            S.op("dve", lambda e, n=n, b=b: e.memset(ss[b][:n, :], 0.0), writes=[("ss", b)])
            S.op("act", lambda e, n=n, b=b: e.activation(junk[:n, :], xt[b][:n, :], AF.Square, accum_out=ss[b][:n, :]), reads=[("xt", b), ("ss", b)], writes=["junk", ("ss", b)])
            S.op("act", lambda e, n=n, b=b: e.activation(ss[b][:n, :], ss[b][:n, :], AF.Sqrt, bias=EPS, scale=1.0 / D), reads=[("ss", b)], writes=[("ss", b)])
            S.op("dve", lambda e, n=n, b=b: e.reciprocal(ss[b][:n, :], ss[b][:n, :]), reads=[("ss", b)], writes=[("ss", b)])
            S.op("dve", lambda e, n=n, b=b: e.scalar_tensor_tensor(hb[b][:n, :], xt[b][:n, :], ss[b][:n, 0:1], g[:n, :], ALU.mult, ALU.mult),
                 reads=[("xt", b), ("ss", b), gk], writes=[("hb", b)])
            S.dma("sp", lambda e, r=r, n=n, b=b: e.dma_start(out=dst[r:r + n, :], in_=hb[b][:n, :]), reads=[("hb", b)], writes=[("dram", dst.name)])
            r += n
            i += 1
        st.done()

    def linear(self, A, K, W, N, evac, nrows=None, colblocks=None):
        S = self.S
        nrows = self.NT if nrows is None else nrows
        KC = K // 128
        TBL = {2048: 1024, 4096: 512, 8192: 256}[K]
        st = self.stage()
        NAT = 2 if K <= 4096 else 1
        ATs = [st.sb([128, KC, TBL + 64], BF16, "AT") for _ in range(NAT)]
        WT = [st.sb([128, KC, 512], BF16, "WT") for _ in range(2)]
        PS = [st.ps([128, 512], F32, "lps") for _ in range(2)]
        self.lin_st = st
        if colblocks is None:
            colblocks = [(c, min(512, N - c)) for c in range(0, N, 512)]
        Wv = W.rearrange("(kc p) n -> p kc n", p=128)
        widx, pidx = 0, 0
        blocks = []
        rb = 0
        while rb < nrows:
            nb = min(TBL, nrows - rb)
            if 0 < nrows - (rb + nb) <= 64:
                nb = nrows - rb
            blocks.append((rb, nb))
            rb += nb

        def load_at(bi):
            rb, nb = blocks[bi]
            ab = bi % NAT
            for kc in range(KC):
                S.dma("sp", lambda e, kc=kc, rb=rb, nb=nb, ab=ab: e.dma_start(out=ATs[ab][:, kc, :nb], in_=A[rb:rb + nb, kc * 128:(kc + 1) * 128], transpose=True),
                      reads=[("dram", A.name)], writes=[("AT", ab, kc)])
        load_at(0)
        for bi, (rb, nb) in enumerate(blocks):
            ab = bi % NAT
            AT = ATs[ab]
            if NAT == 2 and bi + 1 < len(blocks):
                load_at(bi + 1)
            for (c0, cw) in colblocks:
                wbuf = widx % 2
                widx += 1
                S.dma("act", lambda e, c0=c0, cw=cw, wbuf=wbuf: e.dma_start(out=WT[wbuf][:, :, :cw], in_=Wv[:, :, c0:c0 + cw]),
                      reads=[("dram", W.name)], writes=[("WT", wbuf)])
                t0 = 0
                while t0 < nb:
                    n = min(128, nb - t0)
                    pb = pidx % 2
                    pidx += 1

                    def mm(e, t0=t0, n=n, cw=cw, wbuf=wbuf, pb=pb, AT=AT):
                        for kc in range(KC):
                            ins = e.matmul(PS[pb][:n, :cw], AT[:, kc, t0:t0 + n], WT[wbuf][:, kc, :cw], start=(kc == 0), stop=(kc == KC - 1))
                        return ins
                    S.op("pe", mm, reads=[("AT", ab, kc) for kc in range(KC)] + [("WT", wbuf)], writes=[("lps", pb)])
                    evac(st, rb + t0, n, c0, cw, PS[pb], pb)
                    t0 += n
            if NAT == 1 and bi + 1 < len(blocks):
                load_at(bi + 1)
        st.done()

    def evac_store(self, dst, dt, coloff=0, func=None):
        S = self.S
        bufs = {}

        def evac(st, r0, n, c0, cw, ps, pb):
            if "ob" not in bufs:
                bufs["ob"] = [st.sb([128, 512], dt, "ob") for _ in range(2)]
                bufs["i"] = 0
            ob = bufs["ob"][bufs["i"] % 2]
            okey = ("ob", bufs["i"] % 2)
            bufs["i"] += 1
            if func is None:
                S.op("act", lambda e: e.copy(ob[:n, :cw], ps[:n, :cw]), reads=[("lps", pb)], writes=[okey])
            else:
                func(st, n, cw, ps, pb, ob, okey)
            S.dma("sp", lambda e: e.dma_start(out=dst[r0:r0 + n, coloff + c0:coloff + c0 + cw], in_=ob[:n, :cw]), reads=[okey], writes=[("dram", dst.name)])
        return evac

    def segs(self, r0, n):
        out = []
        for (s0, ln, p0, _) in self.seqs:
            a, b = max(r0, s0), min(r0 + n, s0 + ln)
            if a < b:
                out.append((a - r0, b - a, p0 + 3 + (a - s0)))
        return out

    def evac_inproj(self):
        S = self.S
        bufs = {}

        def evac(st, r0, n, c0, cw, ps, pb):
            if "ob" not in bufs:
                bufs["ob"] = [st.sb([128, 512], BF16, "ob") for _ in range(2)]
                bufs["of"] = st.sb([128, 64], F32, "of")
                bufs["i"] = 0
            ob = bufs["ob"][bufs["i"] % 2]
            okey = ("ob", bufs["i"] % 2)
            bufs["i"] += 1
            S.op("act", lambda e: e.copy(ob[:n, :cw], ps[:n, :cw]), reads=[("lps", pb)], writes=[okey])
            tgt = None
            if c0 < C_U:
                tgt, lc = self.pj_qkvg, c0
            elif c0 < C_Z:
                tgt, lc = self.pj_u, c0 - C_U
            elif c0 < C_XBC:
                tgt, lc = self.pj_z, c0 - C_Z
            elif c0 >= C_GATE:
                tgt, lc = self.pj_gate, c0 - C_GATE
            if tgt is not None:
                S.dma("sp", lambda e: e.dma_start(out=tgt[r0:r0 + n, lc:lc + cw], in_=ob[:n, :cw]), reads=[okey], writes=[("dram", tgt.name)])
            if C_XBC <= c0 < C_DT:
                for (o, cnt, prow) in self.segs(r0, n):
                    S.dma("sp", lambda e, o=o, cnt=cnt, prow=prow: e.dma_start(out=self.xpad[prow:prow + cnt, c0 - C_XBC:c0 - C_XBC + cw], in_=ob[o:o + cnt, :cw]),
                          reads=[okey], writes=[("dram", "xpad")])
            if c0 == C_DT:
                of = bufs["of"]
                S.op("dve", lambda e: e.tensor_copy(of[:n, :cw], ps[:n, :cw]), reads=[("lps", pb)], writes=["of"])
                S.dma("sp", lambda e: e.dma_start(out=self.dtf[r0:r0 + n, :], in_=of[:n, :cw]), reads=["of"], writes=[("dram", "dtf")])
        return evac

    def retention(self, l):
        S, nc = self.S, self.nc
        st = self.stage()
        sb, ps = st.sb, st.ps
        mask = sb([64, 8, 64], F32); din = sb([128, 8, 64], F32); dup64 = sb([64, 8], F32); dup16 = sb([16, 8], F32)
        identb = sb([128, 128], BF16); identf = sb([128, 128], F32)
        S.dma("sp", lambda e: e.dma_start(out=mask[:], in_=self.cst["ret_mask"]), writes=["mask"])
        S.dma("sp", lambda e: e.dma_start(out=din[:], in_=self.cst["ret_din"]), writes=["din"])
        S.dma("sp", lambda e: e.dma_start(out=dup64[:], in_=self.cst["ret_dup64"]), writes=["dup64"])
        S.dma("sp", lambda e: e.dma_start(out=dup16[:], in_=self.cst["ret_dup16"]), writes=["dup16"])
        S.dma("sp", lambda e: e.dma_start(out=identf[:], in_=self.cst["ident_in"]), writes=["identf"])
        S.op("dve", lambda e: e.tensor_copy(identb[:], identf[:]), reads=["identf"], writes=["identb"])
        gn, gnk = self.load_row_bcast(st, self.sw["ret_gn"][l:l + 1, :], D, 64)
        qk = sb([64, 2048], BF16); vt = sb([64, 2048], BF16); gt = sb([64, 2048], BF16)
        cs = sb([64, 2048], F32); sn = sb([64, 2048], F32)
        t1 = sb([64, 2048], F32); t2 = sb([64, 2048], F32)
        qkr = sb([64, 2048], BF16); kd = sb([64, 8, 128], BF16)
        qkT = sb([128, 16, 64], BF16); qdT = sb([128, 8, 64], BF16)
        sm = sb([64, 8, 64], BF16)
        o_sb = sb([64, 8, 256], F32); osq = sb([64, 8, 256], F32)
        Sf = sb([128, 8, 256], F32); Sb = sb([128, 8, 256], BF16)
        s1 = sb([64, 8], F32); s2 = sb([64, 8], F32); mean = sb([64, 8], F32); msq = sb([64, 8], F32)
        sg = sb([64, 2048], F32); yb = sb([64, 2048], BF16)
        tq = ps([128, 16, 64], BF16); ps_s = ps([64, 8, 64], F32)
        ps_o = [ps([64, 2, 256], F32) for _ in range(2)]; ps_S = [ps([128, 2, 256], F32) for _ in range(2)]
        for (si, r0, L, first, last) in self.chunks():
            slot = self.seqs[si][3]
            if first:
                if slot == 0:
                    S.op("dve", lambda e: e.memset(Sf[:], 0.0), writes=["Sf"])
                else:
                    S.dma("sp", lambda e, slot=slot: e.dma_start(out=Sf[:], in_=self.st_ret[l, slot - 1].rearrange("h d e -> d h e")), writes=["Sf"])
                S.op("act", lambda e: e.copy(Sb[:], Sf[:]), reads=["Sf"], writes=["Sb"])
            S.dma("sp", lambda e, r0=r0, L=L: e.dma_start(out=qk[:L, :], in_=self.pj_qkvg[r0:r0 + L, C_Q:C_Q + 2048]), reads=[("dram", "pj_qkvg")], writes=["qk"])
            S.dma("sp", lambda e, r0=r0, L=L: e.dma_start(out=vt[:L, :], in_=self.pj_qkvg[r0:r0 + L, C_V:C_V + 2048]), reads=[("dram", "pj_qkvg")], writes=["vt"])
            S.dma("sp", lambda e, r0=r0, L=L: e.dma_start(out=gt[:L, :], in_=self.pj_qkvg[r0:r0 + L, C_G:C_G + 2048]), reads=[("dram", "pj_qkvg")], writes=["gt"])
            S.dma("act", lambda e, r0=r0, L=L: e.dma_start(out=cs[:L, :], in_=self.cst["rope_cs"][r0:r0 + L, :]), writes=["cs"])
            S.dma("act", lambda e, r0=r0, L=L: e.dma_start(out=sn[:L, :], in_=self.cst["rope_sn"][r0:r0 + L, :]), writes=["sn"])
            v4 = lambda t, L=L: t[:L, :].rearrange("p (a two d) -> p a two d", two=2, d=64)
            S.op("dve", lambda e, L=L: e.tensor_tensor(t1[:L, :], qk[:L, :], cs[:L, :], ALU.mult), reads=["qk", "cs"], writes=["t1"])
            S.op("pool", lambda e, L=L, v4=v4: e.tensor_tensor(v4(t2)[:, :, 0, :], v4(qk)[:, :, 1, :], v4(sn)[:, :, 0, :], ALU.mult), reads=["qk", "sn"], writes=["t2a"])
            S.op("pool", lambda e, L=L, v4=v4: e.tensor_tensor(v4(t2)[:, :, 1, :], v4(qk)[:, :, 0, :], v4(sn)[:, :, 1, :], ALU.mult), reads=["qk", "sn"], writes=["t2b"])
            S.op("dve", lambda e, L=L: e.tensor_tensor(qkr[:L, :], t1[:L, :], t2[:L, :], ALU.add), reads=["t1", "t2a", "t2b"], writes=["qkr"])
            dup = dup64 if L == 64 else dup16
            S.op("dve", lambda e, L=L, dup=dup: e.tensor_tensor(kd[:L], qkr[:L, 1024:2048].rearrange("p (h d) -> p h d", h=8),
                                                               dup[:L, :].unsqueeze(2).to_broadcast([L, 8, 128]), ALU.mult),
                 reads=["qkr", "dup64", "dup16"], writes=["kd"])

            def tr(e, L=L):
                for i in range(16):
                    ins = e.transpose(tq[:, i, :L], qkr[:L, i * 128:(i + 1) * 128], identb[:L, :L])
                return ins
            S.op("pe", tr, reads=["qkr", "identb"], writes=["tq"])
            S.op("act", lambda e, L=L: e.copy(qkT[:, :, :L], tq[:, :, :L]), reads=["tq"], writes=["qkT"])
            S.op("dve", lambda e, L=L: e.tensor_tensor(qdT[:, :, :L], qkT[:, 0:8, :L], din[:, :, :L], ALU.mult), reads=["qkT", "din"], writes=["qdT"])

            def sc(e, L=L):
                for h in range(8):
                    ins = e.matmul(ps_s[:L, h, :L], qkT[:, 8 + h, :L], qkT[:, h, :L], start=True, stop=True)
                return ins
            S.op("pe", sc, reads=["qkT"], writes=["ps_s"])
            S.op("dve", lambda e, L=L: e.tensor_tensor(sm[:L, :, :L], ps_s[:L, :, :L], mask[:L, :, :L], ALU.mult), reads=["ps_s", "mask"], writes=["sm"])
            for hp in range(4):
                pb = hp % 2

                def om(e, L=L, hp=hp, pb=pb):
                    for hh in range(2):
                        h = 2 * hp + hh
                        e.matmul(ps_o[pb][:L, hh, :], sm[:L, h, :L], vt[:L, h * 256:(h + 1) * 256], start=True, stop=False)
                        ins = e.matmul(ps_o[pb][:L, hh, :], qdT[:, h, :L], Sb[:, h, :], start=False, stop=True)
                    return ins
                S.op("pe", om, reads=["sm", "vt", "qdT", "Sb"], writes=[("ps_o", pb)])
                S.op("act", lambda e, L=L, hp=hp, pb=pb: e.copy(o_sb[:L, 2 * hp:2 * hp + 2, :], ps_o[pb][:L, :, :]), reads=[("ps_o", pb)], writes=[("o_sb", hp)])
            for hp in range(4):
                pb = hp % 2

                def sm_(e, L=L, hp=hp, pb=pb):
                    for hh in range(2):
                        h = 2 * hp + hh
                        ins = e.matmul(ps_S[pb][:, hh, :], kd[:L, h, :], vt[:L, h * 256:(h + 1) * 256], start=True, stop=True)
                    return ins
                S.op("pe", sm_, reads=["kd", "vt"], writes=[("ps_S", pb)])
                for hh in range(2):
                    h = 2 * hp + hh
                    S.op("dve", lambda e, h=h, hh=hh, pb=pb, L=L: e.scalar_tensor_tensor(Sf[:, h, :], Sf[:, h, :], float(GAMMA[h] ** L), ps_S[pb][:, hh, :], ALU.mult, ALU.add),
                         reads=[("ps_S", pb), "Sf"], writes=["Sf"])
            S.op("act", lambda e: e.copy(Sb[:], Sf[:]), reads=["Sf"], writes=["Sb"])
            if last:
                dst = self.o_ret[l, slot].rearrange("h d e -> d h e")
                S.dma("sp", lambda e, dst=dst: e.dma_start(out=dst, in_=Sf[:]), reads=["Sf"], writes=[("dram", "o_ret")])
            okeys = [("o_sb", hp) for hp in range(4)]
            S.op("dve", lambda e, L=L: e.tensor_reduce(s1[:L, :], o_sb[:L], AX.X, ALU.add), reads=okeys, writes=["s1"])
            S.op("act", lambda e, L=L: e.activation(osq[:L], o_sb[:L], AF.Square), reads=okeys, writes=["osq"])
            S.op("dve", lambda e, L=L: e.tensor_reduce(s2[:L, :], osq[:L], AX.X, ALU.add), reads=["osq"], writes=["s2"])
            S.op("dve", lambda e, L=L: e.tensor_scalar(mean[:L, :], s1[:L, :], 1.0 / 256, None, ALU.mult), reads=["s1"], writes=["mean"])
            S.op("dve", lambda e, L=L: e.tensor_tensor(msq[:L, :], mean[:L, :], mean[:L, :], ALU.mult), reads=["mean"], writes=["msq"])
            S.op("dve", lambda e, L=L: e.scalar_tensor_tensor(s2[:L, :], s2[:L, :], 1.0 / 256, msq[:L, :], ALU.mult, ALU.subtract), reads=["s2", "msq"], writes=["s2"])
            S.op("act", lambda e, L=L: e.activation(s2[:L, :], s2[:L, :], AF.Sqrt, bias=EPS, scale=1.0), reads=["s2"], writes=["s2"])
            S.op("dve", lambda e, L=L: e.reciprocal(s2[:L, :], s2[:L, :]), reads=["s2"], writes=["s2"])
            S.op("dve", lambda e, L=L: e.tensor_tensor(osq[:L], o_sb[:L], mean[:L, :].unsqueeze(2).to_broadcast([L, 8, 256]), ALU.subtract), reads=okeys + ["mean", "osq"], writes=["osq"])
            S.op("dve", lambda e, L=L: e.tensor_tensor(osq[:L], osq[:L], s2[:L, :].unsqueeze(2).to_broadcast([L, 8, 256]), ALU.mult), reads=["osq", "s2"], writes=["osq"])
            S.op("act", lambda e, L=L: e.activation(sg[:L, :], gt[:L, :], AF.Silu), reads=["gt"], writes=["sg"])
            S.op("pool", lambda e, L=L: e.tensor_tensor(sg[:L, :], sg[:L, :], gn[:L, :], ALU.mult), reads=["sg", gnk], writes=["sg"])
            S.op("dve", lambda e, L=L: e.tensor_tensor(yb[:L, :], osq[:L].rearrange("p h e -> p (h e)"), sg[:L, :], ALU.mult), reads=["osq", "sg"], writes=["yb"])
            S.dma("sp", lambda e, r0=r0, L=L: e.dma_start(out=self.ret_y[r0:r0 + L, :], in_=yb[:L, :]), reads=["yb"], writes=[("dram", "ret_y")])
        st.done()


    def trig(self, st, ang, cos_o, sin_o, shape, key_in, key_c, key_s):
        S = self.S
        ki = st.sb(shape, I32, "ki"); kf = st.sb(shape, F32, "kf"); s2 = st.sb(shape, F32, "s2"); s4 = st.sb(shape, F32, "s4")
        k = self.name("trg")
        S.op("dve", lambda e: e.tensor_scalar(ki[:], ang(), 1.0 / (2 * math.pi), None, ALU.mult), reads=[key_in], writes=[k + "ki"])
        S.op("dve", lambda e: e.tensor_copy(kf[:], ki[:]), reads=[k + "ki"], writes=[k + "kf"])
        S.op("dve", lambda e: e.scalar_tensor_tensor(kf[:], kf[:], -2 * math.pi, ang(), ALU.mult, ALU.add), reads=[k + "kf", key_in], writes=[k + "kf"])
        S.op("act", lambda e: e.activation(s2[:], kf[:], AF.Sin, scale=0.5), reads=[k + "kf"], writes=[k + "s2"])
        S.op("act", lambda e: e.activation(s4[:], kf[:], AF.Sin, scale=0.25), reads=[k + "kf"], writes=[k + "s4"])
        S.op("dve", lambda e: e.tensor_tensor(s4[:], s4[:], s4[:], ALU.mult), reads=[k + "s4"], writes=[k + "s4"])
        S.op("dve", lambda e: e.tensor_scalar(s4[:], s4[:], -2.0, 1.0, ALU.mult, ALU.add), reads=[k + "s4"], writes=[k + "s4"])
        S.op("dve", lambda e: e.scalar_tensor_tensor(sin_o(), s2[:], 2.0, s4[:], ALU.mult, ALU.mult), reads=[k + "s2", k + "s4"], writes=[key_s])
        S.op("dve", lambda e: e.tensor_tensor(s2[:], s2[:], s2[:], ALU.mult), reads=[k + "s2", key_s], writes=[k + "s2"])
        S.op("dve", lambda e: e.tensor_scalar(cos_o(), s2[:], -2.0, 1.0, ALU.mult, ALU.add), reads=[k + "s2"], writes=[key_c])

    def s5(self, l):
        S, nc = self.S, self.nc
        st = self.stage()
        sb, ps = st.sb, st.ps
        QH = 32
        identf = sb([128, 128], F32); identb = sb([128, 128], BF16)
        S.dma("sp", lambda e: e.dma_start(out=identf[:], in_=self.cst["ident_in"]), writes=["identf"])
        S.op("dve", lambda e: e.tensor_copy(identb[:], identf[:]), reads=["identf"], writes=["identb"])
        tv = sb([128, 64], F32)
        S.dma("sp", lambda e: e.dma_start(out=tv[:], in_=self.cst["tvec"]), writes=["tv"])
        COS = sb([128, 64, 64], F32); SIN = sb([128, 64, 64], F32); MT = sb([128, 64, 64], F32)
        MT16 = sb([128, 64, 16], F32)
        LB = sb([128, 64, 2, 128], BF16); CB = sb([128, 64, 2, 32], BF16)
        arT = sb([128, 64], F32); aiT = sb([128, 64], F32)
        pst = self.stage()
        psb = pst.sb
        are = psb([64, 128], F32); aim = psb([64, 128], F32); dtq = psb([64, 2], F32); dtE = psb([64, 128], F32)
        S.dma("sp", lambda e: e.dma_start(out=are[:], in_=self.sw["s5_a_re"][l].rearrange("(q g) p -> q (g p)", g=2)), writes=["are"])
        S.dma("sp", lambda e: e.dma_start(out=aim[:], in_=self.sw["s5_a_im"][l].rearrange("(q g) p -> q (g p)", g=2)), writes=["aim"])
        S.dma("sp", lambda e: e.dma_start(out=dtq[:], in_=self.sw["s5_log_dt"][l].rearrange("(q g) -> q g", g=2)), writes=["dtq"])
        S.op("act", lambda e: e.activation(dtq[:], dtq[:], AF.Exp), reads=["dtq"], writes=["dtq"])
        for g2 in range(2):
            S.op("dve", lambda e, g2=g2: e.tensor_copy(dtE[:, g2 * 64:(g2 + 1) * 64], dtq[:, g2:g2 + 1].to_broadcast([64, 64])), reads=["dtq"], writes=["dtE%d" % g2])
        dk = ["dtE0", "dtE1"]
        mag = psb([64, 128], F32); th = psb([64, 128], F32); cth = psb([64, 128], F32); sth = psb([64, 128], F32)
        S.op("dve", lambda e: e.tensor_tensor(mag[:], dtE[:], are[:], ALU.mult), reads=dk + ["are"], writes=["mag"])
        S.op("act", lambda e: e.activation(mag[:], mag[:], AF.Exp), reads=["mag"], writes=["mag"])
        S.op("dve", lambda e: e.tensor_tensor(th[:], dtE[:], aim[:], ALU.mult), reads=dk + ["aim"], writes=["th"])
        self.trig(pst, lambda: th[:], lambda: cth[:], lambda: sth[:], [64, 128], "th", "cth", "sth")
        ar = psb([64, 128], F32); ai = psb([64, 128], F32); nr = psb([64, 128], F32); den = psb([64, 128], F32)
        cr = psb([64, 128], F32); ci = psb([64, 128], F32); tmp = psb([64, 128], F32)
        S.op("dve", lambda e: e.tensor_tensor(ar[:], mag[:], cth[:], ALU.mult), reads=["mag", "cth"], writes=["ar"])
        S.op("dve", lambda e: e.tensor_tensor(ai[:], mag[:], sth[:], ALU.mult), reads=["mag", "sth"], writes=["ai"])
        S.op("dve", lambda e: e.tensor_scalar(nr[:], ar[:], -1.0, None, ALU.add), reads=["ar"], writes=["nr"])
        S.op("dve", lambda e: e.tensor_tensor(den[:], are[:], are[:], ALU.mult), reads=["are"], writes=["den"])
        S.op("dve", lambda e: e.tensor_tensor(tmp[:], aim[:], aim[:], ALU.mult), reads=["aim"], writes=["tmp"])
        S.op("dve", lambda e: e.tensor_tensor(den[:], den[:], tmp[:], ALU.add), reads=["den", "tmp"], writes=["den"])
        S.op("dve", lambda e: e.reciprocal(den[:], den[:]), reads=["den"], writes=["den"])
        S.op("dve", lambda e: e.tensor_tensor(cr[:], nr[:], are[:], ALU.mult), reads=["nr", "are"], writes=["cr"])
        S.op("dve", lambda e: e.tensor_tensor(tmp[:], ai[:], aim[:], ALU.mult), reads=["ai", "aim", "den"], writes=["tmp"])
        S.op("dve", lambda e: e.tensor_tensor(cr[:], cr[:], tmp[:], ALU.add), reads=["cr", "tmp"], writes=["cr"])
        S.op("dve", lambda e: e.tensor_tensor(cr[:], cr[:], den[:], ALU.mult), reads=["cr", "den"], writes=["cr"])
        S.op("dve", lambda e: e.tensor_tensor(ci[:], ai[:], are[:], ALU.mult), reads=["ai", "are"], writes=["ci"])
        S.op("dve", lambda e: e.tensor_tensor(tmp[:], nr[:], aim[:], ALU.mult), reads=["nr", "aim", "cr"], writes=["tmp"])
        S.op("dve", lambda e: e.tensor_tensor(ci[:], ci[:], tmp[:], ALU.subtract), reads=["ci", "tmp"], writes=["ci"])
        S.op("dve", lambda e: e.tensor_tensor(ci[:], ci[:], den[:], ALU.mult), reads=["ci", "den"], writes=["ci"])
        thT = psb([128, 64], F32); mT = psb([128, 64], F32); crT = psb([128, 64], F32); ciT = psb([128, 64], F32)
        ptr = pst.ps([128, 64], F32)
        for (src, sk, dst, dkk) in [(ar, "ar", arT, "arT"), (ai, "ai", aiT, "aiT"), (th, "th", thT, "thT"), (mag, "mag", mT, "mT"), (cr, "cr", crT, "crT"), (ci, "ci", ciT, "ciT")]:
            S.op("pe", lambda e, src=src: e.matmul(ptr[:], src[:], identf[:64, :64], start=True, stop=True), reads=[sk, "identf"], writes=["ptr"])
            S.op("act", lambda e, dst=dst: e.copy(dst[:], ptr[:]), reads=["ptr"], writes=[dkk])
        S.op("dve", lambda e: e.tensor_tensor(MT[:], thT[:].unsqueeze(2).to_broadcast([128, 64, 64]), tv[:].unsqueeze(1).to_broadcast([128, 64, 64]), ALU.mult), reads=["thT", "tv"], writes=["ANG"])
        tst = self.stage()
        self.trig(tst, lambda: MT[:], lambda: COS[:], lambda: SIN[:], [128, 64, 64], "ANG", "COS", "SIN")
        S.barrier(); S.emit(); tst.stack.close()
        S.op("dve", lambda e: e.tensor_copy(MT[:], mT[:].unsqueeze(2).to_broadcast([128, 64, 64])), reads=["mT", "COS", "SIN", "ANG"], writes=["MT"])
        S.op("dve", lambda e: e.memset(MT[:, :, 0:1], 0.0), reads=["MT"], writes=["MT"])
        S.op("dve", lambda e: e.tensor_copy(MT16[:], MT[:, :, 0:16]), reads=["MT"], writes=["MT"])
        Bn = [psb([128, 64, 16], F32, "Bn") for _ in range(2)]
        S.dma("sp", lambda e: e.dma_start(out=Bn[0][:], in_=self.sw["s5_b_re"][l].rearrange("(q g) p j -> (g p) q j", g=2)), writes=["Bn0"])
        S.dma("sp", lambda e: e.dma_start(out=Bn[1][:], in_=self.sw["s5_b_im"][l].rearrange("(q g) p j -> (g p) q j", g=2)), writes=["Bn1"])
        bb = [psb([128, 64, 16], F32, "bb") for _ in range(2)]
        t16 = psb([128, 64, 16], F32)
        bc = lambda t: t[:].unsqueeze(2).to_broadcast([128, 64, 16])
        S.op("dve", lambda e: e.tensor_tensor(bb[0][:], Bn[0][:], bc(crT), ALU.mult), reads=["Bn0", "crT"], writes=["bb0"])
        S.op("dve", lambda e: e.tensor_tensor(t16[:], Bn[1][:], bc(ciT), ALU.mult), reads=["Bn1", "ciT"], writes=["t16"])
        S.op("dve", lambda e: e.tensor_tensor(bb[0][:], bb[0][:], t16[:], ALU.subtract), reads=["bb0", "t16"], writes=["bb0"])
        S.op("dve", lambda e: e.tensor_tensor(bb[1][:], Bn[1][:], bc(crT), ALU.mult), reads=["Bn1", "crT"], writes=["bb1"])
        S.op("dve", lambda e: e.tensor_tensor(t16[:], Bn[0][:], bc(ciT), ALU.mult), reads=["Bn0", "ciT", "bb0"], writes=["t16"])
        S.op("dve", lambda e: e.tensor_tensor(bb[1][:], bb[1][:], t16[:], ALU.add), reads=["bb1", "t16"], writes=["bb1"])
        Z = psb([128, 64, 128], BF16)
        ptz = pst.ps([128, 8, 128], BF16)
        for ri in range(2):
            S.op("dve", lambda e: e.memset(Z[:], 0.0), reads=["Z"], writes=["Z"])
            Zv = Z[:].rearrange("p (c qq) (gl j) -> p c qq gl j", qq=4, j=16)
            bv = bb[ri][:].rearrange("p (c qq) j -> p c qq j", qq=4)
            for qq in range(4):
                for g2 in range(2):
                    S.op("dve", lambda e, qq=qq, g2=g2, Zv=Zv, bv=bv: e.tensor_copy(Zv[g2 * 64:(g2 + 1) * 64, :, qq, 2 * qq + g2, :], bv[g2 * 64:(g2 + 1) * 64, :, qq, :]),
                         reads=["bb%d" % ri, "Z"], writes=["Z"])
            for q8 in range(8):
                def trz(e, q8=q8):
                    for k in range(8):
                        ins = e.transpose(ptz[:, k, :], Z[:, q8 * 8 + k, :], identb[:])
                    return ins
                S.op("pe", trz, reads=["Z", "identb"], writes=["ptz"])
                S.op("act", lambda e, q8=q8, ri=ri: e.copy(LB[:, q8 * 8:(q8 + 1) * 8, ri, :], ptz[:]), reads=["ptz"], writes=["LB"])
        Cn = psb([64, 64, 64], F32); Y = psb([64, 64, 128], BF16)
        ptc = pst.ps([128, 8, 64], BF16)
        for ri, nm in enumerate(["s5_c_re", "s5_c_im"]):
            S.op("dve", lambda e: e.memset(Cn[:], 0.0), reads=["Cn"], writes=["Cn"])
            S.op("dve", lambda e: e.memset(Y[:], 0.0), reads=["Y"], writes=["Y"])
            cv = self.sw[nm][l].rearrange("(q g) i p -> g i q p", g=2)
            for g2 in range(2):
                S.dma("sp", lambda e, g2=g2, cv=cv: e.dma_start(out=Cn[g2 * 32:g2 * 32 + 16, :, :], in_=cv[g2]), reads=["Cn"], writes=["Cn"])
            for g2 in range(2):
                S.op("dve", lambda e, g2=g2, ri=ri: e.tensor_scalar(Y[g2 * 32:(g2 + 1) * 32, :, g2 * 64:(g2 + 1) * 64], Cn[g2 * 32:(g2 + 1) * 32, :, :], (1.0 if ri == 0 else -1.0), None, ALU.mult),
                     reads=["Cn", "Y"], writes=["Y"])
            for q8 in range(8):
                def trc(e, q8=q8):
                    for k in range(8):
                        ins = e.transpose(ptc[:, k, :], Y[:, q8 * 8 + k, :], identb[:64, :64])
                    return ins
                S.op("pe", trc, reads=["Y", "identb"], writes=["ptc"])
                S.op("act", lambda e, q8=q8, ri=ri: e.copy(CB[:, q8 * 8:(q8 + 1) * 8, ri, :].rearrange("p k (g i) -> p k g i", g=2),
                                                         ptc[:].rearrange("p k (g i) -> p k g i", g=2)[:, :, :, 0:16]), reads=["ptc"], writes=["CB"])
        S.barrier(); S.emit(); pst.stack.close()
        dT, dTk = self.load_row_bcast(st, self.sw["s5_d"][l:l + 1, :], D, 64)
        uTs = [sb([128, 16, 64], BF16) for _ in range(2)]; utoks = [sb([64, 2048], BF16) for _ in range(2)]
        Braw = sb([128, QH, 2, 64], F32); BR = sb([128, QH, 64], F32); BI = sb([128, QH, 64], F32); tmpb = sb([128, QH, 64], F32)
        XR = sb([128, QH, 64], BF16); XI = sb([128, QH, 64], BF16)
        xpr = sb([128, 64], F32); xpi = sb([128, 64], F32); fr = sb([128, 64], F32); fi = sb([128, 64], F32); f2 = sb([128, 64], F32)
        yt = sb([64, 2048], F32); yb = sb([64, 2048], BF16)
        xo = sb([64, 128], F32)
        pb_ = [ps([128, 4, 2, 64], F32) for _ in range(2)]
        py = ps([64, 2048], F32)
        pxo = ps([64, 128], F32)
        chs = self.chunks()

        def s5_loads(ci):
            (si_, r0_, n_, _, _) = chs[ci]
            ub_ = ci % 2
            for c in range(16):
                S.dma("sp", lambda e, c=c, r0_=r0_, n_=n_, ub_=ub_: e.dma_start(out=uTs[ub_][:, c, :n_], in_=self.pj_u[r0_:r0_ + n_, c * 128:(c + 1) * 128], transpose=True),
                      reads=[("dram", "pj_u")], writes=[("uT", ub_, c)])
            S.dma("act", lambda e, r0_=r0_, n_=n_, ub_=ub_: e.dma_start(out=utoks[ub_][:n_, :], in_=self.pj_u[r0_:r0_ + n_, :]), reads=[("dram", "pj_u")], writes=[("utok", ub_)])
        for ci, (si, r0, n, first, last) in enumerate(chs):
            slot = self.seqs[si][3]
            if first:
                if slot == 0:
                    S.op("dve", lambda e: e.memset(xpr[:], 0.0), writes=["xpr"])
                    S.op("dve", lambda e: e.memset(xpi[:], 0.0), writes=["xpi"])
                else:
                    for (srcst, dstt, kk) in [(self.st_s5r, xpr, "xpr"), (self.st_s5i, xpi, "xpi")]:
                        S.dma("sp", lambda e, srcst=srcst, slot=slot: e.dma_start(out=xo[:, :], in_=srcst[l, slot - 1].rearrange("(q g) p -> q (g p)", g=2)), reads=["xo"], writes=["xo"])
                        S.op("pe", lambda e: e.matmul(pb_[0][:, 0, 0, :], xo[:, :], identf[:64, :64], start=True, stop=True), reads=["xo", "identf"], writes=[("pb", 0)])
                        S.op("act", lambda e, dstt=dstt: e.copy(dstt[:], pb_[0][:, 0, 0, :]), reads=[("pb", 0)], writes=[kk])
            ub = ci % 2
            uT, utok = uTs[ub], utoks[ub]
            if ci == 0:
                s5_loads(0)
            if ci + 1 < len(chs):
                s5_loads(ci + 1)
            S.op("dve", lambda e: e.tensor_tensor(fr[:], arT[:], xpr[:], ALU.mult), reads=["arT", "xpr"], writes=["fr"])
            S.op("dve", lambda e: e.tensor_tensor(f2[:], aiT[:], xpi[:], ALU.mult), reads=["aiT", "xpi"], writes=["f2"])
            S.op("dve", lambda e: e.tensor_tensor(fr[:], fr[:], f2[:], ALU.subtract), reads=["fr", "f2"], writes=["fr"])
            S.op("dve", lambda e: e.tensor_tensor(fi[:], arT[:], xpi[:], ALU.mult), reads=["arT", "xpi"], writes=["fi"])
            S.op("dve", lambda e: e.tensor_tensor(f2[:], aiT[:], xpr[:], ALU.mult), reads=["aiT", "xpr", "fr"], writes=["f2"])
            S.op("dve", lambda e: e.tensor_tensor(fi[:], fi[:], f2[:], ALU.add), reads=["fi", "f2"], writes=["fi"])
            V = lambda t, n=n: t[:].rearrange("p q t -> p (q t)")[:, :QH * n].rearrange("p (q t) -> p q t", t=n)
            F2 = lambda t, n=n: t[:].rearrange("p q t -> p (q t)")[:, :QH * n]
            BRv, BIv, tmv = V(BR), V(BI), V(tmpb)
            for hf in range(2):
                q0 = hf * QH
                for c8 in range(8):
                    c = hf * 8 + c8
                    pbi = c % 2

                    def bm(e, c=c, pbi=pbi, n=n, uT=uT):
                        for qq in range(4):
                            for ri in range(2):
                                ins = e.matmul(pb_[pbi][:, qq, ri, :n], LB[:, 4 * c + qq, ri, :], uT[:, c, :n], start=True, stop=True)
                        return ins
                    S.op("pe", bm, reads=["LB", ("uT", ub, c)], writes=[("pb", pbi)])
                    S.op("act", lambda e, c8=c8, pbi=pbi, n=n: e.copy(Braw[:, 4 * c8:4 * c8 + 4, :, :n], pb_[pbi][:, :, :, :n]), reads=[("pb", pbi)], writes=["Braw"])
                cosv = COS[:, q0:q0 + QH, :n]; sinv = SIN[:, q0:q0 + QH, :n]; mtv = MT[:, q0:q0 + QH, :n]
                br_, bi_ = Braw[:, :, 0, :n], Braw[:, :, 1, :n]
                S.op("dve", lambda e, cosv=cosv, br_=br_, n=n, BRv=BRv, BIv=BIv, tmv=tmv: e.tensor_tensor(BRv, br_, cosv, ALU.mult), reads=["Braw", "COS"], writes=["BR"])
                S.op("pool", lambda e, sinv=sinv, bi_=bi_, n=n, BRv=BRv, BIv=BIv, tmv=tmv: e.tensor_tensor(tmv, bi_, sinv, ALU.mult), reads=["Braw", "SIN"], writes=["tmpb"])
                S.op("dve", lambda e, n=n, BRv=BRv, BIv=BIv, tmv=tmv: e.tensor_tensor(BRv, BRv, tmv, ALU.add), reads=["BR", "tmpb"], writes=["BR"])
                S.op("dve", lambda e, cosv=cosv, bi_=bi_, n=n, BRv=BRv, BIv=BIv, tmv=tmv: e.tensor_tensor(BIv, bi_, cosv, ALU.mult), reads=["Braw", "COS"], writes=["BI"])
                S.op("pool", lambda e, sinv=sinv, br_=br_, n=n, BRv=BRv, BIv=BIv, tmv=tmv: e.tensor_tensor(tmv, br_, sinv, ALU.mult), reads=["Braw", "SIN", "BR"], writes=["tmpb"])
                S.op("dve", lambda e, n=n, BRv=BRv, BIv=BIv, tmv=tmv: e.tensor_tensor(BIv, BIv, tmv, ALU.subtract), reads=["BI", "tmpb"], writes=["BI"])
                S.op("dve", lambda e, q0=q0, BRv=BRv: e.tensor_tensor(BRv[:, :, 0], BRv[:, :, 0], fr[:, q0:q0 + QH], ALU.add), reads=["BR", "fr"], writes=["BR"])
                S.op("dve", lambda e, q0=q0, BIv=BIv: e.tensor_tensor(BIv[:, :, 0], BIv[:, :, 0], fi[:, q0:q0 + QH], ALU.add), reads=["BI", "fi"], writes=["BI"])
                mt2 = (MT if n == 64 else MT16)[:, q0:q0 + QH, :].rearrange("p q t -> p (q t)")
                S.op("dve", lambda e, mt2=mt2, F2=F2: e.tensor_tensor_scan(F2(BR), mt2, F2(BR), 0.0, ALU.mult, ALU.add), reads=["BR", "MT"], writes=["BR"])
                S.op("dve", lambda e, mt2=mt2, F2=F2: e.tensor_tensor_scan(F2(BI), mt2, F2(BI), 0.0, ALU.mult, ALU.add), reads=["BI", "MT"], writes=["BI"])
                TA, TB = Braw[:, :, 0, :n], Braw[:, :, 1, :n]
                S.op("dve", lambda e, TA=TA, cosv=cosv, n=n, BRv=BRv: e.tensor_tensor(TA, BRv, cosv, ALU.mult), reads=["BR", "COS", "Braw"], writes=["TA"])
                S.op("pool", lambda e, TB=TB, sinv=sinv, n=n, BIv=BIv: e.tensor_tensor(TB, BIv, sinv, ALU.mult), reads=["BI", "SIN", "Braw"], writes=["TB"])
                S.op("dve", lambda e, TA=TA, TB=TB, n=n: e.tensor_tensor(XR[:, :, :n], TA, TB, ALU.subtract), reads=["TA", "TB"], writes=["XR"])
                S.op("dve", lambda e, TA=TA, TB=TB, q0=q0, n=n: e.tensor_tensor(xpr[:, q0:q0 + QH], TA[:, :, n - 1], TB[:, :, n - 1], ALU.subtract), reads=["TA", "TB"], writes=["xpr"])
                S.op("dve", lambda e, TA=TA, sinv=sinv, n=n, BRv=BRv: e.tensor_tensor(TA, BRv, sinv, ALU.mult), reads=["BR", "SIN", "XR", "xpr"], writes=["TA"])
                S.op("pool", lambda e, TB=TB, cosv=cosv, n=n, BIv=BIv: e.tensor_tensor(TB, BIv, cosv, ALU.mult), reads=["BI", "COS", "XR", "xpr"], writes=["TB"])
                S.op("dve", lambda e, TA=TA, TB=TB, n=n: e.tensor_tensor(XI[:, :, :n], TA, TB, ALU.add), reads=["TA", "TB"], writes=["XI"])
                S.op("dve", lambda e, TA=TA, TB=TB, q0=q0, n=n: e.tensor_tensor(xpi[:, q0:q0 + QH], TA[:, :, n - 1], TB[:, :, n - 1], ALU.add), reads=["TA", "TB"], writes=["xpi"])

                def cm(e, q0=q0, n=n):
                    for q in range(QH):
                        e.matmul(py[:n, (q0 + q) * 32:(q0 + q + 1) * 32], XR[:, q, :n], CB[:, q0 + q, 0, :], start=True, stop=False)
                        ins = e.matmul(py[:n, (q0 + q) * 32:(q0 + q + 1) * 32], XI[:, q, :n], CB[:, q0 + q, 1, :], start=False, stop=True)
                    return ins
                S.op("pe", cm, reads=["XR", "XI", "CB"], writes=["py"])
                S.op("dve", lambda e: e.tensor_copy(Braw[:, 0, 0, 0:1], Braw[:, 0, 0, 0:1]), reads=[], writes=["Braw", "TA", "TB"])
            S.op("dve", lambda e, n=n, utok=utok: e.tensor_tensor(yt[:n, :], utok[:n, :], dT[:n, :], ALU.mult), reads=[("utok", ub), dTk], writes=["yt"])
            S.op("dve", lambda e, n=n: e.tensor_tensor(yt[:n, :], yt[:n, :], py[:n, :], ALU.add), reads=["yt", "py"], writes=["yt"])
            S.op("act", lambda e, n=n: e.activation(yb[:n, :], yt[:n, :], AF.Gelu), reads=["yt"], writes=["yb"])
            S.dma("sp", lambda e, r0=r0, n=n: e.dma_start(out=self.s5_y[r0:r0 + n, :], in_=yb[:n, :]), reads=["yb"], writes=[("dram", "s5_y")])
            if last:
                for (srct, dsto, kk) in [(xpr, self.o_s5r, "xpr"), (xpi, self.o_s5i, "xpi")]:
                    S.op("pe", lambda e, srct=srct: e.matmul(pxo[:], srct[:], identf[:], start=True, stop=True), reads=[kk, "identf"], writes=["pxo"])
                    S.op("act", lambda e: e.copy(xo[:], pxo[:]), reads=["pxo"], writes=["xo"])
                    S.dma("sp", lambda e, dsto=dsto, slot=slot: e.dma_start(out=dsto[l, slot].rearrange("(q g) p -> q (g p)", g=2), in_=xo[:]), reads=["xo"], writes=[("dram", dsto.name)])
        st.done()


    def ssd(self, l):
        S, nc = self.S, self.nc
        st = self.stage()
        sb, ps = st.sb, st.ps
        identf = sb([128, 128], F32); identb = sb([128, 128], BF16); tri = sb([64, 64], F32); ones = sb([64, 64], F32)
        sel = {64: sb([64, 128], F32), 16: sb([64, 128], F32)}
        S.dma("sp", lambda e: e.dma_start(out=identf[:], in_=self.cst["ident_in"]), writes=["identf"])
        S.op("dve", lambda e: e.tensor_copy(identb[:], identf[:]), reads=["identf"], writes=["identb"])
        S.dma("sp", lambda e: e.dma_start(out=tri[:], in_=self.cst["tri_in"]), writes=["tri"])
        S.op("dve", lambda e: e.memset(ones[:], 1.0), writes=["ones"])
        for LL in (64, 16):
            S.op("dve", lambda e, LL=LL: e.tensor_copy(sel[LL][:, :], identf[0:64, LL - 1:LL].to_broadcast([64, 128])), reads=["identf"], writes=["sel%d" % LL])
        cw = sb([128, 48, 4], F32); cb = sb([128, 48], F32)
        for w in range(4):
            S.dma("sp", lambda e, w=w: e.dma_start(out=cw[:, :, w], in_=self.sw["ssd_conv_w"][l, w].rearrange("(c p) -> p c", p=128), allow_slow_non_contiguous=True), writes=["cw%d" % w])
        S.dma("sp", lambda e: e.dma_start(out=cb[:], in_=self.sw["ssd_conv_b"][l].rearrange("(c p) -> p c", p=128), allow_slow_non_contiguous=True), writes=["cb"])
        cwk = ["cw%d" % w for w in range(4)]
        dtb, dtbk = self.load_row_bcast(st, self.sw["ssd_dt_bias"][l:l + 1, :], 64, 64)
        aneg, alk = self.load_row_bcast(st, self.sw["ssd_a_log"][l:l + 1, :], 64, 64)
        dsk, dskk = self.load_row_bcast(st, self.sw["ssd_d"][l:l + 1, :], 64, 64)
        ng, ngk = self.load_row_bcast(st, self.sw["ssd_norm"][l:l + 1, :], 4096, 64)
        S.op("act", lambda e: e.activation(aneg[:], aneg[:], AF.Exp), reads=[alk], writes=[alk])
        S.op("dve", lambda e: e.tensor_scalar(aneg[:], aneg[:], -1.0, None, ALU.mult), reads=[alk], writes=[alk])
        xTs = [sb([128, 48, 80], BF16) for _ in range(2)]; acc = sb([128, 48, 64], F32); ctmp = sb([128, 48, 64], F32); xcT = sb([128, 48, 64], BF16)
        xtok = sb([64, 4096], BF16); btok = sb([64, 1024], BF16); zt = sb([64, 4096], BF16)
        dtrs = [sb([64, 64], F32) for _ in range(2)]; dtx = sb([64, 64], F32); dta_ = sb([64, 64], F32); t64 = sb([64, 64], F32); dt = sb([64, 64], F32)
        cs_col = sb([64, 64], F32); ecs = sb([64, 64], F32); wdec = sb([64, 64], F32); csl = sb([128, 64], F32); ecl = sb([128, 64], F32)
        Rgs = [sb([64, 8, 64], F32) for _ in range(2)]; d1s = [sb([64, 8, 64], F32) for _ in range(2)]; cbm = sb([64, 8, 64], F32); M = sb([64, 64, 64], BF16)
        xdt = sb([64, 4096], BF16); xdtw = sb([64, 4096], BF16)
        hT = sb([128, 4096], F32); hTb = sb([128, 4096], BF16)
        yv = sb([64, 4096], F32); ytmp = sb([64, 512], F32); sz = sb([64, 4096], BF16); ssq = sb([64, 8], F32); yb = sb([64, 4096], BF16)
        hio = sb([128, 4, 128], F32)
        p_tr = ps([64, 2048], BF16); p_small = ps([128, 64], F32); p_cs = ps([64, 8, 64], F32); p_cb = ps([64, 8, 64], F32)
        py = ps([64, 512], F32); pys = ps([64, 512], F32); ph = ps([128, 512], F32)
        v3 = lambda t, L: t[:L, :].rearrange("p (h q) -> p h q", q=64)
        chs = self.chunks()

        def ssd_loads(ci):
            (si_, r0_, L_, _, _) = chs[ci]
            r00_, _, p0_, _ = self.seqs[si_]
            prow_ = p0_ + 3 + (r0_ - r00_)
            NR_ = L_ + 16
            xb_ = ci % 2
            for c in range(48):
                S.dma("sp", lambda e, c=c, prow_=prow_, NR_=NR_, xb_=xb_: e.dma_start(out=xTs[xb_][:, c, :NR_], in_=self.xpad[prow_ - 16:prow_ - 16 + NR_, c * 128:(c + 1) * 128], transpose=True),
                      reads=[("dram", "xpad")], writes=[("xT", xb_, c)])
            S.dma("act", lambda e, r0_=r0_, L_=L_, xb_=xb_: e.dma_start(out=dtrs[xb_][:L_, :], in_=self.dtf[r0_:r0_ + L_, :]), reads=[("dram", "dtf")], writes=[("dtr", xb_)])
        for ci, (si, r0, L, first, last) in enumerate(chs):
            r00, ln, p0, slot = self.seqs[si]
            prow = p0 + 3 + (r0 - r00)
            NR = L + 16
            if first:
                if slot == 0:
                    S.op("dve", lambda e: e.memset(hT[:], 0.0), writes=["hT"])
                else:
                    hv = self.st_ssd[l, slot - 1].rearrange("(k q) p n -> (q p) k n", q=2)
                    for k4 in range(8):
                        S.dma("sp", lambda e, k4=k4, hv=hv: e.dma_start(out=hio[:], in_=hv[:, 4 * k4:4 * k4 + 4, :]), reads=["hio"], writes=["hio"])

                        def trh(e):
                            for k in range(4):
                                ins = e.matmul(ph[:, k * 128:(k + 1) * 128], hio[:, k, :], identf[:], start=True, stop=True)
                            return ins
                        S.op("pe", trh, reads=["hio", "identf"], writes=["ph"])
                        S.op("act", lambda e, k4=k4: e.copy(hT[:, k4 * 512:(k4 + 1) * 512], ph[:]), reads=["ph"], writes=["hT"])
                S.op("act", lambda e: e.copy(hTb[:], hT[:]), reads=["hT"], writes=["hTb"])
            xb = ci % 2
            xT, dtr = xTs[xb], dtrs[xb]
            if ci == 0:
                ssd_loads(0)
            S.dma("act", lambda e, r0=r0, L=L: e.dma_start(out=zt[:L, :], in_=self.pj_z[r0:r0 + L, :]), reads=[("dram", "pj_z")], writes=["zt"])
            if ci + 1 < len(chs):
                ssd_loads(ci + 1)
            xk = [("xT", xb, c) for c in range(48)]
            bw = lambda w, L=L: cw[:, :, w:w + 1].to_broadcast([128, 48, L])
            S.op("dve", lambda e, L=L, bw=bw, xT=xT: e.tensor_tensor(acc[:, :, :L], xT[:, :, 13:13 + L], bw(0), ALU.mult), reads=xk + cwk, writes=["acc"])
            for w in range(1, 4):
                S.op("pool", lambda e, L=L, bw=bw, w=w, xT=xT: e.tensor_tensor(ctmp[:, :, :L], xT[:, :, 13 + w:13 + w + L], bw(w), ALU.mult), reads=xk + cwk, writes=["ctmp"])
                S.op("dve", lambda e, L=L: e.tensor_tensor(acc[:, :, :L], acc[:, :, :L], ctmp[:, :, :L], ALU.add), reads=["acc", "ctmp"], writes=["acc"])
            S.op("dve", lambda e, L=L: e.tensor_tensor(acc[:, :, :L], acc[:, :, :L], cb[:].unsqueeze(2).to_broadcast([128, 48, L]), ALU.add), reads=["acc", "cb"], writes=["acc"])
            S.op("act", lambda e, L=L: e.activation(xcT[:, :, :L], acc[:, :, :L], AF.Silu), reads=["acc"], writes=["xcT"])
            for rnd in range(2):
                def trx(e, rnd=rnd, L=L):
                    for k in range(16):
                        ins = e.transpose(p_tr[:L, k * 128:(k + 1) * 128], xcT[:, rnd * 16 + k, :L], identb[:])
                    return ins
                S.op("pe", trx, reads=["xcT", "identb"], writes=["p_tr"])
                S.op("act", lambda e, rnd=rnd, L=L: e.copy(xtok[:L, rnd * 2048:(rnd + 1) * 2048], p_tr[:L, :]), reads=["p_tr"], writes=[("xtok", rnd)])

            def trb(e, L=L):
                for k in range(8):
                    ins = e.transpose(p_tr[:L, k * 128:(k + 1) * 128], xcT[:, 32 + k, :L], identb[:])
                return ins
            S.op("pe", trb, reads=["xcT", "identb"], writes=["p_tr"])
            S.op("act", lambda e, L=L: e.copy(btok[:L, :], p_tr[:L, 0:1024]), reads=["p_tr"], writes=["btok"])
            xtk = [("xtok", 0), ("xtok", 1)]
            S.op("dve", lambda e, L=L, dtr=dtr: e.tensor_tensor(dtx[:L, :], dtr[:L, :], dtb[:L, :], ALU.add), reads=[("dtr", xb), dtbk], writes=["dtx"])
            S.op("act", lambda e, L=L: e.activation(t64[:L, :], dtx[:L, :], AF.Abs), reads=["dtx"], writes=["t64"])
            S.op("act", lambda e, L=L: e.activation(t64[:L, :], t64[:L, :], AF.Exp, scale=-1.0), reads=["t64"], writes=["t64"])
            S.op("act", lambda e, L=L: e.activation(t64[:L, :], t64[:L, :], AF.Ln, bias=1.0), reads=["t64"], writes=["t64"])
            S.op("dve", lambda e, L=L: e.tensor_scalar(dt[:L, :], dtx[:L, :], 0.0, None, ALU.max), reads=["dtx"], writes=["dt"])
            S.op("dve", lambda e, L=L: e.tensor_tensor(dt[:L, :], dt[:L, :], t64[:L, :], ALU.add), reads=["dt", "t64"], writes=["dt"])
            S.op("dve", lambda e, L=L: e.tensor_tensor(dta_[:L, :], dt[:L, :], aneg[:L, :], ALU.mult), reads=["dt", alk], writes=["dta"])
            S.op("pe", lambda e, L=L: e.matmul(p_small[:L, :], tri[:L, :L], dta_[:L, :], start=True, stop=True), reads=["tri", "dta"], writes=["p_small"])
            S.op("act", lambda e, L=L: e.copy(cs_col[:L, :], p_small[:L, :]), reads=["p_small"], writes=["cs_col"])
            S.op("act", lambda e, L=L: e.activation(ecs[:L, :], cs_col[:L, :], AF.Exp), reads=["cs_col"], writes=["ecs"])
            S.op("pe", lambda e, L=L: e.matmul(p_small[:, :], sel[L][:L, :], cs_col[:L, :], start=True, stop=True), reads=["sel%d" % L, "cs_col"], writes=["p_small"])
            S.op("act", lambda e: e.copy(csl[:], p_small[:]), reads=["p_small"], writes=["csl"])
            S.op("act", lambda e: e.activation(ecl[:], csl[:], AF.Exp), reads=["csl"], writes=["ecl"])
            S.op("dve", lambda e, L=L: e.tensor_tensor(wdec[:L, :], csl[:L, :], cs_col[:L, :], ALU.subtract), reads=["csl", "cs_col"], writes=["wdec"])
            S.op("act", lambda e, L=L: e.activation(wdec[:L, :], wdec[:L, :], AF.Exp), reads=["wdec"], writes=["wdec"])
            def cbf(e, L=L):
                for g in range(8):
                    ins = e.matmul(p_cb[:L, g, :L], xcT[:, 32 + g, :L], xcT[:, 40 + g, :L], start=True, stop=True)
                return ins
            S.op("pe", cbf, reads=["xcT"], writes=["p_cb"])
            S.op("dve", lambda e, L=L: e.tensor_tensor(cbm[:L, :, :L], p_cb[:L, :, :L], tri[:L, :L].unsqueeze(1).to_broadcast([L, 8, L]), ALU.mult), reads=["p_cb", "tri"], writes=["cbm"])
            S.op("dve", lambda e, L=L: e.tensor_tensor(v3(xdt, L), v3(xtok, L), dt[:L, :].unsqueeze(2).to_broadcast([L, 64, 64]), ALU.mult), reads=xtk + ["dt"], writes=["xdt"])
            S.op("pool", lambda e, L=L: e.tensor_tensor(v3(xdtw, L), v3(xdt, L), wdec[:L, :].unsqueeze(2).to_broadcast([L, 64, 64]), ALU.mult), reads=["xdt", "wdec"], writes=["xdtw"])
            for g in range(8):
                hs = slice(8 * g, 8 * g + 8)
                Rg, d1 = Rgs[g % 2], d1s[g % 2]
                kR, kd = ("Rg", g % 2), ("d1", g % 2)
                S.op("dve", lambda e, L=L, hs=hs, Rg=Rg: e.tensor_tensor(Rg[:L, :, :L], dta_[:L, hs].unsqueeze(2).to_broadcast([L, 8, L]), tri[:L, :L].unsqueeze(1).to_broadcast([L, 8, L]), ALU.mult),
                     reads=["dta", "tri"], writes=[kR])
                S.op("pe", lambda e, L=L, Rg=Rg: e.matmul(p_cs[:L, :, :L], ones[:L, :L], Rg[:L, :, :L], start=True, stop=True), reads=["ones", kR], writes=["p_cs"])
                S.op("dve", lambda e, L=L, hs=hs, d1=d1: e.tensor_tensor(d1[:L, :, :L], p_cs[:L, :, :L], cs_col[:L, hs].unsqueeze(2).to_broadcast([L, 8, L]), ALU.subtract), reads=["p_cs", "cs_col"], writes=[kd])
                S.op("dve", lambda e, L=L, d1=d1: e.tensor_scalar(d1[:L, :, :L], d1[:L, :, :L], 0.0, None, ALU.min), reads=[kd], writes=[kd])
                S.op("act", lambda e, L=L, d1=d1: e.activation(d1[:L, :, :L], d1[:L, :, :L], AF.Exp), reads=[kd], writes=[kd])
                S.op("dve", lambda e, L=L, g=g, hs=hs, d1=d1: e.tensor_tensor(M[:L, hs, :L], d1[:L, :, :L], cbm[:L, g:g + 1, :L].to_broadcast([L, 8, L]), ALU.mult), reads=[kd, "cbm"], writes=[("M", g)])

                def ym(e, L=L, g=g):
                    for r in range(8):
                        h = 8 * g + r
                        ins = e.matmul(py[:L, r * 64:(r + 1) * 64], M[:L, h, :L], xdt[:L, h * 64:(h + 1) * 64], start=True, stop=True)
                    return ins
                S.op("pe", ym, reads=[("M", g), "xdt"], writes=["py"])
                S.op("pe", lambda e, L=L, g=g: e.matmul(pys[:L, :], xcT[:, 40 + g, :L], hTb[:, g * 512:(g + 1) * 512], start=True, stop=True), reads=["xcT", "hTb"], writes=["pys"])
                S.op("dve", lambda e, L=L, hs=hs: e.tensor_tensor(ytmp[:L, :].rearrange("p (r q) -> p r q", q=64), pys[:L, :].rearrange("p (r q) -> p r q", q=64),
                                                              ecs[:L, hs].unsqueeze(2).to_broadcast([L, 8, 64]), ALU.mult), reads=["pys", "ecs"], writes=["ytmp"])
                S.op("dve", lambda e, L=L, g=g: e.tensor_tensor(yv[:L, g * 512:(g + 1) * 512], ytmp[:L, :], py[:L, :], ALU.add), reads=["ytmp", "py"], writes=[("yv", g)])
                S.op("pe", lambda e, L=L, g=g: e.matmul(ph[:, :], btok[:L, g * 128:(g + 1) * 128], xdtw[:L, g * 512:(g + 1) * 512], start=True, stop=True), reads=["btok", "xdtw"], writes=["ph"])
                hg = hT[:, g * 512:(g + 1) * 512]
                S.op("pool", lambda e, hg=hg, hs=hs: e.tensor_tensor(hg.rearrange("p (r q) -> p r q", q=64), hg.rearrange("p (r q) -> p r q", q=64),
                                                                    ecl[:, hs].unsqueeze(2).to_broadcast([128, 8, 64]), ALU.mult), reads=["hT", "ecl", "hTb"], writes=["hT"])
                S.op("dve", lambda e, hg=hg: e.tensor_tensor(hg, hg, ph[:, :], ALU.add), reads=["hT", "ph"], writes=["hT"])
            S.op("act", lambda e: e.copy(hTb[:], hT[:]), reads=["hT", "pys"], writes=["hTb"])
            yk = [("yv", g) for g in range(8)]
            S.op("pool", lambda e, L=L: e.tensor_tensor(v3(xdtw, L), v3(xtok, L), dsk[:L, :].unsqueeze(2).to_broadcast([L, 64, 64]), ALU.mult), reads=xtk + [dskk, "xdtw", "ph"], writes=["xdtw"])
            S.op("dve", lambda e, L=L: e.tensor_tensor(yv[:L, :], yv[:L, :], xdtw[:L, :], ALU.add), reads=yk + ["xdtw"], writes=["yv"])
            S.op("act", lambda e, L=L: e.activation(sz[:L, :], zt[:L, :], AF.Silu), reads=["zt"], writes=["sz"])
            S.op("dve", lambda e, L=L: e.tensor_tensor(yv[:L, :], yv[:L, :], sz[:L, :], ALU.mult), reads=["yv", "sz"], writes=["yv"])
            S.op("act", lambda e, L=L: e.activation(xdt[:L, :], yv[:L, :], AF.Square), reads=["yv", "xdt", "py"], writes=["xdt"])
            S.op("dve", lambda e, L=L: e.tensor_reduce(ssq[:L, :], xdt[:L, :].rearrange("p (g q) -> p g q", g=8), AX.X, ALU.add), reads=["xdt"], writes=["ssq"])
            S.op("act", lambda e, L=L: e.activation(ssq[:L, :], ssq[:L, :], AF.Sqrt, bias=EPS, scale=1.0 / 512), reads=["ssq"], writes=["ssq"])
            S.op("dve", lambda e, L=L: e.reciprocal(ssq[:L, :], ssq[:L, :]), reads=["ssq"], writes=["ssq"])
            S.op("dve", lambda e, L=L: e.tensor_tensor(yv[:L, :].rearrange("p (g q) -> p g q", g=8), yv[:L, :].rearrange("p (g q) -> p g q", g=8),
                                                     ssq[:L, :].unsqueeze(2).to_broadcast([L, 8, 512]), ALU.mult), reads=["yv", "ssq"], writes=["yv"])
            S.op("dve", lambda e, L=L: e.tensor_tensor(yb[:L, :], yv[:L, :], ng[:L, :], ALU.mult), reads=["yv", ngk], writes=["yb"])
            S.dma("sp", lambda e, r0=r0, L=L: e.dma_start(out=self.ssd_y[r0:r0 + L, :], in_=yb[:L, :]), reads=["yb"], writes=[("dram", "ssd_y")])
            if last:
                ov = self.o_ssd[l, slot].rearrange("(k q) p n -> (q p) k n", q=2)
                for k4 in range(8):
                    def tro(e, k4=k4):
                        for k in range(4):
                            ins = e.matmul(ph[:, k * 128:(k + 1) * 128], hT[:, (4 * k4 + k) * 128:(4 * k4 + k + 1) * 128], identf[:], start=True, stop=True)
                        return ins
                    S.op("pe", tro, reads=["hT", "identf"], writes=["ph"])
                    S.op("act", lambda e: e.copy(hio[:].rearrange("p k n -> p (k n)"), ph[:]), reads=["ph", "hio"], writes=["hio"])
                    S.dma("sp", lambda e, k4=k4, ov=ov: e.dma_start(out=ov[:, 4 * k4:4 * k4 + 4, :], in_=hio[:]), reads=["hio"], writes=[("dram", "o_ssd")])
        st.done()

    def merge(self, l):
        S = self.S
        st = self.stage()
        sb = st.sb
        rt = sb([128, D], F32); gl = sb([128, 2 * D], F32); sd = sb([128, D], F32); gb = sb([128, 3 * D], BF16)
        sg = sb([128, 3 * D], F32); tmp = sb([128, D], F32); ob = sb([128, D], BF16)
        for (r, n) in self.tiles():
            S.dma("sp", lambda e, r=r, n=n: e.dma_start(out=rt[:n, :], in_=self.br_ret[r:r + n, :]), reads=[("dram", "br_ret")], writes=["rt"])
            S.dma("act", lambda e, r=r, n=n: e.dma_start(out=gl[:n, :], in_=self.br_glu[r:r + n, :]), reads=[("dram", "br_glu")], writes=["gl"])
            S.dma("sp", lambda e, r=r, n=n: e.dma_start(out=sd[:n, :], in_=self.br_ssd[r:r + n, :]), reads=[("dram", "br_ssd")], writes=["sd"])
            S.dma("act", lambda e, r=r, n=n: e.dma_start(out=gb[:n, :], in_=self.pj_gate[r:r + n, :]), reads=[("dram", "pj_gate")], writes=["gb"])
            S.op("act", lambda e, n=n: e.activation(sg[:n, :], gb[:n, :], AF.Sigmoid), reads=["gb"], writes=["sg"])
            S.op("act", lambda e, n=n: e.activation(tmp[:n, :], gl[:n, D:2 * D], AF.Sigmoid), reads=["gl"], writes=["tmp"])
            S.op("dve", lambda e, n=n: e.tensor_tensor(tmp[:n, :], tmp[:n, :], gl[:n, 0:D], ALU.mult), reads=["tmp", "gl"], writes=["tmp"])
            S.op("dve", lambda e, n=n: e.tensor_tensor(tmp[:n, :], tmp[:n, :], sg[:n, D:2 * D], ALU.mult), reads=["tmp", "sg"], writes=["tmp"])
            S.op("pool", lambda e, n=n: e.tensor_tensor(rt[:n, :], rt[:n, :], sg[:n, 0:D], ALU.mult), reads=["rt", "sg"], writes=["rt"])
            S.op("pool", lambda e, n=n: e.tensor_tensor(sd[:n, :], sd[:n, :], sg[:n, 2 * D:3 * D], ALU.mult), reads=["sd", "sg"], writes=["sd"])
            S.op("dve", lambda e, n=n: e.tensor_tensor(tmp[:n, :], tmp[:n, :], rt[:n, :], ALU.add), reads=["tmp", "rt"], writes=["tmp"])
            S.op("dve", lambda e, n=n: e.tensor_tensor(ob[:n, :], tmp[:n, :], sd[:n, :], ALU.add), reads=["tmp", "sd"], writes=["ob"])
            S.dma("sp", lambda e, r=r, n=n: e.dma_start(out=self.merged[r:r + n, :], in_=ob[:n, :]), reads=["ob"], writes=[("dram", "merged")])
        st.done()

    def normadd(self, ysrc, g_post, g_next, final=False):
        S = self.S
        st = self.stage()
        sb = st.sb
        gp, gpk = self.load_row_bcast(st, g_post, D)
        if g_next is not None:
            gn, gnk = self.load_row_bcast(st, g_next, D)
        yts = [sb([128, D], F32) for _ in range(2)]; xts = [sb([128, D], F32) for _ in range(2)]
        junk = sb([128, D], BF16); hbs = [sb([128, D], BF16) for _ in range(2)]
        ss = sb([128, 1], F32); ss2 = sb([128, 1], F32)
        tl = self.tiles()

        def loads(i):
            r, n = tl[i]
            b = i % 2
            S.dma("sp", lambda e, r=r, n=n, b=b: e.dma_start(out=yts[b][:n, :], in_=ysrc[r:r + n, :]), reads=[("dram", ysrc.name)], writes=[("yt", b)])
            S.dma("act", lambda e, r=r, n=n, b=b: e.dma_start(out=xts[b][:n, :], in_=self.xres[r:r + n, :]), reads=[("dram", "xres", r)], writes=[("xt", b)])
        loads(0)
        for i, (r, n) in enumerate(tl):
            b = i % 2
            yt, xt, hb = yts[b], xts[b], hbs[b]
            if i + 1 < len(tl):
                loads(i + 1)
            S.op("dve", lambda e, n=n: e.memset(ss[:n, :], 0.0), writes=["ss"])
            S.op("act", lambda e, n=n, yt=yt: e.activation(junk[:n, :], yt[:n, :], AF.Square, accum_out=ss[:n, :]), reads=[("yt", b), "ss"], writes=["junk", "ss"])
            S.op("act", lambda e, n=n: e.activation(ss[:n, :], ss[:n, :], AF.Sqrt, bias=EPS, scale=1.0 / D), reads=["ss"], writes=["ss"])
            S.op("dve", lambda e, n=n: e.reciprocal(ss[:n, :], ss[:n, :]), reads=["ss"], writes=["ss"])
            S.op("dve", lambda e, n=n, yt=yt: e.scalar_tensor_tensor(yt[:n, :], yt[:n, :], ss[:n, 0:1], gp[:n, :], ALU.mult, ALU.mult), reads=[("yt", b), "ss", gpk], writes=[("yt", b)])
            S.op("dve", lambda e, n=n, yt=yt, xt=xt: e.tensor_tensor(xt[:n, :], xt[:n, :], yt[:n, :], ALU.add), reads=[("xt", b), ("yt", b)], writes=[("xt", b)])
            S.dma("sp", lambda e, r=r, n=n, xt=xt: e.dma_start(out=self.xres[r:r + n, :], in_=xt[:n, :]), reads=[("xt", b)], writes=[("dram", "xres", r)])
            if final:
                S.dma("sp", lambda e, r=r, n=n, xt=xt: e.dma_start(out=self.y[r:r + n, :], in_=xt[:n, :]), reads=[("xt", b)], writes=[("dram", "y")])
            if g_next is not None:
                S.op("dve", lambda e, n=n: e.memset(ss2[:n, :], 0.0), writes=["ss2"])
                S.op("act", lambda e, n=n, xt=xt: e.activation(junk[:n, :], xt[:n, :], AF.Square, accum_out=ss2[:n, :]), reads=[("xt", b), "junk", "ss2"], writes=["junk", "ss2"])
                S.op("act", lambda e, n=n: e.activation(ss2[:n, :], ss2[:n, :], AF.Sqrt, bias=EPS, scale=1.0 / D), reads=["ss2"], writes=["ss2"])
                S.op("dve", lambda e, n=n: e.reciprocal(ss2[:n, :], ss2[:n, :]), reads=["ss2"], writes=["ss2"])
                S.op("dve", lambda e, n=n, xt=xt, hb=hb: e.scalar_tensor_tensor(hb[:n, :], xt[:n, :], ss2[:n, 0:1], gn[:n, :], ALU.mult, ALU.mult), reads=[("xt", b), "ss2", gnk], writes=[("hb", b)])
                S.dma("sp", lambda e, r=r, n=n, hb=hb: e.dma_start(out=self.hn[r:r + n, :], in_=hb[:n, :]), reads=[("hb", b)], writes=[("dram", "hn")])
        S.op("dve", lambda e: e.memset(ss[:1, :], 0.0), reads=[("dram", "xres", r) for (r, n) in tl], writes=[("dram", "xres"), "ss"])
        st.done()

    def attention(self):
        S = self.S
        st = self.stage()
        sb, ps = st.sb, st.ps
        identf = sb([128, 128], F32); identb = sb([128, 128], BF16)
        S.dma("sp", lambda e: e.dma_start(out=identf[:], in_=self.cst["ident_in"]), writes=["identf"])
        S.op("dve", lambda e: e.tensor_copy(identb[:], identf[:]), reads=["identf"], writes=["identb"])
        kT = sb([128, 16, 256], BF16); v = sb([128, 2, D], BF16); qT = sb([128, 16, 128], BF16)
        pexp = sb([128, 4, 256], BF16); pT = sb([128, 8, 128], BF16); ob = sb([128, D], BF16)
        mx = sb([128, 4], F32); sm = sb([128, 4], F32)
        ps_sc = ps([128, 4, 256], F32); pt = ps([128, 8, 128], BF16); po = ps([128, D], F32)
        SC = 512 ** -0.5
        for (r0, ln, p0, slot) in self.seqs:
            for c in range(16):
                ksrc = self.kvp_bf[:, 0:D] if slot == 0 else self.kv_bf[slot, 0]
                S.dma("sp", lambda e, c=c, ksrc=ksrc: e.dma_start(out=kT[:, c, :], in_=ksrc[:, c * 128:(c + 1) * 128], transpose=True),
                      reads=[("dram", "kv_bf")], writes=[("kT", c)])
            vsrc = self.kvp_bf[:, D:2 * D] if slot == 0 else self.kv_bf[slot, 1]
            S.dma("sp", lambda e, vsrc=vsrc: e.dma_start(out=v[:], in_=vsrc.rearrange("(mc p) d -> p mc d", p=128)), reads=[("dram", "kv_bf")], writes=["v"])
            t = 0
            while t < ln:
                n = min(128, ln - t)
                rr = r0 + t
                for c in range(16):
                    S.dma("sp", lambda e, c=c, rr=rr, n=n: e.dma_start(out=qT[:, c, :n], in_=self.q_bf[rr:rr + n, c * 128:(c + 1) * 128], transpose=True),
                          reads=[("dram", "q_bf")], writes=[("qT", c)])

                def scm(e, n=n):
                    for h in range(4):
                        for dc in range(4):
                            ins = e.matmul(ps_sc[:n, h, :], qT[:, h * 4 + dc, :n], kT[:, h * 4 + dc, :], start=(dc == 0), stop=(dc == 3))
                    return ins
                S.op("pe", scm, reads=[("qT", c) for c in range(16)] + [("kT", c) for c in range(16)], writes=["ps_sc"])
                S.op("dve", lambda e, n=n: e.tensor_reduce(mx[:n, :], ps_sc[:n], AX.X, ALU.max), reads=["ps_sc"], writes=["mx"])
                S.op("dve", lambda e, n=n: e.tensor_scalar(mx[:n, :], mx[:n, :], -SC, None, ALU.mult), reads=["mx"], writes=["mx"])
                for h in range(4):
                    S.op("act", lambda e, n=n, h=h: e.activation(pexp[:n, h, :], ps_sc[:n, h, :], AF.Exp, bias=mx[:n, h:h + 1], scale=SC),
                         reads=["ps_sc", "mx"], writes=[("pexp", h)])
                S.op("dve", lambda e, n=n: e.tensor_reduce(sm[:n, :], pexp[:n], AX.X, ALU.add), reads=[("pexp", h) for h in range(4)], writes=[("sm", h) for h in range(4)])

                def trp(e, n=n):
                    for h in range(4):
                        for mc in range(2):
                            ins = e.transpose(pt[:, h * 2 + mc, :n], pexp[:n, h, mc * 128:(mc + 1) * 128], identb[:n, :n])
                    return ins
                S.op("pe", trp, reads=[("pexp", h) for h in range(4)] + ["identb"], writes=["pt"])
                S.op("act", lambda e, n=n: e.copy(pT[:, :, :n], pt[:, :, :n]), reads=["pt"], writes=["pT"])

                def om(e, n=n):
                    for h in range(4):
                        for mc in range(2):
                            ins = e.matmul(po[:n, h * 512:(h + 1) * 512], pT[:, h * 2 + mc, :n], v[:, mc, h * 512:(h + 1) * 512], start=(mc == 0), stop=(mc == 1))
                    return ins
                S.op("pe", om, reads=["pT", "v"], writes=["po"])
                smk = [("sm", h) for h in range(4)]
                S.op("dve", lambda e, n=n: e.reciprocal(sm[:n, :], sm[:n, :]), reads=smk, writes=smk)
                S.op("dve", lambda e, n=n: e.tensor_tensor(ob[:n, :].rearrange("p (h d) -> p h d", h=4), po[:n, :].rearrange("p (h d) -> p h d", h=4),
                                                         sm[:n, :].unsqueeze(2).to_broadcast([n, 4, 512]), ALU.mult), reads=["po"] + smk, writes=["ob"])
                S.dma("sp", lambda e, rr=rr, n=n: e.dma_start(out=self.att[rr:rr + n, :], in_=ob[:n, :]), reads=["ob"], writes=[("dram", "att")])
                t += n
        st.done()

    def evac_relu2(self):
        S = self.S

        def f(st, n, cw, ps, pb, ob, okey):
            if not hasattr(st, "r2"):
                st.r2 = st.sb([128, 512], F32, "r2")
            r2 = st.r2
            S.op("act", lambda e: e.activation(r2[:n, :cw], ps[:n, :cw], AF.Relu), reads=[("lps", pb)], writes=["r2"])
            S.op("dve", lambda e: e.tensor_tensor(ob[:n, :cw], r2[:n, :cw], r2[:n, :cw], ALU.mult), reads=["r2"], writes=[okey])
        return f

    def conv_out(self, l):
        S = self.S
        for (r0, ln, p0, slot) in self.seqs:
            S.dma("pool", lambda e, p0=p0, ln=ln, slot=slot: e.dma_start(out=self.o_conv[l, slot], in_=self.xpad[p0 + ln:p0 + ln + 3, :]),
                  reads=[("dram", "xpad")], writes=[("dram", "o_conv")])
        S.barrier()
        S.emit()

    def conv_init(self, l):
        S = self.S
        st = self.stage()
        z = st.sb([96, 192], BF16)
        S.op("dve", lambda e: e.memset(z[:], 0.0), writes=["z"])
        S.dma("sp", lambda e: e.dma_start(out=self.xpad[16:19, :].rearrange("r (a f) -> (r a) f", f=192), in_=z[:]), reads=["z"], writes=[("dram", "xpad")])
        for j in range(NS):
            p0 = self.seqs[1 + j][2]
            S.dma("pool", lambda e, j=j, p0=p0: e.dma_start(out=self.xpad[p0:p0 + 3, :], in_=self.st_conv[l, j]), writes=[("dram", "xpad")])
        st.done()

    def mem_kv(self, l):
        S = self.S
        self.rms_stage(self.mem, self.memn, self.sw["norm_gains"][l, 6:7, :], 256)
        self.linear(self.memn, D, self.wb["w_xkv"][l], 2 * D, self.evac_store(self.mkv_f, F32), nrows=256)
        S.dma("sp", lambda e: e.dma_start(out=self.o_mk[l], in_=self.mkv_f[:, 0:D]), reads=[("dram", "mkv_f")], writes=[("dram", "o_mk")])
        S.dma("sp", lambda e: e.dma_start(out=self.o_mv[l], in_=self.mkv_f[:, D:2 * D]), reads=[("dram", "mkv_f")], writes=[("dram", "o_mv")])
        for kv in range(2):
            if kv == 0:
                S.dma("pool", lambda e: e.dma_start(out=self.kvp_bf, in_=self.mkv_f), reads=[("dram", "mkv_f")], writes=[("dram", "kv_bf")])
            src = self.st_mk if kv == 0 else self.st_mv
            for j in range(NS):
                S.dma("pool", lambda e, kv=kv, j=j, src=src: e.dma_start(out=self.kv_bf[1 + j, kv], in_=src[l, j]), writes=[("dram", "kv_bf")])
        S.barrier()
        S.emit()

    def build(self, upto="all"):
        S = self.S
        flags = upto.split(",")
        G = lambda l, i: self.sw["norm_gains"][l, i:i + 1, :]
        self.cast_weights()
        self.init_copy()
        if "castonly" in flags:
            S.barrier(); S.emit(); S.close()
            return self.nc
        self.rms_stage(self.xres, self.hn, G(0, 0), self.NT)
        for l in range(2):
            self.mem_kv(l)
            self.conv_init(l)
            cbs = [(c, 512) for c in range(0, C_DT, 512)] + [(C_DT, 64)] + [(c, 512) for c in range(C_GATE, INC, 512)]
            self.linear(self.hn, D, self.wb["w_in"][l], INC, self.evac_inproj(), colblocks=cbs)
            self.conv_out(l)
            if "noret" not in flags:
                self.retention(l)
            if "nos5" not in flags:
                self.s5(l)
            if "nossd" not in flags:
                self.ssd(l)
            if "mix" in flags:
                break
            self.linear(self.ret_y, D, self.wb["w_ret_o"][l], D, self.evac_store(self.br_ret, F32))
            self.linear(self.s5_y, D, self.wb["w_s5_glu"][l], 2 * D, self.evac_store(self.br_glu, F32))
            self.linear(self.ssd_y, 2 * D, self.wb["w_ssd_out"][l], D, self.evac_store(self.br_ssd, F32))
            self.merge(l)
            self.linear(self.merged, D, self.wb["w_mix_out"][l], D, self.evac_store(self.lin_out, F32))
            self.normadd(self.lin_out, G(l, 1), G(l, 2))
            self.linear(self.hn, D, self.wb["w_xq"][l], D, self.evac_store(self.q_bf, BF16))
            self.attention()
            self.linear(self.att, D, self.wb["w_xo"][l], D, self.evac_store(self.lin_out, F32))
            self.normadd(self.lin_out, G(l, 3), G(l, 4))
            self.linear(self.hn, D, self.wb["w_up"][l], 4 * D, self.evac_store(self.hmlp, BF16, func=self.evac_relu2()))
            self.linear(self.hmlp, 4 * D, self.wb["w_down"][l], D, self.evac_store(self.lin_out, F32))
            self.normadd(self.lin_out, G(l, 5), G(l + 1, 0) if l == 0 else None, final=(l == 1))
            if "l0" in flags:
                break
        if "l0" in flags or "mix" in flags:
            st = self.stage()
            xt = st.sb([128, D], F32)
            for (r, n) in self.tiles():
                S.dma("sp", lambda e, r=r, n=n: e.dma_start(out=xt[:n, :], in_=self.xres[r:r + n, :]), reads=[("dram", "xres")], writes=["xt"])
                S.dma("sp", lambda e, r=r, n=n: e.dma_start(out=self.y[r:r + n, :], in_=xt[:n, :]), reads=["xt"], writes=[("dram", "y")])
            st.done()
        S.barrier()
        S.emit()
        S.close()
        return self.nc


def make_in_maps(inputs, TP, cores):
    consts = host_consts(TP)
    maps = []
    for c in cores:
        b = c % 2
        sl = slice(NS * c, NS * (c + 1))
        m = {}
        m["x_in"] = np.concatenate([inputs["x_prompt"][b, :TP], inputs["x_sample"][sl].reshape(NS * SL, D)], axis=0)
        m["mem_in"] = inputs["mem_prompt"][b]
        m["st_ret"] = inputs["state_ret"][:, sl]
        m["st_s5r"] = inputs["state_s5_re"][:, sl]
        m["st_s5i"] = inputs["state_s5_im"][:, sl]
        m["st_ssd"] = inputs["state_ssd"][:, sl]
        m["st_conv"] = inputs["cache_ssd_conv"][:, sl]
        m["st_mk"] = inputs["cache_mem_k"][:, sl].reshape(2, NS, 256, D)
        m["st_mv"] = inputs["cache_mem_v"][:, sl].reshape(2, NS, 256, D)
        for k in WSHAPES:
            m[k] = inputs[k]
        for k in SMALLW:
            m[k] = inputs[k]
        m.update(consts)
        maps.append({k: np.ascontiguousarray(v, dtype=np.float32) for k, v in m.items()})
    return maps


KERNEL_FLAGS = "all"


def kernel(**inputs):
    TP = 8192
    inputs = {k: np.asarray(v) for k, v in inputs.items()}
    b = Builder(TP)
    nc = b.build(KERNEL_FLAGS)
    cores = list(range(8))
    maps = make_in_maps(inputs, TP, cores)
    res = run_bass_kernel_spmd(nc, maps, core_ids=cores)
    R = res.results
    f32 = np.float32
    y_p = np.stack([np.asarray(R[bb]["y"], f32)[:TP] for bb in range(2)])
    y_s = np.concatenate([np.asarray(R[c]["y"], f32)[TP:].reshape(NS, SL, D) for c in cores], axis=0)

    def pstate(name, shape):
        return np.stack([np.asarray(R[bb][name], f32)[:, 0] for bb in range(2)], axis=1).reshape(shape)

    def sstate(name, shape):
        return np.concatenate([np.asarray(R[c][name], f32)[:, 1:] for c in cores], axis=1).reshape(shape)
    p_ret = pstate("o_ret", (2, 2, RH, RDK, RDV))
    p_s5r = pstate("o_s5r", (2, 2, 128, 64))
    p_s5i = pstate("o_s5i", (2, 2, 128, 64))
    p_ssd = pstate("o_ssd", (2, 2, 64, 64, 128))
    p_conv = pstate("o_conv", (2, 2, 3, 6144))
    p_mk = np.stack([np.asarray(R[bb]["o_mk"], f32) for bb in range(2)], axis=1).reshape(2, 2, 256, 4, 512)
    p_mv = np.stack([np.asarray(R[bb]["o_mv"], f32) for bb in range(2)], axis=1).reshape(2, 2, 256, 4, 512)
    s_ret = sstate("o_ret", (2, 32, RH, RDK, RDV))
    s_s5r = sstate("o_s5r", (2, 32, 128, 64))
    s_s5i = sstate("o_s5i", (2, 32, 128, 64))
    s_ssd = sstate("o_ssd", (2, 32, 64, 64, 128))
    s_conv = sstate("o_conv", (2, 32, 3, 6144))
    return (y_p, y_s, p_ret, p_s5r, p_s5i, p_ssd, p_conv, p_mk, p_mv, s_ret, s_s5r, s_s5i, s_ssd, s_conv)
```

```python
import math
import numpy as np
import concourse.bass as bass
import concourse.mybir as mybir
from concourse.bass_utils import run_bass_kernel_spmd

F32 = mybir.dt.float32
BF16 = mybir.dt.bfloat16
I32 = mybir.dt.int32
AF = mybir.ActivationFunctionType
ALU = mybir.AluOpType
AX = mybir.AxisListType


class Sched:
    EPOCH = 24000
    NDMA = 16

    def __init__(self, nc):
        self.nc = nc
        self.engs = ["pe", "act", "dve", "pool", "sp"]
        self.items = {e: [] for e in self.engs}
        self.count = {e: 0 for e in self.engs}
        self.sems = {}
        self.dsems = {}
        self.dcount = {e: 0 for e in self.engs}
        self.seen = {e: {} for e in self.engs}
        self.res = {}
        self._semctx = []
        self.dlast = {}

    def _sem(self, key):
        d = self.sems if key[0] == "E" else self.dsems
        if key not in d:
            cm = self.nc.semaphore("s_%s_%s_%d" % key)
            d[key] = cm.__enter__()
            self._semctx.append(cm)
        return d[key]

    def _deps(self, reads, writes):
        ev = []
        for r in reads:
            st = self.res.get(r)
            if st and st[0] is not None:
                ev.append(st[0])
        for w in writes:
            st = self.res.get(w)
            if st:
                if st[0] is not None:
                    ev.append(st[0])
                ev.extend(st[1])
        return ev

    def _waits(self, eng, events):
        best = {}
        for (k, v) in events:
            if best.get(k, 0) < v:
                best[k] = v
        out = []
        for k, v in best.items():
            if self.seen[eng].get(k, 0) < v:
                self.seen[eng][k] = v
                out.append((k, v))
        return out

    def _mark(self, event, reads, writes):
        for r in reads:
            st = self.res.setdefault(r, [None, []])
            st[1].append(event)
        for w in writes:
            self.res[w] = [event, []]

    def op(self, eng, fn, reads=(), writes=()):
        waits = self._waits(eng, self._deps(reads, writes))
        n = self.count[eng]
        epoch, idx = divmod(n, self.EPOCH)
        self.count[eng] = n + 1
        key = ("E", eng, epoch)
        self._sem(key)
        event = (key, idx + 1)
        self.items[eng].append((waits, fn, key, 1))
        self._mark(event, reads, writes)
        return event

    def dma(self, eng, fn, reads=(), writes=()):
        n = self.dcount[eng]
        self.dcount[eng] = n + 1
        k, rnd = n % self.NDMA, n // self.NDMA
        key = ("D", eng, k)
        self._sem(key)
        ev = self._deps(reads, writes)
        if rnd > 0:
            ev.append((key, 16 * rnd))
        waits = self._waits(eng, ev)
        event = (key, 16 * (rnd + 1))
        self.dlast[key] = 16 * (rnd + 1)
        self.items[eng].append((waits, fn, key, 16))
        self._mark(event, reads, writes)
        return event

    def wait_all(self, eng, keys):
        ev = []
        for kk in keys:
            st = self.res.get(kk)
            if st:
                if st[0] is not None:
                    ev.append(st[0])
                ev.extend(st[1])
        waits = self._waits(eng, ev)
        self.items[eng].append((waits, None, None, 0))

    def barrier(self):
        ev = []
        for f in self.engs:
            n = self.count[f]
            if n > 0:
                epoch, idx = divmod(n - 1, self.EPOCH)
                ev.append((("E", f, epoch), idx + 1))
        for k, v in self.dlast.items():
            ev.append((k, v))
        for e in self.engs:
            self.items[e].append((self._waits(e, list(ev)), None, None, 0))

    def emit(self):
        nc = self.nc
        sched = self

        def run(engname, engine):
            for waits, fn, key, inc in sched.items[engname]:
                for (k, v) in waits:
                    engine.wait_ge(sched._sem(k), v)
                if fn is not None:
                    ins = fn(engine)
                    ins.then_inc(sched._sem(key), inc)

        with nc.Block() as block:
            @block.tensor
            def _(e):
                run("pe", e)

            @block.scalar
            def _(e):
                run("act", e)

            @block.vector
            def _(e):
                run("dve", e)

            @block.gpsimd
            def _(e):
                run("pool", e)

            @block.sync
            def _(e):
                run("sp", e)
        self.items = {e: [] for e in self.engs}

    def close(self):
        for cm in reversed(self._semctx):
            cm.__exit__(None, None, None)
        self._semctx = []


D = 2048
NS = 4
SL = 16
PAST = 1024
EPS = 1e-6
RH, RDK, RDV = 8, 128, 256
INC = 24640
C_Q, C_K, C_V, C_G, C_U, C_Z, C_XBC, C_DT, C_GATE = 0, 1024, 2048, 4096, 6144, 8192, 12288, 18432, 18496
GAMMA = [1.0 - 2.0 ** (-5 - h) for h in range(RH)]
WSHAPES = {"w_in": (D, INC), "w_ret_o": (D, D), "w_s5_glu": (D, 2 * D), "w_ssd_out": (2 * D, D), "w_mix_out": (D, D),
           "w_xq": (D, D), "w_xkv": (D, 2 * D), "w_xo": (D, D), "w_up": (D, 4 * D), "w_down": (4 * D, D)}
SMALLW = {"norm_gains": (7, D), "ret_gn": (D,), "s5_a_re": (128, 64), "s5_a_im": (128, 64), "s5_b_re": (128, 64, 16),
          "s5_b_im": (128, 64, 16), "s5_c_re": (128, 16, 64), "s5_c_im": (128, 16, 64), "s5_d": (D,), "s5_log_dt": (128,),
          "ssd_conv_w": (4, 6144), "ssd_conv_b": (6144,), "ssd_dt_bias": (64,), "ssd_a_log": (64,), "ssd_d": (64,),
          "ssd_norm": (4096,)}


def host_consts(TP):
    NT = TP + NS * SL
    pos = np.concatenate([np.arange(TP)] + [PAST + np.arange(SL)] * NS).astype(np.float32)
    half = 64
    inv = np.exp(-math.log(10000.0) * np.arange(half, dtype=np.float32) / half).astype(np.float32)
    ang = pos[:, None] * inv[None]
    cos, sin = np.cos(ang).astype(np.float32), np.sin(ang).astype(np.float32)
    cs = np.zeros((NT, 2, RH, 2, 64), np.float32)
    sn = np.zeros((NT, 2, RH, 2, 64), np.float32)
    for w in range(2):
        sc = 1.0 if w == 0 else RDK ** -0.5
        cs[:, w, :, :, :] = (cos * sc)[:, None, None, :]
        sn[:, w, :, 0, :] = (-sin * sc)[:, None, :]
        sn[:, w, :, 1, :] = (sin * sc)[:, None, :]
    lg = np.log1p(-np.exp2(-5.0 - np.arange(RH))).astype(np.float64)
    idx = np.arange(64)
    mask = np.exp(np.abs(idx[:, None] - idx[None, :])[:, None, :] * lg[None, :, None]).astype(np.float32)
    din = np.exp((idx + 1.0)[None, :] * lg[:, None])[None].repeat(128, 0).astype(np.float32)
    dup64 = np.exp((63.0 - idx)[:, None] * lg[None, :]).astype(np.float32)
    dup16 = np.exp((15.0 - np.arange(16))[:, None] * lg[None, :]).astype(np.float32)
    tv = np.arange(64, dtype=np.float32)[None].repeat(128, 0)
    ident = np.eye(128, dtype=np.float32)
    tri = (idx[:, None] <= idx[None, :]).astype(np.float32)
    return {"rope_cs": cs.reshape(NT, 2048), "rope_sn": sn.reshape(NT, 2048), "ret_mask": mask, "ret_din": din,
            "ret_dup64": dup64, "ret_dup16": dup16, "tvec": tv, "ident_in": ident, "tri_in": tri}


CONST_SHAPES = lambda NT: {"rope_cs": (NT, 2048), "rope_sn": (NT, 2048), "ret_mask": (64, 8, 64), "ret_din": (128, 8, 64),
                           "ret_dup64": (64, 8), "ret_dup16": (16, 8), "tvec": (128, 64), "ident_in": (128, 128),
                           "tri_in": (64, 64)}


from contextlib import ExitStack


class Builder:
    def __init__(self, TP, dbg_out=()):
        self.TP = TP
        self.NT = NT = TP + NS * SL
        self.nc = nc = bass.Bass("TRN2", target_bir_lowering=False)
        self.S = Sched(nc)
        self.uid = 0
        di = lambda name, shape, dt=F32: nc.dram_tensor(name, list(shape), dt, kind="ExternalInput").ap()
        do = lambda name, shape, dt=F32: nc.dram_tensor(name, list(shape), dt, kind="ExternalOutput").ap()
        ds = lambda name, shape, dt: nc.dram_tensor(name, list(shape), dt, kind=("ExternalOutput" if name in dbg_out else "Internal")).ap()
        self.x_in = di("x_in", (NT, D))
        self.mem = di("mem_in", (256, D))
        self.st_ret = di("st_ret", (2, NS, RH, RDK, RDV))
        self.st_s5r = di("st_s5r", (2, NS, 128, 64))
        self.st_s5i = di("st_s5i", (2, NS, 128, 64))
        self.st_ssd = di("st_ssd", (2, NS, 64, 64, 128))
        self.st_conv = di("st_conv", (2, NS, 3, 6144))
        self.st_mk = di("st_mk", (2, NS, 256, D))
        self.st_mv = di("st_mv", (2, NS, 256, D))
        self.w = {k: di(k, (2,) + v) for k, v in WSHAPES.items()}
        self.sw = {k: di(k, (2,) + v) for k, v in SMALLW.items()}
        self.cst = {k: di(k, v) for k, v in CONST_SHAPES(NT).items()}
        self.y = do("y", (NT, D))
        self.o_ret = do("o_ret", (2, 1 + NS, RH, RDK, RDV))
        self.o_s5r = do("o_s5r", (2, 1 + NS, 128, 64))
        self.o_s5i = do("o_s5i", (2, 1 + NS, 128, 64))
        self.o_ssd = do("o_ssd", (2, 1 + NS, 64, 64, 128))
        self.o_conv = do("o_conv", (2, 1 + NS, 3, 6144))
        self.o_mk = do("o_mk", (2, 256, D))
        self.o_mv = do("o_mv", (2, 256, D))
        self.wb = {k: ds(k + "_bf", (2,) + v, BF16) for k, v in WSHAPES.items()}
        self.xres = ds("xres", (NT, D), F32)
        self.hn = ds("hn", (NT, D), BF16)
        self.pj_qkvg = ds("pj_qkvg", (NT, 6144), BF16)
        self.pj_u = ds("pj_u", (NT, 2048), BF16)
        self.pj_z = ds("pj_z", (NT, 4096), BF16)
        self.pj_gate = ds("pj_gate", (NT, 6144), BF16)
        self.dtf = ds("dtf", (NT, 64), F32)
        self.xpad = ds("xpad", (16 + NT + 3 * (1 + NS) + 64, 6144), BF16)
        self.ret_y = ds("ret_y", (NT, D), BF16)
        self.s5_y = ds("s5_y", (NT, D), BF16)
        self.ssd_y = ds("ssd_y", (NT, 2 * D), BF16)
        self.br_ret = ds("br_ret", (NT, D), F32)
        self.br_glu = ds("br_glu", (NT, 2 * D), F32)
        self.br_ssd = ds("br_ssd", (NT, D), F32)
        self.merged = ds("merged", (NT, D), BF16)
        self.lin_out = ds("lin_out", (NT, D), F32)
        self.q_bf = ds("q_bf", (NT, D), BF16)
        self.att = ds("att", (NT, D), BF16)
        self.hmlp = ds("hmlp", (NT, 4 * D), BF16)
        self.memn = ds("memn", (256, D), BF16)
        self.mkv_f = ds("mkv_f", (256, 2 * D), F32)
        self.kvp_bf = ds("kvp_bf", (256, 2 * D), BF16)
        self.kv_bf = ds("kv_bf", (1 + NS, 2, 256, D), BF16)
        self.seqs = [(0, TP, 16, 0)] + [(TP + SL * j, SL, 16 + TP + 3 + (SL + 3) * j, 1 + j) for j in range(NS)]

    def name(self, p):
        self.uid += 1
        return "%s%d" % (p, self.uid)

    def stage(self):
        b = self

        class St:
            def __init__(s):
                s.stack = ExitStack()

            def sb(s, shape, dt=F32, nm="t"):
                return s.stack.enter_context(b.nc.sbuf_tensor(b.name(nm), list(shape), dt))

            def ps(s, shape, dt=F32, nm="p"):
                return s.stack.enter_context(b.nc.psum_tensor(b.name(nm), list(shape), dt))

            def done(s):
                b.S.barrier()
                b.S.emit()
                s.stack.close()
        return St()

    def tiles(self):
        out, r = [], 0
        while r < self.NT:
            n = min(128, self.NT - r)
            out.append((r, n))
            r += n
        return out

    def chunks(self):
        out = []
        for si, (r0, ln, _, _) in enumerate(self.seqs):
            c = min(64, ln)
            for k in range(ln // c):
                out.append((si, r0 + k * c, c, k == 0, k == ln // c - 1))
        return out

    def load_row_bcast(self, st, ap_row, width, npart=128, dt=F32, eng="sp"):
        t = st.sb([npart, width], dt, "rb")
        key = self.name("rbk")
        self.S.dma(eng, lambda e: e.dma_start(out=t[:], in_=ap_row.to_broadcast([npart, width])), writes=[key])
        return t, key

    def cast_weights(self):
        S = self.S
        for k, (rows, cols) in WSHAPES.items():
            step = max(1, (4 << 20) // cols)
            for l in range(2):
                for r in range(0, rows, step):
                    rr = min(step, rows - r)
                    S.dma("pool", lambda e, k=k, l=l, r=r, rr=rr: e.dma_start(out=self.wb[k][l, r:r + rr, :], in_=self.w[k][l, r:r + rr, :]),
                          writes=[("dram", k + "_bf")])
        S.barrier()
        S.emit()

    def init_copy(self):
        S = self.S
        S.dma("sp", lambda e: e.dma_start(out=self.xres, in_=self.x_in), writes=["xres"])
        for l in range(2):
            pass
        S.barrier()
        S.emit()

    def rms_stage(self, src, dst, gain_row, nrows):
        S = self.S
        st = self.stage()
        g, gk = self.load_row_bcast(st, gain_row, D)
        xt = [st.sb([128, D], F32, "xt") for _ in range(2)]
        hb = [st.sb([128, D], BF16, "hb") for _ in range(2)]
        junk = st.sb([128, D], BF16, "junk")
        ss = [st.sb([128, 1], F32, "ss") for _ in range(2)]
        r, i = 0, 0
        while r < nrows:
            n = min(128, nrows - r)
            b = i % 2
            S.dma("sp", lambda e, r=r, n=n, b=b: e.dma_start(out=xt[b][:n, :], in_=src[r:r + n, :]), reads=[("dram", src.name)], writes=[("xt", b)])
            S.op("dve", lambda e, n=n, b=b: e.memset(ss[b][:n, :], 0.0), writes=[("ss", b)])
            S.op("act", lambda e, n=n, b=b: e.activation(junk[:n, :], xt[b][:n, :], AF.Square, accum_out=ss[b][:n, :]), reads=[("xt", b), ("ss", b)], writes=["junk", ("ss", b)])
            S.op("act", lambda e, n=n, b=b: e.activation(ss[b][:n, :], ss[b][:n, :], AF.Sqrt, bias=EPS, scale=1.0 / D), reads=[("ss", b)], writes=[("ss", b)])
            S.op("dve", lambda e, n=n, b=b: e.reciprocal(ss[b][:n, :], ss[b][:n, :]), reads=[("ss", b)], writes=[("ss", b)])
            S.op("dve", lambda e, n=n, b=b: e.scalar_tensor_tensor(hb[b][:n, :], xt[b][:n, :], ss[b][:n, 0:1], g[:n, :], ALU.mult, ALU.mult),
                 reads=[("xt", b), ("ss", b), gk], writes=[("hb", b)])
            S.dma("sp", lambda e, r=r, n=n, b=b: e.dma_start(out=dst[r:r + n, :], in_=hb[b][:n, :]), reads=[("hb", b)], writes=[("dram", dst.name)])
            r += n
            i += 1
        st.done()

    def linear(self, A, K, W, N, evac, nrows=None, colblocks=None):
        S = self.S
        nrows = self.NT if nrows is None else nrows
        KC = K // 128
        TBL = {2048: 1024, 4096: 512, 8192: 256}[K]
        st = self.stage()
        NAT = 2 if K <= 4096 else 1
        ATs = [st.sb([128, KC, TBL + 64], BF16, "AT") for _ in range(NAT)]
        WT = [st.sb([128, KC, 512], BF16, "WT") for _ in range(2)]
        PS = [st.ps([128, 512], F32, "lps") for _ in range(2)]
        self.lin_st = st
        if colblocks is None:
            colblocks = [(c, min(512, N - c)) for c in range(0, N, 512)]
        Wv = W.rearrange("(kc p) n -> p kc n", p=128)
        widx, pidx = 0, 0
        blocks = []
        rb = 0
        while rb < nrows:
            nb = min(TBL, nrows - rb)
            if 0 < nrows - (rb + nb) <= 64:
                nb = nrows - rb
            blocks.append((rb, nb))
            rb += nb

        def load_at(bi):
            rb, nb = blocks[bi]
            ab = bi % NAT
            for kc in range(KC):
                S.dma("sp", lambda e, kc=kc, rb=rb, nb=nb, ab=ab: e.dma_start(out=ATs[ab][:, kc, :nb], in_=A[rb:rb + nb, kc * 128:(kc + 1) * 128], transpose=True),
                      reads=[("dram", A.name)], writes=[("AT", ab, kc)])
        load_at(0)
        for bi, (rb, nb) in enumerate(blocks):
            ab = bi % NAT
            AT = ATs[ab]
            if NAT == 2 and bi + 1 < len(blocks):
                load_at(bi + 1)
            for (c0, cw) in colblocks:
                wbuf = widx % 2
                widx += 1
                S.dma("act", lambda e, c0=c0, cw=cw, wbuf=wbuf: e.dma_start(out=WT[wbuf][:, :, :cw], in_=Wv[:, :, c0:c0 + cw]),
                      reads=[("dram", W.name)], writes=[("WT", wbuf)])
                t0 = 0
                while t0 < nb:
                    n = min(128, nb - t0)
                    pb = pidx % 2
                    pidx += 1

                    def mm(e, t0=t0, n=n, cw=cw, wbuf=wbuf, pb=pb, AT=AT):
                        for kc in range(KC):
                            ins = e.matmul(PS[pb][:n, :cw], AT[:, kc, t0:t0 + n], WT[wbuf][:, kc, :cw], start=(kc == 0), stop=(kc == KC - 1))
                        return ins
                    S.op("pe", mm, reads=[("AT", ab, kc) for kc in range(KC)] + [("WT", wbuf)], writes=[("lps", pb)])
                    evac(st, rb + t0, n, c0, cw, PS[pb], pb)
                    t0 += n
            if NAT == 1 and bi + 1 < len(blocks):
                load_at(bi + 1)
        st.done()

    def evac_store(self, dst, dt, coloff=0, func=None):
        S = self.S
        bufs = {}

        def evac(st, r0, n, c0, cw, ps, pb):
            if "ob" not in bufs:
                bufs["ob"] = [st.sb([128, 512], dt, "ob") for _ in range(2)]
                bufs["i"] = 0
            ob = bufs["ob"][bufs["i"] % 2]
            okey = ("ob", bufs["i"] % 2)
            bufs["i"] += 1
            if func is None:
                S.op("act", lambda e: e.copy(ob[:n, :cw], ps[:n, :cw]), reads=[("lps", pb)], writes=[okey])
            else:
                func(st, n, cw, ps, pb, ob, okey)
            S.dma("sp", lambda e: e.dma_start(out=dst[r0:r0 + n, coloff + c0:coloff + c0 + cw], in_=ob[:n, :cw]), reads=[okey], writes=[("dram", dst.name)])
        return evac

    def segs(self, r0, n):
        out = []
        for (s0, ln, p0, _) in self.seqs:
            a, b = max(r0, s0), min(r0 + n, s0 + ln)
            if a < b:
                out.append((a - r0, b - a, p0 + 3 + (a - s0)))
        return out

    def evac_inproj(self):
        S = self.S
        bufs = {}

        def evac(st, r0, n, c0, cw, ps, pb):
            if "ob" not in bufs:
                bufs["ob"] = [st.sb([128, 512], BF16, "ob") for _ in range(2)]
                bufs["of"] = st.sb([128, 64], F32, "of")
                bufs["i"] = 0
            ob = bufs["ob"][bufs["i"] % 2]
            okey = ("ob", bufs["i"] % 2)
            bufs["i"] += 1
            S.op("act", lambda e: e.copy(ob[:n, :cw], ps[:n, :cw]), reads=[("lps", pb)], writes=[okey])
            tgt = None
            if c0 < C_U:
                tgt, lc = self.pj_qkvg, c0
            elif c0 < C_Z:
                tgt, lc = self.pj_u, c0 - C_U
            elif c0 < C_XBC:
                tgt, lc = self.pj_z, c0 - C_Z
            elif c0 >= C_GATE:
                tgt, lc = self.pj_gate, c0 - C_GATE
            if tgt is not None:
                S.dma("sp", lambda e: e.dma_start(out=tgt[r0:r0 + n, lc:lc + cw], in_=ob[:n, :cw]), reads=[okey], writes=[("dram", tgt.name)])
            if C_XBC <= c0 < C_DT:
                for (o, cnt, prow) in self.segs(r0, n):
                    S.dma("sp", lambda e, o=o, cnt=cnt, prow=prow: e.dma_start(out=self.xpad[prow:prow + cnt, c0 - C_XBC:c0 - C_XBC + cw], in_=ob[o:o + cnt, :cw]),
                          reads=[okey], writes=[("dram", "xpad")])
            if c0 == C_DT:
                of = bufs["of"]
                S.op("dve", lambda e: e.tensor_copy(of[:n, :cw], ps[:n, :cw]), reads=[("lps", pb)], writes=["of"])
                S.dma("sp", lambda e: e.dma_start(out=self.dtf[r0:r0 + n, :], in_=of[:n, :cw]), reads=["of"], writes=[("dram", "dtf")])
        return evac

    def retention(self, l):
        S, nc = self.S, self.nc
        st = self.stage()
        sb, ps = st.sb, st.ps
        mask = sb([64, 8, 64], F32); din = sb([128, 8, 64], F32); dup64 = sb([64, 8], F32); dup16 = sb([16, 8], F32)
        identb = sb([128, 128], BF16); identf = sb([128, 128], F32)
        S.dma("sp", lambda e: e.dma_start(out=mask[:], in_=self.cst["ret_mask"]), writes=["mask"])
        S.dma("sp", lambda e: e.dma_start(out=din[:], in_=self.cst["ret_din"]), writes=["din"])
        S.dma("sp", lambda e: e.dma_start(out=dup64[:], in_=self.cst["ret_dup64"]), writes=["dup64"])
        S.dma("sp", lambda e: e.dma_start(out=dup16[:], in_=self.cst["ret_dup16"]), writes=["dup16"])
        S.dma("sp", lambda e: e.dma_start(out=identf[:], in_=self.cst["ident_in"]), writes=["identf"])
        S.op("dve", lambda e: e.tensor_copy(identb[:], identf[:]), reads=["identf"], writes=["identb"])
        gn, gnk = self.load_row_bcast(st, self.sw["ret_gn"][l:l + 1, :], D, 64)
        qks = [sb([64, 2048], BF16) for _ in range(2)]; vts = [sb([64, 2048], BF16) for _ in range(2)]; gts = [sb([64, 2048], BF16) for _ in range(2)]
        css = [sb([64, 2048], F32) for _ in range(2)]; sns = [sb([64, 2048], F32) for _ in range(2)]
        t1 = sb([64, 2048], F32); t2 = sb([64, 2048], F32)
        qkr = sb([64, 2048], BF16); kd = sb([64, 8, 128], BF16)
        qkT = sb([128, 16, 64], BF16); qdT = sb([128, 8, 64], BF16)
        sm = sb([64, 8, 64], BF16)
        o_sb = sb([64, 8, 256], F32); osq = sb([64, 8, 256], F32)
        Sf = sb([128, 8, 256], F32); Sb = sb([128, 8, 256], BF16)
        s1 = sb([64, 8], F32); s2 = sb([64, 8], F32); mean = sb([64, 8], F32); msq = sb([64, 8], F32)
        sg = sb([64, 2048], F32); yb = sb([64, 2048], BF16)
        tq = ps([128, 16, 64], BF16); ps_s = ps([64, 8, 64], F32)
        ps_o = [ps([64, 2, 256], F32) for _ in range(2)]; ps_S = [ps([128, 2, 256], F32) for _ in range(2)]
        chs = self.chunks()

        def ret_loads(ci):
            (_, r0_, L_, _, _) = chs[ci]
            b_ = ci % 2
            S.dma("sp", lambda e: e.dma_start(out=qks[b_][:L_, :], in_=self.pj_qkvg[r0_:r0_ + L_, C_Q:C_Q + 2048]), reads=[("dram", "pj_qkvg")], writes=[("qk", b_)])
            S.dma("sp", lambda e: e.dma_start(out=vts[b_][:L_, :], in_=self.pj_qkvg[r0_:r0_ + L_, C_V:C_V + 2048]), reads=[("dram", "pj_qkvg")], writes=[("vt", b_)])
            S.dma("sp", lambda e: e.dma_start(out=gts[b_][:L_, :], in_=self.pj_qkvg[r0_:r0_ + L_, C_G:C_G + 2048]), reads=[("dram", "pj_qkvg")], writes=[("gt", b_)])
            S.dma("act", lambda e: e.dma_start(out=css[b_][:L_, :], in_=self.cst["rope_cs"][r0_:r0_ + L_, :]), writes=[("cs", b_)])
            S.dma("act", lambda e: e.dma_start(out=sns[b_][:L_, :], in_=self.cst["rope_sn"][r0_:r0_ + L_, :]), writes=[("sn", b_)])
        for ci, (si, r0, L, first, last) in enumerate(chs):
            slot = self.seqs[si][3]
            bb = ci % 2
            qk, vt, gt, cs, sn = qks[bb], vts[bb], gts[bb], css[bb], sns[bb]
            kqk, kvt, kgt, kcs, ksn = ("qk", bb), ("vt", bb), ("gt", bb), ("cs", bb), ("sn", bb)
            if first:
                if slot == 0:
                    S.op("dve", lambda e: e.memset(Sf[:], 0.0), writes=["Sf"])
                else:
                    S.dma("sp", lambda e, slot=slot: e.dma_start(out=Sf[:], in_=self.st_ret[l, slot - 1].rearrange("h d e -> d h e")), writes=["Sf"])
                S.op("act", lambda e: e.copy(Sb[:], Sf[:]), reads=["Sf"], writes=["Sb"])
            if ci == 0:
                ret_loads(0)
            if ci + 1 < len(chs):
                ret_loads(ci + 1)
            v4 = lambda t, L=L: t[:L, :].rearrange("p (a two d) -> p a two d", two=2, d=64)
            S.op("dve", lambda e, L=L, qk=qk, cs=cs: e.tensor_tensor(t1[:L, :], qk[:L, :], cs[:L, :], ALU.mult), reads=[kqk, kcs], writes=["t1"])
            S.op("pool", lambda e, L=L, v4=v4, qk=qk, sn=sn: e.tensor_tensor(v4(t2)[:, :, 0, :], v4(qk)[:, :, 1, :], v4(sn)[:, :, 0, :], ALU.mult), reads=[kqk, ksn], writes=["t2a"])
            S.op("pool", lambda e, L=L, v4=v4, qk=qk, sn=sn: e.tensor_tensor(v4(t2)[:, :, 1, :], v4(qk)[:, :, 0, :], v4(sn)[:, :, 1, :], ALU.mult), reads=[kqk, ksn], writes=["t2b"])
            S.op("dve", lambda e, L=L: e.tensor_tensor(qkr[:L, :], t1[:L, :], t2[:L, :], ALU.add), reads=["t1", "t2a", "t2b"], writes=["qkr"])
            dup = dup64 if L == 64 else dup16
            S.op("dve", lambda e, L=L, dup=dup: e.tensor_tensor(kd[:L], qkr[:L, 1024:2048].rearrange("p (h d) -> p h d", h=8),
                                                               dup[:L, :].unsqueeze(2).to_broadcast([L, 8, 128]), ALU.mult),
                 reads=["qkr", "dup64", "dup16"], writes=["kd"])

            def tr(e, L=L):
                for i in range(16):
                    ins = e.transpose(tq[:, i, :L], qkr[:L, i * 128:(i + 1) * 128], identb[:L, :L])
                return ins
            S.op("pe", tr, reads=["qkr", "identb"], writes=["tq"])
            S.op("act", lambda e, L=L: e.copy(qkT[:, :, :L], tq[:, :, :L]), reads=["tq"], writes=["qkT"])
            S.op("dve", lambda e, L=L: e.tensor_tensor(qdT[:, :, :L], qkT[:, 0:8, :L], din[:, :, :L], ALU.mult), reads=["qkT", "din"], writes=["qdT"])

            def sc(e, L=L):
                for h in range(8):
                    ins = e.matmul(ps_s[:L, h, :L], qkT[:, 8 + h, :L], qkT[:, h, :L], start=True, stop=True)
                return ins
            S.op("pe", sc, reads=["qkT"], writes=["ps_s"])
            S.op("dve", lambda e, L=L: e.tensor_tensor(sm[:L, :, :L], ps_s[:L, :, :L], mask[:L, :, :L], ALU.mult), reads=["ps_s", "mask"], writes=["sm"])
            for hp in range(4):
                pb = hp % 2

                def om(e, L=L, hp=hp, pb=pb, vt=vt):
                    for hh in range(2):
                        h = 2 * hp + hh
                        e.matmul(ps_o[pb][:L, hh, :], sm[:L, h, :L], vt[:L, h * 256:(h + 1) * 256], start=True, stop=False)
                        ins = e.matmul(ps_o[pb][:L, hh, :], qdT[:, h, :L], Sb[:, h, :], start=False, stop=True)
                    return ins
                S.op("pe", om, reads=["sm", kvt, "qdT", "Sb"], writes=[("ps_o", pb)])
                S.op("act", lambda e, L=L, hp=hp, pb=pb: e.copy(o_sb[:L, 2 * hp:2 * hp + 2, :], ps_o[pb][:L, :, :]), reads=[("ps_o", pb)], writes=[("o_sb", hp)])
            for hp in range(4):
                pb = hp % 2

                def sm_(e, L=L, hp=hp, pb=pb, vt=vt):
                    for hh in range(2):
                        h = 2 * hp + hh
                        ins = e.matmul(ps_S[pb][:, hh, :], kd[:L, h, :], vt[:L, h * 256:(h + 1) * 256], start=True, stop=True)
                    return ins
                S.op("pe", sm_, reads=["kd", kvt], writes=[("ps_S", pb)])
                for hh in range(2):
                    h = 2 * hp + hh
                    S.op("dve", lambda e, h=h, hh=hh, pb=pb, L=L: e.scalar_tensor_tensor(Sf[:, h, :], Sf[:, h, :], float(GAMMA[h] ** L), ps_S[pb][:, hh, :], ALU.mult, ALU.add),
                         reads=[("ps_S", pb), "Sf"], writes=["Sf"])
            S.op("act", lambda e: e.copy(Sb[:], Sf[:]), reads=["Sf"], writes=["Sb"])
            if last:
                dst = self.o_ret[l, slot].rearrange("h d e -> d h e")
                S.dma("sp", lambda e, dst=dst: e.dma_start(out=dst, in_=Sf[:]), reads=["Sf"], writes=[("dram", "o_ret")])
            okeys = [("o_sb", hp) for hp in range(4)]
            S.op("dve", lambda e, L=L: e.tensor_reduce(s1[:L, :], o_sb[:L], AX.X, ALU.add), reads=okeys, writes=["s1"])
            S.op("act", lambda e, L=L: e.activation(osq[:L], o_sb[:L], AF.Square), reads=okeys, writes=["osq"])
            S.op("dve", lambda e, L=L: e.tensor_reduce(s2[:L, :], osq[:L], AX.X, ALU.add), reads=["osq"], writes=["s2"])
            S.op("dve", lambda e, L=L: e.tensor_scalar(mean[:L, :], s1[:L, :], 1.0 / 256, None, ALU.mult), reads=["s1"], writes=["mean"])
            S.op("dve", lambda e, L=L: e.tensor_tensor(msq[:L, :], mean[:L, :], mean[:L, :], ALU.mult), reads=["mean"], writes=["msq"])
            S.op("dve", lambda e, L=L: e.scalar_tensor_tensor(s2[:L, :], s2[:L, :], 1.0 / 256, msq[:L, :], ALU.mult, ALU.subtract), reads=["s2", "msq"], writes=["s2"])
            S.op("act", lambda e, L=L: e.activation(s2[:L, :], s2[:L, :], AF.Sqrt, bias=EPS, scale=1.0), reads=["s2"], writes=["s2"])
            S.op("dve", lambda e, L=L: e.reciprocal(s2[:L, :], s2[:L, :]), reads=["s2"], writes=["s2"])
            S.op("dve", lambda e, L=L: e.tensor_tensor(osq[:L], o_sb[:L], mean[:L, :].unsqueeze(2).to_broadcast([L, 8, 256]), ALU.subtract), reads=okeys + ["mean", "osq"], writes=["osq"])
            S.op("dve", lambda e, L=L: e.tensor_tensor(osq[:L], osq[:L], s2[:L, :].unsqueeze(2).to_broadcast([L, 8, 256]), ALU.mult), reads=["osq", "s2"], writes=["osq"])
            S.op("act", lambda e, L=L, gt=gt: e.activation(sg[:L, :], gt[:L, :], AF.Silu), reads=[kgt], writes=["sg"])
            S.op("pool", lambda e, L=L: e.tensor_tensor(sg[:L, :], sg[:L, :], gn[:L, :], ALU.mult), reads=["sg", gnk], writes=["sg"])
            S.op("dve", lambda e, L=L: e.tensor_tensor(yb[:L, :], osq[:L].rearrange("p h e -> p (h e)"), sg[:L, :], ALU.mult), reads=["osq", "sg"], writes=["yb"])
            S.dma("sp", lambda e, r0=r0, L=L: e.dma_start(out=self.ret_y[r0:r0 + L, :], in_=yb[:L, :]), reads=["yb"], writes=[("dram", "ret_y")])
        st.done()


    def trig(self, st, ang, cos_o, sin_o, shape, key_in, key_c, key_s):
        S = self.S
        ki = st.sb(shape, I32, "ki"); kf = st.sb(shape, F32, "kf"); s2 = st.sb(shape, F32, "s2"); s4 = st.sb(shape, F32, "s4")
        k = self.name("trg")
        S.op("dve", lambda e: e.tensor_scalar(ki[:], ang(), 1.0 / (2 * math.pi), None, ALU.mult), reads=[key_in], writes=[k + "ki"])
        S.op("dve", lambda e: e.tensor_copy(kf[:], ki[:]), reads=[k + "ki"], writes=[k + "kf"])
        S.op("dve", lambda e: e.scalar_tensor_tensor(kf[:], kf[:], -2 * math.pi, ang(), ALU.mult, ALU.add), reads=[k + "kf", key_in], writes=[k + "kf"])
        S.op("act", lambda e: e.activation(s2[:], kf[:], AF.Sin, scale=0.5), reads=[k + "kf"], writes=[k + "s2"])
        S.op("act", lambda e: e.activation(s4[:], kf[:], AF.Sin, scale=0.25), reads=[k + "kf"], writes=[k + "s4"])
        S.op("dve", lambda e: e.tensor_tensor(s4[:], s4[:], s4[:], ALU.mult), reads=[k + "s4"], writes=[k + "s4"])
        S.op("dve", lambda e: e.tensor_scalar(s4[:], s4[:], -2.0, 1.0, ALU.mult, ALU.add), reads=[k + "s4"], writes=[k + "s4"])
        S.op("dve", lambda e: e.scalar_tensor_tensor(sin_o(), s2[:], 2.0, s4[:], ALU.mult, ALU.mult), reads=[k + "s2", k + "s4"], writes=[key_s])
        S.op("dve", lambda e: e.tensor_tensor(s2[:], s2[:], s2[:], ALU.mult), reads=[k + "s2", key_s], writes=[k + "s2"])
        S.op("dve", lambda e: e.tensor_scalar(cos_o(), s2[:], -2.0, 1.0, ALU.mult, ALU.add), reads=[k + "s2"], writes=[key_c])

    def s5(self, l):
        S, nc = self.S, self.nc
        st = self.stage()
        sb, ps = st.sb, st.ps
        QH = 32
        identf = sb([128, 128], F32); identb = sb([128, 128], BF16)
        S.dma("sp", lambda e: e.dma_start(out=identf[:], in_=self.cst["ident_in"]), writes=["identf"])
        S.op("dve", lambda e: e.tensor_copy(identb[:], identf[:]), reads=["identf"], writes=["identb"])
        tv = sb([128, 64], F32)
        S.dma("sp", lambda e: e.dma_start(out=tv[:], in_=self.cst["tvec"]), writes=["tv"])
        COS = sb([128, 64, 64], F32); SIN = sb([128, 64, 64], F32); MT = sb([128, 64, 64], F32)
        MT16 = sb([128, 64, 16], F32)
        LB = sb([128, 64, 2, 128], BF16); CB = sb([128, 64, 2, 32], BF16)
        arT = sb([128, 64], F32); aiT = sb([128, 64], F32)
        pst = self.stage()
        psb = pst.sb
        are = psb([64, 128], F32); aim = psb([64, 128], F32); dtq = psb([64, 2], F32); dtE = psb([64, 128], F32)
        S.dma("sp", lambda e: e.dma_start(out=are[:], in_=self.sw["s5_a_re"][l].rearrange("(q g) p -> q (g p)", g=2)), writes=["are"])
        S.dma("sp", lambda e: e.dma_start(out=aim[:], in_=self.sw["s5_a_im"][l].rearrange("(q g) p -> q (g p)", g=2)), writes=["aim"])
        S.dma("sp", lambda e: e.dma_start(out=dtq[:], in_=self.sw["s5_log_dt"][l].rearrange("(q g) -> q g", g=2)), writes=["dtq"])
        S.op("act", lambda e: e.activation(dtq[:], dtq[:], AF.Exp), reads=["dtq"], writes=["dtq"])
        for g2 in range(2):
            S.op("dve", lambda e, g2=g2: e.tensor_copy(dtE[:, g2 * 64:(g2 + 1) * 64], dtq[:, g2:g2 + 1].to_broadcast([64, 64])), reads=["dtq"], writes=["dtE%d" % g2])
        dk = ["dtE0", "dtE1"]
        mag = psb([64, 128], F32); th = psb([64, 128], F32); cth = psb([64, 128], F32); sth = psb([64, 128], F32)
        S.op("dve", lambda e: e.tensor_tensor(mag[:], dtE[:], are[:], ALU.mult), reads=dk + ["are"], writes=["mag"])
        S.op("act", lambda e: e.activation(mag[:], mag[:], AF.Exp), reads=["mag"], writes=["mag"])
        S.op("dve", lambda e: e.tensor_tensor(th[:], dtE[:], aim[:], ALU.mult), reads=dk + ["aim"], writes=["th"])
        self.trig(pst, lambda: th[:], lambda: cth[:], lambda: sth[:], [64, 128], "th", "cth", "sth")
        ar = psb([64, 128], F32); ai = psb([64, 128], F32); nr = psb([64, 128], F32); den = psb([64, 128], F32)
        cr = psb([64, 128], F32); ci = psb([64, 128], F32); tmp = psb([64, 128], F32)
        S.op("dve", lambda e: e.tensor_tensor(ar[:], mag[:], cth[:], ALU.mult), reads=["mag", "cth"], writes=["ar"])
        S.op("dve", lambda e: e.tensor_tensor(ai[:], mag[:], sth[:], ALU.mult), reads=["mag", "sth"], writes=["ai"])
        S.op("dve", lambda e: e.tensor_scalar(nr[:], ar[:], -1.0, None, ALU.add), reads=["ar"], writes=["nr"])
        S.op("dve", lambda e: e.tensor_tensor(den[:], are[:], are[:], ALU.mult), reads=["are"], writes=["den"])
        S.op("dve", lambda e: e.tensor_tensor(tmp[:], aim[:], aim[:], ALU.mult), reads=["aim"], writes=["tmp"])
        S.op("dve", lambda e: e.tensor_tensor(den[:], den[:], tmp[:], ALU.add), reads=["den", "tmp"], writes=["den"])
        S.op("dve", lambda e: e.reciprocal(den[:], den[:]), reads=["den"], writes=["den"])
        S.op("dve", lambda e: e.tensor_tensor(cr[:], nr[:], are[:], ALU.mult), reads=["nr", "are"], writes=["cr"])
        S.op("dve", lambda e: e.tensor_tensor(tmp[:], ai[:], aim[:], ALU.mult), reads=["ai", "aim", "den"], writes=["tmp"])
        S.op("dve", lambda e: e.tensor_tensor(cr[:], cr[:], tmp[:], ALU.add), reads=["cr", "tmp"], writes=["cr"])
        S.op("dve", lambda e: e.tensor_tensor(cr[:], cr[:], den[:], ALU.mult), reads=["cr", "den"], writes=["cr"])
        S.op("dve", lambda e: e.tensor_tensor(ci[:], ai[:], are[:], ALU.mult), reads=["ai", "are"], writes=["ci"])
        S.op("dve", lambda e: e.tensor_tensor(tmp[:], nr[:], aim[:], ALU.mult), reads=["nr", "aim", "cr"], writes=["tmp"])
        S.op("dve", lambda e: e.tensor_tensor(ci[:], ci[:], tmp[:], ALU.subtract), reads=["ci", "tmp"], writes=["ci"])
        S.op("dve", lambda e: e.tensor_tensor(ci[:], ci[:], den[:], ALU.mult), reads=["ci", "den"], writes=["ci"])
        thT = psb([128, 64], F32); mT = psb([128, 64], F32); crT = psb([128, 64], F32); ciT = psb([128, 64], F32)
        ptr = pst.ps([128, 64], F32)
        for (src, sk, dst, dkk) in [(ar, "ar", arT, "arT"), (ai, "ai", aiT, "aiT"), (th, "th", thT, "thT"), (mag, "mag", mT, "mT"), (cr, "cr", crT, "crT"), (ci, "ci", ciT, "ciT")]:
            S.op("pe", lambda e, src=src: e.matmul(ptr[:], src[:], identf[:64, :64], start=True, stop=True), reads=[sk, "identf"], writes=["ptr"])
            S.op("act", lambda e, dst=dst: e.copy(dst[:], ptr[:]), reads=["ptr"], writes=[dkk])
        S.op("dve", lambda e: e.tensor_tensor(MT[:], thT[:].unsqueeze(2).to_broadcast([128, 64, 64]), tv[:].unsqueeze(1).to_broadcast([128, 64, 64]), ALU.mult), reads=["thT", "tv"], writes=["ANG"])
        tst = self.stage()
        self.trig(tst, lambda: MT[:], lambda: COS[:], lambda: SIN[:], [128, 64, 64], "ANG", "COS", "SIN")
        S.barrier(); S.emit(); tst.stack.close()
        S.op("dve", lambda e: e.tensor_copy(MT[:], mT[:].unsqueeze(2).to_broadcast([128, 64, 64])), reads=["mT", "COS", "SIN", "ANG"], writes=["MT"])
        S.op("dve", lambda e: e.memset(MT[:, :, 0:1], 0.0), reads=["MT"], writes=["MT"])
        S.op("dve", lambda e: e.tensor_copy(MT16[:], MT[:, :, 0:16]), reads=["MT"], writes=["MT"])
        Bn = [psb([128, 64, 16], F32, "Bn") for _ in range(2)]
        S.dma("sp", lambda e: e.dma_start(out=Bn[0][:], in_=self.sw["s5_b_re"][l].rearrange("(q g) p j -> (g p) q j", g=2)), writes=["Bn0"])
        S.dma("sp", lambda e: e.dma_start(out=Bn[1][:], in_=self.sw["s5_b_im"][l].rearrange("(q g) p j -> (g p) q j", g=2)), writes=["Bn1"])
        bb = [psb([128, 64, 16], F32, "bb") for _ in range(2)]
        t16 = psb([128, 64, 16], F32)
        bc = lambda t: t[:].unsqueeze(2).to_broadcast([128, 64, 16])
        S.op("dve", lambda e: e.tensor_tensor(bb[0][:], Bn[0][:], bc(crT), ALU.mult), reads=["Bn0", "crT"], writes=["bb0"])
        S.op("dve", lambda e: e.tensor_tensor(t16[:], Bn[1][:], bc(ciT), ALU.mult), reads=["Bn1", "ciT"], writes=["t16"])
        S.op("dve", lambda e: e.tensor_tensor(bb[0][:], bb[0][:], t16[:], ALU.subtract), reads=["bb0", "t16"], writes=["bb0"])
        S.op("dve", lambda e: e.tensor_tensor(bb[1][:], Bn[1][:], bc(crT), ALU.mult), reads=["Bn1", "crT"], writes=["bb1"])
        S.op("dve", lambda e: e.tensor_tensor(t16[:], Bn[0][:], bc(ciT), ALU.mult), reads=["Bn0", "ciT", "bb0"], writes=["t16"])
        S.op("dve", lambda e: e.tensor_tensor(bb[1][:], bb[1][:], t16[:], ALU.add), reads=["bb1", "t16"], writes=["bb1"])
        Z = psb([128, 64, 128], BF16)
        ptz = pst.ps([128, 8, 128], BF16)
        for ri in range(2):
            S.op("dve", lambda e: e.memset(Z[:], 0.0), reads=["Z"], writes=["Z"])
            Zv = Z[:].rearrange("p (c qq) (gl j) -> p c qq gl j", qq=4, j=16)
            bv = bb[ri][:].rearrange("p (c qq) j -> p c qq j", qq=4)
            for qq in range(4):
                for g2 in range(2):
                    S.op("dve", lambda e, qq=qq, g2=g2, Zv=Zv, bv=bv: e.tensor_copy(Zv[g2 * 64:(g2 + 1) * 64, :, qq, 2 * qq + g2, :], bv[g2 * 64:(g2 + 1) * 64, :, qq, :]),
                         reads=["bb%d" % ri, "Z"], writes=["Z"])
            for q8 in range(8):
                def trz(e, q8=q8):
                    for k in range(8):
                        ins = e.transpose(ptz[:, k, :], Z[:, q8 * 8 + k, :], identb[:])
                    return ins
                S.op("pe", trz, reads=["Z", "identb"], writes=["ptz"])
                S.op("act", lambda e, q8=q8, ri=ri: e.copy(LB[:, q8 * 8:(q8 + 1) * 8, ri, :], ptz[:]), reads=["ptz"], writes=["LB"])
        Cn = psb([64, 64, 64], F32); Y = psb([64, 64, 128], BF16)
        ptc = pst.ps([128, 8, 64], BF16)
        for ri, nm in enumerate(["s5_c_re", "s5_c_im"]):
            S.op("dve", lambda e: e.memset(Cn[:], 0.0), reads=["Cn"], writes=["Cn"])
            S.op("dve", lambda e: e.memset(Y[:], 0.0), reads=["Y"], writes=["Y"])
            cv = self.sw[nm][l].rearrange("(q g) i p -> g i q p", g=2)
            for g2 in range(2):
                S.dma("sp", lambda e, g2=g2, cv=cv: e.dma_start(out=Cn[g2 * 32:g2 * 32 + 16, :, :], in_=cv[g2]), reads=["Cn"], writes=["Cn"])
            for g2 in range(2):
                S.op("dve", lambda e, g2=g2, ri=ri: e.tensor_scalar(Y[g2 * 32:(g2 + 1) * 32, :, g2 * 64:(g2 + 1) * 64], Cn[g2 * 32:(g2 + 1) * 32, :, :], (1.0 if ri == 0 else -1.0), None, ALU.mult),
                     reads=["Cn", "Y"], writes=["Y"])
            for q8 in range(8):
                def trc(e, q8=q8):
                    for k in range(8):
                        ins = e.transpose(ptc[:, k, :], Y[:, q8 * 8 + k, :], identb[:64, :64])
                    return ins
                S.op("pe", trc, reads=["Y", "identb"], writes=["ptc"])
                S.op("act", lambda e, q8=q8, ri=ri: e.copy(CB[:, q8 * 8:(q8 + 1) * 8, ri, :].rearrange("p k (g i) -> p k g i", g=2),
                                                         ptc[:].rearrange("p k (g i) -> p k g i", g=2)[:, :, :, 0:16]), reads=["ptc"], writes=["CB"])
        S.barrier(); S.emit(); pst.stack.close()
        dT, dTk = self.load_row_bcast(st, self.sw["s5_d"][l:l + 1, :], D, 64)
        uTs = [sb([128, 16, 64], BF16) for _ in range(2)]; utoks = [sb([64, 2048], BF16) for _ in range(2)]
        Braw = sb([128, QH, 2, 64], F32); BR = sb([128, QH, 64], F32); BI = sb([128, QH, 64], F32); tmpb = sb([128, QH, 64], F32)
        XR = sb([128, QH, 64], BF16); XI = sb([128, QH, 64], BF16)
        xpr = sb([128, 64], F32); xpi = sb([128, 64], F32); fr = sb([128, 64], F32); fi = sb([128, 64], F32); f2 = sb([128, 64], F32)
        yt = sb([64, 2048], F32); yb = sb([64, 2048], BF16)
        xo = sb([64, 128], F32)
        pb_ = [ps([128, 4, 2, 64], F32) for _ in range(2)]
        py = ps([64, 2048], F32)
        pxo = ps([64, 128], F32)
        chs = self.chunks()

        def s5_loads(ci):
            (si_, r0_, n_, _, _) = chs[ci]
            ub_ = ci % 2
            for c in range(16):
                S.dma("sp", lambda e, c=c, r0_=r0_, n_=n_, ub_=ub_: e.dma_start(out=uTs[ub_][:, c, :n_], in_=self.pj_u[r0_:r0_ + n_, c * 128:(c + 1) * 128], transpose=True),
                      reads=[("dram", "pj_u")], writes=[("uT", ub_, c)])
            S.dma("act", lambda e, r0_=r0_, n_=n_, ub_=ub_: e.dma_start(out=utoks[ub_][:n_, :], in_=self.pj_u[r0_:r0_ + n_, :]), reads=[("dram", "pj_u")], writes=[("utok", ub_)])
        for ci, (si, r0, n, first, last) in enumerate(chs):
            slot = self.seqs[si][3]
            if first:
                if slot == 0:
                    S.op("dve", lambda e: e.memset(xpr[:], 0.0), writes=["xpr"])
                    S.op("dve", lambda e: e.memset(xpi[:], 0.0), writes=["xpi"])
                else:
                    for (srcst, dstt, kk) in [(self.st_s5r, xpr, "xpr"), (self.st_s5i, xpi, "xpi")]:
                        S.dma("sp", lambda e, srcst=srcst, slot=slot: e.dma_start(out=xo[:, :], in_=srcst[l, slot - 1].rearrange("(q g) p -> q (g p)", g=2)), reads=["xo"], writes=["xo"])
                        S.op("pe", lambda e: e.matmul(pb_[0][:, 0, 0, :], xo[:, :], identf[:64, :64], start=True, stop=True), reads=["xo", "identf"], writes=[("pb", 0)])
                        S.op("act", lambda e, dstt=dstt: e.copy(dstt[:], pb_[0][:, 0, 0, :]), reads=[("pb", 0)], writes=[kk])
            ub = ci % 2
            uT, utok = uTs[ub], utoks[ub]
            if ci == 0:
                s5_loads(0)
            if ci + 1 < len(chs):
                s5_loads(ci + 1)
            S.op("dve", lambda e: e.tensor_tensor(fr[:], arT[:], xpr[:], ALU.mult), reads=["arT", "xpr"], writes=["fr"])
            S.op("dve", lambda e: e.tensor_tensor(f2[:], aiT[:], xpi[:], ALU.mult), reads=["aiT", "xpi"], writes=["f2"])
            S.op("dve", lambda e: e.tensor_tensor(fr[:], fr[:], f2[:], ALU.subtract), reads=["fr", "f2"], writes=["fr"])
            S.op("dve", lambda e: e.tensor_tensor(fi[:], arT[:], xpi[:], ALU.mult), reads=["arT", "xpi"], writes=["fi"])
            S.op("dve", lambda e: e.tensor_tensor(f2[:], aiT[:], xpr[:], ALU.mult), reads=["aiT", "xpr", "fr"], writes=["f2"])
            S.op("dve", lambda e: e.tensor_tensor(fi[:], fi[:], f2[:], ALU.add), reads=["fi", "f2"], writes=["fi"])
            V = lambda t, n=n: t[:].rearrange("p q t -> p (q t)")[:, :QH * n].rearrange("p (q t) -> p q t", t=n)
            F2 = lambda t, n=n: t[:].rearrange("p q t -> p (q t)")[:, :QH * n]
            BRv, BIv, tmv = V(BR), V(BI), V(tmpb)
            for hf in range(2):
                q0 = hf * QH
                for c8 in range(8):
                    c = hf * 8 + c8
                    pbi = c % 2

                    def bm(e, c=c, pbi=pbi, n=n, uT=uT):
                        for qq in range(4):
                            for ri in range(2):
                                ins = e.matmul(pb_[pbi][:, qq, ri, :n], LB[:, 4 * c + qq, ri, :], uT[:, c, :n], start=True, stop=True)
                        return ins
                    S.op("pe", bm, reads=["LB", ("uT", ub, c)], writes=[("pb", pbi)])
                    S.op("act", lambda e, c8=c8, pbi=pbi, n=n: e.copy(Braw[:, 4 * c8:4 * c8 + 4, :, :n], pb_[pbi][:, :, :, :n]), reads=[("pb", pbi)], writes=["Braw"])
                cosv = COS[:, q0:q0 + QH, :n]; sinv = SIN[:, q0:q0 + QH, :n]; mtv = MT[:, q0:q0 + QH, :n]
                br_, bi_ = Braw[:, :, 0, :n], Braw[:, :, 1, :n]
                S.op("dve", lambda e, cosv=cosv, br_=br_, n=n, BRv=BRv, BIv=BIv, tmv=tmv: e.tensor_tensor(BRv, br_, cosv, ALU.mult), reads=["Braw", "COS"], writes=["BR"])
                S.op("pool", lambda e, sinv=sinv, bi_=bi_, n=n, BRv=BRv, BIv=BIv, tmv=tmv: e.tensor_tensor(tmv, bi_, sinv, ALU.mult), reads=["Braw", "SIN"], writes=["tmpb"])
                S.op("dve", lambda e, n=n, BRv=BRv, BIv=BIv, tmv=tmv: e.tensor_tensor(BRv, BRv, tmv, ALU.add), reads=["BR", "tmpb"], writes=["BR"])
                S.op("dve", lambda e, cosv=cosv, bi_=bi_, n=n, BRv=BRv, BIv=BIv, tmv=tmv: e.tensor_tensor(BIv, bi_, cosv, ALU.mult), reads=["Braw", "COS"], writes=["BI"])
                S.op("pool", lambda e, sinv=sinv, br_=br_, n=n, BRv=BRv, BIv=BIv, tmv=tmv: e.tensor_tensor(tmv, br_, sinv, ALU.mult), reads=["Braw", "SIN", "BR"], writes=["tmpb"])
                S.op("dve", lambda e, n=n, BRv=BRv, BIv=BIv, tmv=tmv: e.tensor_tensor(BIv, BIv, tmv, ALU.subtract), reads=["BI", "tmpb"], writes=["BI"])
                S.op("dve", lambda e, q0=q0, BRv=BRv: e.tensor_tensor(BRv[:, :, 0], BRv[:, :, 0], fr[:, q0:q0 + QH], ALU.add), reads=["BR", "fr"], writes=["BR"])
                S.op("dve", lambda e, q0=q0, BIv=BIv: e.tensor_tensor(BIv[:, :, 0], BIv[:, :, 0], fi[:, q0:q0 + QH], ALU.add), reads=["BI", "fi"], writes=["BI"])
                mt2 = (MT if n == 64 else MT16)[:, q0:q0 + QH, :].rearrange("p q t -> p (q t)")
                S.op("dve", lambda e, mt2=mt2, F2=F2: e.tensor_tensor_scan(F2(BR), mt2, F2(BR), 0.0, ALU.mult, ALU.add), reads=["BR", "MT"], writes=["BR"])
                S.op("dve", lambda e, mt2=mt2, F2=F2: e.tensor_tensor_scan(F2(BI), mt2, F2(BI), 0.0, ALU.mult, ALU.add), reads=["BI", "MT"], writes=["BI"])
                TA, TB = Braw[:, :, 0, :n], Braw[:, :, 1, :n]
                S.op("dve", lambda e, TA=TA, cosv=cosv, n=n, BRv=BRv: e.tensor_tensor(TA, BRv, cosv, ALU.mult), reads=["BR", "COS", "Braw"], writes=["TA"])
                S.op("pool", lambda e, TB=TB, sinv=sinv, n=n, BIv=BIv: e.tensor_tensor(TB, BIv, sinv, ALU.mult), reads=["BI", "SIN", "Braw"], writes=["TB"])
                S.op("dve", lambda e, TA=TA, TB=TB, n=n: e.tensor_tensor(XR[:, :, :n], TA, TB, ALU.subtract), reads=["TA", "TB"], writes=["XR"])
                S.op("dve", lambda e, TA=TA, TB=TB, q0=q0, n=n: e.tensor_tensor(xpr[:, q0:q0 + QH], TA[:, :, n - 1], TB[:, :, n - 1], ALU.subtract), reads=["TA", "TB"], writes=["xpr"])
                S.op("dve", lambda e, TA=TA, sinv=sinv, n=n, BRv=BRv: e.tensor_tensor(TA, BRv, sinv, ALU.mult), reads=["BR", "SIN", "XR", "xpr"], writes=["TA"])
                S.op("pool", lambda e, TB=TB, cosv=cosv, n=n, BIv=BIv: e.tensor_tensor(TB, BIv, cosv, ALU.mult), reads=["BI", "COS", "XR", "xpr"], writes=["TB"])
                S.op("dve", lambda e, TA=TA, TB=TB, n=n: e.tensor_tensor(XI[:, :, :n], TA, TB, ALU.add), reads=["TA", "TB"], writes=["XI"])
                S.op("dve", lambda e, TA=TA, TB=TB, q0=q0, n=n: e.tensor_tensor(xpi[:, q0:q0 + QH], TA[:, :, n - 1], TB[:, :, n - 1], ALU.add), reads=["TA", "TB"], writes=["xpi"])

                def cm(e, q0=q0, n=n):
                    for q in range(QH):
                        e.matmul(py[:n, (q0 + q) * 32:(q0 + q + 1) * 32], XR[:, q, :n], CB[:, q0 + q, 0, :], start=True, stop=False)
                        ins = e.matmul(py[:n, (q0 + q) * 32:(q0 + q + 1) * 32], XI[:, q, :n], CB[:, q0 + q, 1, :], start=False, stop=True)
                    return ins
                S.op("pe", cm, reads=["XR", "XI", "CB"], writes=["py"])
                S.op("dve", lambda e: e.tensor_copy(Braw[:, 0, 0, 0:1], Braw[:, 0, 0, 0:1]), reads=[], writes=["Braw", "TA", "TB"])
            S.op("dve", lambda e, n=n, utok=utok: e.tensor_tensor(yt[:n, :], utok[:n, :], dT[:n, :], ALU.mult), reads=[("utok", ub), dTk], writes=["yt"])
            S.op("dve", lambda e, n=n: e.tensor_tensor(yt[:n, :], yt[:n, :], py[:n, :], ALU.add), reads=["yt", "py"], writes=["yt"])
            S.op("act", lambda e, n=n: e.activation(yb[:n, :], yt[:n, :], AF.Gelu), reads=["yt"], writes=["yb"])
            S.dma("sp", lambda e, r0=r0, n=n: e.dma_start(out=self.s5_y[r0:r0 + n, :], in_=yb[:n, :]), reads=["yb"], writes=[("dram", "s5_y")])
            if last:
                for (srct, dsto, kk) in [(xpr, self.o_s5r, "xpr"), (xpi, self.o_s5i, "xpi")]:
                    S.op("pe", lambda e, srct=srct: e.matmul(pxo[:], srct[:], identf[:], start=True, stop=True), reads=[kk, "identf"], writes=["pxo"])
                    S.op("act", lambda e: e.copy(xo[:], pxo[:]), reads=["pxo"], writes=["xo"])
                    S.dma("sp", lambda e, dsto=dsto, slot=slot: e.dma_start(out=dsto[l, slot].rearrange("(q g) p -> q (g p)", g=2), in_=xo[:]), reads=["xo"], writes=[("dram", dsto.name)])
        st.done()


    def ssd(self, l):
        S, nc = self.S, self.nc
        st = self.stage()
        sb, ps = st.sb, st.ps
        identf = sb([128, 128], F32); identb = sb([128, 128], BF16); tri = sb([64, 64], F32); ones = sb([64, 64], F32)
        sel = {64: sb([64, 128], F32), 16: sb([64, 128], F32)}
        S.dma("sp", lambda e: e.dma_start(out=identf[:], in_=self.cst["ident_in"]), writes=["identf"])
        S.op("dve", lambda e: e.tensor_copy(identb[:], identf[:]), reads=["identf"], writes=["identb"])
        S.dma("sp", lambda e: e.dma_start(out=tri[:], in_=self.cst["tri_in"]), writes=["tri"])
        S.op("dve", lambda e: e.memset(ones[:], 1.0), writes=["ones"])
        for LL in (64, 16):
            S.op("dve", lambda e, LL=LL: e.tensor_copy(sel[LL][:, :], identf[0:64, LL - 1:LL].to_broadcast([64, 128])), reads=["identf"], writes=["sel%d" % LL])
        cw = sb([128, 48, 4], F32); cb = sb([128, 48], F32)
        for w in range(4):
            S.dma("sp", lambda e, w=w: e.dma_start(out=cw[:, :, w], in_=self.sw["ssd_conv_w"][l, w].rearrange("(c p) -> p c", p=128), allow_slow_non_contiguous=True), writes=["cw%d" % w])
        S.dma("sp", lambda e: e.dma_start(out=cb[:], in_=self.sw["ssd_conv_b"][l].rearrange("(c p) -> p c", p=128), allow_slow_non_contiguous=True), writes=["cb"])
        cwk = ["cw%d" % w for w in range(4)]
        dtb, dtbk = self.load_row_bcast(st, self.sw["ssd_dt_bias"][l:l + 1, :], 64, 64)
        aneg, alk = self.load_row_bcast(st, self.sw["ssd_a_log"][l:l + 1, :], 64, 64)
        dsk, dskk = self.load_row_bcast(st, self.sw["ssd_d"][l:l + 1, :], 64, 64)
        ng, ngk = self.load_row_bcast(st, self.sw["ssd_norm"][l:l + 1, :], 4096, 64)
        S.op("act", lambda e: e.activation(aneg[:], aneg[:], AF.Exp), reads=[alk], writes=[alk])
        S.op("dve", lambda e: e.tensor_scalar(aneg[:], aneg[:], -1.0, None, ALU.mult), reads=[alk], writes=[alk])
        xTs = [sb([128, 48, 80], BF16) for _ in range(2)]; acc = sb([128, 48, 64], F32); ctmp = sb([128, 48, 64], F32); xcT = sb([128, 48, 64], BF16)
        xtok = sb([64, 4096], BF16); btok = sb([64, 1024], BF16); zt = sb([64, 4096], BF16)
        dtrs = [sb([64, 64], F32) for _ in range(2)]; dtx = sb([64, 64], F32); dta_ = sb([64, 64], F32); t64 = sb([64, 64], F32); dt = sb([64, 64], F32)
        cs_col = sb([64, 64], F32); ecs = sb([64, 64], F32); wdec = sb([64, 64], F32); csl = sb([128, 64], F32); ecl = sb([128, 64], F32)
        Rgs = [sb([64, 8, 64], F32) for _ in range(2)]; d1s = [sb([64, 8, 64], F32) for _ in range(2)]; cbm = sb([64, 8, 64], F32); M = sb([64, 64, 64], BF16)
        xdt = sb([64, 4096], BF16); xdtw = sb([64, 4096], BF16)
        hT = sb([128, 4096], F32); hTb = sb([128, 4096], BF16)
        yv = sb([64, 4096], F32); ytmp = sb([64, 512], F32); sz = sb([64, 4096], BF16); ssq = sb([64, 8], F32); yb = sb([64, 4096], BF16)
        hio = sb([128, 4, 128], F32)
        p_tr = ps([64, 2048], BF16); p_small = ps([128, 64], F32); p_cs = ps([64, 8, 64], F32); p_cb = ps([64, 8, 64], F32)
        py = ps([64, 512], F32); pys = ps([64, 512], F32); ph = ps([128, 512], F32)
        v3 = lambda t, L: t[:L, :].rearrange("p (h q) -> p h q", q=64)
        chs = self.chunks()

        def ssd_loads(ci):
            (si_, r0_, L_, _, _) = chs[ci]
            r00_, _, p0_, _ = self.seqs[si_]
            prow_ = p0_ + 3 + (r0_ - r00_)
            NR_ = L_ + 16
            xb_ = ci % 2
            for c in range(48):
                S.dma("sp", lambda e, c=c, prow_=prow_, NR_=NR_, xb_=xb_: e.dma_start(out=xTs[xb_][:, c, :NR_], in_=self.xpad[prow_ - 16:prow_ - 16 + NR_, c * 128:(c + 1) * 128], transpose=True),
                      reads=[("dram", "xpad")], writes=[("xT", xb_, c)])
            S.dma("act", lambda e, r0_=r0_, L_=L_, xb_=xb_: e.dma_start(out=dtrs[xb_][:L_, :], in_=self.dtf[r0_:r0_ + L_, :]), reads=[("dram", "dtf")], writes=[("dtr", xb_)])
        for ci, (si, r0, L, first, last) in enumerate(chs):
            r00, ln, p0, slot = self.seqs[si]
            prow = p0 + 3 + (r0 - r00)
            NR = L + 16
            if first:
                if slot == 0:
                    S.op("dve", lambda e: e.memset(hT[:], 0.0), writes=["hT"])
                else:
                    hv = self.st_ssd[l, slot - 1].rearrange("(k q) p n -> (q p) k n", q=2)
                    for k4 in range(8):
                        S.dma("sp", lambda e, k4=k4, hv=hv: e.dma_start(out=hio[:], in_=hv[:, 4 * k4:4 * k4 + 4, :]), reads=["hio"], writes=["hio"])

                        def trh(e):
                            for k in range(4):
                                ins = e.matmul(ph[:, k * 128:(k + 1) * 128], hio[:, k, :], identf[:], start=True, stop=True)
                            return ins
                        S.op("pe", trh, reads=["hio", "identf"], writes=["ph"])
                        S.op("act", lambda e, k4=k4: e.copy(hT[:, k4 * 512:(k4 + 1) * 512], ph[:]), reads=["ph"], writes=["hT"])
                S.op("act", lambda e: e.copy(hTb[:], hT[:]), reads=["hT"], writes=["hTb"])
            xb = ci % 2
            xT, dtr = xTs[xb], dtrs[xb]
            if ci == 0:
                ssd_loads(0)
            S.dma("act", lambda e, r0=r0, L=L: e.dma_start(out=zt[:L, :], in_=self.pj_z[r0:r0 + L, :]), reads=[("dram", "pj_z")], writes=["zt"])
            if ci + 1 < len(chs):
                ssd_loads(ci + 1)
            xk = [("xT", xb, c) for c in range(48)]
            bw = lambda w, L=L: cw[:, :, w:w + 1].to_broadcast([128, 48, L])
            S.op("dve", lambda e, L=L, bw=bw, xT=xT: e.tensor_tensor(acc[:, :, :L], xT[:, :, 13:13 + L], bw(0), ALU.mult), reads=xk + cwk, writes=["acc"])
            for w in range(1, 4):
                S.op("pool", lambda e, L=L, bw=bw, w=w, xT=xT: e.tensor_tensor(ctmp[:, :, :L], xT[:, :, 13 + w:13 + w + L], bw(w), ALU.mult), reads=xk + cwk, writes=["ctmp"])
                S.op("dve", lambda e, L=L: e.tensor_tensor(acc[:, :, :L], acc[:, :, :L], ctmp[:, :, :L], ALU.add), reads=["acc", "ctmp"], writes=["acc"])
            S.op("dve", lambda e, L=L: e.tensor_tensor(acc[:, :, :L], acc[:, :, :L], cb[:].unsqueeze(2).to_broadcast([128, 48, L]), ALU.add), reads=["acc", "cb"], writes=["acc"])
            S.op("act", lambda e, L=L: e.activation(xcT[:, :, :L], acc[:, :, :L], AF.Silu), reads=["acc"], writes=["xcT"])
            for rnd in range(2):
                def trx(e, rnd=rnd, L=L):
                    for k in range(16):
                        ins = e.transpose(p_tr[:L, k * 128:(k + 1) * 128], xcT[:, rnd * 16 + k, :L], identb[:])
                    return ins
                S.op("pe", trx, reads=["xcT", "identb"], writes=["p_tr"])
                S.op("act", lambda e, rnd=rnd, L=L: e.copy(xtok[:L, rnd * 2048:(rnd + 1) * 2048], p_tr[:L, :]), reads=["p_tr"], writes=[("xtok", rnd)])

            def trb(e, L=L):
                for k in range(8):
                    ins = e.transpose(p_tr[:L, k * 128:(k + 1) * 128], xcT[:, 32 + k, :L], identb[:])
                return ins
            S.op("pe", trb, reads=["xcT", "identb"], writes=["p_tr"])
            S.op("act", lambda e, L=L: e.copy(btok[:L, :], p_tr[:L, 0:1024]), reads=["p_tr"], writes=["btok"])
            xtk = [("xtok", 0), ("xtok", 1)]
            S.op("dve", lambda e, L=L, dtr=dtr: e.tensor_tensor(dtx[:L, :], dtr[:L, :], dtb[:L, :], ALU.add), reads=[("dtr", xb), dtbk], writes=["dtx"])
            S.op("act", lambda e, L=L: e.activation(t64[:L, :], dtx[:L, :], AF.Abs), reads=["dtx"], writes=["t64"])
            S.op("act", lambda e, L=L: e.activation(t64[:L, :], t64[:L, :], AF.Exp, scale=-1.0), reads=["t64"], writes=["t64"])
            S.op("act", lambda e, L=L: e.activation(t64[:L, :], t64[:L, :], AF.Ln, bias=1.0), reads=["t64"], writes=["t64"])
            S.op("dve", lambda e, L=L: e.tensor_scalar(dt[:L, :], dtx[:L, :], 0.0, None, ALU.max), reads=["dtx"], writes=["dt"])
            S.op("dve", lambda e, L=L: e.tensor_tensor(dt[:L, :], dt[:L, :], t64[:L, :], ALU.add), reads=["dt", "t64"], writes=["dt"])
            S.op("dve", lambda e, L=L: e.tensor_tensor(dta_[:L, :], dt[:L, :], aneg[:L, :], ALU.mult), reads=["dt", alk], writes=["dta"])
            S.op("pe", lambda e, L=L: e.matmul(p_small[:L, :], tri[:L, :L], dta_[:L, :], start=True, stop=True), reads=["tri", "dta"], writes=["p_small"])
            S.op("act", lambda e, L=L: e.copy(cs_col[:L, :], p_small[:L, :]), reads=["p_small"], writes=["cs_col"])
            S.op("act", lambda e, L=L: e.activation(ecs[:L, :], cs_col[:L, :], AF.Exp), reads=["cs_col"], writes=["ecs"])
            S.op("pe", lambda e, L=L: e.matmul(p_small[:, :], sel[L][:L, :], cs_col[:L, :], start=True, stop=True), reads=["sel%d" % L, "cs_col"], writes=["p_small"])
            S.op("act", lambda e: e.copy(csl[:], p_small[:]), reads=["p_small"], writes=["csl"])
            S.op("act", lambda e: e.activation(ecl[:], csl[:], AF.Exp), reads=["csl"], writes=["ecl"])
            S.op("dve", lambda e, L=L: e.tensor_tensor(wdec[:L, :], csl[:L, :], cs_col[:L, :], ALU.subtract), reads=["csl", "cs_col"], writes=["wdec"])
            S.op("act", lambda e, L=L: e.activation(wdec[:L, :], wdec[:L, :], AF.Exp), reads=["wdec"], writes=["wdec"])
            def cbf(e, L=L):
                for g in range(8):
                    ins = e.matmul(p_cb[:L, g, :L], xcT[:, 32 + g, :L], xcT[:, 40 + g, :L], start=True, stop=True)
                return ins
            S.op("pe", cbf, reads=["xcT"], writes=["p_cb"])
            S.op("dve", lambda e, L=L: e.tensor_tensor(cbm[:L, :, :L], p_cb[:L, :, :L], tri[:L, :L].unsqueeze(1).to_broadcast([L, 8, L]), ALU.mult), reads=["p_cb", "tri"], writes=["cbm"])
            S.op("dve", lambda e, L=L: e.tensor_tensor(v3(xdt, L), v3(xtok, L), dt[:L, :].unsqueeze(2).to_broadcast([L, 64, 64]), ALU.mult), reads=xtk + ["dt"], writes=["xdt"])
            S.op("pool", lambda e, L=L: e.tensor_tensor(v3(xdtw, L), v3(xdt, L), wdec[:L, :].unsqueeze(2).to_broadcast([L, 64, 64]), ALU.mult), reads=["xdt", "wdec"], writes=["xdtw"])
            for g in range(8):
                hs = slice(8 * g, 8 * g + 8)
                Rg, d1 = Rgs[g % 2], d1s[g % 2]
                kR, kd = ("Rg", g % 2), ("d1", g % 2)
                S.op("dve", lambda e, L=L, hs=hs, Rg=Rg: e.tensor_tensor(Rg[:L, :, :L], dta_[:L, hs].unsqueeze(2).to_broadcast([L, 8, L]), tri[:L, :L].unsqueeze(1).to_broadcast([L, 8, L]), ALU.mult),
                     reads=["dta", "tri"], writes=[kR])
                S.op("pe", lambda e, L=L, Rg=Rg: e.matmul(p_cs[:L, :, :L], ones[:L, :L], Rg[:L, :, :L], start=True, stop=True), reads=["ones", kR], writes=["p_cs"])
                S.op("dve", lambda e, L=L, hs=hs, d1=d1: e.tensor_tensor(d1[:L, :, :L], p_cs[:L, :, :L], cs_col[:L, hs].unsqueeze(2).to_broadcast([L, 8, L]), ALU.subtract), reads=["p_cs", "cs_col"], writes=[kd])
                S.op("dve", lambda e, L=L, d1=d1: e.tensor_scalar(d1[:L, :, :L], d1[:L, :, :L], 0.0, None, ALU.min), reads=[kd], writes=[kd])
                S.op("act", lambda e, L=L, d1=d1: e.activation(d1[:L, :, :L], d1[:L, :, :L], AF.Exp), reads=[kd], writes=[kd])
                S.op("dve", lambda e, L=L, g=g, hs=hs, d1=d1: e.tensor_tensor(M[:L, hs, :L], d1[:L, :, :L], cbm[:L, g:g + 1, :L].to_broadcast([L, 8, L]), ALU.mult), reads=[kd, "cbm"], writes=[("M", g)])

                def ym(e, L=L, g=g):
                    for r in range(8):
                        h = 8 * g + r
                        ins = e.matmul(py[:L, r * 64:(r + 1) * 64], M[:L, h, :L], xdt[:L, h * 64:(h + 1) * 64], start=True, stop=True)
                    return ins
                S.op("pe", ym, reads=[("M", g), "xdt"], writes=["py"])
                S.op("pe", lambda e, L=L, g=g: e.matmul(pys[:L, :], xcT[:, 40 + g, :L], hTb[:, g * 512:(g + 1) * 512], start=True, stop=True), reads=["xcT", "hTb"], writes=["pys"])
                S.op("dve", lambda e, L=L, hs=hs: e.tensor_tensor(ytmp[:L, :].rearrange("p (r q) -> p r q", q=64), pys[:L, :].rearrange("p (r q) -> p r q", q=64),
                                                              ecs[:L, hs].unsqueeze(2).to_broadcast([L, 8, 64]), ALU.mult), reads=["pys", "ecs"], writes=["ytmp"])
                S.op("dve", lambda e, L=L, g=g: e.tensor_tensor(yv[:L, g * 512:(g + 1) * 512], ytmp[:L, :], py[:L, :], ALU.add), reads=["ytmp", "py"], writes=[("yv", g)])
                S.op("pe", lambda e, L=L, g=g: e.matmul(ph[:, :], btok[:L, g * 128:(g + 1) * 128], xdtw[:L, g * 512:(g + 1) * 512], start=True, stop=True), reads=["btok", "xdtw"], writes=["ph"])
                hg = hT[:, g * 512:(g + 1) * 512]
                S.op("pool", lambda e, hg=hg, hs=hs: e.tensor_tensor(hg.rearrange("p (r q) -> p r q", q=64), hg.rearrange("p (r q) -> p r q", q=64),
                                                                    ecl[:, hs].unsqueeze(2).to_broadcast([128, 8, 64]), ALU.mult), reads=["hT", "ecl", "hTb"], writes=["hT"])
                S.op("dve", lambda e, hg=hg: e.tensor_tensor(hg, hg, ph[:, :], ALU.add), reads=["hT", "ph"], writes=["hT"])
            S.op("act", lambda e: e.copy(hTb[:], hT[:]), reads=["hT", "pys"], writes=["hTb"])
            yk = [("yv", g) for g in range(8)]
            S.op("pool", lambda e, L=L: e.tensor_tensor(v3(xdtw, L), v3(xtok, L), dsk[:L, :].unsqueeze(2).to_broadcast([L, 64, 64]), ALU.mult), reads=xtk + [dskk, "xdtw", "ph"], writes=["xdtw"])
            S.op("dve", lambda e, L=L: e.tensor_tensor(yv[:L, :], yv[:L, :], xdtw[:L, :], ALU.add), reads=yk + ["xdtw"], writes=["yv"])
            S.op("act", lambda e, L=L: e.activation(sz[:L, :], zt[:L, :], AF.Silu), reads=["zt"], writes=["sz"])
            S.op("dve", lambda e, L=L: e.tensor_tensor(yv[:L, :], yv[:L, :], sz[:L, :], ALU.mult), reads=["yv", "sz"], writes=["yv"])
            S.op("act", lambda e, L=L: e.activation(xdt[:L, :], yv[:L, :], AF.Square), reads=["yv", "xdt", "py"], writes=["xdt"])
            S.op("dve", lambda e, L=L: e.tensor_reduce(ssq[:L, :], xdt[:L, :].rearrange("p (g q) -> p g q", g=8), AX.X, ALU.add), reads=["xdt"], writes=["ssq"])
            S.op("act", lambda e, L=L: e.activation(ssq[:L, :], ssq[:L, :], AF.Sqrt, bias=EPS, scale=1.0 / 512), reads=["ssq"], writes=["ssq"])
            S.op("dve", lambda e, L=L: e.reciprocal(ssq[:L, :], ssq[:L, :]), reads=["ssq"], writes=["ssq"])
            S.op("dve", lambda e, L=L: e.tensor_tensor(yv[:L, :].rearrange("p (g q) -> p g q", g=8), yv[:L, :].rearrange("p (g q) -> p g q", g=8),
                                                     ssq[:L, :].unsqueeze(2).to_broadcast([L, 8, 512]), ALU.mult), reads=["yv", "ssq"], writes=["yv"])
            S.op("dve", lambda e, L=L: e.tensor_tensor(yb[:L, :], yv[:L, :], ng[:L, :], ALU.mult), reads=["yv", ngk], writes=["yb"])
            S.dma("sp", lambda e, r0=r0, L=L: e.dma_start(out=self.ssd_y[r0:r0 + L, :], in_=yb[:L, :]), reads=["yb"], writes=[("dram", "ssd_y")])
            if last:
                ov = self.o_ssd[l, slot].rearrange("(k q) p n -> (q p) k n", q=2)
                for k4 in range(8):
                    def tro(e, k4=k4):
                        for k in range(4):
                            ins = e.matmul(ph[:, k * 128:(k + 1) * 128], hT[:, (4 * k4 + k) * 128:(4 * k4 + k + 1) * 128], identf[:], start=True, stop=True)
                        return ins
                    S.op("pe", tro, reads=["hT", "identf"], writes=["ph"])
                    S.op("act", lambda e: e.copy(hio[:].rearrange("p k n -> p (k n)"), ph[:]), reads=["ph", "hio"], writes=["hio"])
                    S.dma("sp", lambda e, k4=k4, ov=ov: e.dma_start(out=ov[:, 4 * k4:4 * k4 + 4, :], in_=hio[:]), reads=["hio"], writes=[("dram", "o_ssd")])
        st.done()

    def merge(self, l):
        S = self.S
        st = self.stage()
        sb = st.sb
        rt = sb([128, D], F32); gl = sb([128, 2 * D], F32); sd = sb([128, D], F32); gb = sb([128, 3 * D], BF16)
        sg = sb([128, 3 * D], F32); tmp = sb([128, D], F32); ob = sb([128, D], BF16)
        for (r, n) in self.tiles():
            S.dma("sp", lambda e, r=r, n=n: e.dma_start(out=rt[:n, :], in_=self.br_ret[r:r + n, :]), reads=[("dram", "br_ret")], writes=["rt"])
            S.dma("act", lambda e, r=r, n=n: e.dma_start(out=gl[:n, :], in_=self.br_glu[r:r + n, :]), reads=[("dram", "br_glu")], writes=["gl"])
            S.dma("sp", lambda e, r=r, n=n: e.dma_start(out=sd[:n, :], in_=self.br_ssd[r:r + n, :]), reads=[("dram", "br_ssd")], writes=["sd"])
            S.dma("act", lambda e, r=r, n=n: e.dma_start(out=gb[:n, :], in_=self.pj_gate[r:r + n, :]), reads=[("dram", "pj_gate")], writes=["gb"])
            S.op("act", lambda e, n=n: e.activation(sg[:n, :], gb[:n, :], AF.Sigmoid), reads=["gb"], writes=["sg"])
            S.op("act", lambda e, n=n: e.activation(tmp[:n, :], gl[:n, D:2 * D], AF.Sigmoid), reads=["gl"], writes=["tmp"])
            S.op("dve", lambda e, n=n: e.tensor_tensor(tmp[:n, :], tmp[:n, :], gl[:n, 0:D], ALU.mult), reads=["tmp", "gl"], writes=["tmp"])
            S.op("dve", lambda e, n=n: e.tensor_tensor(tmp[:n, :], tmp[:n, :], sg[:n, D:2 * D], ALU.mult), reads=["tmp", "sg"], writes=["tmp"])
            S.op("pool", lambda e, n=n: e.tensor_tensor(rt[:n, :], rt[:n, :], sg[:n, 0:D], ALU.mult), reads=["rt", "sg"], writes=["rt"])
            S.op("pool", lambda e, n=n: e.tensor_tensor(sd[:n, :], sd[:n, :], sg[:n, 2 * D:3 * D], ALU.mult), reads=["sd", "sg"], writes=["sd"])
            S.op("dve", lambda e, n=n: e.tensor_tensor(tmp[:n, :], tmp[:n, :], rt[:n, :], ALU.add), reads=["tmp", "rt"], writes=["tmp"])
            S.op("dve", lambda e, n=n: e.tensor_tensor(ob[:n, :], tmp[:n, :], sd[:n, :], ALU.add), reads=["tmp", "sd"], writes=["ob"])
            S.dma("sp", lambda e, r=r, n=n: e.dma_start(out=self.merged[r:r + n, :], in_=ob[:n, :]), reads=["ob"], writes=[("dram", "merged")])
        st.done()

    def normadd(self, ysrc, g_post, g_next, final=False):
        S = self.S
        st = self.stage()
        sb = st.sb
        gp, gpk = self.load_row_bcast(st, g_post, D)
        if g_next is not None:
            gn, gnk = self.load_row_bcast(st, g_next, D)
        yts = [sb([128, D], F32) for _ in range(2)]; xts = [sb([128, D], F32) for _ in range(2)]
        junk = sb([128, D], BF16); hbs = [sb([128, D], BF16) for _ in range(2)]
        ss = sb([128, 1], F32); ss2 = sb([128, 1], F32)
        tl = self.tiles()

        def loads(i):
            r, n = tl[i]
            b = i % 2
            S.dma("sp", lambda e, r=r, n=n, b=b: e.dma_start(out=yts[b][:n, :], in_=ysrc[r:r + n, :]), reads=[("dram", ysrc.name)], writes=[("yt", b)])
            S.dma("act", lambda e, r=r, n=n, b=b: e.dma_start(out=xts[b][:n, :], in_=self.xres[r:r + n, :]), reads=[("dram", "xres", r)], writes=[("xt", b)])
        loads(0)
        for i, (r, n) in enumerate(tl):
            b = i % 2
            yt, xt, hb = yts[b], xts[b], hbs[b]
            if i + 1 < len(tl):
                loads(i + 1)
            S.op("dve", lambda e, n=n: e.memset(ss[:n, :], 0.0), writes=["ss"])
            S.op("act", lambda e, n=n, yt=yt: e.activation(junk[:n, :], yt[:n, :], AF.Square, accum_out=ss[:n, :]), reads=[("yt", b), "ss"], writes=["junk", "ss"])
            S.op("act", lambda e, n=n: e.activation(ss[:n, :], ss[:n, :], AF.Sqrt, bias=EPS, scale=1.0 / D), reads=["ss"], writes=["ss"])
            S.op("dve", lambda e, n=n: e.reciprocal(ss[:n, :], ss[:n, :]), reads=["ss"], writes=["ss"])
            S.op("dve", lambda e, n=n, yt=yt: e.scalar_tensor_tensor(yt[:n, :], yt[:n, :], ss[:n, 0:1], gp[:n, :], ALU.mult, ALU.mult), reads=[("yt", b), "ss", gpk], writes=[("yt", b)])
            S.op("dve", lambda e, n=n, yt=yt, xt=xt: e.tensor_tensor(xt[:n, :], xt[:n, :], yt[:n, :], ALU.add), reads=[("xt", b), ("yt", b)], writes=[("xt", b)])
            S.dma("sp", lambda e, r=r, n=n, xt=xt: e.dma_start(out=self.xres[r:r + n, :], in_=xt[:n, :]), reads=[("xt", b)], writes=[("dram", "xres", r)])
            if final:
                S.dma("sp", lambda e, r=r, n=n, xt=xt: e.dma_start(out=self.y[r:r + n, :], in_=xt[:n, :]), reads=[("xt", b)], writes=[("dram", "y")])
            if g_next is not None:
                S.op("dve", lambda e, n=n: e.memset(ss2[:n, :], 0.0), writes=["ss2"])
                S.op("act", lambda e, n=n, xt=xt: e.activation(junk[:n, :], xt[:n, :], AF.Square, accum_out=ss2[:n, :]), reads=[("xt", b), "junk", "ss2"], writes=["junk", "ss2"])
                S.op("act", lambda e, n=n: e.activation(ss2[:n, :], ss2[:n, :], AF.Sqrt, bias=EPS, scale=1.0 / D), reads=["ss2"], writes=["ss2"])
                S.op("dve", lambda e, n=n: e.reciprocal(ss2[:n, :], ss2[:n, :]), reads=["ss2"], writes=["ss2"])
                S.op("dve", lambda e, n=n, xt=xt, hb=hb: e.scalar_tensor_tensor(hb[:n, :], xt[:n, :], ss2[:n, 0:1], gn[:n, :], ALU.mult, ALU.mult), reads=[("xt", b), "ss2", gnk], writes=[("hb", b)])
                S.dma("sp", lambda e, r=r, n=n, hb=hb: e.dma_start(out=self.hn[r:r + n, :], in_=hb[:n, :]), reads=[("hb", b)], writes=[("dram", "hn")])
        S.op("dve", lambda e: e.memset(ss[:1, :], 0.0), reads=[("dram", "xres", r) for (r, n) in tl], writes=[("dram", "xres"), "ss"])
        st.done()

    def attention(self):
        S = self.S
        st = self.stage()
        sb, ps = st.sb, st.ps
        identf = sb([128, 128], F32); identb = sb([128, 128], BF16)
        S.dma("sp", lambda e: e.dma_start(out=identf[:], in_=self.cst["ident_in"]), writes=["identf"])
        S.op("dve", lambda e: e.tensor_copy(identb[:], identf[:]), reads=["identf"], writes=["identb"])
        kT = sb([128, 16, 256], BF16); v = sb([128, 2, D], BF16); qT = sb([128, 16, 128], BF16)
        pexp = sb([128, 4, 256], BF16); pT = sb([128, 8, 128], BF16); ob = sb([128, D], BF16)
        mx = sb([128, 4], F32); sm = sb([128, 4], F32)
        ps_sc = ps([128, 4, 256], F32); pt = ps([128, 8, 128], BF16); po = ps([128, D], F32)
        SC = 512 ** -0.5
        for (r0, ln, p0, slot) in self.seqs:
            for c in range(16):
                ksrc = self.kvp_bf[:, 0:D] if slot == 0 else self.kv_bf[slot, 0]
                S.dma("sp", lambda e, c=c, ksrc=ksrc: e.dma_start(out=kT[:, c, :], in_=ksrc[:, c * 128:(c + 1) * 128], transpose=True),
                      reads=[("dram", "kv_bf")], writes=[("kT", c)])
            vsrc = self.kvp_bf[:, D:2 * D] if slot == 0 else self.kv_bf[slot, 1]
            S.dma("sp", lambda e, vsrc=vsrc: e.dma_start(out=v[:], in_=vsrc.rearrange("(mc p) d -> p mc d", p=128)), reads=[("dram", "kv_bf")], writes=["v"])
            t = 0
            while t < ln:
                n = min(128, ln - t)
                rr = r0 + t
                for c in range(16):
                    S.dma("sp", lambda e, c=c, rr=rr, n=n: e.dma_start(out=qT[:, c, :n], in_=self.q_bf[rr:rr + n, c * 128:(c + 1) * 128], transpose=True),
                          reads=[("dram", "q_bf")], writes=[("qT", c)])

                def scm(e, n=n):
                    for h in range(4):
                        for dc in range(4):
                            ins = e.matmul(ps_sc[:n, h, :], qT[:, h * 4 + dc, :n], kT[:, h * 4 + dc, :], start=(dc == 0), stop=(dc == 3))
                    return ins
                S.op("pe", scm, reads=[("qT", c) for c in range(16)] + [("kT", c) for c in range(16)], writes=["ps_sc"])
                S.op("dve", lambda e, n=n: e.tensor_reduce(mx[:n, :], ps_sc[:n], AX.X, ALU.max), reads=["ps_sc"], writes=["mx"])
                S.op("dve", lambda e, n=n: e.tensor_scalar(mx[:n, :], mx[:n, :], -SC, None, ALU.mult), reads=["mx"], writes=["mx"])
                for h in range(4):
                    S.op("act", lambda e, n=n, h=h: e.activation(pexp[:n, h, :], ps_sc[:n, h, :], AF.Exp, bias=mx[:n, h:h + 1], scale=SC),
                         reads=["ps_sc", "mx"], writes=[("pexp", h)])
                S.op("dve", lambda e, n=n: e.tensor_reduce(sm[:n, :], pexp[:n], AX.X, ALU.add), reads=[("pexp", h) for h in range(4)], writes=[("sm", h) for h in range(4)])

                def trp(e, n=n):
                    for h in range(4):
                        for mc in range(2):
                            ins = e.transpose(pt[:, h * 2 + mc, :n], pexp[:n, h, mc * 128:(mc + 1) * 128], identb[:n, :n])
                    return ins
                S.op("pe", trp, reads=[("pexp", h) for h in range(4)] + ["identb"], writes=["pt"])
                S.op("act", lambda e, n=n: e.copy(pT[:, :, :n], pt[:, :, :n]), reads=["pt"], writes=["pT"])

                def om(e, n=n):
                    for h in range(4):
                        for mc in range(2):
                            ins = e.matmul(po[:n, h * 512:(h + 1) * 512], pT[:, h * 2 + mc, :n], v[:, mc, h * 512:(h + 1) * 512], start=(mc == 0), stop=(mc == 1))
                    return ins
                S.op("pe", om, reads=["pT", "v"], writes=["po"])
                smk = [("sm", h) for h in range(4)]
                S.op("dve", lambda e, n=n: e.reciprocal(sm[:n, :], sm[:n, :]), reads=smk, writes=smk)
                S.op("dve", lambda e, n=n: e.tensor_tensor(ob[:n, :].rearrange("p (h d) -> p h d", h=4), po[:n, :].rearrange("p (h d) -> p h d", h=4),
                                                         sm[:n, :].unsqueeze(2).to_broadcast([n, 4, 512]), ALU.mult), reads=["po"] + smk, writes=["ob"])
                S.dma("sp", lambda e, rr=rr, n=n: e.dma_start(out=self.att[rr:rr + n, :], in_=ob[:n, :]), reads=["ob"], writes=[("dram", "att")])
                t += n
        st.done()

    def evac_relu2(self):
        S = self.S

        def f(st, n, cw, ps, pb, ob, okey):
            if not hasattr(st, "r2"):
                st.r2 = st.sb([128, 512], F32, "r2")
            r2 = st.r2
            S.op("act", lambda e: e.activation(r2[:n, :cw], ps[:n, :cw], AF.Relu), reads=[("lps", pb)], writes=["r2"])
            S.op("dve", lambda e: e.tensor_tensor(ob[:n, :cw], r2[:n, :cw], r2[:n, :cw], ALU.mult), reads=["r2"], writes=[okey])
        return f

    def conv_out(self, l):
        S = self.S
        for (r0, ln, p0, slot) in self.seqs:
            S.dma("pool", lambda e, p0=p0, ln=ln, slot=slot: e.dma_start(out=self.o_conv[l, slot], in_=self.xpad[p0 + ln:p0 + ln + 3, :]),
                  reads=[("dram", "xpad")], writes=[("dram", "o_conv")])
        S.barrier()
        S.emit()

    def conv_init(self, l):
        S = self.S
        st = self.stage()
        z = st.sb([96, 192], BF16)
        S.op("dve", lambda e: e.memset(z[:], 0.0), writes=["z"])
        S.dma("sp", lambda e: e.dma_start(out=self.xpad[16:19, :].rearrange("r (a f) -> (r a) f", f=192), in_=z[:]), reads=["z"], writes=[("dram", "xpad")])
        for j in range(NS):
            p0 = self.seqs[1 + j][2]
            S.dma("pool", lambda e, j=j, p0=p0: e.dma_start(out=self.xpad[p0:p0 + 3, :], in_=self.st_conv[l, j]), writes=[("dram", "xpad")])
        st.done()

    def mem_kv(self, l):
        S = self.S
        self.rms_stage(self.mem, self.memn, self.sw["norm_gains"][l, 6:7, :], 256)
        self.linear(self.memn, D, self.wb["w_xkv"][l], 2 * D, self.evac_store(self.mkv_f, F32), nrows=256)
        S.dma("sp", lambda e: e.dma_start(out=self.o_mk[l], in_=self.mkv_f[:, 0:D]), reads=[("dram", "mkv_f")], writes=[("dram", "o_mk")])
        S.dma("sp", lambda e: e.dma_start(out=self.o_mv[l], in_=self.mkv_f[:, D:2 * D]), reads=[("dram", "mkv_f")], writes=[("dram", "o_mv")])
        for kv in range(2):
            if kv == 0:
                S.dma("pool", lambda e: e.dma_start(out=self.kvp_bf, in_=self.mkv_f), reads=[("dram", "mkv_f")], writes=[("dram", "kv_bf")])
            src = self.st_mk if kv == 0 else self.st_mv
            for j in range(NS):
                S.dma("pool", lambda e, kv=kv, j=j, src=src: e.dma_start(out=self.kv_bf[1 + j, kv], in_=src[l, j]), writes=[("dram", "kv_bf")])
        S.barrier()
        S.emit()

    def build(self, upto="all"):
        S = self.S
        flags = upto.split(",")
        G = lambda l, i: self.sw["norm_gains"][l, i:i + 1, :]
        self.cast_weights()
        self.init_copy()
        if "castonly" in flags:
            S.barrier(); S.emit(); S.close()
            return self.nc
        self.rms_stage(self.xres, self.hn, G(0, 0), self.NT)
        for l in range(2):
            self.mem_kv(l)
            self.conv_init(l)
            cbs = [(c, 512) for c in range(0, C_DT, 512)] + [(C_DT, 64)] + [(c, 512) for c in range(C_GATE, INC, 512)]
            self.linear(self.hn, D, self.wb["w_in"][l], INC, self.evac_inproj(), colblocks=cbs)
            self.conv_out(l)
            if "noret" not in flags:
                self.retention(l)
            if "nos5" not in flags:
                self.s5(l)
            if "nossd" not in flags:
                self.ssd(l)
            if "mix" in flags:
                break
            self.linear(self.ret_y, D, self.wb["w_ret_o"][l], D, self.evac_store(self.br_ret, F32))
            self.linear(self.s5_y, D, self.wb["w_s5_glu"][l], 2 * D, self.evac_store(self.br_glu, F32))
            self.linear(self.ssd_y, 2 * D, self.wb["w_ssd_out"][l], D, self.evac_store(self.br_ssd, F32))
            self.merge(l)
            self.linear(self.merged, D, self.wb["w_mix_out"][l], D, self.evac_store(self.lin_out, F32))
            self.normadd(self.lin_out, G(l, 1), G(l, 2))
            self.linear(self.hn, D, self.wb["w_xq"][l], D, self.evac_store(self.q_bf, BF16))
            self.attention()
            self.linear(self.att, D, self.wb["w_xo"][l], D, self.evac_store(self.lin_out, F32))
            self.normadd(self.lin_out, G(l, 3), G(l, 4))
            self.linear(self.hn, D, self.wb["w_up"][l], 4 * D, self.evac_store(self.hmlp, BF16, func=self.evac_relu2()))
            self.linear(self.hmlp, 4 * D, self.wb["w_down"][l], D, self.evac_store(self.lin_out, F32))
            self.normadd(self.lin_out, G(l, 5), G(l + 1, 0) if l == 0 else None, final=(l == 1))
            if "l0" in flags:
                break
        if "l0" in flags or "mix" in flags:
            st = self.stage()
            xt = st.sb([128, D], F32)
            for (r, n) in self.tiles():
                S.dma("sp", lambda e, r=r, n=n: e.dma_start(out=xt[:n, :], in_=self.xres[r:r + n, :]), reads=[("dram", "xres")], writes=["xt"])
                S.dma("sp", lambda e, r=r, n=n: e.dma_start(out=self.y[r:r + n, :], in_=xt[:n, :]), reads=["xt"], writes=[("dram", "y")])
            st.done()
        S.barrier()
        S.emit()
        S.close()
        return self.nc


def make_in_maps(inputs, TP, cores):
    consts = host_consts(TP)
    maps = []
    for c in cores:
        b = c % 2
        sl = slice(NS * c, NS * (c + 1))
        m = {}
        m["x_in"] = np.concatenate([inputs["x_prompt"][b, :TP], inputs["x_sample"][sl].reshape(NS * SL, D)], axis=0)
        m["mem_in"] = inputs["mem_prompt"][b]
        m["st_ret"] = inputs["state_ret"][:, sl]
        m["st_s5r"] = inputs["state_s5_re"][:, sl]
        m["st_s5i"] = inputs["state_s5_im"][:, sl]
        m["st_ssd"] = inputs["state_ssd"][:, sl]
        m["st_conv"] = inputs["cache_ssd_conv"][:, sl]
        m["st_mk"] = inputs["cache_mem_k"][:, sl].reshape(2, NS, 256, D)
        m["st_mv"] = inputs["cache_mem_v"][:, sl].reshape(2, NS, 256, D)
        for k in WSHAPES:
            m[k] = inputs[k]
        for k in SMALLW:
            m[k] = inputs[k]
        m.update(consts)
        maps.append({k: np.ascontiguousarray(v, dtype=np.float32) for k, v in m.items()})
    return maps


KERNEL_FLAGS = "all"


def kernel(**inputs):
    TP = 8192
    inputs = {k: np.asarray(v) for k, v in inputs.items()}
    b = Builder(TP)
    nc = b.build(KERNEL_FLAGS)
    cores = list(range(8))
    maps = make_in_maps(inputs, TP, cores)
    res = run_bass_kernel_spmd(nc, maps, core_ids=cores)
    R = res.results
    f32 = np.float32
    y_p = np.stack([np.asarray(R[bb]["y"], f32)[:TP] for bb in range(2)])
    y_s = np.concatenate([np.asarray(R[c]["y"], f32)[TP:].reshape(NS, SL, D) for c in cores], axis=0)

    def pstate(name, shape):
        return np.stack([np.asarray(R[bb][name], f32)[:, 0] for bb in range(2)], axis=1).reshape(shape)

    def sstate(name, shape):
        return np.concatenate([np.asarray(R[c][name], f32)[:, 1:] for c in cores], axis=1).reshape(shape)
    p_ret = pstate("o_ret", (2, 2, RH, RDK, RDV))
    p_s5r = pstate("o_s5r", (2, 2, 128, 64))
    p_s5i = pstate("o_s5i", (2, 2, 128, 64))
    p_ssd = pstate("o_ssd", (2, 2, 64, 64, 128))
    p_conv = pstate("o_conv", (2, 2, 3, 6144))
    p_mk = np.stack([np.asarray(R[bb]["o_mk"], f32) for bb in range(2)], axis=1).reshape(2, 2, 256, 4, 512)
    p_mv = np.stack([np.asarray(R[bb]["o_mv"], f32) for bb in range(2)], axis=1).reshape(2, 2, 256, 4, 512)
    s_ret = sstate("o_ret", (2, 32, RH, RDK, RDV))
    s_s5r = sstate("o_s5r", (2, 32, 128, 64))
    s_s5i = sstate("o_s5i", (2, 32, 128, 64))
    s_ssd = sstate("o_ssd", (2, 32, 64, 64, 128))
    s_conv = sstate("o_conv", (2, 32, 3, 6144))
    return (y_p, y_s, p_ret, p_s5r, p_s5i, p_ssd, p_conv, p_mk, p_mv, s_ret, s_s5r, s_s5i, s_ssd, s_conv)
```
